# Optimizing a Trainium2 kernel written in Bass

```python
import math
import jax, jax.numpy as jnp
from jax import lax
import numpy as np

D_MODEL = 1024
BATCH = 16
SEQ = 256
DEPTH = 2
DEC_BATCH = 2
DEC_SEQ = 2048
PAST_LEN = 512

GRID_W = 64
MIX_W = D_MODEL
HG_W = MIX_W // 2
HG_HEADS = 4
HG_DK = HG_W // HG_HEADS
HG_DV = HG_W // HG_HEADS
HG_CHUNK = 16
S5_W = MIX_W - HG_W
S5_CH = 16
S5_GROUPS = S5_W // S5_CH
S5_P = 64
N_DIR = 2
IN_W = 5 * HG_W + S5_W
D_FF = -(-8 * D_MODEL // (3 * 256)) * 256
EPS = 1e-6

kernel_name = "hymba_hgrn2_s5_prefix_dit_step"


def rmsnorm(x, gain):
    xf = x.astype(jnp.float32)
    y = xf * lax.rsqrt(jnp.mean(xf * xf, axis=-1, keepdims=True) + EPS)
    return (y * gain.astype(jnp.float32)).astype(x.dtype)


def grid_pos_embed(n_tokens, dim):
    rows = n_tokens // GRID_W
    t = jnp.arange(rows * GRID_W)
    r = (t // GRID_W).astype(jnp.float32)
    col = (t % GRID_W).astype(jnp.float32)
    nf = dim // 4
    omega = 1.0 / (10000.0 ** (jnp.arange(nf, dtype=jnp.float32) / nf))
    def enc(p):
        a = p[:, None] * omega[None, :]
        return jnp.concatenate([jnp.sin(a), jnp.cos(a)], axis=-1)
    return jnp.concatenate([enc(r), enc(col)], axis=-1)


def hgrn2_scan(q, k, logf, v, s0):
    B, L = q.shape[0], q.shape[1]
    N = L // HG_CHUNK
    def chunk(a):
        return a.reshape(B, N, HG_CHUNK, HG_HEADS, a.shape[-1]).transpose(0, 3, 1, 2, 4)
    q, k, logf, v = chunk(q), chunk(k), chunk(logf), chunk(v)
    b = jnp.cumsum(logf, axis=3)
    b_last = b[:, :, :, -1:, :]
    causal = jnp.tril(jnp.ones((HG_CHUNK, HG_CHUNK), dtype=bool))
    diff = b[:, :, :, :, None, :] - b[:, :, :, None, :, :]
    decay = jnp.exp(jnp.where(causal[:, :, None], diff, -jnp.inf))
    scores = jnp.einsum('bhnik,bhnjk,bhnijk->bhnij', q, k, decay)
    o_intra = jnp.einsum('bhnij,bhnjv->bhniv', scores, v)
    k_to_end = k * jnp.exp(b_last - b)
    u = jnp.einsum('bhnjk,bhnjv->bhnkv', k_to_end, v)
    g = jnp.exp(b_last[:, :, :, 0, :])
    def step(s, xs):
        g_n, u_n = xs
        return g_n[..., None] * s + u_n, s
    s_final, s_enter = lax.scan(step, s0, (jnp.moveaxis(g, 2, 0), jnp.moveaxis(u, 2, 0)))
    s_enter = jnp.moveaxis(s_enter, 0, 2)
    o_inter = jnp.einsum('bhnik,bhnkv->bhniv', q * jnp.exp(b), s_enter)
    o = (o_intra + o_inter).transpose(0, 2, 3, 1, 4).reshape(B, L, HG_HEADS, HG_DV)
    return o, s_final


def hgrn2_mixer(q, f_fwd, f_bwd, i, g, lb, norm_gain, s0):
    f32 = jnp.float32
    B, L = q.shape[0], q.shape[1]
    def split(a):
        return a.reshape(B, L, HG_HEADS, -1).astype(f32)
    qh = jax.nn.silu(split(q)) * (HG_DK ** -0.5)
    vh = split(i)
    o_sum = None
    finals = []
    for d, f in enumerate((f_fwd, f_bwd)):
        lb_d = lb[d].astype(f32).reshape(HG_HEADS, HG_DK)
        fl = split(f)
        forget = lb_d + (1.0 - lb_d) * jax.nn.sigmoid(fl)
        logf = jnp.log(forget)
        key_in = (1.0 - lb_d) * jax.nn.sigmoid(-fl)
        args = (qh, key_in, logf, vh)
        if d == 1:
            args = tuple(jnp.flip(a, axis=1) for a in args)
        o, sf = hgrn2_scan(*args, s0[:, d].astype(f32))
        if d == 1:
            o = jnp.flip(o, axis=1)
        o_sum = o if o_sum is None else o_sum + o
        finals.append(sf)
    o = o_sum * lax.rsqrt(jnp.mean(o_sum * o_sum, axis=-1, keepdims=True) + EPS) * norm_gain.astype(f32)
    o = o * jax.nn.silu(split(g))
    return o.reshape(B, L, HG_W), jnp.stack(finals, axis=1)


def s5_combine(e1, e2):
    a1r, a1i, b1r, b1i = e1
    a2r, a2i, b2r, b2i = e2
    return (a2r * a1r - a2i * a1i, a2r * a1i + a2i * a1r,
            a2r * b1r - a2i * b1i + b2r, a2r * b1i + a2i * b1r + b2i)


def s5_mixer(u, lam_re, lam_im, log_dt, b_re, b_im, c_re, c_im, d_skip, w_glu, h0):
    f32 = jnp.float32
    B, L = u.shape[0], u.shape[1]
    uf = u.astype(f32).reshape(B, L, S5_GROUPS, S5_CH)
    y = d_skip.astype(f32) * uf
    finals = []
    for d in range(N_DIR):
        lr = jnp.minimum(lam_re[d].astype(f32), -1e-4)
        li = lam_im[d].astype(f32)
        dt = jnp.exp(log_dt[d].astype(f32))[:, None]
        mag = jnp.exp(lr * dt)
        ab_re, ab_im = mag * jnp.cos(li * dt), mag * jnp.sin(li * dt)
        nr, ni = ab_re - 1.0, ab_im
        den = lr * lr + li * li
        z_re, z_im = (nr * lr + ni * li) / den, (ni * lr - nr * li) / den
        br, bi = b_re[d].astype(f32), b_im[d].astype(f32)
        bb_re = z_re[..., None] * br - z_im[..., None] * bi
        bb_im = z_re[..., None] * bi + z_im[..., None] * br
        x_in = jnp.flip(uf, axis=1) if d == 1 else uf
        bu_re = jnp.einsum('gpc,blgc->blgp', bb_re, x_in)
        bu_im = jnp.einsum('gpc,blgc->blgp', bb_im, x_in)
        h0r = h0[:, d, :, :, 0].astype(f32)
        h0i = h0[:, d, :, :, 1].astype(f32)
        bu_re = bu_re.at[:, 0].add(ab_re * h0r - ab_im * h0i)
        bu_im = bu_im.at[:, 0].add(ab_re * h0i + ab_im * h0r)
        a_re = jnp.broadcast_to(ab_re, bu_re.shape)
        a_im = jnp.broadcast_to(ab_im, bu_im.shape)
        _, _, hr, hi = lax.associative_scan(s5_combine, (a_re, a_im, bu_re, bu_im), axis=1)
        yd = jnp.einsum('gcp,blgp->blgc', c_re[d].astype(f32), hr) - jnp.einsum('gcp,blgp->blgc', c_im[d].astype(f32), hi)
        if d == 1:
            yd = jnp.flip(yd, axis=1)
        y = y + yd
        finals.append(jnp.stack([hr[:, -1], hi[:, -1]], axis=-1))
    y = jax.nn.gelu(y.reshape(B, L, S5_W))
    y = y * jax.nn.sigmoid(y @ w_glu.astype(f32))
    return y, jnp.stack(finals, axis=1)


def setup_inputs(seed: int = 0) -> dict:
    key = jax.random.key(seed)
    ks = iter(jax.random.split(key, 40))
    f32 = jnp.float32
    def nrm(shape, scale):
        return jax.random.normal(next(ks), shape, f32) * scale
    inp = {}
    inp["x_prompt"] = nrm((BATCH, SEQ, D_MODEL), 1.0)
    inp["x_sample"] = nrm((DEC_BATCH, DEC_SEQ, D_MODEL), 1.0)
    inp["state_hgrn"] = nrm((DEC_BATCH, DEPTH, N_DIR, HG_HEADS, HG_DK, HG_DV), 0.5)
    inp["state_s5"] = nrm((DEC_BATCH, DEPTH, N_DIR, S5_GROUPS, S5_P, 2), 0.1)
    inp["c"] = nrm((DEC_BATCH, D_MODEL), 1.0)
    inp["c_ctx"] = nrm((D_MODEL,), 1.0)
    inp["w_mod"] = nrm((DEPTH, D_MODEL, 6 * D_MODEL), 0.5 * D_MODEL ** -0.5)
    inp["b_mod"] = nrm((DEPTH, 6 * D_MODEL), 0.02)
    inp["norm_mix"] = 1.0 + nrm((DEPTH, D_MODEL), 0.02)
    inp["norm_ffn"] = 1.0 + nrm((DEPTH, D_MODEL), 0.02)
    inp["norm_final"] = 1.0 + nrm((D_MODEL,), 0.02)
    inp["w_in"] = nrm((DEPTH, D_MODEL, IN_W), D_MODEL ** -0.5)
    inp["w_out"] = nrm((DEPTH, MIX_W, D_MODEL), MIX_W ** -0.5)
    inp["hg_lb_logits"] = nrm((N_DIR, DEPTH, HG_W), 1.0)
    inp["hg_norm"] = 1.0 + nrm((DEPTH, HG_DV), 0.02)
    inp["s5_lam_re"] = -0.5 + nrm((DEPTH, N_DIR, S5_GROUPS, S5_P), 0.01)
    inp["s5_lam_im"] = math.pi * jnp.arange(S5_P, dtype=f32) + nrm((DEPTH, N_DIR, S5_GROUPS, S5_P), 0.01)
    inp["s5_log_dt"] = jax.random.uniform(next(ks), (DEPTH, N_DIR, S5_GROUPS), f32, math.log(1e-3), math.log(1e-1))
    inp["s5_b_re"] = nrm((DEPTH, N_DIR, S5_GROUPS, S5_P, S5_CH), S5_CH ** -0.5)
    inp["s5_b_im"] = nrm((DEPTH, N_DIR, S5_GROUPS, S5_P, S5_CH), S5_CH ** -0.5)
    inp["s5_c_re"] = nrm((DEPTH, N_DIR, S5_GROUPS, S5_CH, S5_P), S5_P ** -0.5)
    inp["s5_c_im"] = nrm((DEPTH, N_DIR, S5_GROUPS, S5_CH, S5_P), S5_P ** -0.5)
    inp["s5_d"] = nrm((DEPTH, S5_GROUPS, S5_CH), 1.0)
    inp["s5_w_glu"] = nrm((DEPTH, S5_W, S5_W), S5_W ** -0.5)
    inp["w_gate"] = nrm((DEPTH, D_MODEL, D_FF), D_MODEL ** -0.5)
    inp["w_up"] = nrm((DEPTH, D_MODEL, D_FF), D_MODEL ** -0.5)
    inp["w_down"] = nrm((DEPTH, D_FF, D_MODEL), D_FF ** -0.5)
    return inp


def reference(x_prompt, x_sample, state_hgrn, state_s5, c, c_ctx, w_mod, b_mod, norm_mix, norm_ffn,
              norm_final, w_in, w_out, hg_lb_logits, hg_norm, s5_lam_re, s5_lam_im, s5_log_dt,
              s5_b_re, s5_b_im, s5_c_re, s5_c_im, s5_d, s5_w_glu, w_gate, w_up, w_down):
    f32 = jnp.float32
    lb_soft = jax.nn.softmax(hg_lb_logits.astype(f32), axis=1)
    lower_bounds = jnp.cumsum(lb_soft, axis=1) - lb_soft[:, :1]

    def layer(x, cond, s_hg0, s_s50, l):
        mod = jax.nn.silu(cond) @ w_mod[l] + b_mod[l]
        sh1, sc1, g1, sh2, sc2, g2 = jnp.split(mod[:, None, :], 6, axis=-1)
        h = rmsnorm(x, norm_mix[l]) * (1.0 + sc1) + sh1
        proj = h @ w_in[l]
        q, ff, fb, iv, og, u = jnp.split(proj, [HG_W, 2 * HG_W, 3 * HG_W, 4 * HG_W, 5 * HG_W], axis=-1)
        o_hg, s_hg = hgrn2_mixer(q, ff, fb, iv, og, lower_bounds[:, l], hg_norm[l], s_hg0)
        o_s5, s_s5 = s5_mixer(u, s5_lam_re[l], s5_lam_im[l], s5_log_dt[l], s5_b_re[l], s5_b_im[l],
                              s5_c_re[l], s5_c_im[l], s5_d[l], s5_w_glu[l], s_s50)
        mix = jnp.concatenate([o_hg, o_s5], axis=-1).astype(x.dtype) @ w_out[l]
        x = x + g1 * mix
        h = rmsnorm(x, norm_ffn[l]) * (1.0 + sc2) + sh2
        x = x + g2 * ((jax.nn.silu(h @ w_gate[l]) * (h @ w_up[l])) @ w_down[l])
        return x, s_hg, s_s5

    bp = x_prompt.shape[0]
    zero_hg = jnp.zeros((bp, N_DIR, HG_HEADS, HG_DK, HG_DV), f32)
    zero_s5 = jnp.zeros((bp, N_DIR, S5_GROUPS, S5_P, 2), f32)
    xc = x_prompt
    hg_states, s5_states = [], []
    for l in range(DEPTH):
        xc, s_hg, s_s5 = layer(xc, c_ctx[None, :], zero_hg, zero_s5, l)
        hg_states.append(s_hg)
        s5_states.append(s_s5)
    y_prompt = rmsnorm(xc, norm_final)
    new_state_hgrn = jnp.stack(hg_states, axis=1)
    new_state_s5 = jnp.stack(s5_states, axis=1)

    xs = x_sample + grid_pos_embed(x_sample.shape[1], D_MODEL).astype(x_sample.dtype)
    for l in range(DEPTH):
        xs, _, _ = layer(xs, c, state_hgrn[:, l], state_s5[:, l], l)
    y_sample = rmsnorm(xs, norm_final)
    return (y_prompt, y_sample, new_state_hgrn, new_state_s5)
```

```python
import math
from contextlib import ExitStack

import numpy as np
import concourse.bass as bass
import concourse.mybir as mybir
from concourse.bass_utils import run_bass_kernel_spmd

F32 = mybir.dt.float32
BF16 = mybir.dt.bfloat16
AF = mybir.ActivationFunctionType
ALU = mybir.AluOpType

ENGS = ("pe", "act", "dve", "pool", "sp")


class Slot:
    __slots__ = ("name", "w", "r", "al")

    def __init__(self, name):
        self.name = name
        self.w = None
        self.r = []
        self.al = [self]


def alias(*slots):
    grp = []
    for s in slots:
        for a in s.al:
            if a not in grp:
                grp.append(a)
    for s in grp:
        s.al = grp


class Op:
    __slots__ = ("eng", "fn", "deps", "dma", "idx", "milestone", "mcount", "dsem", "dval", "inc")


class Prog:
    def __init__(self, nc, n_dma_sems=8, sync_same_engine=True):
        self.nc = nc
        self.ops = []
        self.n_dma_sems = n_dma_sems
        self.sync_same = sync_same_engine

    def add(self, eng, fn, reads=(), writes=(), dma=False, inc=16):
        op = Op()
        op.eng, op.fn, op.dma, op.inc = eng, fn, dma, inc
        op.deps = set()
        op.milestone = False
        op.mcount = 0
        op.dsem = None
        op.dval = 0
        op.idx = len(self.ops)
        for s0 in reads:
            for s in s0.al:
                if s.w is not None:
                    op.deps.add(s.w)
        for s0 in writes:
            for s in s0.al:
                if s.w is not None:
                    op.deps.add(s.w)
                op.deps.update(s.r)
        for s in reads:
            s.r.append(op.idx)
        for s in writes:
            s.w = op.idx
            s.r = []
        op.deps.discard(op.idx)
        self.ops.append(op)
        return op

    def pe(self, fn, reads=(), writes=()):
        return self.add("pe", fn, reads, writes)

    def act(self, fn, reads=(), writes=()):
        return self.add("act", fn, reads, writes)

    def dve(self, fn, reads=(), writes=()):
        return self.add("dve", fn, reads, writes)

    def dma(self, eng, fn, reads=(), writes=(), inc=16):
        return self.add(eng, fn, reads, writes, dma=True, inc=inc)

    def emit(self, stack):
        nc = self.nc
        ops = self.ops
        for op in ops:
            for d in op.deps:
                dop = ops[d]
                if dop.dma:
                    continue
                if dop.eng == op.eng and not op.dma:
                    if dop.eng == "pe" or not self.sync_same:
                        continue
                dop.milestone = True
        cnt = {e: 0 for e in ENGS}
        for op in ops:
            if not op.dma and op.milestone:
                cnt[op.eng] += 1
            op.mcount = cnt[op.eng]
        esem = {e: stack.enter_context(nc.semaphore("s_" + e)) for e in ENGS}
        dsems = {e: None for e in ENGS}
        dcount = {}
        dn = {e: 0 for e in ENGS}
        for op in ops:
            if op.dma:
                if dsems[op.eng] is None:
                    dsems[op.eng] = [stack.enter_context(nc.semaphore("d_%s_%d" % (op.eng, i)))
                                     for i in range(self.n_dma_sems)]
                k = dn[op.eng]
                dn[op.eng] += 1
                op.dsem = (op.eng, k % self.n_dma_sems)
                dcount[op.dsem] = dcount.get(op.dsem, 0) + op.inc
                op.dval = dcount[op.dsem]
        per = {e: [o for o in ops if o.eng == e] for e in ENGS}
        block = stack.enter_context(nc.Block())
        sync_same = self.sync_same

        def make(e):
            def body(eng):
                waited = {}

                def wait(key, sem, val):
                    if waited.get(key, 0) >= val:
                        return
                    waited[key] = val
                    eng.wait_ge(sem, val)

                for op in per[e]:
                    for d in sorted(op.deps):
                        dop = ops[d]
                        if dop.dma:
                            wait(("d",) + dop.dsem, dsems[dop.dsem[0]][dop.dsem[1]], dop.dval)
                        else:
                            if dop.eng == e and not op.dma and (e == "pe" or not sync_same):
                                continue
                            wait(("e", dop.eng), esem[dop.eng], dop.mcount)
                    if op.dma:
                        prev = op.dval - op.inc
                        if prev > 0:
                            wait(("d",) + op.dsem, dsems[op.dsem[0]][op.dsem[1]], prev)
                        ins = op.fn(eng)
                        ins.then_inc(dsems[op.dsem[0]][op.dsem[1]], op.inc)
                    else:
                        ins = op.fn(eng)
                        if op.milestone:
                            ins.then_inc(esem[e], 1)
                if dsems[e] is not None:
                    for i, s in enumerate(dsems[e]):
                        v = dcount.get((e, i), 0)
                        if v:
                            wait(("d", e, i), s, v)
            return body

        block.tensor(make("pe"))
        block.scalar(make("act"))
        block.vector(make("dve"))
        block.gpsimd(make("pool"))
        block.sync(make("sp"))


D = 1024
KD = 8
T = 1024
NT = 8
DEPTH = 2
HG_W = 512
NH = 4
DK = 128
S5_W = 512
NG = 32
SP = 64
IN_W = 3072
DFF = 2816
NF = 22
EPS = 1e-6
CH = 32
NCH = T // CH
SEGS = [(0, 8), (8, 16), (16, 32)]
WSLOT = 4096
N_WSLOT = 4
CCW = 1096
ARENA_B = 90 * 1024 // 2

I32 = mybir.dt.int32
STAGE = {"hg": True, "s5": True, "s5_stop": 99}
DEBUG = False
_DBG = {}


class TT:
    def __init__(self, t, slot):
        self.t = t
        self.s = slot

    def __getitem__(self, k):
        return self.t[k]


def build_program():
    nc = bass.Bass("TRN2", target_bir_lowering=False)
    st = ExitStack()
    P = Prog(nc)

    def din(name, shape):
        return nc.dram_tensor(name, list(shape), F32, kind="ExternalInput").ap()

    def dout(name, shape):
        return nc.dram_tensor(name, list(shape), F32, kind="ExternalOutput").ap()

    x_in = din("x", [T, D])
    cond_in = din("cond", [2, D])
    w_mod = din("w_mod", [DEPTH, D, 6 * D])
    b_mod = din("b_mod", [DEPTH, 6 * D])
    norm_mix = din("norm_mix", [DEPTH, D])
    norm_ffn = din("norm_ffn", [DEPTH, D])
    norm_final = din("norm_final", [D])
    w_in = din("w_in", [DEPTH, D, IN_W])
    w_out = din("w_out", [DEPTH, D, D])
    w_gate = din("w_gate", [DEPTH, D, DFF])
    w_up = din("w_up", [DEPTH, D, DFF])
    w_down = din("w_down", [DEPTH, DFF, D])
    hg_lb = din("hg_lb_logits", [2, DEPTH, HG_W])
    hg_norm = din("hg_norm", [DEPTH, DK])
    st_hg = din("st_hg", [DEPTH, 2, NH, DK, DK])
    cst = din("cst", [128, 512])
    cst2_in = din("cst2", [128, 1024])
    cst3_in = din("cst3", [128, 1024])
    s5_lam_re = din("s5_lam_re", [DEPTH, 2, NG, SP])
    s5_lam_im = din("s5_lam_im", [DEPTH, 2, NG, SP])
    s5_log_dt = din("s5_log_dt", [DEPTH, 2, NG])
    s5_b_re = din("s5_b_re", [DEPTH, 2, NG, SP, 16])
    s5_b_im = din("s5_b_im", [DEPTH, 2, NG, SP, 16])
    s5_c_re = din("s5_c_re", [DEPTH, 2, NG, 16, SP])
    s5_c_im = din("s5_c_im", [DEPTH, 2, NG, 16, SP])
    s5_d = din("s5_d", [DEPTH, NG, 16])
    s5_w_glu = din("s5_w_glu", [DEPTH, S5_W, S5_W])
    st_s5 = din("st_s5", [DEPTH, 2, NG, SP, 2])
    y_out = dout("y", [T, D])
    ns_hg = dout("ns_hg", [2, DEPTH, 2, NH, DK, DK])
    ns_s5 = dout("ns_s5", [2, DEPTH, 2, NG, SP, 2])
    cc5_in = [nc.dram_tensor("cc5_in%d" % i, [128, 64], F32, kind="Internal").ap() for i in range(DEPTH)]
    cc5_out = [nc.dram_tensor("cc5_out%d" % i, [4 * 128, 64], F32, kind="Internal").ap() for i in range(DEPTH)]
    cc_in = [nc.dram_tensor("cc_in%d" % i, [128, CCW], F32, kind="Internal").ap() for i in range(2 * DEPTH)]
    cc_out = [nc.dram_tensor("cc_out%d" % i, [4 * 128, CCW], F32, kind="Internal").ap() for i in range(2 * DEPTH)]

    _n = [0]
    dbg_list = []

    def dbg(name, ap, slot, shape, dtype=F32):
        if not DEBUG:
            return
        t = nc.dram_tensor("dbg_" + name, list(shape), dtype, kind="ExternalOutput").ap()
        P.dma("sp", lambda e: e.dma_start(out=t, in_=ap), reads=[slot], writes=[])

    def sb(shape, dtype, name=None):
        _n[0] += 1
        name = "sb_" + (name or "t%d" % _n[0])
        t = st.enter_context(nc.sbuf_tensor(name, list(shape), dtype))
        return TT(t, Slot(name))

    banks = [TT(st.enter_context(nc.psum_tensor("ps%d" % i, [128, 512], F32)), Slot("ps%d" % i)) for i in range(8)]
    _pb = [0]

    def psum():
        b = banks[_pb[0] % 6]
        _pb[0] += 1
        return b

    wslots = [sb([128, WSLOT], BF16, "wslot%d" % i) for i in range(N_WSLOT)]
    _ws = [0]

    def wload(src_ap, a, b, slot=None):
        if slot is None:
            w = wslots[_ws[0] % N_WSLOT]
            _ws[0] += 1
        else:
            w = wslots[slot]
        view = bass.AP(w.t, 0, [[WSLOT, 128], [b, a], [1, b]])
        P.dma("pool", lambda e, v=view, s=src_ap: e.dma_start(out=v, in_=s), writes=[w.s])
        return view, w.s

    arena_t = st.enter_context(nc.sbuf_tensor("arena", [128, ARENA_B], BF16))
    ar = {"off": 0, "live": []}

    def aalloc(free_shape, dtype, name):
        n = 1
        for v in free_shape:
            n *= v
        nb = n * (1 if dtype == BF16 else 2)
        nb = (nb + 1) // 2 * 2
        off = ar["off"]
        assert off + nb <= ARENA_B, ("arena overflow", name, off, nb)
        ar["off"] = off + nb
        v = arena_t[:, off:off + nb]
        if dtype != BF16:
            v = v.bitcast(dtype)
        if len(free_shape) > 1:
            names = "abcdefg"[:len(free_shape)]
            kw = {names[i]: free_shape[i] for i in range(1, len(free_shape))}
            v = v.rearrange("p (%s) -> p %s" % (" ".join(names), " ".join(names)), **kw)
        s = Slot(name)
        ar["live"].append(s)
        return TT(v, s)

    fence_t = sb([128, 2], F32, "fence")

    def aphase(new_names_hint=None):
        old = ar["live"]
        ar["live"] = []
        ar["off"] = 0
        ar["pending"] = old

    def afence():
        old = ar.get("pending", [])
        new = list(ar["live"])
        P.dve(lambda e: e.memset(fence_t[:], 0.0), reads=[], writes=old + new + [fence_t.s])
        ar["pending"] = []

    cst_f = sb([128, 512], F32, "cst_f")
    P.dma("sp", lambda e: e.dma_start(out=cst_f[:], in_=cst), writes=[cst_f.s])
    ident_f = cst_f[:, 0:128]
    ident_b = sb([128, 128], BF16, "ident_b")
    P.dve(lambda e: e.tensor_copy(out=ident_b[:], in_=cst_f[:, 0:128]), reads=[cst_f.s], writes=[ident_b.s])
    ones_b = sb([128, 128], BF16, "ones_b")
    P.dve(lambda e: e.memset(ones_b[:], 1.0), writes=[ones_b.s])
    zeros_f = sb([128, 128], F32, "zeros_f")
    P.dve(lambda e: e.memset(zeros_f[:], 0.0), writes=[zeros_f.s])
    epsc = sb([128, 2], F32, "epsc")
    P.dve(lambda e: e.memset(epsc[:, 0:1], float(D * EPS)), writes=[epsc.s])
    P.dve(lambda e: e.memset(epsc[:, 1:2], float(DK * EPS)), reads=[epsc.s], writes=[epsc.s])
    lnc = sb([128, 1], F32, "lnc")
    P.dve(lambda e: e.memset(lnc[:], float(math.log(DK ** -0.5))), writes=[lnc.s])

    xT = sb([128, KD, T], F32, "xT")
    xs = [[Slot("xT%d_%d" % (k, h)) for h in range(2)] for k in range(KD)]
    hT = sb([128, KD, T], BF16, "hT")
    hs = [[Slot("hT%d_%d" % (k, h)) for h in range(2)] for k in range(KD)]
    mixT = sb([128, KD, T], BF16, "mixT")
    mixs = [[Slot("mix%d_%d" % (k, h)) for h in range(2)] for k in range(KD)]

    def sin_turns(out, u, ki, kf, ap=lambda t: t[:]):
        P.dve(lambda e: e.tensor_copy(out=ap(ki), in_=ap(u)), reads=[u.s], writes=[ki.s])
        P.dve(lambda e: e.tensor_copy(out=ap(kf), in_=ap(ki)), reads=[ki.s], writes=[kf.s])
        P.dve(lambda e: e.tensor_tensor(out=ap(kf), in0=ap(u), in1=ap(kf), op=ALU.subtract), reads=[u.s, kf.s], writes=[kf.s])
        P.act(lambda e: e.activation(out=ap(out), in_=ap(kf), func=AF.Sin, scale=2 * math.pi), reads=[kf.s], writes=[out.s])

    aphase()
    xtok = [aalloc([D], F32, "xtok%d" % i) for i in range(NT)]
    posarg = aalloc([512], F32, "posarg")
    cst2 = aalloc([1024], F32, "cst2")
    posk_i = aalloc([512], I32, "posk_i")
    posk_f = aalloc([512], F32, "posk_f")
    afence()
    P.dma("sp", lambda e: e.dma_start(out=cst2[:], in_=cst2_in), writes=[cst2.s])
    for tt in range(NT):
        P.dma("sp", lambda e, tt=tt: e.dma_start(out=xtok[tt][:], in_=x_in[tt * 128:(tt + 1) * 128, :]),
              writes=[xtok[tt].s])
    for k in range(KD):
        for h in range(2):
            b = psum()
            for j in range(4):
                tt = h * 4 + j
                P.pe(lambda e, b=b, j=j, tt=tt, k=k: e.transpose(b[:, j * 128:(j + 1) * 128],
                                                                   xtok[tt][:, k * 128:(k + 1) * 128], ident_f),
                     reads=[xtok[tt].s, cst_f.s], writes=[b.s])
            P.act(lambda e, b=b, k=k, h=h: e.activation(out=xT[:, k, h * 512:(h + 1) * 512], in_=b[:], func=AF.Copy),
                  reads=[b.s], writes=[xs[k][h]])

    posi = aalloc_late = None
    for k in range(KD):
        blk = k // 2
        pos_src = cst2[:, 0:512] if blk < 2 else cst2[:, 512:1024]
        om = cst_f[:, 388 + (k % 2):389 + (k % 2)]
        P.dve(lambda e, pos_src=pos_src, om=om: e.tensor_scalar(
            out=posarg[:], in0=pos_src, scalar1=om, scalar2=1.0 / (2 * math.pi), op0=ALU.mult, op1=ALU.mult),
            reads=[cst2.s, cst_f.s], writes=[posarg.s])
        if blk % 2 == 1:
            P.dve(lambda e: e.tensor_scalar(out=posarg[:], in0=posarg[:], scalar1=0.25, scalar2=None, op0=ALU.add),
                  reads=[posarg.s], writes=[posarg.s])
        sin_turns(posarg, posarg, posk_i, posk_f)
        P.dve(lambda e, k=k: e.tensor_tensor(out=xT[:, k, 512:1024], in0=xT[:, k, 512:1024], in1=posarg[:], op=ALU.add),
              reads=[posarg.s, xs[k][1]], writes=[xs[k][1]])

    condf = sb([128, KD, 2], F32, "condf")
    condb = sb([128, KD, 2], BF16, "condb")
    for j in range(2):
        P.dma("sp", lambda e, j=j: e.dma_start(out=condf[:, :, j], in_=cond_in[j].rearrange("(k p) -> p k", p=128)),
              writes=[condf.s])
    P.act(lambda e: e.activation(out=condb[:], in_=condf[:], func=AF.Silu), reads=[condf.s], writes=[condb.s])

    nmix = sb([128, DEPTH, KD], F32, "nmix")
    nffn = sb([128, DEPTH, KD], F32, "nffn")
    nfin = sb([128, KD], F32, "nfin")
    bmod = sb([128, DEPTH, 48], F32, "bmod")
    P.dma("sp", lambda e: e.dma_start(out=nmix[:], in_=norm_mix.rearrange("l (k p) -> p l k", p=128)), writes=[nmix.s])
    P.dma("sp", lambda e: e.dma_start(out=nffn[:], in_=norm_ffn.rearrange("l (k p) -> p l k", p=128)), writes=[nffn.s])
    P.dma("sp", lambda e: e.dma_start(out=nfin[:], in_=norm_final.rearrange("(k p) -> p k", p=128)), writes=[nfin.s])
    P.dma("sp", lambda e: e.dma_start(out=bmod[:], in_=b_mod.rearrange("l (k p) -> p l k", p=128)), writes=[bmod.s])

    lbl = sb([128, 2, DEPTH, NH], F32, "lbl")
    for d in range(2):
        for l in range(DEPTH):
            P.dma("sp", lambda e, d=d, l=l: e.dma_start(out=lbl[:, d, l, :], in_=hg_lb[d, l].rearrange("(h p) -> p h", p=128)),
                  writes=[lbl.s])
    lb = sb([128, DEPTH, 2, NH], F32, "lb")
    oml = sb([128, DEPTH, 2, NH], F32, "oml")
    noml = sb([128, DEPTH, 2, NH], F32, "noml")
    P.dve(lambda e: e.memset(lb[:], 0.0), writes=[lb.s])
    P.dve(lambda e: e.tensor_tensor(out=lb[:, 1], in0=lbl[:, :, 1, :], in1=lbl[:, :, 0, :], op=ALU.subtract),
          reads=[lbl.s, lb.s], writes=[lb.s])
    P.act(lambda e: e.activation(out=lb[:, 1], in_=lb[:, 1], func=AF.Sigmoid), reads=[lb.s], writes=[lb.s])
    P.dve(lambda e: e.tensor_scalar(out=oml[:], in0=lb[:], scalar1=-1.0, scalar2=1.0, op0=ALU.mult, op1=ALU.add),
          reads=[lb.s], writes=[oml.s])
    P.dve(lambda e: e.tensor_scalar(out=noml[:], in0=lb[:], scalar1=1.0, scalar2=-1.0, op0=ALU.mult, op1=ALU.add),
          reads=[lb.s], writes=[noml.s])
    gn = sb([128, DEPTH], F32, "gn")
    P.dma("sp", lambda e: e.dma_start(out=gn[:], in_=hg_norm.rearrange("l p -> p l")), writes=[gn.s])
    P.dve(lambda e: e.tensor_scalar(out=gn[:], in0=gn[:], scalar1=float(math.sqrt(DK)), scalar2=None, op0=ALU.mult),
          reads=[gn.s], writes=[gn.s])

    mod = sb([128, DEPTH, 48, 2], F32, "mod")
    coef = sb([128, DEPTH, 2, KD, 2], F32, "coef")

    def compute_mod(l):
        bk = psum()
        for cc in range(12):
            wv, wsl = wload(w_mod[l][:, cc * 512:(cc + 1) * 512].rearrange("(k p) c -> p k c", p=128), KD, 512)
            for j in range(4):
                ft = cc * 4 + j
                for k in range(KD):
                    P.pe(lambda e, wv=wv, j=j, k=k, ft=ft, bk=bk: e.matmul(
                        bk[:, ft * 2:ft * 2 + 2], wv[:, k, j * 128:(j + 1) * 128], condb[:, k, :],
                        start=(k == 0), stop=(k == KD - 1)),
                        reads=[wsl, condb.s], writes=[bk.s])
        P.dve(lambda e, bk=bk, l=l: e.tensor_tensor(
            out=mod[:, l], in0=bk[:, 0:96].rearrange("p (f c) -> p f c", c=2),
            in1=bmod[:, l].unsqueeze(2).broadcast_to([128, 48, 2]), op=ALU.add),
            reads=[bk.s, bmod.s], writes=[mod.s])
        for n, (gt, base) in enumerate(((nmix, 8), (nffn, 32))):
            P.dve(lambda e, n=n, base=base, l=l: e.tensor_scalar(
                out=coef[:, l, n], in0=mod[:, l, base:base + 8, :], scalar1=1.0, scalar2=32.0,
                op0=ALU.add, op1=ALU.mult), reads=[mod.s], writes=[coef.s])
            P.dve(lambda e, n=n, gt=gt, l=l: e.tensor_tensor(
                out=coef[:, l, n], in0=coef[:, l, n], in1=gt[:, l].unsqueeze(2).broadcast_to([128, KD, 2]),
                op=ALU.mult), reads=[coef.s, gt.s], writes=[coef.s])

    sq = [sb([128, 512], BF16, "sq%d" % i) for i in range(2)]
    rstd = sb([128, 512], F32, "rstd")
    tmpn = [sb([128, 512], F32, "tmpn%d" % i) for i in range(2)]

    def rms_stats(h):
        bk = psum()
        for k in range(KD):
            s = sq[k % 2]
            P.act(lambda e, k=k, h=h, s=s: e.activation(out=s[:], in_=xT[:, k, h * 512:(h + 1) * 512], func=AF.Square),
                  reads=[xs[k][h]], writes=[s.s])
            P.pe(lambda e, k=k, bk=bk, s=s: e.matmul(bk[:], ones_b[:], s[:], start=(k == 0), stop=(k == KD - 1)),
                 reads=[s.s, ones_b.s], writes=[bk.s])
        P.act(lambda e, bk=bk: e.activation(out=rstd[:], in_=bk[:], func=AF.Ln, bias=epsc[:, 0:1]),
              reads=[bk.s, epsc.s], writes=[rstd.s])
        P.act(lambda e: e.activation(out=rstd[:], in_=rstd[:], func=AF.Exp, scale=-0.5), reads=[rstd.s], writes=[rstd.s])

    def norm_mod(l, n, shift_base):
        for h in range(2):
            rms_stats(h)
            for k in range(KD):
                tm = tmpn[k % 2]
                P.dve(lambda e, k=k, h=h, tm=tm: e.tensor_tensor(out=tm[:], in0=xT[:, k, h * 512:(h + 1) * 512],
                                                                 in1=rstd[:], op=ALU.mult),
                      reads=[xs[k][h], rstd.s], writes=[tm.s])
                P.act(lambda e, k=k, h=h, l=l, n=n, tm=tm: e.activation(
                    out=hT[:, k, h * 512:(h + 1) * 512], in_=tm[:], func=AF.Identity,
                    bias=mod[:, l, shift_base + k, h:h + 1], scale=coef[:, l, n, k, h:h + 1]),
                    reads=[tm.s, mod.s, coef.s], writes=[hs[k][h]])

    def residual_proj(wdram, l, gate_base, srcT, src_slots, nk):
        per = WSLOT // 256
        for c4 in range(4):
            wv = []
            for k0 in range(0, nk, per):
                kk = min(per, nk - k0)
                v, s = wload(wdram[k0 * 128:(k0 + kk) * 128, c4 * 256:(c4 + 1) * 256].rearrange("(k p) c -> p k c", p=128),
                             kk, 256)
                wv.append((k0, kk, v, s))
            for j in range(2):
                dt_ = c4 * 2 + j
                for h in range(2):
                    bk = psum()
                    for (k0, kk, v, s) in wv:
                        for k in range(kk):
                            kg = k0 + k
                            P.pe(lambda e, v=v, k=k, j=j, kg=kg, h=h, bk=bk: e.matmul(
                                bk[:], v[:, k, j * 128:(j + 1) * 128], srcT[:, kg, h * 512:(h + 1) * 512],
                                start=(kg == 0), stop=(kg == nk - 1)),
                                reads=[s, src_slots[kg][h]], writes=[bk.s])
                    P.dve(lambda e, bk=bk, dt_=dt_, h=h, l=l: e.scalar_tensor_tensor(
                        out=xT[:, dt_, h * 512:(h + 1) * 512], in0=bk[:], scalar=mod[:, l, gate_base + dt_, h:h + 1],
                        in1=xT[:, dt_, h * 512:(h + 1) * 512], op0=ALU.mult, op1=ALU.add),
                        reads=[bk.s, mod.s, xs[dt_][h]], writes=[xs[dt_][h]])


    def ffn(l):
        aphase()
        h1 = aalloc([NF, T], BF16, "h1")
        sgate = [aalloc([512], F32, "sgate%d" % i) for i in range(2)]
        h1s = [[Slot("h1_%d_%d" % (f, h)) for h in range(2)] for f in range(NF)]
        ar["live"].extend([s for row in h1s for s in row])
        afence()
        it = 0
        for c in range(6):
            ncol = 512 if c < 5 else 256
            vg, sg_ = wload(w_gate[l][:, c * 512:c * 512 + ncol].rearrange("(k p) c -> p k c", p=128), KD, ncol)
            vu, su_ = wload(w_up[l][:, c * 512:c * 512 + ncol].rearrange("(k p) c -> p k c", p=128), KD, ncol)
            for j in range(ncol // 128):
                f = c * 4 + j
                for h in range(2):
                    bg = psum()
                    bu = psum()
                    for k in range(KD):
                        P.pe(lambda e, vg=vg, k=k, j=j, h=h, bg=bg: e.matmul(
                            bg[:], vg[:, k, j * 128:(j + 1) * 128], hT[:, k, h * 512:(h + 1) * 512],
                            start=(k == 0), stop=(k == KD - 1)), reads=[sg_, hs[k][h]], writes=[bg.s])
                    for k in range(KD):
                        P.pe(lambda e, vu=vu, k=k, j=j, h=h, bu=bu: e.matmul(
                            bu[:], vu[:, k, j * 128:(j + 1) * 128], hT[:, k, h * 512:(h + 1) * 512],
                            start=(k == 0), stop=(k == KD - 1)), reads=[su_, hs[k][h]], writes=[bu.s])
                    sgt = sgate[it % 2]
                    it += 1
                    P.act(lambda e, bg=bg, sgt=sgt: e.activation(out=sgt[:], in_=bg[:], func=AF.Silu),
                          reads=[bg.s], writes=[sgt.s])
                    P.dve(lambda e, bu=bu, f=f, h=h, sgt=sgt: e.tensor_tensor(
                        out=h1[:, f, h * 512:(h + 1) * 512], in0=sgt[:], in1=bu[:], op=ALU.mult),
                        reads=[sgt.s, bu.s], writes=[h1s[f][h]])
        residual_proj(w_down[l], l, 40, h1, h1s, NF)

    def proj_feat(wv, wsl, c0, h):
        bk = psum()
        for k in range(KD):
            P.pe(lambda e, k=k, bk=bk: e.matmul(bk[:], wv[:, k, c0:c0 + 128], hT[:, k, h * 512:(h + 1) * 512],
                                                start=(k == 0), stop=(k == KD - 1)),
                 reads=[wsl, hs[k][h]], writes=[bk.s])
        return bk

    def hgrn(l):
        aphase()
        qk = [[[aalloc([T], BF16, "qk%d%d%d" % (h, d, w)) for w in range(2)] for d in range(2)] for h in range(NH)]
        kendT = [[aalloc([NT, DK], BF16, "kendT%d%d" % (h, d)) for d in range(2)] for h in range(NH)]
        V = [aalloc([HG_W], BF16, "V%d" % tt) for tt in range(NT)]
        gch = aalloc([NH * 2, NCH], F32, "gch")
        gch_s = [[Slot("gch%d%d" % (h, d)) for d in range(2)] for h in range(NH)]
        ar["live"].extend([s for row in gch_s for s in row])
        R1 = ar["off"]
        qs = [aalloc([512], F32, "qs%d" % hf) for hf in range(2)]
        rmask = aalloc([512], F32, "rmask")
        tmp = [[aalloc([512], F32, "gt%d_%d" % (i, j)) for j in range(5)] for i in range(2)]
        kend_t = [aalloc([512], BF16, "kend%d" % i) for i in range(2)]
        R1_end = ar["off"]
        afence()
        P.dve(lambda e: e.memset(rmask[:], 1.0), writes=[rmask.s])
        P.dve(lambda e: e.memset(rmask[:, 0:512:CH], 0.0), reads=[rmask.s], writes=[rmask.s])

        wv_iv, ws_iv = None, None

        def load_in(c):
            return wload(w_in[l][:, c * 512:(c + 1) * 512].rearrange("(k p) c -> p k c", p=128), KD, 512)

        wq, wqs = load_in(0)
        wf = [None, None]
        wf[0] = load_in(1)
        wf[1] = load_in(2)
        it = 0
        for h in range(NH):
            for hf in range(2):
                bk = proj_feat(wq, wqs, h * 128, hf)
                P.act(lambda e, bk=bk, hf=hf: e.activation(out=qs[hf][:], in_=bk[:], func=AF.Silu),
                      reads=[bk.s], writes=[qs[hf].s])
            for d in range(2):
                for hf in range(2):
                    t_sig, t_a, t_b, t_c, t_e = tmp[it % 2]
                    ke = kend_t[it % 2]
                    it += 1
                    bk = proj_feat(wf[d][0], wf[d][1], h * 128, hf)
                    lb_ = lb[:, l, d, h:h + 1]
                    oml_ = oml[:, l, d, h:h + 1]
                    noml_ = noml[:, l, d, h:h + 1]
                    P.act(lambda e, bk=bk, t_sig=t_sig: e.activation(out=t_sig[:], in_=bk[:], func=AF.Sigmoid),
                          reads=[bk.s], writes=[t_sig.s])
                    P.dve(lambda e, t_sig=t_sig, t_a=t_a, lb_=lb_, oml_=oml_: e.tensor_scalar(
                        out=t_a[:], in0=t_sig[:], scalar1=oml_, scalar2=lb_, op0=ALU.mult, op1=ALU.add),
                        reads=[t_sig.s, lb.s, oml.s], writes=[t_a.s])
                    P.act(lambda e, t_a=t_a: e.activation(out=t_a[:], in_=t_a[:], func=AF.Ln), reads=[t_a.s], writes=[t_a.s])
                    P.dve(lambda e, t_a=t_a, t_b=t_b: e.tensor_tensor_scan(
                        out=t_b[:], data0=rmask[:], data1=t_a[:], initial=0.0, op0=ALU.mult, op1=ALU.add),
                        reads=[t_a.s, rmask.s], writes=[t_b.s])
                    P.dve(lambda e, t_sig=t_sig, noml_=noml_, oml_=oml_: e.tensor_scalar(
                        out=t_sig[:], in0=t_sig[:], scalar1=noml_, scalar2=oml_, op0=ALU.mult, op1=ALU.add),
                        reads=[t_sig.s, noml.s, oml.s], writes=[t_sig.s])
                    P.act(lambda e, t_b=t_b, h=h, d=d, hf=hf: e.activation(
                        out=gch[:, h * 2 + d, hf * (512 // CH):(hf + 1) * (512 // CH)], in_=t_b[:, CH - 1:512:CH], func=AF.Exp),
                        reads=[t_b.s], writes=[gch_s[h][d]])
                    tb3 = t_b[:].rearrange("p (c j) -> p c j", j=CH)
                    tc3 = t_c[:].rearrange("p (c j) -> p c j", j=CH)
                    ta3 = t_a[:].rearrange("p (c j) -> p c j", j=CH)
                    tot_b = tb3[:, :, CH - 1:CH].broadcast_to([128, 512 // CH, CH])
                    if d == 0:
                        P.dve(lambda e, tc3=tc3, tb3=tb3, tot_b=tot_b: e.tensor_tensor(
                            out=tc3, in0=tb3, in1=tot_b, op=ALU.subtract), reads=[t_b.s], writes=[t_c.s])
                    else:
                        P.dve(lambda e, tc3=tc3, ta3=ta3, tb3=tb3: e.tensor_tensor(
                            out=tc3, in0=ta3, in1=tb3, op=ALU.subtract), reads=[t_a.s, t_b.s], writes=[t_c.s])
                        P.dve(lambda e, tc3=tc3, tb3=tb3, tot_b=tot_b, ta3=ta3: e.tensor_tensor(
                            out=ta3, in0=tc3, in1=tot_b, op=ALU.add), reads=[t_c.s, t_b.s], writes=[t_a.s])
                    beta = t_b if d == 0 else t_a
                    P.act(lambda e, beta=beta, t_e=t_e: e.activation(out=t_e[:], in_=beta[:], func=AF.Exp, bias=lnc[:, 0:1]),
                          reads=[beta.s, lnc.s], writes=[t_e.s])
                    P.dve(lambda e, t_e=t_e, h=h, d=d, hf=hf: e.tensor_tensor(
                        out=qk[h][d][0][:, hf * 512:(hf + 1) * 512], in0=qs[hf][:], in1=t_e[:], op=ALU.mult),
                        reads=[t_e.s, qs[hf].s], writes=[qk[h][d][0].s])
                    P.dve(lambda e, beta=beta, t_e=t_e: e.tensor_scalar(out=t_e[:], in0=beta[:], scalar1=-75.0, scalar2=None, op0=ALU.max),
                          reads=[beta.s], writes=[t_e.s])
                    P.act(lambda e, t_e=t_e: e.activation(out=t_e[:], in_=t_e[:], func=AF.Exp, scale=-1.0),
                          reads=[t_e.s], writes=[t_e.s])
                    P.dve(lambda e, t_e=t_e, t_sig=t_sig, h=h, d=d, hf=hf: e.tensor_tensor(
                        out=qk[h][d][1][:, hf * 512:(hf + 1) * 512], in0=t_sig[:], in1=t_e[:], op=ALU.mult),
                        reads=[t_e.s, t_sig.s], writes=[qk[h][d][1].s])
                    P.act(lambda e, t_c=t_c, t_e=t_e: e.activation(out=t_e[:], in_=t_c[:], func=AF.Exp, scale=-1.0),
                          reads=[t_c.s], writes=[t_e.s])
                    P.dve(lambda e, t_e=t_e, t_sig=t_sig, ke=ke: e.tensor_tensor(
                        out=ke[:], in0=t_sig[:], in1=t_e[:], op=ALU.mult), reads=[t_e.s, t_sig.s], writes=[ke.s])
                    bk2 = psum()
                    for j in range(4):
                        P.pe(lambda e, bk2=bk2, j=j, ke=ke: e.matmul(bk2[:, j * 128:(j + 1) * 128], ke[:, j * 128:(j + 1) * 128],
                                                                     ident_b[:], start=True, stop=True),
                             reads=[ke.s, ident_b.s], writes=[bk2.s])
                    P.act(lambda e, bk2=bk2, h=h, d=d, hf=hf: e.activation(
                        out=kendT[h][d][:, hf * 4:(hf + 1) * 4, :], in_=bk2[:].rearrange("p (j k) -> p j k", k=128), func=AF.Copy),
                        reads=[bk2.s], writes=[kendT[h][d].s])
        wiv, wivs = load_in(3)
        for tt in range(NT):
            bk = psum()
            hf = tt // 4
            for k in range(KD):
                P.pe(lambda e, k=k, bk=bk, tt=tt: e.matmul(bk[:], hT[:, k, tt * 128:(tt + 1) * 128], wiv[:, k, :],
                                                            start=(k == 0), stop=(k == KD - 1)),
                     reads=[wivs, hs[k][hf]], writes=[bk.s])
            P.act(lambda e, bk=bk, tt=tt: e.activation(out=V[tt][:], in_=bk[:], func=AF.Copy), reads=[bk.s], writes=[V[tt].s])

        old_tmp = [t.s for grp in tmp for t in grp] + [k.s for k in kend_t] + [q.s for q in qs] + [rmask.s]
        ar["off"] = R1
        S = [aalloc([DK], F32, "S%d" % i) for i in range(2)]
        Sent = aalloc([NCH, DK], BF16, "Sent")
        o_t = aalloc([512], F32, "o_t")
        on_t = aalloc([512], F32, "on_t")
        sg_t = aalloc([512], F32, "sg_t")
        sq_t = aalloc([512], BF16, "sq_t")
        PT = [aalloc([128], BF16, "PT%d" % i) for i in range(2)]
        gat = aalloc([4, DK], F32, "gat")
        gatG = aalloc([4, 8], F32, "gatG")
        s0t = aalloc([DK], F32, "s0t")
        Pc = [aalloc([DK], F32, "Pc%d" % i) for i in range(2)]
        Sinit = aalloc([2 * NH, DK], F32, "Sinit")
        Sinit_s = [[Slot("Sinit%d%d" % (h, d)) for d in range(2)] for h in range(NH)]
        gtot = aalloc([NH * 2, NCH // 2], F32, "gtot")
        new2 = [t.s for t in S + PT + Pc] + [Sent.s, o_t.s, on_t.s, sg_t.s, sq_t.s, gat.s, gatG.s, s0t.s, Sinit.s, gtot.s] + \
               [s_ for row in Sinit_s for s_ in row]
        ar["live"].extend([s_ for row in Sinit_s for s_ in row])
        P.dve(lambda e: e.memset(fence_t[:], 0.0), reads=[], writes=old_tmp + new2 + [fence_t.s])

        CPT = 128 // CH

        def u_matmul(h, d, c):
            tt, p0 = c // CPT, (c % CPT) * CH
            bk = psum()
            P.pe(lambda e, bk=bk: e.matmul(bk[:, 0:128], kendT[h][d][p0:p0 + CH, tt, :],
                                           V[tt][p0:p0 + CH, h * 128:(h + 1) * 128],
                                           start=True, stop=True, tile_position=(p0, 0)),
                 reads=[kendT[h][d].s, V[tt].s], writes=[bk.s])
            return bk

        def scan_order(c0, c1, d):
            return list(range(c0, c1)) if d == 0 else list(range(c1 - 1, c0 - 1, -1))

        si = [0]
        SC0, SC1 = SEGS[2]

        ci = l * 2
        ccs_in, ccs_out = Slot("ccin"), Slot("ccout")
        for h in range(NH):
            for d in range(2):
                hd = h * 2 + d
                Sx = S[si[0] % 2]
                si[0] += 1
                order = scan_order(SC0, SC1, d)
                for i, c in enumerate(order):
                    bk = u_matmul(h, d, c)
                    if i == 0:
                        P.dve(lambda e, bk=bk, Sx=Sx: e.tensor_copy(out=Sx[:], in_=bk[:, 0:128]), reads=[bk.s], writes=[Sx.s])
                    else:
                        P.dve(lambda e, bk=bk, Sx=Sx, c=c, hd=hd: e.scalar_tensor_tensor(
                            out=Sx[:], in0=Sx[:], scalar=gch[:, hd, c:c + 1], in1=bk[:, 0:128],
                            op0=ALU.mult, op1=ALU.add), reads=[bk.s, Sx.s, gch_s[h][d]], writes=[Sx.s])
                P.dma("sp", lambda e, Sx=Sx, hd=hd: e.dma_start(out=cc_in[ci][:, hd * 128:(hd + 1) * 128], in_=Sx[:]),
                      reads=[Sx.s], writes=[ccs_in])
                P.dve(lambda e, hd=hd: e.tensor_tensor_scan(
                    out=gtot[:, hd, :], data0=gch[:, hd, SC0:SC1], data1=zeros_f[:, 0:SC1 - SC0], initial=1.0,
                    op0=ALU.mult, op1=ALU.add), reads=[gch_s[h][d], zeros_f.s], writes=[gtot.s])
        P.dma("sp", lambda e: e.dma_start(out=cc_in[ci][:, 1024:1032], in_=gtot[:, :, SC1 - SC0 - 1]),
              reads=[gtot.s], writes=[ccs_in])
        P.dma("sp", lambda e: e.dma_start(out=cc_in[ci][:, 1032:CCW], in_=zeros_f[:, 0:CCW - 1032]),
              reads=[zeros_f.s], writes=[ccs_in])
        P.dma("pool", lambda e: e.collective_compute("AllGather", ALU.bypass, replica_groups=[[0, 1, 2, 3], [4, 5, 6, 7]],
                                                     ins=[cc_in[ci]], outs=[cc_out[ci]]),
              reads=[ccs_in], writes=[ccs_out], inc=1)
        ccv = cc_out[ci].rearrange("(r p) c -> p r c", p=128)
        P.dma("sp", lambda e: e.dma_start(out=gatG[:], in_=ccv[:, :, 1024:1032]), reads=[ccs_out], writes=[gatG.s])
        for h in range(NH):
            for d in range(2):
                hd = h * 2 + d
                col = slice(hd * 128, (hd + 1) * 128)
                P.dma("sp", lambda e, col=col: e.dma_start(out=gat[:], in_=ccv[:, :, col]), reads=[ccs_out], writes=[gat.s])
                P.dma("sp", lambda e, d=d, h=h: e.dma_start(out=s0t[:], in_=st_hg[l, d, h]), writes=[s0t.s])
                dst = Sinit[:, hd, :]
                ranks = [0, 1, 2, 3] if d == 0 else [3, 2, 1, 0]
                prev, prev_s = s0t[:], s0t.s
                P.dve(lambda e, dst=dst, prev=prev, r=ranks[0]: e.tensor_scalar(
                    out=dst, in0=prev, scalar1=cst_f[:, 384 + r:385 + r], scalar2=None, op0=ALU.mult),
                    reads=[prev_s, cst_f.s], writes=[Sinit_s[h][d]])
                for i in range(3):
                    r = ranks[i]
                    nxt = Pc[i % 2]
                    P.dve(lambda e, nxt=nxt, prev=prev, r=r, hd=hd: e.scalar_tensor_tensor(
                        out=nxt[:], in0=prev, scalar=gatG[:, r, hd:hd + 1], in1=gat[:, r, :],
                        op0=ALU.mult, op1=ALU.add), reads=[prev_s, gat.s, gatG.s], writes=[nxt.s])
                    rn = ranks[i + 1]
                    P.dve(lambda e, nxt=nxt, dst=dst, rn=rn: e.scalar_tensor_tensor(
                        out=dst, in0=nxt[:], scalar=cst_f[:, 384 + rn:385 + rn], in1=dst, op0=ALU.mult, op1=ALU.add),
                        reads=[nxt.s, cst_f.s, Sinit_s[h][d]], writes=[Sinit_s[h][d]])
                    prev, prev_s = nxt[:], nxt.s

        wg_v, wg_s = load_in(4)
        it2 = 0
        for h in range(NH):
            bo = [banks[6], banks[7]]
            for d in range(2):
                hd = h * 2 + d
                for (c0, c1) in SEGS:
                    Sx = S[si[0] % 2]
                    si[0] += 1
                    order = scan_order(c0, c1, d)
                    is_sample = (c0 == SC0)
                    for i, c in enumerate(order):
                        if i == 0:
                            src = Sinit[:, hd, :] if is_sample else zeros_f[:]
                            src_s = Sinit_s[h][d] if is_sample else zeros_f.s
                        else:
                            src, src_s = Sx[:], Sx.s
                        P.act(lambda e, src=src, c=c: e.activation(out=Sent[:, c, :], in_=src, func=AF.Copy),
                              reads=[src_s], writes=[Sent.s])
                        last = (i == len(order) - 1)
                        if last and is_sample:
                            continue
                        bk = u_matmul(h, d, c)
                        if i == 0 and not is_sample:
                            P.dve(lambda e, bk=bk, Sx=Sx: e.tensor_copy(out=Sx[:], in_=bk[:, 0:128]), reads=[bk.s], writes=[Sx.s])
                        else:
                            P.dve(lambda e, bk=bk, Sx=Sx, src=src, c=c, hd=hd: e.scalar_tensor_tensor(
                                out=Sx[:], in0=src, scalar=gch[:, hd, c:c + 1], in1=bk[:, 0:128],
                                op0=ALU.mult, op1=ALU.add), reads=[bk.s, src_s, gch_s[h][d]], writes=[Sx.s])
                    if not is_sample:
                        seq = 0 if c0 == 0 else 1
                        P.dma("sp", lambda e, Sx=Sx, seq=seq, d=d, h=h: e.dma_start(out=ns_hg[seq, l, d, h], in_=Sx[:]),
                              reads=[Sx.s], writes=[])
                for hf in range(2):
                    for j in range(4):
                        tt = hf * 4 + j
                        tok = slice(tt * 128, (tt + 1) * 128)
                        bs = psum()
                        P.pe(lambda e, bs=bs, tok=tok, d=d, h=h: e.matmul(bs[:, 0:128], qk[h][d][1][:, tok], qk[h][d][0][:, tok],
                                                                    start=True, stop=True),
                             reads=[qk[h][d][0].s, qk[h][d][1].s], writes=[bs.s])
                        pt = PT[it2 % 2]
                        it2 += 1
                        P.dve(lambda e, bs=bs, pt=pt, d=d: e.tensor_tensor(
                            out=pt[:], in0=bs[:, 0:128], in1=cst_f[:, 128 + d * 128:256 + d * 128], op=ALU.mult),
                            reads=[bs.s, cst_f.s], writes=[pt.s])
                        oc = slice(j * 128, (j + 1) * 128)
                        P.pe(lambda e, pt=pt, tt=tt, oc=oc, hf=hf, d=d, j=j, h=h: e.matmul(
                            bo[hf][:, oc], V[tt][:, h * 128:(h + 1) * 128], pt[:], start=(d == 0 and j == 0), stop=False),
                            reads=[V[tt].s, pt.s], writes=[bo[hf].s])
                        for sub in range(CPT):
                            c = tt * CPT + sub
                            cs = slice(j * 128 + sub * CH, j * 128 + (sub + 1) * CH)
                            ts = slice(tt * 128 + sub * CH, tt * 128 + (sub + 1) * CH)
                            P.pe(lambda e, c=c, cs=cs, ts=ts, hf=hf, d=d, h=h: e.matmul(
                                bo[hf][:, cs], Sent[:, c, :], qk[h][d][0][:, ts], start=False, stop=(d == 1)),
                                reads=[Sent.s, qk[h][d][0].s], writes=[bo[hf].s])
            for hf in range(2):
                P.act(lambda e, hf=hf: e.activation(out=o_t[:], in_=bo[hf][:], func=AF.Copy), reads=[bo[hf].s], writes=[o_t.s])
                P.act(lambda e, hf=hf: e.activation(out=sq_t[:], in_=bo[hf][:], func=AF.Square), reads=[bo[hf].s], writes=[sq_t.s])
                if l == 0 and h == 0 and hf == 0:
                    dbg("o", o_t[:], o_t.s, [128, 512])
                    dbg("qf", qk[0][0][0][:], qk[0][0][0].s, [128, T], BF16)
                    dbg("kf", qk[0][0][1][:], qk[0][0][1].s, [128, T], BF16)
                    dbg("qb", qk[0][1][0][:], qk[0][1][0].s, [128, T], BF16)
                    dbg("kb", qk[0][1][1][:], qk[0][1][1].s, [128, T], BF16)
                br = psum()
                P.pe(lambda e, br=br: e.matmul(br[:], ones_b[:], sq_t[:], start=True, stop=True),
                     reads=[sq_t.s, ones_b.s], writes=[br.s])
                P.act(lambda e, br=br: e.activation(out=on_t[:], in_=br[:], func=AF.Ln, bias=epsc[:, 1:2]),
                      reads=[br.s, epsc.s], writes=[on_t.s])
                P.act(lambda e: e.activation(out=on_t[:], in_=on_t[:], func=AF.Exp, scale=-0.5), reads=[on_t.s], writes=[on_t.s])
                P.dve(lambda e: e.tensor_tensor(out=on_t[:], in0=o_t[:], in1=on_t[:], op=ALU.mult),
                      reads=[o_t.s, on_t.s], writes=[on_t.s])
                bg = proj_feat(wg_v, wg_s, h * 128, hf)
                P.act(lambda e, bg=bg: e.activation(out=sg_t[:], in_=bg[:], func=AF.Silu), reads=[bg.s], writes=[sg_t.s])
                P.dve(lambda e, hf=hf, h=h: e.scalar_tensor_tensor(
                    out=mixT[:, h, hf * 512:(hf + 1) * 512], in0=on_t[:], scalar=gn[:, l:l + 1], in1=sg_t[:],
                    op0=ALU.mult, op1=ALU.mult), reads=[on_t.s, sg_t.s, gn.s], writes=[mixs[h][hf]])

    def TTop(out, in0, in1, op, reads, writes):
        return P.dve(lambda e: e.tensor_tensor(out=out, in0=in0, in1=in1, op=op), reads=reads, writes=writes)

    def TSop(out, in0, s1, s2, op0, op1, reads, writes):
        if s2 is None:
            return P.dve(lambda e: e.tensor_scalar(out=out, in0=in0, scalar1=s1, scalar2=None, op0=op0), reads=reads, writes=writes)
        return P.dve(lambda e: e.tensor_scalar(out=out, in0=in0, scalar1=s1, scalar2=s2, op0=op0, op1=op1), reads=reads, writes=writes)

    def STTop(out, in0, scalar, in1, op0, op1, reads, writes):
        return P.dve(lambda e: e.scalar_tensor_tensor(out=out, in0=in0, scalar=scalar, in1=in1, op0=op0, op1=op1),
                     reads=reads, writes=writes)

    def ACTop(out, in_, func, reads, writes, bias=None, scale=None):
        kw = {}
        if bias is not None:
            kw["bias"] = bias
        if scale is not None:
            kw["scale"] = scale
        return P.act(lambda e: e.activation(out=out, in_=in_, func=func, **kw), reads=reads, writes=writes)

    def MM(out, lhsT, rhs, start, stop, reads, writes, tp=None):
        if tp is None:
            return P.pe(lambda e: e.matmul(out, lhsT, rhs, start=start, stop=stop), reads=reads, writes=writes)
        return P.pe(lambda e: e.matmul(out, lhsT, rhs, start=start, stop=stop, tile_position=tp), reads=reads, writes=writes)

    def CPY(out, in_, reads, writes):
        return P.dve(lambda e: e.tensor_copy(out=out, in_=in_), reads=reads, writes=writes)

    def MSET(out, val, reads, writes):
        return P.dve(lambda e: e.memset(out, val), reads=reads, writes=writes)

    def SDMA(out, in_, reads, writes):
        return P.dma("sp", lambda e: e.dma_start(out=out, in_=in_), reads=reads, writes=writes)

    NCK = 128
    SEG8 = [(0, 32), (32, 64), (64, 128)]
    TWO_PI = 2.0 * math.pi

    def s5(l):
        aphase()
        c3 = aalloc([1024], F32, "c3")
        asel = aalloc([8, 240], BF16, "asel")
        UT = aalloc([NG, NCK], BF16, "UT")
        Hb = aalloc([2, 2, 16, NCK], BF16, "Hb")
        par = aalloc([2, 5, 16], F32, "par")
        tab = aalloc([3, 16, 65], F32, "tab")
        hin = aalloc([2, 16, 2], F32, "hin")
        sloc = aalloc([2, 2, 16], F32, "sloc")
        hent = aalloc([2, 2, 16], F32, "hent")
        fst = aalloc([2, 2, 16, 2], F32, "fst")
        dsk = aalloc([NG], F32, "dsk")
        gat5 = aalloc([4, 2, 2, 16], F32, "gat5")
        sm = [aalloc([16], F32, "sm%d" % i) for i in range(8)]
        W0 = ar["off"]
        Bt = aalloc([NG, 2, 64], BF16, "Bt")
        bb = aalloc([2, 2, 16, 16], F32, "bb")
        craw = aalloc([2, 2, 16, 16], F32, "craw")
        pw = aalloc([2, 2, 16, 17], F32, "pw")
        R2 = ar["off"]
        uT = aalloc([4, T], BF16, "uT")
        cnat = aalloc([16, 64], F32, "cnat")
        prs = [aalloc([16, 17], F32, "prs%d" % i) for i in range(3)]
        pri = aalloc([16, 17], I32, "pri")
        afence()
        CtS = [wslots[1], wslots[2]]
        CtV = [bass.AP(w.t, 0, [[WSLOT, 128], [512, 8], [256, 2], [128, 2], [1, 128]]) for w in CtS]
        DtS = wslots[3]
        DtV = bass.AP(DtS.t, 0, [[WSLOT, 128], [128, NG], [1, 128]])

        def Ct_(gp):
            return CtV[gp // 8], gp % 8, CtS[gp // 8].s

        def _stop(k):
            if STAGE["s5_stop"] <= k:
                for kk in range(4, 8):
                    for hh in range(2):
                        MSET(mixT[:, kk, hh * 512:(hh + 1) * 512], 0.0, [], [mixs[kk][hh]])
                return True
            return False

        SDMA(c3[:], cst3_in, [], [c3.s])
        for g8 in range(8):
            TSop(asel[:, g8, :], c3[:, 0:240], c3[:, 240 + g8:241 + g8], None, ALU.mult, None, [c3.s], [asel.s])
        EV = c3[:, 608:625]
        K8 = c3[:, 640:705]
        R_even = c3[:, 480:544]
        R_odd = c3[:, 544:608]
        M5 = c3[:, 768:1024]
        for d in range(2):
            SDMA(par[:, d, 0, :], s5_lam_re[l, d].rearrange("(gp g2) p -> (g2 p) gp", g2=2), [], [par.s])
            SDMA(par[:, d, 1, :], s5_lam_im[l, d].rearrange("(gp g2) p -> (g2 p) gp", g2=2), [], [par.s])
            for g2 in range(2):
                SDMA(par[64 * g2:64 * g2 + 64, d, 2, :],
                     s5_log_dt[l, d].rearrange("(gp g2) -> g2 gp", g2=2)[g2].partition_broadcast(64), [], [par.s])
            for ri, src in enumerate((s5_b_re, s5_b_im)):
                SDMA(bb[:, d, ri], src[l, d].rearrange("(gp g2) p c -> (g2 p) gp c", g2=2), [], [bb.s])
            SDMA(hin[:, d], bass.AP(st_s5.tensor, st_s5[l, d].offset, [[2, 128], [256, 16], [1, 2]]), [], [hin.s])
        for s_ in range(8):
            SDMA(dsk[16 * s_:16 * s_ + 16, :], s5_d[l].rearrange("g c -> c g"), [], [dsk.s])
        for d in range(2):
            for ri, src in enumerate((s5_c_re, s5_c_im)):
                x0 = (d * 2 + ri) * 4
                SDMA(cnat[:, x0:x0 + 4, :], src[l, d].rearrange("(ct g8) c p -> (g8 c) ct p", g8=8), [], [cnat.s])
        for d in range(2):
            for ri in range(2):
                bk = psum()
                for ct in range(4):
                    x = (d * 2 + ri) * 4 + ct
                    MM(bk[0:64, ct * 64:(ct + 1) * 64], cnat[:, x, :], R_even, True, True, [cnat.s, c3.s], [bk.s], tp=(0, 0))
                    MM(bk[64:128, ct * 64:(ct + 1) * 64], cnat[:, x, :], R_odd, True, True, [cnat.s, c3.s], [bk.s], tp=(0, 64))
                ACTop(craw[:, d, ri].rearrange("p a b -> p (a b)"), bk[:, 0:256], AF.Copy, [bk.s], [craw.s])

        if _stop(1):
            return
        for d in range(2):
            lr, li, dt_, a_, th_ = (par[:, d, i, :] for i in range(5))
            TSop(lr, lr, -1e-4, None, ALU.min, None, [par.s], [par.s])
            ACTop(dt_, dt_, AF.Exp, [par.s], [par.s])
            TTop(a_, lr, dt_, ALU.mult, [par.s], [par.s])
            TTop(th_, li, dt_, ALU.mult, [par.s], [par.s])
            TSop(th_, th_, 1.0 / TWO_PI, None, ALU.mult, None, [par.s], [par.s])

        def powers(out_r, out_i, out_m, a_ap, th_ap, evals, ng_, ne, tr, ti_, tk_i, tk_f, rs, ws):
            sh = [128, ng_, ne]
            ev_b = evals.unsqueeze(1).broadcast_to(sh)
            TTop(tr, th_ap.unsqueeze(2).broadcast_to(sh), ev_b, ALU.mult, rs + ws, ws)
            TSop(ti_, tr, 0.25, None, ALU.add, None, ws, ws)
            for (dst, src) in ((out_i, tr), (out_r, ti_)):
                CPY(tk_i, src, ws, ws)
                CPY(tk_f, tk_i, ws, ws)
                TTop(tk_f, src, tk_f, ALU.subtract, ws, ws)
                ACTop(dst, tk_f, AF.Sin, ws, ws, scale=TWO_PI)
            TTop(tr, a_ap.unsqueeze(2).broadcast_to(sh), ev_b, ALU.mult, rs + ws, ws)
            ACTop(out_m, tr, AF.Exp, ws, ws)

        for d in range(2):
            ws = [pw.s, pri.s] + [p_.s for p_ in prs]
            tkf_ = cnat[:].rearrange("p a b -> p (a b)")[:, 0:272].rearrange("p (a b) -> p a b", b=17)
            powers(pw[:, d, 0], pw[:, d, 1], prs[2][:], par[:, d, 3, :], par[:, d, 4, :], EV, 16, 17, prs[0][:], prs[1][:],
                   pri[:], tkf_, [par.s, c3.s, craw.s], ws + [cnat.s])
            TTop(pw[:, d, 0], pw[:, d, 0], prs[2][:], ALU.mult, ws, ws)
            TTop(pw[:, d, 1], pw[:, d, 1], prs[2][:], ALU.mult, ws, ws)

        for d in range(2):
            lr, li = par[:, d, 0, :], par[:, d, 1, :]
            abr, abi = pw[:, d, 0, :, 9], pw[:, d, 1, :, 9]
            nr, den, zr, zi, t1, t2 = (sm[i][:] for i in range(6))
            ws = [s_.s for s_ in sm]
            rs = [par.s, pw.s] + ws
            TSop(nr, abr, -1.0, None, ALU.add, None, rs, ws)
            TTop(t1, lr, lr, ALU.mult, rs, ws)
            TTop(t2, li, li, ALU.mult, rs, ws)
            TTop(den, t1, t2, ALU.add, rs, ws)
            P.dve(lambda e, den=den: e.reciprocal(out=den, in_=den), reads=rs, writes=ws)
            TTop(t1, nr, lr, ALU.mult, rs, ws)
            TTop(t2, abi, li, ALU.mult, rs, ws)
            TTop(zr, t1, t2, ALU.add, rs, ws)
            TTop(zr, zr, den, ALU.mult, rs, ws)
            TTop(t1, abi, lr, ALU.mult, rs, ws)
            TTop(t2, nr, li, ALU.mult, rs, ws)
            TTop(zi, t1, t2, ALU.subtract, rs, ws)
            TTop(zi, zi, den, ALU.mult, rs, ws)
            zrb = zr.unsqueeze(2).broadcast_to([128, 16, 16])
            zib = zi.unsqueeze(2).broadcast_to([128, 16, 16])
            cf = cnat[:].rearrange("p a b -> p (a b)")
            t3 = cf[:, 0:256].rearrange("p (a b) -> p a b", b=16)
            t4 = cf[:, 256:512].rearrange("p (a b) -> p a b", b=16)
            t5 = cf[:, 512:768].rearrange("p (a b) -> p a b", b=16)
            br_, bi_ = bb[:, d, 0], bb[:, d, 1]
            rs2 = rs + [bb.s, cnat.s, craw.s]
            ws2 = [bb.s, cnat.s]
            TTop(t3, br_, zrb, ALU.mult, rs2, ws2)
            TTop(t4, bi_, zib, ALU.mult, rs2, ws2)
            TTop(t5, br_, zib, ALU.mult, rs2, ws2)
            TTop(t3, t3, t4, ALU.subtract, rs2, ws2)
            TTop(t4, bi_, zrb, ALU.mult, rs2, ws2)
            TTop(bi_, t4, t5, ALU.add, rs2, ws2)
            CPY(br_, t3, rs2, ws2)

        if l == 0:
            dbg("par", par[:].rearrange("p a b c -> p (a b c)"), par.s, [128, 160])
            dbg("bb", bb[:].rearrange("p a b c d -> p (a b c d)"), bb.s, [128, 1024])
            dbg("pw", pw[:].rearrange("p a b c d -> p (a b c d)"), pw.s, [128, 1088])
            dbg("craw", craw[:].rearrange("p a b c d -> p (a b c d)"), craw.s, [128, 1024])
        if _stop(2):
            return
        old_r2 = [uT.s, cnat.s, pri.s] + [p_.s for p_ in prs]
        ar["off"] = R2
        uT = aalloc([4, T], BF16, "uT2")
        LA = aalloc([4, 128], F32, "LA")
        LB = aalloc([4, 128], F32, "LB")
        mnat = [aalloc([4, 128], BF16, "mnat%d" % i) for i in range(2)]
        Dacc = aalloc([8, 128], F32, "Dacc")
        new_r2 = [uT.s, LA.s, LB.s, mnat[0].s, mnat[1].s, Dacc.s]
        P.dve(lambda e: e.memset(fence_t[:], 0.0), reads=[], writes=old_r2 + new_r2 + [fence_t.s])

        def lifted(dst_r, dst_i, coef_r, coef_i, d, e_idx, conj_sign, gp0, ws):
            sh = [128, 4, 8, 16]
            pr = pw[:, d, 0, gp0:gp0 + 4, e_idx].unsqueeze(3).broadcast_to(sh)
            pi_ = pw[:, d, 1, gp0:gp0 + 4, e_idx].unsqueeze(3).broadcast_to(sh)
            cr = coef_r[:, gp0:gp0 + 4, :].unsqueeze(2).broadcast_to(sh)
            ci = coef_i[:, gp0:gp0 + 4, :].unsqueeze(2).broadcast_to(sh)
            t1 = LA[:].rearrange("p a (j c) -> p a j c", c=16)
            t2 = LB[:].rearrange("p a (j c) -> p a j c", c=16)
            rs = [pw.s, bb.s, craw.s, LA.s, LB.s]
            TTop(t1, cr, pr, ALU.mult, rs, [LA.s])
            TTop(t2, ci, pi_, ALU.mult, rs, [LB.s])
            TTop(dst_r.rearrange("p a (j c) -> p a j c", c=16), t1, t2, ALU.subtract, rs, ws)
            TTop(t1, cr, pi_, ALU.mult, rs, [LA.s])
            TTop(t2, ci, pr, ALU.mult, rs, [LB.s])
            if conj_sign > 0:
                TTop(dst_i.rearrange("p a (j c) -> p a j c", c=16), t1, t2, ALU.add, rs, ws)
            else:
                STTop(dst_i.rearrange("p a (j c) -> p a j c", c=16), t1, -1.0, t2, ALU.mult, ALU.subtract, rs, ws)

        E_B = [slice(15, 7, -1), slice(8, 16)]
        E_C = [slice(9, 17), slice(16, 8, -1)]
        E_N = [slice(7, None, -1), slice(0, 8)]

        for d in range(2):
            for q in range(4):
                gp0 = q * 4
                cv, g8_, cs_ = Ct_(gp0)
                lifted(cv[:, g8_:g8_ + 4, d, 0, :], cv[:, g8_:g8_ + 4, d, 1, :], craw[:, d, 0], craw[:, d, 1], d, E_C[d], -1, gp0, [cs_])
        for q in range(4):
            gp0 = q * 4
            for d in range(2):
                lifted(mnat[0][:], mnat[1][:], bb[:, d, 0], bb[:, d, 1], d, E_N[d], +1, gp0, [mnat[0].s, mnat[1].s])
                for gi in range(8):
                    gl, g2 = gi // 2, gi % 2
                    gp = gp0 + gl
                    cv, g8_, cs_ = Ct_(gp)
                    bk = psum()
                    for ri in range(2):
                        MM(bk[:, 0:128], mnat[ri][64 * g2:64 * g2 + 64, gl, :], cv[64 * g2:64 * g2 + 64, g8_, d, ri, :],
                           (ri == 0), (ri == 1), [mnat[ri].s, cs_], [bk.s])
                    msk = M5[:, d * 128:(d + 1) * 128]
                    if d == 0:
                        TTop(Dacc[:, gi, :], bk[:, 0:128], msk, ALU.mult, [bk.s, c3.s], [Dacc.s])
                    else:
                        tmpv = LA[:, gl, :] if g2 == 0 else LB[:, gl, :]
                        tmps = LA.s if g2 == 0 else LB.s
                        TTop(tmpv, bk[:, 0:128], msk, ALU.mult, [bk.s, c3.s, mnat[0].s, mnat[1].s], [tmps])
                        TTop(Dacc[:, gi, :], Dacc[:, gi, :], tmpv, ALU.add, [tmps, Dacc.s], [Dacc.s])
                        g = 2 * gp + g2
                        STTop(DtV[:, g, :], ident_f, dsk[:, g:g + 1], Dacc[:, gi, :], ALU.mult, ALU.add,
                              [cst_f.s, dsk.s, Dacc.s], [DtS.s])

        if _stop(3):
            return
        wu, wus = wload(w_in[l][:, 2560:3072].rearrange("(k p) c -> p k c", p=128), KD, 512, slot=0)
        for ct in range(4):
            for hf in range(2):
                bk = proj_feat(wu, wus, ct * 128, hf)
                ACTop(uT[:, ct, hf * 512:(hf + 1) * 512], bk[:], AF.Copy, [bk.s], [uT.s])
        for g0 in range(0, NG, 4):
            bk = psum()
            for gi in range(4):
                g = g0 + gi
                ct, g8 = g // 8, g % 8
                for s_ in range(8):
                    MM(bk[:, gi * 128:(gi + 1) * 128], asel[:, g8, 112 - 16 * s_:240 - 16 * s_],
                       uT[:, ct, s_:T:8], (gi == 0 and s_ == 0), (s_ == 7), [asel.s, uT.s], [bk.s])
            ACTop(UT[:, g0:g0 + 4, :], bk[:].rearrange("p (g n) -> p g n", n=128), AF.Copy, [bk.s], [UT.s])

        if _stop(4):
            return
        old_r2 = new_r2
        ar["off"] = R2
        XR = aalloc([4, NCK], F32, "XR")
        XI = aalloc([4, NCK], F32, "XI")
        A1 = aalloc([4, NCK], F32, "A1")
        B2 = aalloc([4, NCK], F32, "B2")
        C2 = aalloc([4, NCK], F32, "C2")
        RC = aalloc([4, NCK], F32, "RC")
        LA = aalloc([4, 128], F32, "LA2")
        LB = aalloc([4, 128], F32, "LB2")
        mnat = [aalloc([4, 128], BF16, "mnat2_%d" % i) for i in range(2)]
        new_r2 = [XR.s, XI.s, A1.s, B2.s, C2.s, RC.s, LA.s, LB.s, mnat[0].s, mnat[1].s]
        tsc = [A1, B2, C2]
        tsi = RC
        P.dve(lambda e: e.memset(fence_t[:], 0.0), reads=[], writes=old_r2 + new_r2 + [fence_t.s])

        def tables(d, tsc, tsi):
            ws = [tab.s, tsi.s] + [t_.s for t_ in tsc]
            for q in range(4):
                g_ = slice(q * 4, q * 4 + 4)
                powers(tab[:, 0, g_, :], tab[:, 1, g_, :], tab[:, 2, g_, :], par[:, d, 3, g_], par[:, d, 4, g_], K8, 4, 65,
                       tsc[0][:, :, 0:65], tsc[1][:, :, 0:65], tsi[:, :, 0:65].bitcast(I32), tsc[2][:, :, 0:65], [par.s, c3.s], ws)

        def seg_views(buf, gsl, n0, n1, d, shift):
            if d == 0:
                if shift == 0:
                    return buf[:, gsl, n0:n1]
                return buf[:, gsl, n0 + 1:n1] if shift > 0 else buf[:, gsl, n0:n1 - 1]
            lo = None if n0 == 0 else n0 - 1
            if shift == 0:
                return buf[:, gsl, n1 - 1:lo:-1]
            if shift > 0:
                return buf[:, gsl, n1 - 2:lo:-1]
            return buf[:, gsl, n1 - 1:n0:-1]

        for d in range(2):
            tables(d, tsc, tsi)
            if l == 0 and d == 0:
                dbg("tab", tab[:].rearrange("p a b c -> p (a b c)"), tab.s, [128, 3 * 16 * 65])
            if _stop(4.2):
                return
            for q in range(4):
                gp0 = q * 4
                tsl = slice(gp0, gp0 + 4)
                lifted(mnat[0][:], mnat[1][:], bb[:, d, 0], bb[:, d, 1], d, E_B[d], +1, gp0, [mnat[0].s, mnat[1].s])
                for ri in range(2):
                    bk = psum()
                    for gl in range(4):
                        MM(bk[:, gl * 128:(gl + 1) * 128], mnat[ri][:, gl, :], ident_b[:], True, True,
                           [mnat[ri].s, ident_b.s], [bk.s])
                    ACTop(Bt[:, 2 * gp0:2 * gp0 + 8, ri, :], bk[:].rearrange("p (g q) -> p g q", q=64), AF.Copy, [bk.s], [Bt.s])
                if _stop(4.3):
                    return
                for gl in range(4):
                    gp = gp0 + gl
                    bk = psum()
                    for g2 in range(2):
                        g = 2 * gp + g2
                        for ri in range(2):
                            MM(bk[64 * g2:64 * g2 + 64, ri * 128:(ri + 1) * 128], Bt[:, g, ri, :], UT[:, g, :], True, True,
                               [Bt.s, UT.s], [bk.s], tp=(0, 64 * g2))
                    ACTop(XR[:, gl, :], bk[:, 0:128], AF.Copy, [bk.s], [XR.s])
                    ACTop(XI[:, gl, :], bk[:, 128:256], AF.Copy, [bk.s], [XI.s])
                if l == 0 and d == 0:
                    dbg("XR%d" % q, XR[:].rearrange("p a b -> p (a b)"), XR.s, [128, 512])
                    dbg("Bt%d" % q, Bt[:, 2 * gp0:2 * gp0 + 8].rearrange("p a b c -> p (a b c)"), Bt.s, [128, 1024], BF16)
                    dbg("UT%d" % q, UT[:, 2 * gp0:2 * gp0 + 8].rearrange("p a b -> p (a b)"), UT.s, [128, 1024], BF16)
                if _stop(4.4):
                    return
                CPY(RC[:], tab[:, 2, tsl, 1:2].broadcast_to([128, 4, NCK]), [tab.s], [RC.s])
                for (n0, n1) in SEG8:
                    first = n0 if d == 0 else n1 - 1
                    MSET(RC[:, :, first:first + 1], 0.0, [RC.s], [RC.s])
                gsl = slice(0, 4)
                for (n0, n1) in SEG8:
                    L = n1 - n0
                    xr, xi = seg_views(XR, gsl, n0, n1, d, 0), seg_views(XI, gsl, n0, n1, d, 0)
                    a1, b2, c2 = seg_views(A1, gsl, n0, n1, d, 0), seg_views(B2, gsl, n0, n1, d, 0), seg_views(C2, gsl, n0, n1, d, 0)
                    cs_, sn_ = tab[:, 0, tsl, 1:L + 1], tab[:, 1, tsl, 1:L + 1]
                    rs = [XR.s, XI.s, tab.s, A1.s, B2.s, C2.s]
                    TTop(a1, xr, cs_, ALU.mult, rs, [A1.s])
                    TTop(c2, xi, sn_, ALU.mult, rs, [C2.s])
                    TTop(a1, a1, c2, ALU.add, rs, [A1.s])
                    TTop(b2, xi, cs_, ALU.mult, rs, [B2.s])
                    TTop(c2, xr, sn_, ALU.mult, rs, [C2.s])
                    TTop(b2, b2, c2, ALU.subtract, rs, [B2.s])

                if _stop(4.5):
                    return

                def fl(t_):
                    v = t_[:].rearrange("p a b -> p (a b)")
                    return v if d == 0 else v[:, ::-1]
                o_r, o_i, i_r, i_i, cf_ = fl(XR), fl(XI), fl(A1), fl(B2), fl(RC)
                P.dve(lambda e, o_r=o_r, i_r=i_r, cf_=cf_: e.tensor_tensor_scan(out=o_r, data0=cf_, data1=i_r, initial=0.0,
                                                                                  op0=ALU.mult, op1=ALU.add),
                      reads=[RC.s, A1.s], writes=[XR.s])
                P.dve(lambda e, o_i=o_i, i_i=i_i, cf_=cf_: e.tensor_tensor_scan(out=o_i, data0=cf_, data1=i_i, initial=0.0,
                                                                                  op0=ALU.mult, op1=ALU.add),
                      reads=[RC.s, B2.s], writes=[XI.s])
                if _stop(4.6):
                    return
                for si_, (n0, n1) in enumerate(SEG8):
                    L = n1 - n0
                    gr, gi_ = seg_views(XR, gsl, n0, n1, d, -1), seg_views(XI, gsl, n0, n1, d, -1)
                    a1, b2 = seg_views(A1, gsl, n0, n1, d, 1), seg_views(B2, gsl, n0, n1, d, 1)
                    hr = seg_views(Hb[:, d, 0], tsl, n0, n1, d, 1)
                    hi = seg_views(Hb[:, d, 1], tsl, n0, n1, d, 1)
                    cs_, sn_ = tab[:, 0, tsl, 1:L], tab[:, 1, tsl, 1:L]
                    rs = [XR.s, XI.s, tab.s, A1.s, B2.s]
                    TTop(a1, gr, cs_, ALU.mult, rs, [A1.s])
                    TTop(b2, gi_, sn_, ALU.mult, rs, [B2.s])
                    TTop(hr, a1, b2, ALU.subtract, rs, [Hb.s])
                    TTop(a1, gr, sn_, ALU.mult, rs, [A1.s])
                    TTop(b2, gi_, cs_, ALU.mult, rs, [B2.s])
                    TTop(hi, a1, b2, ALU.add, rs, [Hb.s])
                    first = n0 if d == 0 else n1 - 1
                    MSET(Hb[:, d, :, tsl, first:first + 1], 0.0, [Hb.s], [Hb.s])
                    last = n1 - 1 if d == 0 else n0
                    glr, gli = XR[:, :, last], XI[:, :, last]
                    cL, sL = tab[:, 0, tsl, L], tab[:, 1, tsl, L]
                    t1, t2 = sm[6][:, 0:4], sm[7][:, 0:4]
                    if si_ < 2:
                        dr, di = fst[:, si_, d, tsl, 0], fst[:, si_, d, tsl, 1]
                        dsl = fst.s
                    else:
                        dr, di = sloc[:, d, 0, tsl], sloc[:, d, 1, tsl]
                        dsl = sloc.s
                    rs = [XR.s, XI.s, tab.s, sm[6].s, sm[7].s, dsl]
                    TTop(t1, glr, cL, ALU.mult, rs, [sm[6].s])
                    TTop(t2, gli, sL, ALU.mult, rs, [sm[7].s])
                    TTop(dr, t1, t2, ALU.subtract, rs, [dsl])
                    TTop(t1, glr, sL, ALU.mult, rs, [sm[6].s])
                    TTop(t2, gli, cL, ALU.mult, rs, [sm[7].s])
                    TTop(di, t1, t2, ALU.add, rs, [dsl])
        if _stop(4.8):
            return
        for seq in range(2):
            for d in range(2):
                SDMA(bass.AP(ns_s5.tensor, ns_s5[seq, l, d].offset, [[2, 128], [256, 16], [1, 2]]), fst[:, seq, d], [fst.s], [])

        if _stop(5):
            return
        ccs_in, ccs_out = Slot("cc5in"), Slot("cc5out")
        SDMA(cc5_in[l], sloc[:].rearrange("p a b c -> p (a b c)"), [sloc.s], [ccs_in])
        P.dma("pool", lambda e: e.collective_compute("AllGather", ALU.bypass, replica_groups=[[0, 1, 2, 3], [4, 5, 6, 7]],
                                                     ins=[cc5_in[l]], outs=[cc5_out[l]]),
              reads=[ccs_in], writes=[ccs_out], inc=1)
        SDMA(gat5[:].rearrange("p r a b c -> p r (a b c)"), cc5_out[l].rearrange("(r p) c -> p r c", p=128), [ccs_out], [gat5.s])

        old_w0 = [Bt.s, bb.s, craw.s, pw.s] + new_r2
        ar["off"] = W0
        DH = aalloc([16, 2, 2, 64], BF16, "DH")
        W1 = ar["off"]
        tsc2 = [aalloc([4, 128], F32, "tscb%d" % i) for i in range(3)]
        tsi2 = aalloc([4, 128], F32, "tsib")
        TRt = aalloc([16, 64], F32, "TRt")
        TIt = aalloc([16, 64], F32, "TIt")
        U1 = aalloc([16, 64], F32, "U1")
        U2 = aalloc([16, 64], F32, "U2")
        pc = [aalloc([16], F32, "pc%d" % i) for i in range(6)]
        new_w0 = [DH.s, tsi2.s, TRt.s, TIt.s, U1.s, U2.s] + [t_.s for t_ in tsc2] + [p_.s for p_ in pc]
        P.dve(lambda e: e.memset(fence_t[:], 0.0), reads=[], writes=old_w0 + new_w0 + [fence_t.s])

        def cmul(dr, di, ar_, ai_, br_, bi_, t1, t2, rs, ws):
            TTop(t1, ar_, br_, ALU.mult, rs, ws)
            TTop(t2, ai_, bi_, ALU.mult, rs, ws)
            TTop(dr, t1, t2, ALU.subtract, rs, ws)
            TTop(t1, ar_, bi_, ALU.mult, rs, ws)
            TTop(t2, ai_, br_, ALU.mult, rs, ws)
            TTop(di, t1, t2, ALU.add, rs, ws)

        for d in range(2):
            tables(d, tsc2, tsi2)
            atr, ati, cr_, ci_, t1, t2 = (p_[:] for p_ in pc)
            ws = [p_.s for p_ in pc] + [sm[0].s, sm[1].s]
            rs = [tab.s, gat5.s, hin.s, hent.s, cst_f.s] + ws
            TTop(atr, tab[:, 0, :, 64], tab[:, 2, :, 64], ALU.mult, rs, ws)
            TTop(ati, tab[:, 1, :, 64], tab[:, 2, :, 64], ALU.mult, rs, ws)
            ranks = [0, 1, 2, 3] if d == 0 else [3, 2, 1, 0]
            CPY(cr_, hin[:, d, :, 0], rs, ws)
            CPY(ci_, hin[:, d, :, 1], rs, ws)
            TSop(hent[:, d, 0], cr_, cst_f[:, 384 + ranks[0]:385 + ranks[0]], None, ALU.mult, None, rs, [hent.s])
            TSop(hent[:, d, 1], ci_, cst_f[:, 384 + ranks[0]:385 + ranks[0]], None, ALU.mult, None, rs, [hent.s])
            for i in range(3):
                r = ranks[i]
                nr_, ni_ = sm[0][:], sm[1][:]
                cmul(nr_, ni_, atr, ati, cr_, ci_, t1, t2, rs, ws)
                TTop(cr_, nr_, gat5[:, r, d, 0, :], ALU.add, rs, ws)
                TTop(ci_, ni_, gat5[:, r, d, 1, :], ALU.add, rs, ws)
                rn = ranks[i + 1]
                STTop(hent[:, d, 0], cr_, cst_f[:, 384 + rn:385 + rn], hent[:, d, 0], ALU.mult, ALU.add, rs, [hent.s])
                STTop(hent[:, d, 1], ci_, cst_f[:, 384 + rn:385 + rn], hent[:, d, 1], ALU.mult, ALU.add, rs, [hent.s])
            rs = [tab.s, hent.s, TRt.s, TIt.s, U1.s, U2.s]
            TTop(TRt[:], tab[:, 0, :, 0:64], tab[:, 2, :, 0:64], ALU.mult, rs, [TRt.s])
            TTop(TIt[:], tab[:, 1, :, 0:64], tab[:, 2, :, 0:64], ALU.mult, rs, [TIt.s])
            her = hent[:, d, 0].unsqueeze(2).broadcast_to([128, 16, 64])
            hei = hent[:, d, 1].unsqueeze(2).broadcast_to([128, 16, 64])
            dhr = DH[:, :, d, 0, :] if d == 0 else DH[:, :, d, 0, ::-1]
            dhi = DH[:, :, d, 1, :] if d == 0 else DH[:, :, d, 1, ::-1]
            TTop(U1[:], TRt[:], her, ALU.mult, rs, [U1.s])
            TTop(U2[:], TIt[:], hei, ALU.mult, rs, [U2.s])
            TTop(dhr, U1[:], U2[:], ALU.subtract, rs, [DH.s])
            TTop(U1[:], TRt[:], hei, ALU.mult, rs, [U1.s])
            TTop(U2[:], TIt[:], her, ALU.mult, rs, [U2.s])
            TTop(dhi, U1[:], U2[:], ALU.add, rs, [DH.s])

        if _stop(6):
            return
        old_w1 = new_w0[1:]
        ar["off"] = W1
        YA = aalloc([NG, NCK], BF16, "YA")
        yT = aalloc([4, T], BF16, "yT5")
        gl_t = [aalloc([512], F32, "gl%d" % i) for i in range(4)]
        new_w1 = [YA.s, yT.s] + [g_.s for g_ in gl_t]
        P.dve(lambda e: e.memset(fence_t[:], 0.0), reads=[], writes=old_w1 + new_w1 + [fence_t.s])

        for g0 in range(0, NG, 4):
            bk = psum()
            for gi in range(4):
                g = g0 + gi
                gp, g2 = g // 2, g % 2
                cv, g8_, cs_ = Ct_(gp)
                cols = slice(gi * 128, (gi + 1) * 128)
                MM(bk[:, cols], DtV[:, g, :], UT[:, g, :], (gi == 0), False, [DtS.s, UT.s], [bk.s])
                for d in range(2):
                    for ri in range(2):
                        MM(bk[:, cols], cv[64 * g2:64 * g2 + 64, g8_, d, ri, :], Hb[64 * g2:64 * g2 + 64, d, ri, gp, :], False, False,
                           [cs_, Hb.s], [bk.s])
                for d in range(2):
                    for ri in range(2):
                        MM(bk[:, gi * 128 + 64:(gi + 1) * 128], cv[64 * g2:64 * g2 + 64, g8_, d, ri, :],
                           DH[64 * g2:64 * g2 + 64, gp, d, ri, :], False, (d == 1 and ri == 1), [cs_, DH.s], [bk.s])
            xs_, sq_, u_, sg_ = gl_t
            ACTop(xs_[:], bk[:], AF.Copy, [bk.s], [xs_.s])
            ACTop(sq_[:], bk[:], AF.Square, [bk.s], [sq_.s])
            TSop(sq_[:], sq_[:], 0.044715, 1.0, ALU.mult, ALU.add, [sq_.s], [sq_.s])
            TTop(u_[:], sq_[:], xs_[:], ALU.mult, [sq_.s, xs_.s], [u_.s])
            ACTop(sg_[:], u_[:], AF.Sigmoid, [u_.s], [sg_.s], scale=2.0 * math.sqrt(2.0 / math.pi))
            TTop(YA[:, g0:g0 + 4, :].rearrange("p g n -> p (g n)"), xs_[:], sg_[:], ALU.mult, [xs_.s, sg_.s], [YA.s])

        for ct in range(4):
            for t0 in range(0, 8, 4):
                bk = psum()
                for ti in range(4):
                    t_ = t0 + ti
                    for g8 in range(8):
                        g = ct * 8 + g8
                        MM(bk[:, ti * 128:(ti + 1) * 128], asel[:, t_, 112 - 16 * g8:240 - 16 * g8],
                           YA[:, g, :], (ti == 0 and g8 == 0), (g8 == 7), [asel.s, YA.s], [bk.s])
                ACTop(yT[:, ct, :].rearrange("p (n t) -> p t n", t=8)[:, t0:t0 + 4, :],
                      bk[:].rearrange("p (t n) -> p t n", n=128), AF.Copy, [bk.s], [yT.s])

        wgl, wgls = wload(s5_w_glu[l].rearrange("(k p) c -> p k c", p=128), 4, 512, slot=0)
        for c2 in range(4):
            for hf in range(2):
                bk = psum()
                for ct in range(4):
                    MM(bk[:], wgl[:, ct, c2 * 128:(c2 + 1) * 128], yT[:, ct, hf * 512:(hf + 1) * 512], (ct == 0), (ct == 3),
                       [wgls, yT.s], [bk.s])
                sgl = gl_t[(c2 * 2 + hf) % 2]
                ACTop(sgl[:], bk[:], AF.Sigmoid, [bk.s], [sgl.s])
                TTop(mixT[:, 4 + c2, hf * 512:(hf + 1) * 512], yT[:, c2, hf * 512:(hf + 1) * 512], sgl[:], ALU.mult,
                     [yT.s, sgl.s], [mixs[4 + c2][hf]])


    for l in range(DEPTH):
        compute_mod(l)
        norm_mod(l, 0, 0)
        if STAGE["s5"]:
            s5(l)
        else:
            for k in range(4, 8):
                for h in range(2):
                    P.dve(lambda e, k=k, h=h: e.memset(mixT[:, k, h * 512:(h + 1) * 512], 0.0), writes=[mixs[k][h]])
        if STAGE["hg"]:
            hgrn(l)
        else:
            for k in range(0, 4):
                for h in range(2):
                    P.dve(lambda e, k=k, h=h: e.memset(mixT[:, k, h * 512:(h + 1) * 512], 0.0), writes=[mixs[k][h]])
        residual_proj(w_out[l], l, 16, mixT, mixs, KD)
        norm_mod(l, 1, 24)
        ffn(l)

    nfin32 = sb([128, KD], F32, "nfin32")
    P.dve(lambda e: e.tensor_scalar(out=nfin32[:], in0=nfin[:], scalar1=32.0, scalar2=None, op0=ALU.mult),
          reads=[nfin.s], writes=[nfin32.s])
    aphase()
    ytok = [aalloc([D], F32, "ytok%d" % j) for j in range(4)]
    yT = aalloc([512], F32, "yT")
    afence()
    for h in range(2):
        rms_stats(h)
        for k in range(KD):
            P.dve(lambda e, k=k, h=h: e.scalar_tensor_tensor(
                out=yT[:], in0=xT[:, k, h * 512:(h + 1) * 512], scalar=nfin32[:, k:k + 1], in1=rstd[:],
                op0=ALU.mult, op1=ALU.mult), reads=[xs[k][h], nfin32.s, rstd.s], writes=[yT.s])
            bk = psum()
            for j in range(4):
                P.pe(lambda e, bk=bk, j=j: e.transpose(bk[:, j * 128:(j + 1) * 128], yT[:, j * 128:(j + 1) * 128], ident_f),
                     reads=[yT.s, cst_f.s], writes=[bk.s])
            for j in range(4):
                P.act(lambda e, bk=bk, j=j, k=k: e.activation(out=ytok[j][:, k * 128:(k + 1) * 128],
                                                               in_=bk[:, j * 128:(j + 1) * 128], func=AF.Copy),
                      reads=[bk.s], writes=[ytok[j].s])
        for j in range(4):
            tt = h * 4 + j
            P.dma("sp", lambda e, j=j, tt=tt: e.dma_start(out=y_out[tt * 128:(tt + 1) * 128, :], in_=ytok[j][:]),
                  reads=[ytok[j].s], writes=[])

    with nc.allow_non_contiguous_dma(reason="small strided parameter loads"):
        P.emit(st)
    st.close()
    return nc


def _consts(core):
    q = core % 4
    c = np.zeros((128, 512), np.float32)
    c[:, 0:128] = np.eye(128, dtype=np.float32)
    j = np.arange(128)[:, None]
    i = np.arange(128)[None, :]
    same = (j // CH) == (i // CH)
    c[:, 128:256] = (same & (j <= i)).astype(np.float32)
    c[:, 256:384] = (same & (j >= i)).astype(np.float32)
    c[:, 384 + q] = 1.0
    nf = D // 4
    p = np.arange(128, dtype=np.float32)
    for par in range(2):
        kf = par * 128 + p
        c[:, 388 + par] = (1.0 / (np.float32(10000.0) ** (kf / np.float32(nf)))).astype(np.float32)
    t = q * 512 + np.arange(512)
    c2 = np.zeros((128, 1024), np.float32)
    c2[:, 0:512] = (t // 64).astype(np.float32)[None, :]
    c2[:, 512:1024] = (t % 64).astype(np.float32)[None, :]
    c3 = np.zeros((128, 1024), np.float32)
    for p_ in range(128):
        c3[p_, 112 + p_ % 16] = 1.0
        c3[p_, 240 + p_ // 16] = 1.0
    for g4 in range(4):
        for cc in range(16):
            c3[(2 * g4) * 16 + cc, 480 + g4 * 16 + cc] = 1.0
            c3[(2 * g4 + 1) * 16 + cc, 544 + g4 * 16 + cc] = 1.0
    c3[:, 608:625] = np.arange(-8, 9, dtype=np.float32)[None, :]
    c3[:, 640:705] = (8.0 * np.arange(65, dtype=np.float32))[None, :]
    sI = (np.arange(128) // 16)[:, None]
    tI = (np.arange(128) // 16)[None, :]
    c3[:, 768:896] = (sI <= tI).astype(np.float32)
    c3[:, 896:1024] = (sI >= tI).astype(np.float32)
    return c, c2, c3


_NC_CACHE = {}


def kernel(**inp):
    inp = {k: np.asarray(v) for k, v in inp.items()}
    if "nc" not in _NC_CACHE:
        _NC_CACHE["nc"] = build_program()
    nc = _NC_CACHE["nc"]
    xp = inp["x_prompt"]
    xsm = inp["x_sample"]
    in_maps = []
    shared = {k: np.ascontiguousarray(inp[k], dtype=np.float32) for k in
              ("w_mod", "b_mod", "norm_mix", "norm_ffn", "norm_final", "w_in", "w_out", "w_gate", "w_up", "w_down",
               "hg_lb_logits", "hg_norm", "s5_lam_re", "s5_lam_im", "s5_log_dt", "s5_b_re", "s5_b_im", "s5_c_re", "s5_c_im",
               "s5_d", "s5_w_glu")}
    for core in range(8):
        b, q = core // 4, core % 4
        x = np.concatenate([xp[2 * core], xp[2 * core + 1], xsm[b, q * 512:(q + 1) * 512]], axis=0)
        cond = np.stack([inp["c_ctx"], inp["c"][b]], axis=0)
        c1, c2, c3 = _consts(core)
        m = dict(shared)
        m.update({"x": np.ascontiguousarray(x, dtype=np.float32), "cond": np.ascontiguousarray(cond, dtype=np.float32),
                  "st_hg": np.ascontiguousarray(inp["state_hgrn"][b], dtype=np.float32), "cst": c1, "cst2": c2, "cst3": c3,
                  "st_s5": np.ascontiguousarray(inp["state_s5"][b], dtype=np.float32)})
        in_maps.append(m)
    res = run_bass_kernel_spmd(nc, in_maps, core_ids=list(range(8)))
    outs = res.results
    _DBG["outs"] = outs
    y_prompt = np.zeros_like(xp)
    y_sample = np.zeros_like(xsm)
    ns_hg = np.zeros((16, DEPTH, 2, NH, DK, DK), np.float32)
    ns_s5 = np.zeros((16, DEPTH, 2, NG, SP, 2), np.float32)
    for core in range(8):
        b, q = core // 4, core % 4
        y = outs[core]["y"]
        y_prompt[2 * core] = y[0:256]
        y_prompt[2 * core + 1] = y[256:512]
        y_sample[b, q * 512:(q + 1) * 512] = y[512:1024]
        ns_hg[2 * core:2 * core + 2] = outs[core]["ns_hg"]
        ns_s5[2 * core:2 * core + 2] = outs[core]["ns_s5"]
    return (y_prompt, y_sample, ns_hg, ns_s5)
```

```python
import math
from contextlib import ExitStack

import numpy as np
import concourse.bass as bass
import concourse.mybir as mybir
from concourse.bass_utils import run_bass_kernel_spmd

F32 = mybir.dt.float32
BF16 = mybir.dt.bfloat16
AF = mybir.ActivationFunctionType
ALU = mybir.AluOpType

ENGS = ("pe", "act", "dve", "pool", "sp")
SAME_ENGINE_RAW_DIST = 2


class Slot:
    __slots__ = ("name", "w", "r", "al")

    def __init__(self, name):
        self.name = name
        self.w = None
        self.r = []
        self.al = [self]


def alias(*slots):
    grp = []
    for s in slots:
        for a in s.al:
            if a not in grp:
                grp.append(a)
    for s in grp:
        s.al = grp


class Op:
    __slots__ = ("eng", "fn", "deps", "raw", "dma", "idx", "milestone", "mcount", "dsem", "dval", "inc", "eidx")


class Prog:
    def __init__(self, nc, n_dma_sems=8, sync_same_engine=True):
        self.nc = nc
        self.ops = []
        self.n_dma_sems = n_dma_sems
        self.sync_same = sync_same_engine

    def add(self, eng, fn, reads=(), writes=(), dma=False, inc=16):
        op = Op()
        op.eng, op.fn, op.dma, op.inc = eng, fn, dma, inc
        op.deps = set()
        op.raw = set()
        op.milestone = False
        op.mcount = 0
        op.dsem = None
        op.dval = 0
        op.idx = len(self.ops)
        for s0 in reads:
            for s in s0.al:
                if s.w is not None:
                    op.deps.add(s.w)
                    op.raw.add(s.w)
        for s0 in writes:
            for s in s0.al:
                if s.w is not None:
                    op.deps.add(s.w)
                op.deps.update(s.r)
        for s in reads:
            s.r.append(op.idx)
        for s in writes:
            s.w = op.idx
            s.r = []
        op.deps.discard(op.idx)
        self.ops.append(op)
        return op

    def pe(self, fn, reads=(), writes=()):
        return self.add("pe", fn, reads, writes)

    def act(self, fn, reads=(), writes=()):
        return self.add("act", fn, reads, writes)

    def dve(self, fn, reads=(), writes=()):
        return self.add("dve", fn, reads, writes)

    def dma(self, eng, fn, reads=(), writes=(), inc=16):
        return self.add(eng, fn, reads, writes, dma=True, inc=inc)

    def emit(self, stack):
        nc = self.nc
        ops = self.ops
        ecount = {e: 0 for e in ENGS}
        for op in ops:
            op.eidx = ecount[op.eng]
            ecount[op.eng] += 1

        def needs_sync(op, dop):
            if dop.dma or op.dma or dop.eng != op.eng:
                return True
            if dop.eng == "pe" or not self.sync_same:
                return False
            return (dop.idx in op.raw) and (op.eidx - dop.eidx < SAME_ENGINE_RAW_DIST)

        self.needs_sync = needs_sync
        for op in ops:
            for d in op.deps:
                dop = ops[d]
                if dop.dma:
                    continue
                if not needs_sync(op, dop):
                    continue
                dop.milestone = True
        cnt = {e: 0 for e in ENGS}
        for op in ops:
            if not op.dma and op.milestone:
                cnt[op.eng] += 1
            op.mcount = cnt[op.eng]
        esem = {e: stack.enter_context(nc.semaphore("s_" + e)) for e in ENGS}
        dsems = {e: None for e in ENGS}
        dcount = {}
        dn = {e: 0 for e in ENGS}
        for op in ops:
            if op.dma:
                if dsems[op.eng] is None:
                    dsems[op.eng] = [stack.enter_context(nc.semaphore("d_%s_%d" % (op.eng, i)))
                                     for i in range(self.n_dma_sems)]
                k = dn[op.eng]
                dn[op.eng] += 1
                op.dsem = (op.eng, k % self.n_dma_sems)
                dcount[op.dsem] = dcount.get(op.dsem, 0) + op.inc
                op.dval = dcount[op.dsem]
        per = {e: [o for o in ops if o.eng == e] for e in ENGS}
        block = stack.enter_context(nc.Block())
        sync_same = self.sync_same

        def make(e):
            def body(eng):
                waited = {}

                def wait(key, sem, val):
                    if waited.get(key, 0) >= val:
                        return
                    waited[key] = val
                    eng.wait_ge(sem, val)

                for op in per[e]:
                    for d in sorted(op.deps):
                        dop = ops[d]
                        if dop.dma:
                            wait(("d",) + dop.dsem, dsems[dop.dsem[0]][dop.dsem[1]], dop.dval)
                        else:
                            if not self.needs_sync(op, dop):
                                continue
                            wait(("e", dop.eng), esem[dop.eng], dop.mcount)
                    if op.dma:
                        prev = op.dval - op.inc
                        if prev > 0:
                            wait(("d",) + op.dsem, dsems[op.dsem[0]][op.dsem[1]], prev)
                        ins = op.fn(eng)
                        ins.then_inc(dsems[op.dsem[0]][op.dsem[1]], op.inc)
                    else:
                        ins = op.fn(eng)
                        if op.milestone:
                            ins.then_inc(esem[e], 1)
                if dsems[e] is not None:
                    for i, s in enumerate(dsems[e]):
                        v = dcount.get((e, i), 0)
                        if v:
                            wait(("d", e, i), s, v)
            return body

        block.tensor(make("pe"))
        block.scalar(make("act"))
        block.vector(make("dve"))
        block.gpsimd(make("pool"))
        block.sync(make("sp"))


D = 1024
KD = 8
T = 1024
NT = 8
DEPTH = 2
HG_W = 512
NH = 4
DK = 128
S5_W = 512
NG = 32
SP = 64
IN_W = 3072
DFF = 2816
NF = 22
EPS = 1e-6
CH = 32
NCH = T // CH
SEGS = [(0, 8), (8, 16), (16, 32)]
WSLOT = 4096
N_WSLOT = 4
CCW = 1096
ARENA_B = 90 * 1024 // 2

I32 = mybir.dt.int32
STAGE = {"hg": True, "s5": True, "s5_stop": 99}
DEBUG = False
_DBG = {}


class TT:
    def __init__(self, t, slot):
        self.t = t
        self.s = slot

    def __getitem__(self, k):
        return self.t[k]


def build_program():
    nc = bass.Bass("TRN2", target_bir_lowering=False)
    st = ExitStack()
    P = Prog(nc)

    def din(name, shape):
        return nc.dram_tensor(name, list(shape), F32, kind="ExternalInput").ap()

    def dout(name, shape):
        return nc.dram_tensor(name, list(shape), F32, kind="ExternalOutput").ap()

    x_in = din("x", [T, D])
    cond_in = din("cond", [2, D])
    w_mod = din("w_mod", [DEPTH, D, 6 * D])
    b_mod = din("b_mod", [DEPTH, 6 * D])
    norm_mix = din("norm_mix", [DEPTH, D])
    norm_ffn = din("norm_ffn", [DEPTH, D])
    norm_final = din("norm_final", [D])
    w_in = din("w_in", [DEPTH, D, IN_W])
    w_out = din("w_out", [DEPTH, D, D])
    w_gate = din("w_gate", [DEPTH, D, DFF])
    w_up = din("w_up", [DEPTH, D, DFF])
    w_down = din("w_down", [DEPTH, DFF, D])
    hg_lb = din("hg_lb_logits", [2, DEPTH, HG_W])
    hg_norm = din("hg_norm", [DEPTH, DK])
    st_hg = din("st_hg", [DEPTH, 2, NH, DK, DK])
    cst = din("cst", [128, 512])
    cst2_in = din("cst2", [128, 1024])
    cst3_in = din("cst3", [128, 1024])
    s5_lam_re = din("s5_lam_re", [DEPTH, 2, NG, SP])
    s5_lam_im = din("s5_lam_im", [DEPTH, 2, NG, SP])
    s5_log_dt = din("s5_log_dt", [DEPTH, 2, NG])
    s5_b_re = din("s5_b_re", [DEPTH, 2, NG, SP, 16])
    s5_b_im = din("s5_b_im", [DEPTH, 2, NG, SP, 16])
    s5_c_re = din("s5_c_re", [DEPTH, 2, NG, 16, SP])
    s5_c_im = din("s5_c_im", [DEPTH, 2, NG, 16, SP])
    s5_d = din("s5_d", [DEPTH, NG, 16])
    s5_w_glu = din("s5_w_glu", [DEPTH, S5_W, S5_W])
    st_s5 = din("st_s5", [DEPTH, 2, NG, SP, 2])
    y_out = dout("y", [T, D])
    ns_hg = dout("ns_hg", [2, DEPTH, 2, NH, DK, DK])
    ns_s5 = dout("ns_s5", [2, DEPTH, 2, NG, SP, 2])
    cc5_in = [nc.dram_tensor("cc5_in%d" % i, [128, 64], F32, kind="Internal").ap() for i in range(DEPTH)]
    cc5_out = [nc.dram_tensor("cc5_out%d" % i, [4 * 128, 64], F32, kind="Internal").ap() for i in range(DEPTH)]
    cc_in = [nc.dram_tensor("cc_in%d" % i, [128, CCW], F32, kind="Internal").ap() for i in range(2 * DEPTH)]
    cc_out = [nc.dram_tensor("cc_out%d" % i, [4 * 128, CCW], F32, kind="Internal").ap() for i in range(2 * DEPTH)]

    _n = [0]
    dbg_list = []

    def dbg(name, ap, slot, shape, dtype=F32):
        if not DEBUG:
            return
        t = nc.dram_tensor("dbg_" + name, list(shape), dtype, kind="ExternalOutput").ap()
        P.dma("sp", lambda e: e.dma_start(out=t, in_=ap), reads=[slot], writes=[])

    def sb(shape, dtype, name=None):
        _n[0] += 1
        name = "sb_" + (name or "t%d" % _n[0])
        t = st.enter_context(nc.sbuf_tensor(name, list(shape), dtype))
        return TT(t, Slot(name))

    banks = [TT(st.enter_context(nc.psum_tensor("ps%d" % i, [128, 512], F32)), Slot("ps%d" % i)) for i in range(8)]
    _pb = [0]

    def psum():
        b = banks[_pb[0] % 6]
        _pb[0] += 1
        return b

    wslots = [sb([128, WSLOT], BF16, "wslot%d" % i) for i in range(N_WSLOT)]
    _ws = [0]

    def wload(src_ap, a, b, slot=None):
        if slot is None:
            w = wslots[_ws[0] % N_WSLOT]
            _ws[0] += 1
        else:
            w = wslots[slot]
        view = bass.AP(w.t, 0, [[WSLOT, 128], [b, a], [1, b]])
        P.dma("pool", lambda e, v=view, s=src_ap: e.dma_start(out=v, in_=s), writes=[w.s])
        return view, w.s

    arena_t = st.enter_context(nc.sbuf_tensor("arena", [128, ARENA_B], BF16))
    ar = {"off": 0, "live": []}

    def aalloc(free_shape, dtype, name):
        n = 1
        for v in free_shape:
            n *= v
        nb = n * (1 if dtype == BF16 else 2)
        nb = (nb + 1) // 2 * 2
        off = ar["off"]
        assert off + nb <= ARENA_B, ("arena overflow", name, off, nb)
        ar["off"] = off + nb
        v = arena_t[:, off:off + nb]
        if dtype != BF16:
            v = v.bitcast(dtype)
        if len(free_shape) > 1:
            names = "abcdefg"[:len(free_shape)]
            kw = {names[i]: free_shape[i] for i in range(1, len(free_shape))}
            v = v.rearrange("p (%s) -> p %s" % (" ".join(names), " ".join(names)), **kw)
        s = Slot(name)
        ar["live"].append(s)
        return TT(v, s)

    fence_t = sb([128, 2], F32, "fence")

    def aphase(new_names_hint=None):
        old = ar["live"]
        ar["live"] = []
        ar["off"] = 0
        ar["pending"] = old

    def afence():
        old = ar.get("pending", [])
        new = list(ar["live"])
        P.dve(lambda e: e.memset(fence_t[:], 0.0), reads=[], writes=old + new + [fence_t.s])
        ar["pending"] = []

    cst_f = sb([128, 512], F32, "cst_f")
    P.dma("sp", lambda e: e.dma_start(out=cst_f[:], in_=cst), writes=[cst_f.s])
    ident_f = cst_f[:, 0:128]
    ident_b = sb([128, 128], BF16, "ident_b")
    P.dve(lambda e: e.tensor_copy(out=ident_b[:], in_=cst_f[:, 0:128]), reads=[cst_f.s], writes=[ident_b.s])
    ones_b = sb([128, 128], BF16, "ones_b")
    P.dve(lambda e: e.memset(ones_b[:], 1.0), writes=[ones_b.s])
    zeros_f = sb([128, 128], F32, "zeros_f")
    P.dve(lambda e: e.memset(zeros_f[:], 0.0), writes=[zeros_f.s])
    epsc = sb([128, 2], F32, "epsc")
    P.dve(lambda e: e.memset(epsc[:, 0:1], float(D * EPS)), writes=[epsc.s])
    P.dve(lambda e: e.memset(epsc[:, 1:2], float(DK * EPS)), reads=[epsc.s], writes=[epsc.s])
    lnc = sb([128, 1], F32, "lnc")
    P.dve(lambda e: e.memset(lnc[:], float(math.log(DK ** -0.5))), writes=[lnc.s])

    xT = sb([128, KD, T], F32, "xT")
    xs = [[Slot("xT%d_%d" % (k, h)) for h in range(2)] for k in range(KD)]
    hT = sb([128, KD, T], BF16, "hT")
    hs = [[Slot("hT%d_%d" % (k, h)) for h in range(2)] for k in range(KD)]
    mixT = sb([128, KD, T], BF16, "mixT")
    mixs = [[Slot("mix%d_%d" % (k, h)) for h in range(2)] for k in range(KD)]

    def sin_turns(out, u, ki, kf, ap=lambda t: t[:]):
        P.dve(lambda e: e.tensor_copy(out=ap(ki), in_=ap(u)), reads=[u.s], writes=[ki.s])
        P.dve(lambda e: e.tensor_copy(out=ap(kf), in_=ap(ki)), reads=[ki.s], writes=[kf.s])
        P.dve(lambda e: e.tensor_tensor(out=ap(kf), in0=ap(u), in1=ap(kf), op=ALU.subtract), reads=[u.s, kf.s], writes=[kf.s])
        P.act(lambda e: e.activation(out=ap(out), in_=ap(kf), func=AF.Sin, scale=2 * math.pi), reads=[kf.s], writes=[out.s])

    aphase()
    xtok = [aalloc([D], F32, "xtok%d" % i) for i in range(NT)]
    posarg = aalloc([512], F32, "posarg")
    cst2 = aalloc([1024], F32, "cst2")
    posk_i = aalloc([512], I32, "posk_i")
    posk_f = aalloc([512], F32, "posk_f")
    afence()
    P.dma("sp", lambda e: e.dma_start(out=cst2[:], in_=cst2_in), writes=[cst2.s])
    for tt in range(NT):
        P.dma("sp", lambda e, tt=tt: e.dma_start(out=xtok[tt][:], in_=x_in[tt * 128:(tt + 1) * 128, :]),
              writes=[xtok[tt].s])
    for k in range(KD):
        for h in range(2):
            b = psum()
            for j in range(4):
                tt = h * 4 + j
                P.pe(lambda e, b=b, j=j, tt=tt, k=k: e.transpose(b[:, j * 128:(j + 1) * 128],
                                                                   xtok[tt][:, k * 128:(k + 1) * 128], ident_f),
                     reads=[xtok[tt].s, cst_f.s], writes=[b.s])
            P.act(lambda e, b=b, k=k, h=h: e.activation(out=xT[:, k, h * 512:(h + 1) * 512], in_=b[:], func=AF.Copy),
                  reads=[b.s], writes=[xs[k][h]])

    posi = aalloc_late = None
    for k in range(KD):
        blk = k // 2
        pos_src = cst2[:, 0:512] if blk < 2 else cst2[:, 512:1024]
        om = cst_f[:, 388 + (k % 2):389 + (k % 2)]
        P.dve(lambda e, pos_src=pos_src, om=om: e.tensor_scalar(
            out=posarg[:], in0=pos_src, scalar1=om, scalar2=1.0 / (2 * math.pi), op0=ALU.mult, op1=ALU.mult),
            reads=[cst2.s, cst_f.s], writes=[posarg.s])
        if blk % 2 == 1:
            P.dve(lambda e: e.tensor_scalar(out=posarg[:], in0=posarg[:], scalar1=0.25, scalar2=None, op0=ALU.add),
                  reads=[posarg.s], writes=[posarg.s])
        sin_turns(posarg, posarg, posk_i, posk_f)
        P.dve(lambda e, k=k: e.tensor_tensor(out=xT[:, k, 512:1024], in0=xT[:, k, 512:1024], in1=posarg[:], op=ALU.add),
              reads=[posarg.s, xs[k][1]], writes=[xs[k][1]])

    condf = sb([128, KD, 2], F32, "condf")
    condb = sb([128, KD, 2], BF16, "condb")
    for j in range(2):
        P.dma("sp", lambda e, j=j: e.dma_start(out=condf[:, :, j], in_=cond_in[j].rearrange("(k p) -> p k", p=128)),
              writes=[condf.s])
    P.act(lambda e: e.activation(out=condb[:], in_=condf[:], func=AF.Silu), reads=[condf.s], writes=[condb.s])

    nmix = sb([128, DEPTH, KD], F32, "nmix")
    nffn = sb([128, DEPTH, KD], F32, "nffn")
    nfin = sb([128, KD], F32, "nfin")
    bmod = sb([128, DEPTH, 48], F32, "bmod")
    P.dma("sp", lambda e: e.dma_start(out=nmix[:], in_=norm_mix.rearrange("l (k p) -> p l k", p=128)), writes=[nmix.s])
    P.dma("sp", lambda e: e.dma_start(out=nffn[:], in_=norm_ffn.rearrange("l (k p) -> p l k", p=128)), writes=[nffn.s])
    P.dma("sp", lambda e: e.dma_start(out=nfin[:], in_=norm_final.rearrange("(k p) -> p k", p=128)), writes=[nfin.s])
    P.dma("sp", lambda e: e.dma_start(out=bmod[:], in_=b_mod.rearrange("l (k p) -> p l k", p=128)), writes=[bmod.s])

    lbl = sb([128, 2, DEPTH, NH], F32, "lbl")
    for d in range(2):
        for l in range(DEPTH):
            P.dma("sp", lambda e, d=d, l=l: e.dma_start(out=lbl[:, d, l, :], in_=hg_lb[d, l].rearrange("(h p) -> p h", p=128)),
                  writes=[lbl.s])
    lb = sb([128, DEPTH, 2, NH], F32, "lb")
    oml = sb([128, DEPTH, 2, NH], F32, "oml")
    noml = sb([128, DEPTH, 2, NH], F32, "noml")
    P.dve(lambda e: e.memset(lb[:], 0.0), writes=[lb.s])
    P.dve(lambda e: e.tensor_tensor(out=lb[:, 1], in0=lbl[:, :, 1, :], in1=lbl[:, :, 0, :], op=ALU.subtract),
          reads=[lbl.s, lb.s], writes=[lb.s])
    P.act(lambda e: e.activation(out=lb[:, 1], in_=lb[:, 1], func=AF.Sigmoid), reads=[lb.s], writes=[lb.s])
    P.dve(lambda e: e.tensor_scalar(out=oml[:], in0=lb[:], scalar1=-1.0, scalar2=1.0, op0=ALU.mult, op1=ALU.add),
          reads=[lb.s], writes=[oml.s])
    P.dve(lambda e: e.tensor_scalar(out=noml[:], in0=lb[:], scalar1=1.0, scalar2=-1.0, op0=ALU.mult, op1=ALU.add),
          reads=[lb.s], writes=[noml.s])
    gn = sb([128, DEPTH], F32, "gn")
    P.dma("sp", lambda e: e.dma_start(out=gn[:], in_=hg_norm.rearrange("l p -> p l")), writes=[gn.s])
    P.dve(lambda e: e.tensor_scalar(out=gn[:], in0=gn[:], scalar1=float(math.sqrt(DK)), scalar2=None, op0=ALU.mult),
          reads=[gn.s], writes=[gn.s])

    mod = sb([128, DEPTH, 48, 2], F32, "mod")
    coef = sb([128, DEPTH, 2, KD, 2], F32, "coef")

    def compute_mod(l):
        bk = psum()
        for cc in range(12):
            wv, wsl = wload(w_mod[l][:, cc * 512:(cc + 1) * 512].rearrange("(k p) c -> p k c", p=128), KD, 512)
            for j in range(4):
                ft = cc * 4 + j
                for k in range(KD):
                    P.pe(lambda e, wv=wv, j=j, k=k, ft=ft, bk=bk: e.matmul(
                        bk[:, ft * 2:ft * 2 + 2], wv[:, k, j * 128:(j + 1) * 128], condb[:, k, :],
                        start=(k == 0), stop=(k == KD - 1)),
                        reads=[wsl, condb.s], writes=[bk.s])
        P.dve(lambda e, bk=bk, l=l: e.tensor_tensor(
            out=mod[:, l], in0=bk[:, 0:96].rearrange("p (f c) -> p f c", c=2),
            in1=bmod[:, l].unsqueeze(2).broadcast_to([128, 48, 2]), op=ALU.add),
            reads=[bk.s, bmod.s], writes=[mod.s])
        for n, (gt, base) in enumerate(((nmix, 8), (nffn, 32))):
            P.dve(lambda e, n=n, base=base, l=l: e.tensor_scalar(
                out=coef[:, l, n], in0=mod[:, l, base:base + 8, :], scalar1=1.0, scalar2=32.0,
                op0=ALU.add, op1=ALU.mult), reads=[mod.s], writes=[coef.s])
            P.dve(lambda e, n=n, gt=gt, l=l: e.tensor_tensor(
                out=coef[:, l, n], in0=coef[:, l, n], in1=gt[:, l].unsqueeze(2).broadcast_to([128, KD, 2]),
                op=ALU.mult), reads=[coef.s, gt.s], writes=[coef.s])

    sq = [sb([128, 512], BF16, "sq%d" % i) for i in range(2)]
    rstd = sb([128, 512], F32, "rstd")
    tmpn = [sb([128, 512], F32, "tmpn%d" % i) for i in range(2)]

    def rms_stats(h):
        bk = psum()
        for k in range(KD):
            s = sq[k % 2]
            P.act(lambda e, k=k, h=h, s=s: e.activation(out=s[:], in_=xT[:, k, h * 512:(h + 1) * 512], func=AF.Square),
                  reads=[xs[k][h]], writes=[s.s])
            P.pe(lambda e, k=k, bk=bk, s=s: e.matmul(bk[:], ones_b[:], s[:], start=(k == 0), stop=(k == KD - 1)),
                 reads=[s.s, ones_b.s], writes=[bk.s])
        P.act(lambda e, bk=bk: e.activation(out=rstd[:], in_=bk[:], func=AF.Ln, bias=epsc[:, 0:1]),
              reads=[bk.s, epsc.s], writes=[rstd.s])
        P.act(lambda e: e.activation(out=rstd[:], in_=rstd[:], func=AF.Exp, scale=-0.5), reads=[rstd.s], writes=[rstd.s])

    def norm_mod(l, n, shift_base):
        for h in range(2):
            rms_stats(h)
            for k in range(KD):
                tm = tmpn[k % 2]
                P.dve(lambda e, k=k, h=h, tm=tm: e.tensor_tensor(out=tm[:], in0=xT[:, k, h * 512:(h + 1) * 512],
                                                                 in1=rstd[:], op=ALU.mult),
                      reads=[xs[k][h], rstd.s], writes=[tm.s])
                P.act(lambda e, k=k, h=h, l=l, n=n, tm=tm: e.activation(
                    out=hT[:, k, h * 512:(h + 1) * 512], in_=tm[:], func=AF.Identity,
                    bias=mod[:, l, shift_base + k, h:h + 1], scale=coef[:, l, n, k, h:h + 1]),
                    reads=[tm.s, mod.s, coef.s], writes=[hs[k][h]])

    def residual_proj(wdram, l, gate_base, srcT, src_slots, nk):
        per = WSLOT // 256
        for c4 in range(4):
            wv = []
            for k0 in range(0, nk, per):
                kk = min(per, nk - k0)
                v, s = wload(wdram[k0 * 128:(k0 + kk) * 128, c4 * 256:(c4 + 1) * 256].rearrange("(k p) c -> p k c", p=128),
                             kk, 256)
                wv.append((k0, kk, v, s))
            for j in range(2):
                dt_ = c4 * 2 + j
                for h in range(2):
                    bk = psum()
                    for (k0, kk, v, s) in wv:
                        for k in range(kk):
                            kg = k0 + k
                            P.pe(lambda e, v=v, k=k, j=j, kg=kg, h=h, bk=bk: e.matmul(
                                bk[:], v[:, k, j * 128:(j + 1) * 128], srcT[:, kg, h * 512:(h + 1) * 512],
                                start=(kg == 0), stop=(kg == nk - 1)),
                                reads=[s, src_slots[kg][h]], writes=[bk.s])
                    P.dve(lambda e, bk=bk, dt_=dt_, h=h, l=l: e.scalar_tensor_tensor(
                        out=xT[:, dt_, h * 512:(h + 1) * 512], in0=bk[:], scalar=mod[:, l, gate_base + dt_, h:h + 1],
                        in1=xT[:, dt_, h * 512:(h + 1) * 512], op0=ALU.mult, op1=ALU.add),
                        reads=[bk.s, mod.s, xs[dt_][h]], writes=[xs[dt_][h]])


    def ffn(l):
        aphase()
        h1 = aalloc([NF, T], BF16, "h1")
        sgate = [aalloc([512], F32, "sgate%d" % i) for i in range(2)]
        h1s = [[Slot("h1_%d_%d" % (f, h)) for h in range(2)] for f in range(NF)]
        ar["live"].extend([s for row in h1s for s in row])
        afence()
        it = 0
        for c in range(6):
            ncol = 512 if c < 5 else 256
            vg, sg_ = wload(w_gate[l][:, c * 512:c * 512 + ncol].rearrange("(k p) c -> p k c", p=128), KD, ncol)
            vu, su_ = wload(w_up[l][:, c * 512:c * 512 + ncol].rearrange("(k p) c -> p k c", p=128), KD, ncol)
            for j in range(ncol // 128):
                f = c * 4 + j
                for h in range(2):
                    bg = psum()
                    bu = psum()
                    for k in range(KD):
                        P.pe(lambda e, vg=vg, k=k, j=j, h=h, bg=bg: e.matmul(
                            bg[:], vg[:, k, j * 128:(j + 1) * 128], hT[:, k, h * 512:(h + 1) * 512],
                            start=(k == 0), stop=(k == KD - 1)), reads=[sg_, hs[k][h]], writes=[bg.s])
                    for k in range(KD):
                        P.pe(lambda e, vu=vu, k=k, j=j, h=h, bu=bu: e.matmul(
                            bu[:], vu[:, k, j * 128:(j + 1) * 128], hT[:, k, h * 512:(h + 1) * 512],
                            start=(k == 0), stop=(k == KD - 1)), reads=[su_, hs[k][h]], writes=[bu.s])
                    sgt = sgate[it % 2]
                    it += 1
                    P.act(lambda e, bg=bg, sgt=sgt: e.activation(out=sgt[:], in_=bg[:], func=AF.Silu),
                          reads=[bg.s], writes=[sgt.s])
                    P.dve(lambda e, bu=bu, f=f, h=h, sgt=sgt: e.tensor_tensor(
                        out=h1[:, f, h * 512:(h + 1) * 512], in0=sgt[:], in1=bu[:], op=ALU.mult),
                        reads=[sgt.s, bu.s], writes=[h1s[f][h]])
        residual_proj(w_down[l], l, 40, h1, h1s, NF)

    def proj_feat(wv, wsl, c0, h):
        bk = psum()
        for k in range(KD):
            P.pe(lambda e, k=k, bk=bk: e.matmul(bk[:], wv[:, k, c0:c0 + 128], hT[:, k, h * 512:(h + 1) * 512],
                                                start=(k == 0), stop=(k == KD - 1)),
                 reads=[wsl, hs[k][h]], writes=[bk.s])
        return bk

    def hgrn(l):
        aphase()
        qk = [[[aalloc([T], BF16, "qk%d%d%d" % (h, d, w)) for w in range(2)] for d in range(2)] for h in range(NH)]
        kendT = [[aalloc([NT, DK], BF16, "kendT%d%d" % (h, d)) for d in range(2)] for h in range(NH)]
        V = [aalloc([HG_W], BF16, "V%d" % tt) for tt in range(NT)]
        gch = aalloc([NH * 2, NCH], F32, "gch")
        gch_s = [[Slot("gch%d%d" % (h, d)) for d in range(2)] for h in range(NH)]
        ar["live"].extend([s for row in gch_s for s in row])
        R1 = ar["off"]
        qs = [aalloc([512], F32, "qs%d" % hf) for hf in range(2)]
        rmask = aalloc([512], F32, "rmask")
        tmp = [[aalloc([512], F32, "gt%d_%d" % (i, j)) for j in range(5)] for i in range(2)]
        kend_t = [aalloc([512], BF16, "kend%d" % i) for i in range(2)]
        R1_end = ar["off"]
        afence()
        P.dve(lambda e: e.memset(rmask[:], 1.0), writes=[rmask.s])
        P.dve(lambda e: e.memset(rmask[:, 0:512:CH], 0.0), reads=[rmask.s], writes=[rmask.s])

        wv_iv, ws_iv = None, None

        def load_in(c):
            return wload(w_in[l][:, c * 512:(c + 1) * 512].rearrange("(k p) c -> p k c", p=128), KD, 512)

        wq, wqs = load_in(0)
        wf = [None, None]
        wf[0] = load_in(1)
        wf[1] = load_in(2)
        it = 0
        for h in range(NH):
            for hf in range(2):
                bk = proj_feat(wq, wqs, h * 128, hf)
                P.act(lambda e, bk=bk, hf=hf: e.activation(out=qs[hf][:], in_=bk[:], func=AF.Silu),
                      reads=[bk.s], writes=[qs[hf].s])
            for d in range(2):
                for hf in range(2):
                    t_sig, t_a, t_b, t_c, t_e = tmp[it % 2]
                    ke = kend_t[it % 2]
                    it += 1
                    bk = proj_feat(wf[d][0], wf[d][1], h * 128, hf)
                    lb_ = lb[:, l, d, h:h + 1]
                    oml_ = oml[:, l, d, h:h + 1]
                    noml_ = noml[:, l, d, h:h + 1]
                    P.act(lambda e, bk=bk, t_sig=t_sig: e.activation(out=t_sig[:], in_=bk[:], func=AF.Sigmoid),
                          reads=[bk.s], writes=[t_sig.s])
                    P.dve(lambda e, t_sig=t_sig, t_a=t_a, lb_=lb_, oml_=oml_: e.tensor_scalar(
                        out=t_a[:], in0=t_sig[:], scalar1=oml_, scalar2=lb_, op0=ALU.mult, op1=ALU.add),
                        reads=[t_sig.s, lb.s, oml.s], writes=[t_a.s])
                    P.act(lambda e, t_a=t_a: e.activation(out=t_a[:], in_=t_a[:], func=AF.Ln), reads=[t_a.s], writes=[t_a.s])
                    P.dve(lambda e, t_a=t_a, t_b=t_b: e.tensor_tensor_scan(
                        out=t_b[:], data0=rmask[:], data1=t_a[:], initial=0.0, op0=ALU.mult, op1=ALU.add),
                        reads=[t_a.s, rmask.s], writes=[t_b.s])
                    P.dve(lambda e, t_sig=t_sig, noml_=noml_, oml_=oml_: e.tensor_scalar(
                        out=t_sig[:], in0=t_sig[:], scalar1=noml_, scalar2=oml_, op0=ALU.mult, op1=ALU.add),
                        reads=[t_sig.s, noml.s, oml.s], writes=[t_sig.s])
                    P.act(lambda e, t_b=t_b, h=h, d=d, hf=hf: e.activation(
                        out=gch[:, h * 2 + d, hf * (512 // CH):(hf + 1) * (512 // CH)], in_=t_b[:, CH - 1:512:CH], func=AF.Exp),
                        reads=[t_b.s], writes=[gch_s[h][d]])
                    tb3 = t_b[:].rearrange("p (c j) -> p c j", j=CH)
                    tc3 = t_c[:].rearrange("p (c j) -> p c j", j=CH)
                    ta3 = t_a[:].rearrange("p (c j) -> p c j", j=CH)
                    tot_b = tb3[:, :, CH - 1:CH].broadcast_to([128, 512 // CH, CH])
                    if d == 0:
                        P.dve(lambda e, tc3=tc3, tb3=tb3, tot_b=tot_b: e.tensor_tensor(
                            out=tc3, in0=tb3, in1=tot_b, op=ALU.subtract), reads=[t_b.s], writes=[t_c.s])
                    else:
                        P.dve(lambda e, tc3=tc3, ta3=ta3, tb3=tb3: e.tensor_tensor(
                            out=tc3, in0=ta3, in1=tb3, op=ALU.subtract), reads=[t_a.s, t_b.s], writes=[t_c.s])
                        P.dve(lambda e, tc3=tc3, tb3=tb3, tot_b=tot_b, ta3=ta3: e.tensor_tensor(
                            out=ta3, in0=tc3, in1=tot_b, op=ALU.add), reads=[t_c.s, t_b.s], writes=[t_a.s])
                    beta = t_b if d == 0 else t_a
                    P.act(lambda e, beta=beta, t_e=t_e: e.activation(out=t_e[:], in_=beta[:], func=AF.Exp, bias=lnc[:, 0:1]),
                          reads=[beta.s, lnc.s], writes=[t_e.s])
                    P.dve(lambda e, t_e=t_e, h=h, d=d, hf=hf: e.tensor_tensor(
                        out=qk[h][d][0][:, hf * 512:(hf + 1) * 512], in0=qs[hf][:], in1=t_e[:], op=ALU.mult),
                        reads=[t_e.s, qs[hf].s], writes=[qk[h][d][0].s])
                    P.dve(lambda e, beta=beta, t_e=t_e: e.tensor_scalar(out=t_e[:], in0=beta[:], scalar1=-75.0, scalar2=None, op0=ALU.max),
                          reads=[beta.s], writes=[t_e.s])
                    P.act(lambda e, t_e=t_e: e.activation(out=t_e[:], in_=t_e[:], func=AF.Exp, scale=-1.0),
                          reads=[t_e.s], writes=[t_e.s])
                    P.dve(lambda e, t_e=t_e, t_sig=t_sig, h=h, d=d, hf=hf: e.tensor_tensor(
                        out=qk[h][d][1][:, hf * 512:(hf + 1) * 512], in0=t_sig[:], in1=t_e[:], op=ALU.mult),
                        reads=[t_e.s, t_sig.s], writes=[qk[h][d][1].s])
                    P.act(lambda e, t_c=t_c, t_e=t_e: e.activation(out=t_e[:], in_=t_c[:], func=AF.Exp, scale=-1.0),
                          reads=[t_c.s], writes=[t_e.s])
                    P.dve(lambda e, t_e=t_e, t_sig=t_sig, ke=ke: e.tensor_tensor(
                        out=ke[:], in0=t_sig[:], in1=t_e[:], op=ALU.mult), reads=[t_e.s, t_sig.s], writes=[ke.s])
                    bk2 = psum()
                    for j in range(4):
                        P.pe(lambda e, bk2=bk2, j=j, ke=ke: e.matmul(bk2[:, j * 128:(j + 1) * 128], ke[:, j * 128:(j + 1) * 128],
                                                                     ident_b[:], start=True, stop=True),
                             reads=[ke.s, ident_b.s], writes=[bk2.s])
                    P.act(lambda e, bk2=bk2, h=h, d=d, hf=hf: e.activation(
                        out=kendT[h][d][:, hf * 4:(hf + 1) * 4, :], in_=bk2[:].rearrange("p (j k) -> p j k", k=128), func=AF.Copy),
                        reads=[bk2.s], writes=[kendT[h][d].s])
        wiv, wivs = load_in(3)
        for tt in range(NT):
            bk = psum()
            hf = tt // 4
            for k in range(KD):
                P.pe(lambda e, k=k, bk=bk, tt=tt: e.matmul(bk[:], hT[:, k, tt * 128:(tt + 1) * 128], wiv[:, k, :],
                                                            start=(k == 0), stop=(k == KD - 1)),
                     reads=[wivs, hs[k][hf]], writes=[bk.s])
            P.act(lambda e, bk=bk, tt=tt: e.activation(out=V[tt][:], in_=bk[:], func=AF.Copy), reads=[bk.s], writes=[V[tt].s])

        old_tmp = [t.s for grp in tmp for t in grp] + [k.s for k in kend_t] + [q.s for q in qs] + [rmask.s]
        ar["off"] = R1
        S = [aalloc([DK], F32, "S%d" % i) for i in range(2)]
        Sent = aalloc([NCH, DK], BF16, "Sent")
        o_t = aalloc([512], F32, "o_t")
        on_t = aalloc([512], F32, "on_t")
        sg_t = aalloc([512], F32, "sg_t")
        sq_t = aalloc([512], BF16, "sq_t")
        PT = [aalloc([128], BF16, "PT%d" % i) for i in range(2)]
        gat = aalloc([4, DK], F32, "gat")
        gatG = aalloc([4, 8], F32, "gatG")
        s0t = aalloc([DK], F32, "s0t")
        Pc = [aalloc([DK], F32, "Pc%d" % i) for i in range(2)]
        Sinit = aalloc([2 * NH, DK], F32, "Sinit")
        Sinit_s = [[Slot("Sinit%d%d" % (h, d)) for d in range(2)] for h in range(NH)]
        gtot = aalloc([NH * 2, NCH // 2], F32, "gtot")
        new2 = [t.s for t in S + PT + Pc] + [Sent.s, o_t.s, on_t.s, sg_t.s, sq_t.s, gat.s, gatG.s, s0t.s, Sinit.s, gtot.s] + \
               [s_ for row in Sinit_s for s_ in row]
        ar["live"].extend([s_ for row in Sinit_s for s_ in row])
        P.dve(lambda e: e.memset(fence_t[:], 0.0), reads=[], writes=old_tmp + new2 + [fence_t.s])

        CPT = 128 // CH

        def u_matmul(h, d, c):
            tt, p0 = c // CPT, (c % CPT) * CH
            bk = psum()
            P.pe(lambda e, bk=bk: e.matmul(bk[:, 0:128], kendT[h][d][p0:p0 + CH, tt, :],
                                           V[tt][p0:p0 + CH, h * 128:(h + 1) * 128],
                                           start=True, stop=True, tile_position=(p0, 0)),
                 reads=[kendT[h][d].s, V[tt].s], writes=[bk.s])
            return bk

        def scan_order(c0, c1, d):
            return list(range(c0, c1)) if d == 0 else list(range(c1 - 1, c0 - 1, -1))

        si = [0]
        SC0, SC1 = SEGS[2]

        ci = l * 2
        ccs_in, ccs_out = Slot("ccin"), Slot("ccout")
        for h in range(NH):
            for d in range(2):
                hd = h * 2 + d
                Sx = S[si[0] % 2]
                si[0] += 1
                order = scan_order(SC0, SC1, d)
                for i, c in enumerate(order):
                    bk = u_matmul(h, d, c)
                    if i == 0:
                        P.dve(lambda e, bk=bk, Sx=Sx: e.tensor_copy(out=Sx[:], in_=bk[:, 0:128]), reads=[bk.s], writes=[Sx.s])
                    else:
                        P.dve(lambda e, bk=bk, Sx=Sx, c=c, hd=hd: e.scalar_tensor_tensor(
                            out=Sx[:], in0=Sx[:], scalar=gch[:, hd, c:c + 1], in1=bk[:, 0:128],
                            op0=ALU.mult, op1=ALU.add), reads=[bk.s, Sx.s, gch_s[h][d]], writes=[Sx.s])
                P.dma("sp", lambda e, Sx=Sx, hd=hd: e.dma_start(out=cc_in[ci][:, hd * 128:(hd + 1) * 128], in_=Sx[:]),
                      reads=[Sx.s], writes=[ccs_in])
                P.dve(lambda e, hd=hd: e.tensor_tensor_scan(
                    out=gtot[:, hd, :], data0=gch[:, hd, SC0:SC1], data1=zeros_f[:, 0:SC1 - SC0], initial=1.0,
                    op0=ALU.mult, op1=ALU.add), reads=[gch_s[h][d], zeros_f.s], writes=[gtot.s])
        P.dma("sp", lambda e: e.dma_start(out=cc_in[ci][:, 1024:1032], in_=gtot[:, :, SC1 - SC0 - 1]),
              reads=[gtot.s], writes=[ccs_in])
        P.dma("sp", lambda e: e.dma_start(out=cc_in[ci][:, 1032:CCW], in_=zeros_f[:, 0:CCW - 1032]),
              reads=[zeros_f.s], writes=[ccs_in])
        P.dma("pool", lambda e: e.collective_compute("AllGather", ALU.bypass, replica_groups=[[0, 1, 2, 3], [4, 5, 6, 7]],
                                                     ins=[cc_in[ci]], outs=[cc_out[ci]]),
              reads=[ccs_in], writes=[ccs_out], inc=1)
        ccv = cc_out[ci].rearrange("(r p) c -> p r c", p=128)
        P.dma("sp", lambda e: e.dma_start(out=gatG[:], in_=ccv[:, :, 1024:1032]), reads=[ccs_out], writes=[gatG.s])
        for h in range(NH):
            for d in range(2):
                hd = h * 2 + d
                col = slice(hd * 128, (hd + 1) * 128)
                P.dma("sp", lambda e, col=col: e.dma_start(out=gat[:], in_=ccv[:, :, col]), reads=[ccs_out], writes=[gat.s])
                P.dma("sp", lambda e, d=d, h=h: e.dma_start(out=s0t[:], in_=st_hg[l, d, h]), writes=[s0t.s])
                dst = Sinit[:, hd, :]
                ranks = [0, 1, 2, 3] if d == 0 else [3, 2, 1, 0]
                prev, prev_s = s0t[:], s0t.s
                P.dve(lambda e, dst=dst, prev=prev, r=ranks[0]: e.tensor_scalar(
                    out=dst, in0=prev, scalar1=cst_f[:, 384 + r:385 + r], scalar2=None, op0=ALU.mult),
                    reads=[prev_s, cst_f.s], writes=[Sinit_s[h][d]])
                for i in range(3):
                    r = ranks[i]
                    nxt = Pc[i % 2]
                    P.dve(lambda e, nxt=nxt, prev=prev, r=r, hd=hd: e.scalar_tensor_tensor(
                        out=nxt[:], in0=prev, scalar=gatG[:, r, hd:hd + 1], in1=gat[:, r, :],
                        op0=ALU.mult, op1=ALU.add), reads=[prev_s, gat.s, gatG.s], writes=[nxt.s])
                    rn = ranks[i + 1]
                    P.dve(lambda e, nxt=nxt, dst=dst, rn=rn: e.scalar_tensor_tensor(
                        out=dst, in0=nxt[:], scalar=cst_f[:, 384 + rn:385 + rn], in1=dst, op0=ALU.mult, op1=ALU.add),
                        reads=[nxt.s, cst_f.s, Sinit_s[h][d]], writes=[Sinit_s[h][d]])
                    prev, prev_s = nxt[:], nxt.s

        wg_v, wg_s = load_in(4)
        it2 = 0
        for h in range(NH):
            bo = [banks[6], banks[7]]
            for d in range(2):
                hd = h * 2 + d
                for (c0, c1) in SEGS:
                    Sx = S[si[0] % 2]
                    si[0] += 1
                    order = scan_order(c0, c1, d)
                    is_sample = (c0 == SC0)
                    for i, c in enumerate(order):
                        if i == 0:
                            src = Sinit[:, hd, :] if is_sample else zeros_f[:]
                            src_s = Sinit_s[h][d] if is_sample else zeros_f.s
                        else:
                            src, src_s = Sx[:], Sx.s
                        P.act(lambda e, src=src, c=c: e.activation(out=Sent[:, c, :], in_=src, func=AF.Copy),
                              reads=[src_s], writes=[Sent.s])
                        last = (i == len(order) - 1)
                        if last and is_sample:
                            continue
                        bk = u_matmul(h, d, c)
                        if i == 0 and not is_sample:
                            P.dve(lambda e, bk=bk, Sx=Sx: e.tensor_copy(out=Sx[:], in_=bk[:, 0:128]), reads=[bk.s], writes=[Sx.s])
                        else:
                            P.dve(lambda e, bk=bk, Sx=Sx, src=src, c=c, hd=hd: e.scalar_tensor_tensor(
                                out=Sx[:], in0=src, scalar=gch[:, hd, c:c + 1], in1=bk[:, 0:128],
                                op0=ALU.mult, op1=ALU.add), reads=[bk.s, src_s, gch_s[h][d]], writes=[Sx.s])
                    if not is_sample:
                        seq = 0 if c0 == 0 else 1
                        P.dma("sp", lambda e, Sx=Sx, seq=seq, d=d, h=h: e.dma_start(out=ns_hg[seq, l, d, h], in_=Sx[:]),
                              reads=[Sx.s], writes=[])
                for hf in range(2):
                    for j in range(4):
                        tt = hf * 4 + j
                        tok = slice(tt * 128, (tt + 1) * 128)
                        bs = psum()
                        P.pe(lambda e, bs=bs, tok=tok, d=d, h=h: e.matmul(bs[:, 0:128], qk[h][d][1][:, tok], qk[h][d][0][:, tok],
                                                                    start=True, stop=True),
                             reads=[qk[h][d][0].s, qk[h][d][1].s], writes=[bs.s])
                        pt = PT[it2 % 2]
                        it2 += 1
                        P.dve(lambda e, bs=bs, pt=pt, d=d: e.tensor_tensor(
                            out=pt[:], in0=bs[:, 0:128], in1=cst_f[:, 128 + d * 128:256 + d * 128], op=ALU.mult),
                            reads=[bs.s, cst_f.s], writes=[pt.s])
                        oc = slice(j * 128, (j + 1) * 128)
                        P.pe(lambda e, pt=pt, tt=tt, oc=oc, hf=hf, d=d, j=j, h=h: e.matmul(
                            bo[hf][:, oc], V[tt][:, h * 128:(h + 1) * 128], pt[:], start=(d == 0 and j == 0), stop=False),
                            reads=[V[tt].s, pt.s], writes=[bo[hf].s])
                        for sub in range(CPT):
                            c = tt * CPT + sub
                            cs = slice(j * 128 + sub * CH, j * 128 + (sub + 1) * CH)
                            ts = slice(tt * 128 + sub * CH, tt * 128 + (sub + 1) * CH)
                            P.pe(lambda e, c=c, cs=cs, ts=ts, hf=hf, d=d, h=h: e.matmul(
                                bo[hf][:, cs], Sent[:, c, :], qk[h][d][0][:, ts], start=False, stop=(d == 1)),
                                reads=[Sent.s, qk[h][d][0].s], writes=[bo[hf].s])
            for hf in range(2):
                P.act(lambda e, hf=hf: e.activation(out=o_t[:], in_=bo[hf][:], func=AF.Copy), reads=[bo[hf].s], writes=[o_t.s])
                P.act(lambda e, hf=hf: e.activation(out=sq_t[:], in_=bo[hf][:], func=AF.Square), reads=[bo[hf].s], writes=[sq_t.s])
                if l == 0 and h == 0 and hf == 0:
                    dbg("o", o_t[:], o_t.s, [128, 512])
                    dbg("qf", qk[0][0][0][:], qk[0][0][0].s, [128, T], BF16)
                    dbg("kf", qk[0][0][1][:], qk[0][0][1].s, [128, T], BF16)
                    dbg("qb", qk[0][1][0][:], qk[0][1][0].s, [128, T], BF16)
                    dbg("kb", qk[0][1][1][:], qk[0][1][1].s, [128, T], BF16)
                br = psum()
                P.pe(lambda e, br=br: e.matmul(br[:], ones_b[:], sq_t[:], start=True, stop=True),
                     reads=[sq_t.s, ones_b.s], writes=[br.s])
                P.act(lambda e, br=br: e.activation(out=on_t[:], in_=br[:], func=AF.Ln, bias=epsc[:, 1:2]),
                      reads=[br.s, epsc.s], writes=[on_t.s])
                P.act(lambda e: e.activation(out=on_t[:], in_=on_t[:], func=AF.Exp, scale=-0.5), reads=[on_t.s], writes=[on_t.s])
                P.dve(lambda e: e.tensor_tensor(out=on_t[:], in0=o_t[:], in1=on_t[:], op=ALU.mult),
                      reads=[o_t.s, on_t.s], writes=[on_t.s])
                bg = proj_feat(wg_v, wg_s, h * 128, hf)
                P.act(lambda e, bg=bg: e.activation(out=sg_t[:], in_=bg[:], func=AF.Silu), reads=[bg.s], writes=[sg_t.s])
                P.dve(lambda e, hf=hf, h=h: e.scalar_tensor_tensor(
                    out=mixT[:, h, hf * 512:(hf + 1) * 512], in0=on_t[:], scalar=gn[:, l:l + 1], in1=sg_t[:],
                    op0=ALU.mult, op1=ALU.mult), reads=[on_t.s, sg_t.s, gn.s], writes=[mixs[h][hf]])

    def TTop(out, in0, in1, op, reads, writes):
        return P.dve(lambda e: e.tensor_tensor(out=out, in0=in0, in1=in1, op=op), reads=reads, writes=writes)

    def TSop(out, in0, s1, s2, op0, op1, reads, writes):
        if s2 is None:
            return P.dve(lambda e: e.tensor_scalar(out=out, in0=in0, scalar1=s1, scalar2=None, op0=op0), reads=reads, writes=writes)
        return P.dve(lambda e: e.tensor_scalar(out=out, in0=in0, scalar1=s1, scalar2=s2, op0=op0, op1=op1), reads=reads, writes=writes)

    def STTop(out, in0, scalar, in1, op0, op1, reads, writes):
        return P.dve(lambda e: e.scalar_tensor_tensor(out=out, in0=in0, scalar=scalar, in1=in1, op0=op0, op1=op1),
                     reads=reads, writes=writes)

    def ACTop(out, in_, func, reads, writes, bias=None, scale=None):
        kw = {}
        if bias is not None:
            kw["bias"] = bias
        if scale is not None:
            kw["scale"] = scale
        return P.act(lambda e: e.activation(out=out, in_=in_, func=func, **kw), reads=reads, writes=writes)

    def MM(out, lhsT, rhs, start, stop, reads, writes, tp=None):
        if tp is None:
            return P.pe(lambda e: e.matmul(out, lhsT, rhs, start=start, stop=stop), reads=reads, writes=writes)
        return P.pe(lambda e: e.matmul(out, lhsT, rhs, start=start, stop=stop, tile_position=tp), reads=reads, writes=writes)

    def CPY(out, in_, reads, writes):
        return P.dve(lambda e: e.tensor_copy(out=out, in_=in_), reads=reads, writes=writes)

    def MSET(out, val, reads, writes):
        return P.dve(lambda e: e.memset(out, val), reads=reads, writes=writes)

    def SDMA(out, in_, reads, writes):
        return P.dma("sp", lambda e: e.dma_start(out=out, in_=in_), reads=reads, writes=writes)

    NCK = 128
    SEG8 = [(0, 32), (32, 64), (64, 128)]
    TWO_PI = 2.0 * math.pi

    def s5(l):
        aphase()
        c3 = aalloc([1024], F32, "c3")
        asel = aalloc([8, 240], BF16, "asel")
        UT = aalloc([NG, NCK], BF16, "UT")
        Hb = aalloc([2, 2, 16, NCK], BF16, "Hb")
        par = aalloc([2, 5, 16], F32, "par")
        tab = aalloc([3, 16, 65], F32, "tab")
        hin = aalloc([2, 16, 2], F32, "hin")
        sloc = aalloc([2, 2, 16], F32, "sloc")
        hent = aalloc([2, 2, 16], F32, "hent")
        fst = aalloc([2, 2, 16, 2], F32, "fst")
        dsk = aalloc([NG], F32, "dsk")
        gat5 = aalloc([4, 2, 2, 16], F32, "gat5")
        sm = [aalloc([16], F32, "sm%d" % i) for i in range(8)]
        W0 = ar["off"]
        Bt = aalloc([NG, 2, 64], BF16, "Bt")
        bb = aalloc([2, 2, 16, 16], F32, "bb")
        craw = aalloc([2, 2, 16, 16], F32, "craw")
        pw = aalloc([2, 2, 16, 17], F32, "pw")
        R2 = ar["off"]
        uT = aalloc([4, T], BF16, "uT")
        cnat = aalloc([16, 64], F32, "cnat")
        prs = [aalloc([16, 17], F32, "prs%d" % i) for i in range(3)]
        pri = aalloc([16, 17], I32, "pri")
        afence()
        CtS = [wslots[1], wslots[2]]
        CtV = [bass.AP(w.t, 0, [[WSLOT, 128], [512, 8], [256, 2], [128, 2], [1, 128]]) for w in CtS]
        DtS = wslots[3]
        DtV = bass.AP(DtS.t, 0, [[WSLOT, 128], [128, NG], [1, 128]])

        def Ct_(gp):
            return CtV[gp // 8], gp % 8, CtS[gp // 8].s

        def _stop(k):
            if STAGE["s5_stop"] <= k:
                for kk in range(4, 8):
                    for hh in range(2):
                        MSET(mixT[:, kk, hh * 512:(hh + 1) * 512], 0.0, [], [mixs[kk][hh]])
                return True
            return False

        SDMA(c3[:], cst3_in, [], [c3.s])
        for g8 in range(8):
            TSop(asel[:, g8, :], c3[:, 0:240], c3[:, 240 + g8:241 + g8], None, ALU.mult, None, [c3.s], [asel.s])
        EV = c3[:, 608:625]
        K8 = c3[:, 640:705]
        R_even = c3[:, 480:544]
        R_odd = c3[:, 544:608]
        M5 = c3[:, 768:1024]
        for d in range(2):
            SDMA(par[:, d, 0, :], s5_lam_re[l, d].rearrange("(gp g2) p -> (g2 p) gp", g2=2), [], [par.s])
            SDMA(par[:, d, 1, :], s5_lam_im[l, d].rearrange("(gp g2) p -> (g2 p) gp", g2=2), [], [par.s])
            for g2 in range(2):
                SDMA(par[64 * g2:64 * g2 + 64, d, 2, :],
                     s5_log_dt[l, d].rearrange("(gp g2) -> g2 gp", g2=2)[g2].partition_broadcast(64), [], [par.s])
            for ri, src in enumerate((s5_b_re, s5_b_im)):
                SDMA(bb[:, d, ri], src[l, d].rearrange("(gp g2) p c -> (g2 p) gp c", g2=2), [], [bb.s])
            SDMA(hin[:, d], bass.AP(st_s5.tensor, st_s5[l, d].offset, [[2, 128], [256, 16], [1, 2]]), [], [hin.s])
        for s_ in range(8):
            SDMA(dsk[16 * s_:16 * s_ + 16, :], s5_d[l].rearrange("g c -> c g"), [], [dsk.s])
        for d in range(2):
            for ri, src in enumerate((s5_c_re, s5_c_im)):
                x0 = (d * 2 + ri) * 4
                SDMA(cnat[:, x0:x0 + 4, :], src[l, d].rearrange("(ct g8) c p -> (g8 c) ct p", g8=8), [], [cnat.s])
        for d in range(2):
            for ri in range(2):
                bk = psum()
                for ct in range(4):
                    x = (d * 2 + ri) * 4 + ct
                    MM(bk[0:64, ct * 64:(ct + 1) * 64], cnat[:, x, :], R_even, True, True, [cnat.s, c3.s], [bk.s], tp=(0, 0))
                    MM(bk[64:128, ct * 64:(ct + 1) * 64], cnat[:, x, :], R_odd, True, True, [cnat.s, c3.s], [bk.s], tp=(0, 64))
                ACTop(craw[:, d, ri].rearrange("p a b -> p (a b)"), bk[:, 0:256], AF.Copy, [bk.s], [craw.s])

        if _stop(1):
            return
        for d in range(2):
            lr, li, dt_, a_, th_ = (par[:, d, i, :] for i in range(5))
            TSop(lr, lr, -1e-4, None, ALU.min, None, [par.s], [par.s])
            ACTop(dt_, dt_, AF.Exp, [par.s], [par.s])
            TTop(a_, lr, dt_, ALU.mult, [par.s], [par.s])
            TTop(th_, li, dt_, ALU.mult, [par.s], [par.s])
            TSop(th_, th_, 1.0 / TWO_PI, None, ALU.mult, None, [par.s], [par.s])

        def powers(out_r, out_i, out_m, a_ap, th_ap, evals, ng_, ne, tr, ti_, tk_i, tk_f, rs, ws):
            sh = [128, ng_, ne]
            ev_b = evals.unsqueeze(1).broadcast_to(sh)
            TTop(tr, th_ap.unsqueeze(2).broadcast_to(sh), ev_b, ALU.mult, rs + ws, ws)
            TSop(ti_, tr, 0.25, None, ALU.add, None, ws, ws)
            for (dst, src) in ((out_i, tr), (out_r, ti_)):
                CPY(tk_i, src, ws, ws)
                CPY(tk_f, tk_i, ws, ws)
                TTop(tk_f, src, tk_f, ALU.subtract, ws, ws)
                ACTop(dst, tk_f, AF.Sin, ws, ws, scale=TWO_PI)
            TTop(tr, a_ap.unsqueeze(2).broadcast_to(sh), ev_b, ALU.mult, rs + ws, ws)
            ACTop(out_m, tr, AF.Exp, ws, ws)

        for d in range(2):
            ws = [pw.s, pri.s] + [p_.s for p_ in prs]
            tkf_ = cnat[:].rearrange("p a b -> p (a b)")[:, 0:272].rearrange("p (a b) -> p a b", b=17)
            powers(pw[:, d, 0], pw[:, d, 1], prs[2][:], par[:, d, 3, :], par[:, d, 4, :], EV, 16, 17, prs[0][:], prs[1][:],
                   pri[:], tkf_, [par.s, c3.s, craw.s], ws + [cnat.s])
            TTop(pw[:, d, 0], pw[:, d, 0], prs[2][:], ALU.mult, ws, ws)
            TTop(pw[:, d, 1], pw[:, d, 1], prs[2][:], ALU.mult, ws, ws)

        for d in range(2):
            lr, li = par[:, d, 0, :], par[:, d, 1, :]
            abr, abi = pw[:, d, 0, :, 9], pw[:, d, 1, :, 9]
            nr, den, zr, zi, t1, t2 = (sm[i][:] for i in range(6))
            ws = [s_.s for s_ in sm]
            rs = [par.s, pw.s] + ws
            TSop(nr, abr, -1.0, None, ALU.add, None, rs, ws)
            TTop(t1, lr, lr, ALU.mult, rs, ws)
            TTop(t2, li, li, ALU.mult, rs, ws)
            TTop(den, t1, t2, ALU.add, rs, ws)
            P.dve(lambda e, den=den: e.reciprocal(out=den, in_=den), reads=rs, writes=ws)
            TTop(t1, nr, lr, ALU.mult, rs, ws)
            TTop(t2, abi, li, ALU.mult, rs, ws)
            TTop(zr, t1, t2, ALU.add, rs, ws)
            TTop(zr, zr, den, ALU.mult, rs, ws)
            TTop(t1, abi, lr, ALU.mult, rs, ws)
            TTop(t2, nr, li, ALU.mult, rs, ws)
            TTop(zi, t1, t2, ALU.subtract, rs, ws)
            TTop(zi, zi, den, ALU.mult, rs, ws)
            zrb = zr.unsqueeze(2).broadcast_to([128, 16, 16])
            zib = zi.unsqueeze(2).broadcast_to([128, 16, 16])
            cf = cnat[:].rearrange("p a b -> p (a b)")
            t3 = cf[:, 0:256].rearrange("p (a b) -> p a b", b=16)
            t4 = cf[:, 256:512].rearrange("p (a b) -> p a b", b=16)
            t5 = cf[:, 512:768].rearrange("p (a b) -> p a b", b=16)
            br_, bi_ = bb[:, d, 0], bb[:, d, 1]
            rs2 = rs + [bb.s, cnat.s, craw.s]
            ws2 = [bb.s, cnat.s]
            TTop(t3, br_, zrb, ALU.mult, rs2, ws2)
            TTop(t4, bi_, zib, ALU.mult, rs2, ws2)
            TTop(t5, br_, zib, ALU.mult, rs2, ws2)
            TTop(t3, t3, t4, ALU.subtract, rs2, ws2)
            TTop(t4, bi_, zrb, ALU.mult, rs2, ws2)
            TTop(bi_, t4, t5, ALU.add, rs2, ws2)
            CPY(br_, t3, rs2, ws2)

        if l == 0:
            dbg("par", par[:].rearrange("p a b c -> p (a b c)"), par.s, [128, 160])
            dbg("bb", bb[:].rearrange("p a b c d -> p (a b c d)"), bb.s, [128, 1024])
            dbg("pw", pw[:].rearrange("p a b c d -> p (a b c d)"), pw.s, [128, 1088])
            dbg("craw", craw[:].rearrange("p a b c d -> p (a b c d)"), craw.s, [128, 1024])
        if _stop(2):
            return
        old_r2 = [uT.s, cnat.s, pri.s] + [p_.s for p_ in prs]
        ar["off"] = R2
        uT = aalloc([4, T], BF16, "uT2")
        LA = aalloc([4, 128], F32, "LA")
        LB = aalloc([4, 128], F32, "LB")
        mnat = [aalloc([4, 128], BF16, "mnat%d" % i) for i in range(2)]
        Dacc = aalloc([8, 128], F32, "Dacc")
        new_r2 = [uT.s, LA.s, LB.s, mnat[0].s, mnat[1].s, Dacc.s]
        P.dve(lambda e: e.memset(fence_t[:], 0.0), reads=[], writes=old_r2 + new_r2 + [fence_t.s])

        def lifted(dst_r, dst_i, coef_r, coef_i, d, e_idx, conj_sign, gp0, ws):
            sh = [128, 4, 8, 16]
            pr = pw[:, d, 0, gp0:gp0 + 4, e_idx].unsqueeze(3).broadcast_to(sh)
            pi_ = pw[:, d, 1, gp0:gp0 + 4, e_idx].unsqueeze(3).broadcast_to(sh)
            cr = coef_r[:, gp0:gp0 + 4, :].unsqueeze(2).broadcast_to(sh)
            ci = coef_i[:, gp0:gp0 + 4, :].unsqueeze(2).broadcast_to(sh)
            t1 = LA[:].rearrange("p a (j c) -> p a j c", c=16)
            t2 = LB[:].rearrange("p a (j c) -> p a j c", c=16)
            rs = [pw.s, bb.s, craw.s, LA.s, LB.s]
            TTop(t1, cr, pr, ALU.mult, rs, [LA.s])
            TTop(t2, ci, pi_, ALU.mult, rs, [LB.s])
            TTop(dst_r.rearrange("p a (j c) -> p a j c", c=16), t1, t2, ALU.subtract, rs, ws)
            TTop(t1, cr, pi_, ALU.mult, rs, [LA.s])
            TTop(t2, ci, pr, ALU.mult, rs, [LB.s])
            if conj_sign > 0:
                TTop(dst_i.rearrange("p a (j c) -> p a j c", c=16), t1, t2, ALU.add, rs, ws)
            else:
                STTop(dst_i.rearrange("p a (j c) -> p a j c", c=16), t1, -1.0, t2, ALU.mult, ALU.subtract, rs, ws)

        E_B = [slice(15, 7, -1), slice(8, 16)]
        E_C = [slice(9, 17), slice(16, 8, -1)]
        E_N = [slice(7, None, -1), slice(0, 8)]

        for d in range(2):
            for q in range(4):
                gp0 = q * 4
                cv, g8_, cs_ = Ct_(gp0)
                lifted(cv[:, g8_:g8_ + 4, d, 0, :], cv[:, g8_:g8_ + 4, d, 1, :], craw[:, d, 0], craw[:, d, 1], d, E_C[d], -1, gp0, [cs_])
        for q in range(4):
            gp0 = q * 4
            for d in range(2):
                lifted(mnat[0][:], mnat[1][:], bb[:, d, 0], bb[:, d, 1], d, E_N[d], +1, gp0, [mnat[0].s, mnat[1].s])
                for gi in range(8):
                    gl, g2 = gi // 2, gi % 2
                    gp = gp0 + gl
                    cv, g8_, cs_ = Ct_(gp)
                    bk = psum()
                    for ri in range(2):
                        MM(bk[:, 0:128], mnat[ri][64 * g2:64 * g2 + 64, gl, :], cv[64 * g2:64 * g2 + 64, g8_, d, ri, :],
                           (ri == 0), (ri == 1), [mnat[ri].s, cs_], [bk.s])
                    msk = M5[:, d * 128:(d + 1) * 128]
                    if d == 0:
                        TTop(Dacc[:, gi, :], bk[:, 0:128], msk, ALU.mult, [bk.s, c3.s], [Dacc.s])
                    else:
                        tmpv = LA[:, gl, :] if g2 == 0 else LB[:, gl, :]
                        tmps = LA.s if g2 == 0 else LB.s
                        TTop(tmpv, bk[:, 0:128], msk, ALU.mult, [bk.s, c3.s, mnat[0].s, mnat[1].s], [tmps])
                        TTop(Dacc[:, gi, :], Dacc[:, gi, :], tmpv, ALU.add, [tmps, Dacc.s], [Dacc.s])
                        g = 2 * gp + g2
                        STTop(DtV[:, g, :], ident_f, dsk[:, g:g + 1], Dacc[:, gi, :], ALU.mult, ALU.add,
                              [cst_f.s, dsk.s, Dacc.s], [DtS.s])

        if _stop(3):
            return
        wu, wus = wload(w_in[l][:, 2560:3072].rearrange("(k p) c -> p k c", p=128), KD, 512, slot=0)
        for ct in range(4):
            for hf in range(2):
                bk = proj_feat(wu, wus, ct * 128, hf)
                ACTop(uT[:, ct, hf * 512:(hf + 1) * 512], bk[:], AF.Copy, [bk.s], [uT.s])
        for g0 in range(0, NG, 4):
            bk = psum()
            for gi in range(4):
                g = g0 + gi
                ct, g8 = g // 8, g % 8
                for s_ in range(8):
                    MM(bk[:, gi * 128:(gi + 1) * 128], asel[:, g8, 112 - 16 * s_:240 - 16 * s_],
                       uT[:, ct, s_:T:8], (gi == 0 and s_ == 0), (s_ == 7), [asel.s, uT.s], [bk.s])
            ACTop(UT[:, g0:g0 + 4, :], bk[:].rearrange("p (g n) -> p g n", n=128), AF.Copy, [bk.s], [UT.s])

        if _stop(4):
            return
        old_r2 = new_r2
        ar["off"] = R2
        XR = aalloc([4, NCK], F32, "XR")
        XI = aalloc([4, NCK], F32, "XI")
        A1 = aalloc([4, NCK], F32, "A1")
        B2 = aalloc([4, NCK], F32, "B2")
        C2 = aalloc([4, NCK], F32, "C2")
        RC = aalloc([4, NCK], F32, "RC")
        LA = aalloc([4, 128], F32, "LA2")
        LB = aalloc([4, 128], F32, "LB2")
        mnat = [aalloc([4, 128], BF16, "mnat2_%d" % i) for i in range(2)]
        new_r2 = [XR.s, XI.s, A1.s, B2.s, C2.s, RC.s, LA.s, LB.s, mnat[0].s, mnat[1].s]
        tsc = [A1, B2, C2]
        tsi = RC
        P.dve(lambda e: e.memset(fence_t[:], 0.0), reads=[], writes=old_r2 + new_r2 + [fence_t.s])

        def tables(d, tsc, tsi):
            ws = [tab.s, tsi.s] + [t_.s for t_ in tsc]
            for q in range(4):
                g_ = slice(q * 4, q * 4 + 4)
                powers(tab[:, 0, g_, :], tab[:, 1, g_, :], tab[:, 2, g_, :], par[:, d, 3, g_], par[:, d, 4, g_], K8, 4, 65,
                       tsc[0][:, :, 0:65], tsc[1][:, :, 0:65], tsi[:, :, 0:65].bitcast(I32), tsc[2][:, :, 0:65], [par.s, c3.s], ws)

        def seg_views(buf, gsl, n0, n1, d, shift):
            if d == 0:
                if shift == 0:
                    return buf[:, gsl, n0:n1]
                return buf[:, gsl, n0 + 1:n1] if shift > 0 else buf[:, gsl, n0:n1 - 1]
            lo = None if n0 == 0 else n0 - 1
            if shift == 0:
                return buf[:, gsl, n1 - 1:lo:-1]
            if shift > 0:
                return buf[:, gsl, n1 - 2:lo:-1]
            return buf[:, gsl, n1 - 1:n0:-1]

        for d in range(2):
            tables(d, tsc, tsi)
            if l == 0 and d == 0:
                dbg("tab", tab[:].rearrange("p a b c -> p (a b c)"), tab.s, [128, 3 * 16 * 65])
            if _stop(4.2):
                return
            for q in range(4):
                gp0 = q * 4
                tsl = slice(gp0, gp0 + 4)
                lifted(mnat[0][:], mnat[1][:], bb[:, d, 0], bb[:, d, 1], d, E_B[d], +1, gp0, [mnat[0].s, mnat[1].s])
                for ri in range(2):
                    bk = psum()
                    for gl in range(4):
                        MM(bk[:, gl * 128:(gl + 1) * 128], mnat[ri][:, gl, :], ident_b[:], True, True,
                           [mnat[ri].s, ident_b.s], [bk.s])
                    ACTop(Bt[:, 2 * gp0:2 * gp0 + 8, ri, :], bk[:].rearrange("p (g q) -> p g q", q=64), AF.Copy, [bk.s], [Bt.s])
                if _stop(4.3):
                    return
                for gl in range(4):
                    gp = gp0 + gl
                    bk = psum()
                    for g2 in range(2):
                        g = 2 * gp + g2
                        for ri in range(2):
                            MM(bk[64 * g2:64 * g2 + 64, ri * 128:(ri + 1) * 128], Bt[:, g, ri, :], UT[:, g, :], True, True,
                               [Bt.s, UT.s], [bk.s], tp=(0, 64 * g2))
                    ACTop(XR[:, gl, :], bk[:, 0:128], AF.Copy, [bk.s], [XR.s])
                    ACTop(XI[:, gl, :], bk[:, 128:256], AF.Copy, [bk.s], [XI.s])
                if l == 0 and d == 0:
                    dbg("XR%d" % q, XR[:].rearrange("p a b -> p (a b)"), XR.s, [128, 512])
                    dbg("Bt%d" % q, Bt[:, 2 * gp0:2 * gp0 + 8].rearrange("p a b c -> p (a b c)"), Bt.s, [128, 1024], BF16)
                    dbg("UT%d" % q, UT[:, 2 * gp0:2 * gp0 + 8].rearrange("p a b -> p (a b)"), UT.s, [128, 1024], BF16)
                if _stop(4.4):
                    return
                CPY(RC[:], tab[:, 2, tsl, 1:2].broadcast_to([128, 4, NCK]), [tab.s], [RC.s])
                for (n0, n1) in SEG8:
                    first = n0 if d == 0 else n1 - 1
                    MSET(RC[:, :, first:first + 1], 0.0, [RC.s], [RC.s])
                gsl = slice(0, 4)
                for (n0, n1) in SEG8:
                    L = n1 - n0
                    xr, xi = seg_views(XR, gsl, n0, n1, d, 0), seg_views(XI, gsl, n0, n1, d, 0)
                    a1, b2, c2 = seg_views(A1, gsl, n0, n1, d, 0), seg_views(B2, gsl, n0, n1, d, 0), seg_views(C2, gsl, n0, n1, d, 0)
                    cs_, sn_ = tab[:, 0, tsl, 1:L + 1], tab[:, 1, tsl, 1:L + 1]
                    rs = [XR.s, XI.s, tab.s, A1.s, B2.s, C2.s]
                    TTop(a1, xr, cs_, ALU.mult, rs, [A1.s])
                    TTop(c2, xi, sn_, ALU.mult, rs, [C2.s])
                    TTop(a1, a1, c2, ALU.add, rs, [A1.s])
                    TTop(b2, xi, cs_, ALU.mult, rs, [B2.s])
                    TTop(c2, xr, sn_, ALU.mult, rs, [C2.s])
                    TTop(b2, b2, c2, ALU.subtract, rs, [B2.s])

                if _stop(4.5):
                    return

                def fl(t_):
                    v = t_[:].rearrange("p a b -> p (a b)")
                    return v if d == 0 else v[:, ::-1]
                o_r, o_i, i_r, i_i, cf_ = fl(XR), fl(XI), fl(A1), fl(B2), fl(RC)
                P.dve(lambda e, o_r=o_r, i_r=i_r, cf_=cf_: e.tensor_tensor_scan(out=o_r, data0=cf_, data1=i_r, initial=0.0,
                                                                                  op0=ALU.mult, op1=ALU.add),
                      reads=[RC.s, A1.s], writes=[XR.s])
                P.dve(lambda e, o_i=o_i, i_i=i_i, cf_=cf_: e.tensor_tensor_scan(out=o_i, data0=cf_, data1=i_i, initial=0.0,
                                                                                  op0=ALU.mult, op1=ALU.add),
                      reads=[RC.s, B2.s], writes=[XI.s])
                if _stop(4.6):
                    return
                for si_, (n0, n1) in enumerate(SEG8):
                    L = n1 - n0
                    gr, gi_ = seg_views(XR, gsl, n0, n1, d, -1), seg_views(XI, gsl, n0, n1, d, -1)
                    a1, b2 = seg_views(A1, gsl, n0, n1, d, 1), seg_views(B2, gsl, n0, n1, d, 1)
                    hr = seg_views(Hb[:, d, 0], tsl, n0, n1, d, 1)
                    hi = seg_views(Hb[:, d, 1], tsl, n0, n1, d, 1)
                    cs_, sn_ = tab[:, 0, tsl, 1:L], tab[:, 1, tsl, 1:L]
                    rs = [XR.s, XI.s, tab.s, A1.s, B2.s]
                    TTop(a1, gr, cs_, ALU.mult, rs, [A1.s])
                    TTop(b2, gi_, sn_, ALU.mult, rs, [B2.s])
                    TTop(hr, a1, b2, ALU.subtract, rs, [Hb.s])
                    TTop(a1, gr, sn_, ALU.mult, rs, [A1.s])
                    TTop(b2, gi_, cs_, ALU.mult, rs, [B2.s])
                    TTop(hi, a1, b2, ALU.add, rs, [Hb.s])
                    first = n0 if d == 0 else n1 - 1
                    MSET(Hb[:, d, :, tsl, first:first + 1], 0.0, [Hb.s], [Hb.s])
                    last = n1 - 1 if d == 0 else n0
                    glr, gli = XR[:, :, last], XI[:, :, last]
                    cL, sL = tab[:, 0, tsl, L], tab[:, 1, tsl, L]
                    t1, t2 = sm[6][:, 0:4], sm[7][:, 0:4]
                    if si_ < 2:
                        dr, di = fst[:, si_, d, tsl, 0], fst[:, si_, d, tsl, 1]
                        dsl = fst.s
                    else:
                        dr, di = sloc[:, d, 0, tsl], sloc[:, d, 1, tsl]
                        dsl = sloc.s
                    rs = [XR.s, XI.s, tab.s, sm[6].s, sm[7].s, dsl]
                    TTop(t1, glr, cL, ALU.mult, rs, [sm[6].s])
                    TTop(t2, gli, sL, ALU.mult, rs, [sm[7].s])
                    TTop(dr, t1, t2, ALU.subtract, rs, [dsl])
                    TTop(t1, glr, sL, ALU.mult, rs, [sm[6].s])
                    TTop(t2, gli, cL, ALU.mult, rs, [sm[7].s])
                    TTop(di, t1, t2, ALU.add, rs, [dsl])
        if _stop(4.8):
            return
        for seq in range(2):
            for d in range(2):
                SDMA(bass.AP(ns_s5.tensor, ns_s5[seq, l, d].offset, [[2, 128], [256, 16], [1, 2]]), fst[:, seq, d], [fst.s], [])

        if _stop(5):
            return
        ccs_in, ccs_out = Slot("cc5in"), Slot("cc5out")
        SDMA(cc5_in[l], sloc[:].rearrange("p a b c -> p (a b c)"), [sloc.s], [ccs_in])
        P.dma("pool", lambda e: e.collective_compute("AllGather", ALU.bypass, replica_groups=[[0, 1, 2, 3], [4, 5, 6, 7]],
                                                     ins=[cc5_in[l]], outs=[cc5_out[l]]),
              reads=[ccs_in], writes=[ccs_out], inc=1)
        SDMA(gat5[:].rearrange("p r a b c -> p r (a b c)"), cc5_out[l].rearrange("(r p) c -> p r c", p=128), [ccs_out], [gat5.s])

        old_w0 = [Bt.s, bb.s, craw.s, pw.s] + new_r2
        ar["off"] = W0
        DH = aalloc([16, 2, 2, 64], BF16, "DH")
        W1 = ar["off"]
        tsc2 = [aalloc([4, 128], F32, "tscb%d" % i) for i in range(3)]
        tsi2 = aalloc([4, 128], F32, "tsib")
        TRt = aalloc([16, 64], F32, "TRt")
        TIt = aalloc([16, 64], F32, "TIt")
        U1 = aalloc([16, 64], F32, "U1")
        U2 = aalloc([16, 64], F32, "U2")
        pc = [aalloc([16], F32, "pc%d" % i) for i in range(6)]
        new_w0 = [DH.s, tsi2.s, TRt.s, TIt.s, U1.s, U2.s] + [t_.s for t_ in tsc2] + [p_.s for p_ in pc]
        P.dve(lambda e: e.memset(fence_t[:], 0.0), reads=[], writes=old_w0 + new_w0 + [fence_t.s])

        def cmul(dr, di, ar_, ai_, br_, bi_, t1, t2, rs, ws):
            TTop(t1, ar_, br_, ALU.mult, rs, ws)
            TTop(t2, ai_, bi_, ALU.mult, rs, ws)
            TTop(dr, t1, t2, ALU.subtract, rs, ws)
            TTop(t1, ar_, bi_, ALU.mult, rs, ws)
            TTop(t2, ai_, br_, ALU.mult, rs, ws)
            TTop(di, t1, t2, ALU.add, rs, ws)

        for d in range(2):
            tables(d, tsc2, tsi2)
            atr, ati, cr_, ci_, t1, t2 = (p_[:] for p_ in pc)
            ws = [p_.s for p_ in pc] + [sm[0].s, sm[1].s]
            rs = [tab.s, gat5.s, hin.s, hent.s, cst_f.s] + ws
            TTop(atr, tab[:, 0, :, 64], tab[:, 2, :, 64], ALU.mult, rs, ws)
            TTop(ati, tab[:, 1, :, 64], tab[:, 2, :, 64], ALU.mult, rs, ws)
            ranks = [0, 1, 2, 3] if d == 0 else [3, 2, 1, 0]
            CPY(cr_, hin[:, d, :, 0], rs, ws)
            CPY(ci_, hin[:, d, :, 1], rs, ws)
            TSop(hent[:, d, 0], cr_, cst_f[:, 384 + ranks[0]:385 + ranks[0]], None, ALU.mult, None, rs, [hent.s])
            TSop(hent[:, d, 1], ci_, cst_f[:, 384 + ranks[0]:385 + ranks[0]], None, ALU.mult, None, rs, [hent.s])
            for i in range(3):
                r = ranks[i]
                nr_, ni_ = sm[0][:], sm[1][:]
                cmul(nr_, ni_, atr, ati, cr_, ci_, t1, t2, rs, ws)
                TTop(cr_, nr_, gat5[:, r, d, 0, :], ALU.add, rs, ws)
                TTop(ci_, ni_, gat5[:, r, d, 1, :], ALU.add, rs, ws)
                rn = ranks[i + 1]
                STTop(hent[:, d, 0], cr_, cst_f[:, 384 + rn:385 + rn], hent[:, d, 0], ALU.mult, ALU.add, rs, [hent.s])
                STTop(hent[:, d, 1], ci_, cst_f[:, 384 + rn:385 + rn], hent[:, d, 1], ALU.mult, ALU.add, rs, [hent.s])
            rs = [tab.s, hent.s, TRt.s, TIt.s, U1.s, U2.s]
            TTop(TRt[:], tab[:, 0, :, 0:64], tab[:, 2, :, 0:64], ALU.mult, rs, [TRt.s])
            TTop(TIt[:], tab[:, 1, :, 0:64], tab[:, 2, :, 0:64], ALU.mult, rs, [TIt.s])
            her = hent[:, d, 0].unsqueeze(2).broadcast_to([128, 16, 64])
            hei = hent[:, d, 1].unsqueeze(2).broadcast_to([128, 16, 64])
            dhr = DH[:, :, d, 0, :] if d == 0 else DH[:, :, d, 0, ::-1]
            dhi = DH[:, :, d, 1, :] if d == 0 else DH[:, :, d, 1, ::-1]
            TTop(U1[:], TRt[:], her, ALU.mult, rs, [U1.s])
            TTop(U2[:], TIt[:], hei, ALU.mult, rs, [U2.s])
            TTop(dhr, U1[:], U2[:], ALU.subtract, rs, [DH.s])
            TTop(U1[:], TRt[:], hei, ALU.mult, rs, [U1.s])
            TTop(U2[:], TIt[:], her, ALU.mult, rs, [U2.s])
            TTop(dhi, U1[:], U2[:], ALU.add, rs, [DH.s])

        if _stop(6):
            return
        old_w1 = new_w0[1:]
        ar["off"] = W1
        YA = aalloc([NG, NCK], BF16, "YA")
        yT = aalloc([4, T], BF16, "yT5")
        gl_t = [aalloc([512], F32, "gl%d" % i) for i in range(4)]
        new_w1 = [YA.s, yT.s] + [g_.s for g_ in gl_t]
        P.dve(lambda e: e.memset(fence_t[:], 0.0), reads=[], writes=old_w1 + new_w1 + [fence_t.s])

        for g0 in range(0, NG, 4):
            bk = psum()
            for gi in range(4):
                g = g0 + gi
                gp, g2 = g // 2, g % 2
                cv, g8_, cs_ = Ct_(gp)
                cols = slice(gi * 128, (gi + 1) * 128)
                MM(bk[:, cols], DtV[:, g, :], UT[:, g, :], (gi == 0), False, [DtS.s, UT.s], [bk.s])
                for d in range(2):
                    for ri in range(2):
                        MM(bk[:, cols], cv[64 * g2:64 * g2 + 64, g8_, d, ri, :], Hb[64 * g2:64 * g2 + 64, d, ri, gp, :], False, False,
                           [cs_, Hb.s], [bk.s])
                for d in range(2):
                    for ri in range(2):
                        MM(bk[:, gi * 128 + 64:(gi + 1) * 128], cv[64 * g2:64 * g2 + 64, g8_, d, ri, :],
                           DH[64 * g2:64 * g2 + 64, gp, d, ri, :], False, (d == 1 and ri == 1), [cs_, DH.s], [bk.s])
            xs_, sq_, u_, sg_ = gl_t
            ACTop(xs_[:], bk[:], AF.Copy, [bk.s], [xs_.s])
            ACTop(sq_[:], bk[:], AF.Square, [bk.s], [sq_.s])
            TSop(sq_[:], sq_[:], 0.044715, 1.0, ALU.mult, ALU.add, [sq_.s], [sq_.s])
            TTop(u_[:], sq_[:], xs_[:], ALU.mult, [sq_.s, xs_.s], [u_.s])
            ACTop(sg_[:], u_[:], AF.Sigmoid, [u_.s], [sg_.s], scale=2.0 * math.sqrt(2.0 / math.pi))
            TTop(YA[:, g0:g0 + 4, :].rearrange("p g n -> p (g n)"), xs_[:], sg_[:], ALU.mult, [xs_.s, sg_.s], [YA.s])

        for ct in range(4):
            for t0 in range(0, 8, 4):
                bk = psum()
                for ti in range(4):
                    t_ = t0 + ti
                    for g8 in range(8):
                        g = ct * 8 + g8
                        MM(bk[:, ti * 128:(ti + 1) * 128], asel[:, t_, 112 - 16 * g8:240 - 16 * g8],
                           YA[:, g, :], (ti == 0 and g8 == 0), (g8 == 7), [asel.s, YA.s], [bk.s])
                ACTop(yT[:, ct, :].rearrange("p (n t) -> p t n", t=8)[:, t0:t0 + 4, :],
                      bk[:].rearrange("p (t n) -> p t n", n=128), AF.Copy, [bk.s], [yT.s])

        wgl, wgls = wload(s5_w_glu[l].rearrange("(k p) c -> p k c", p=128), 4, 512, slot=0)
        for c2 in range(4):
            for hf in range(2):
                bk = psum()
                for ct in range(4):
                    MM(bk[:], wgl[:, ct, c2 * 128:(c2 + 1) * 128], yT[:, ct, hf * 512:(hf + 1) * 512], (ct == 0), (ct == 3),
                       [wgls, yT.s], [bk.s])
                sgl = gl_t[(c2 * 2 + hf) % 2]
                ACTop(sgl[:], bk[:], AF.Sigmoid, [bk.s], [sgl.s])
                TTop(mixT[:, 4 + c2, hf * 512:(hf + 1) * 512], yT[:, c2, hf * 512:(hf + 1) * 512], sgl[:], ALU.mult,
                     [yT.s, sgl.s], [mixs[4 + c2][hf]])


    for l in range(DEPTH):
        compute_mod(l)
        norm_mod(l, 0, 0)
        if STAGE["s5"]:
            s5(l)
        else:
            for k in range(4, 8):
                for h in range(2):
                    P.dve(lambda e, k=k, h=h: e.memset(mixT[:, k, h * 512:(h + 1) * 512], 0.0), writes=[mixs[k][h]])
        if STAGE["hg"]:
            hgrn(l)
        else:
            for k in range(0, 4):
                for h in range(2):
                    P.dve(lambda e, k=k, h=h: e.memset(mixT[:, k, h * 512:(h + 1) * 512], 0.0), writes=[mixs[k][h]])
        residual_proj(w_out[l], l, 16, mixT, mixs, KD)
        norm_mod(l, 1, 24)
        ffn(l)

    nfin32 = sb([128, KD], F32, "nfin32")
    P.dve(lambda e: e.tensor_scalar(out=nfin32[:], in0=nfin[:], scalar1=32.0, scalar2=None, op0=ALU.mult),
          reads=[nfin.s], writes=[nfin32.s])
    aphase()
    ytok = [aalloc([D], F32, "ytok%d" % j) for j in range(4)]
    yT = aalloc([512], F32, "yT")
    afence()
    for h in range(2):
        rms_stats(h)
        for k in range(KD):
            P.dve(lambda e, k=k, h=h: e.scalar_tensor_tensor(
                out=yT[:], in0=xT[:, k, h * 512:(h + 1) * 512], scalar=nfin32[:, k:k + 1], in1=rstd[:],
                op0=ALU.mult, op1=ALU.mult), reads=[xs[k][h], nfin32.s, rstd.s], writes=[yT.s])
            bk = psum()
            for j in range(4):
                P.pe(lambda e, bk=bk, j=j: e.transpose(bk[:, j * 128:(j + 1) * 128], yT[:, j * 128:(j + 1) * 128], ident_f),
                     reads=[yT.s, cst_f.s], writes=[bk.s])
            for j in range(4):
                P.act(lambda e, bk=bk, j=j, k=k: e.activation(out=ytok[j][:, k * 128:(k + 1) * 128],
                                                               in_=bk[:, j * 128:(j + 1) * 128], func=AF.Copy),
                      reads=[bk.s], writes=[ytok[j].s])
        for j in range(4):
            tt = h * 4 + j
            P.dma("sp", lambda e, j=j, tt=tt: e.dma_start(out=y_out[tt * 128:(tt + 1) * 128, :], in_=ytok[j][:]),
                  reads=[ytok[j].s], writes=[])

    with nc.allow_non_contiguous_dma(reason="small strided parameter loads"):
        P.emit(st)
    st.close()
    return nc


def _consts(core):
    q = core % 4
    c = np.zeros((128, 512), np.float32)
    c[:, 0:128] = np.eye(128, dtype=np.float32)
    j = np.arange(128)[:, None]
    i = np.arange(128)[None, :]
    same = (j // CH) == (i // CH)
    c[:, 128:256] = (same & (j <= i)).astype(np.float32)
    c[:, 256:384] = (same & (j >= i)).astype(np.float32)
    c[:, 384 + q] = 1.0
    nf = D // 4
    p = np.arange(128, dtype=np.float32)
    for par in range(2):
        kf = par * 128 + p
        c[:, 388 + par] = (1.0 / (np.float32(10000.0) ** (kf / np.float32(nf)))).astype(np.float32)
    t = q * 512 + np.arange(512)
    c2 = np.zeros((128, 1024), np.float32)
    c2[:, 0:512] = (t // 64).astype(np.float32)[None, :]
    c2[:, 512:1024] = (t % 64).astype(np.float32)[None, :]
    c3 = np.zeros((128, 1024), np.float32)
    for p_ in range(128):
        c3[p_, 112 + p_ % 16] = 1.0
        c3[p_, 240 + p_ // 16] = 1.0
    for g4 in range(4):
        for cc in range(16):
            c3[(2 * g4) * 16 + cc, 480 + g4 * 16 + cc] = 1.0
            c3[(2 * g4 + 1) * 16 + cc, 544 + g4 * 16 + cc] = 1.0
    c3[:, 608:625] = np.arange(-8, 9, dtype=np.float32)[None, :]
    c3[:, 640:705] = (8.0 * np.arange(65, dtype=np.float32))[None, :]
    sI = (np.arange(128) // 16)[:, None]
    tI = (np.arange(128) // 16)[None, :]
    c3[:, 768:896] = (sI <= tI).astype(np.float32)
    c3[:, 896:1024] = (sI >= tI).astype(np.float32)
    return c, c2, c3


_NC_CACHE = {}


def kernel(**inp):
    inp = {k: np.asarray(v) for k, v in inp.items()}
    if "nc" not in _NC_CACHE:
        _NC_CACHE["nc"] = build_program()
    nc = _NC_CACHE["nc"]
    xp = inp["x_prompt"]
    xsm = inp["x_sample"]
    in_maps = []
    shared = {k: np.ascontiguousarray(inp[k], dtype=np.float32) for k in
              ("w_mod", "b_mod", "norm_mix", "norm_ffn", "norm_final", "w_in", "w_out", "w_gate", "w_up", "w_down",
               "hg_lb_logits", "hg_norm", "s5_lam_re", "s5_lam_im", "s5_log_dt", "s5_b_re", "s5_b_im", "s5_c_re", "s5_c_im",
               "s5_d", "s5_w_glu")}
    for core in range(8):
        b, q = core // 4, core % 4
        x = np.concatenate([xp[2 * core], xp[2 * core + 1], xsm[b, q * 512:(q + 1) * 512]], axis=0)
        cond = np.stack([inp["c_ctx"], inp["c"][b]], axis=0)
        c1, c2, c3 = _consts(core)
        m = dict(shared)
        m.update({"x": np.ascontiguousarray(x, dtype=np.float32), "cond": np.ascontiguousarray(cond, dtype=np.float32),
                  "st_hg": np.ascontiguousarray(inp["state_hgrn"][b], dtype=np.float32), "cst": c1, "cst2": c2, "cst3": c3,
                  "st_s5": np.ascontiguousarray(inp["state_s5"][b], dtype=np.float32)})
        in_maps.append(m)
    res = run_bass_kernel_spmd(nc, in_maps, core_ids=list(range(8)))
    outs = res.results
    _DBG["outs"] = outs
    y_prompt = np.zeros_like(xp)
    y_sample = np.zeros_like(xsm)
    ns_hg = np.zeros((16, DEPTH, 2, NH, DK, DK), np.float32)
    ns_s5 = np.zeros((16, DEPTH, 2, NG, SP, 2), np.float32)
    for core in range(8):
        b, q = core // 4, core % 4
        y = outs[core]["y"]
        y_prompt[2 * core] = y[0:256]
        y_prompt[2 * core + 1] = y[256:512]
        y_sample[b, q * 512:(q + 1) * 512] = y[512:1024]
        ns_hg[2 * core:2 * core + 2] = outs[core]["ns_hg"]
        ns_s5[2 * core:2 * core + 2] = outs[core]["ns_s5"]
    return (y_prompt, y_sample, ns_hg, ns_s5)
```

```python
import math
from contextlib import ExitStack

import numpy as np
import concourse.bass as bass
import concourse.mybir as mybir
from concourse.bass_utils import run_bass_kernel_spmd

F32 = mybir.dt.float32
BF16 = mybir.dt.bfloat16
AF = mybir.ActivationFunctionType
ALU = mybir.AluOpType

ENGS = ("pe", "act", "dve", "pool", "sp")
SAME_ENGINE_RAW_DIST = 2


class Slot:
    __slots__ = ("name", "w", "r", "al")

    def __init__(self, name):
        self.name = name
        self.w = None
        self.r = []
        self.al = [self]


def alias(*slots):
    grp = []
    for s in slots:
        for a in s.al:
            if a not in grp:
                grp.append(a)
    for s in grp:
        s.al = grp


class Op:
    __slots__ = ("eng", "fn", "deps", "raw", "dma", "idx", "milestone", "mcount", "dsem", "dval", "inc", "eidx")


class Prog:
    def __init__(self, nc, n_dma_sems=8, sync_same_engine=True):
        self.nc = nc
        self.ops = []
        self.n_dma_sems = n_dma_sems
        self.sync_same = sync_same_engine

    def add(self, eng, fn, reads=(), writes=(), dma=False, inc=16):
        op = Op()
        op.eng, op.fn, op.dma, op.inc = eng, fn, dma, inc
        op.deps = set()
        op.raw = set()
        op.milestone = False
        op.mcount = 0
        op.dsem = None
        op.dval = 0
        op.idx = len(self.ops)
        for s0 in reads:
            for s in s0.al:
                if s.w is not None:
                    op.deps.add(s.w)
                    op.raw.add(s.w)
        for s0 in writes:
            for s in s0.al:
                if s.w is not None:
                    op.deps.add(s.w)
                op.deps.update(s.r)
        for s in reads:
            s.r.append(op.idx)
        for s in writes:
            s.w = op.idx
            s.r = []
        op.deps.discard(op.idx)
        self.ops.append(op)
        return op

    def pe(self, fn, reads=(), writes=()):
        return self.add("pe", fn, reads, writes)

    def act(self, fn, reads=(), writes=()):
        return self.add("act", fn, reads, writes)

    def dve(self, fn, reads=(), writes=()):
        return self.add("dve", fn, reads, writes)

    def dma(self, eng, fn, reads=(), writes=(), inc=16):
        return self.add(eng, fn, reads, writes, dma=True, inc=inc)

    def emit(self, stack):
        nc = self.nc
        ops = self.ops
        ecount = {e: 0 for e in ENGS}
        for op in ops:
            op.eidx = ecount[op.eng]
            ecount[op.eng] += 1

        def needs_sync(op, dop):
            if dop.dma or op.dma or dop.eng != op.eng:
                return True
            if dop.eng == "pe" or not self.sync_same:
                return False
            return (dop.idx in op.raw) and (op.eidx - dop.eidx < SAME_ENGINE_RAW_DIST)

        self.needs_sync = needs_sync
        for op in ops:
            for d in op.deps:
                dop = ops[d]
                if dop.dma:
                    continue
                if not needs_sync(op, dop):
                    continue
                dop.milestone = True
        cnt = {e: 0 for e in ENGS}
        for op in ops:
            if not op.dma and op.milestone:
                cnt[op.eng] += 1
            op.mcount = cnt[op.eng]
        esem = {e: stack.enter_context(nc.semaphore("s_" + e)) for e in ENGS}
        dsems = {e: None for e in ENGS}
        dcount = {}
        dn = {e: 0 for e in ENGS}
        for op in ops:
            if op.dma:
                if dsems[op.eng] is None:
                    dsems[op.eng] = [stack.enter_context(nc.semaphore("d_%s_%d" % (op.eng, i)))
                                     for i in range(self.n_dma_sems)]
                k = dn[op.eng]
                dn[op.eng] += 1
                op.dsem = (op.eng, k % self.n_dma_sems)
                dcount[op.dsem] = dcount.get(op.dsem, 0) + op.inc
                op.dval = dcount[op.dsem]
        per = {e: [o for o in ops if o.eng == e] for e in ENGS}
        block = stack.enter_context(nc.Block())
        sync_same = self.sync_same

        def make(e):
            def body(eng):
                waited = {}

                def wait(key, sem, val):
                    if waited.get(key, 0) >= val:
                        return
                    waited[key] = val
                    eng.wait_ge(sem, val)

                for op in per[e]:
                    for d in sorted(op.deps):
                        dop = ops[d]
                        if dop.dma:
                            wait(("d",) + dop.dsem, dsems[dop.dsem[0]][dop.dsem[1]], dop.dval)
                        else:
                            if not self.needs_sync(op, dop):
                                continue
                            wait(("e", dop.eng), esem[dop.eng], dop.mcount)
                    if op.dma:
                        prev = op.dval - op.inc
                        if prev > 0:
                            wait(("d",) + op.dsem, dsems[op.dsem[0]][op.dsem[1]], prev)
                        ins = op.fn(eng)
                        ins.then_inc(dsems[op.dsem[0]][op.dsem[1]], op.inc)
                    else:
                        ins = op.fn(eng)
                        if op.milestone:
                            ins.then_inc(esem[e], 1)
                if dsems[e] is not None:
                    for i, s in enumerate(dsems[e]):
                        v = dcount.get((e, i), 0)
                        if v:
                            wait(("d", e, i), s, v)
            return body

        block.tensor(make("pe"))
        block.scalar(make("act"))
        block.vector(make("dve"))
        block.gpsimd(make("pool"))
        block.sync(make("sp"))


D = 1024
KD = 8
T = 1024
NT = 8
DEPTH = 2
HG_W = 512
NH = 4
DK = 128
S5_W = 512
NG = 32
SP = 64
IN_W = 3072
DFF = 2816
NF = 22
EPS = 1e-6
CH = 32
NCH = T // CH
SEGS = [(0, 8), (8, 16), (16, 32)]
WSLOT = 4096
N_WSLOT = 4
CCW = 1096
ARENA_B = 90 * 1024 // 2

I32 = mybir.dt.int32
STAGE = {"hg": True, "s5": True, "s5_stop": 99}
DEBUG = False
_DBG = {}


class TT:
    def __init__(self, t, slot):
        self.t = t
        self.s = slot

    def __getitem__(self, k):
        return self.t[k]


def build_program():
    nc = bass.Bass("TRN2", target_bir_lowering=False)
    st = ExitStack()
    P = Prog(nc)

    def din(name, shape):
        return nc.dram_tensor(name, list(shape), F32, kind="ExternalInput").ap()

    def dout(name, shape):
        return nc.dram_tensor(name, list(shape), F32, kind="ExternalOutput").ap()

    x_in = din("x", [T, D])
    cond_in = din("cond", [2, D])
    w_mod = din("w_mod", [DEPTH, D, 6 * D])
    b_mod = din("b_mod", [DEPTH, 6 * D])
    norm_mix = din("norm_mix", [DEPTH, D])
    norm_ffn = din("norm_ffn", [DEPTH, D])
    norm_final = din("norm_final", [D])
    w_in = din("w_in", [DEPTH, D, IN_W])
    w_out = din("w_out", [DEPTH, D, D])
    w_gate = din("w_gate", [DEPTH, D, DFF])
    w_up = din("w_up", [DEPTH, D, DFF])
    w_down = din("w_down", [DEPTH, DFF, D])
    hg_lb = din("hg_lb_logits", [2, DEPTH, HG_W])
    hg_norm = din("hg_norm", [DEPTH, DK])
    st_hg = din("st_hg", [DEPTH, 2, NH, DK, DK])
    cst = din("cst", [128, 512])
    cst2_in = din("cst2", [128, 1024])
    cst3_in = din("cst3", [128, 1024])
    s5_lam_re = din("s5_lam_re", [DEPTH, 2, NG, SP])
    s5_lam_im = din("s5_lam_im", [DEPTH, 2, NG, SP])
    s5_log_dt = din("s5_log_dt", [DEPTH, 2, NG])
    s5_b_re = din("s5_b_re", [DEPTH, 2, NG, SP, 16])
    s5_b_im = din("s5_b_im", [DEPTH, 2, NG, SP, 16])
    s5_c_re = din("s5_c_re", [DEPTH, 2, NG, 16, SP])
    s5_c_im = din("s5_c_im", [DEPTH, 2, NG, 16, SP])
    s5_d = din("s5_d", [DEPTH, NG, 16])
    s5_w_glu = din("s5_w_glu", [DEPTH, S5_W, S5_W])
    st_s5 = din("st_s5", [DEPTH, 2, NG, SP, 2])
    y_out = dout("y", [T, D])
    ns_hg = dout("ns_hg", [2, DEPTH, 2, NH, DK, DK])
    ns_s5 = dout("ns_s5", [2, DEPTH, 2, NG, SP, 2])
    cc5_in = [nc.dram_tensor("cc5_in%d" % i, [128, 64], F32, kind="Internal").ap() for i in range(DEPTH)]
    cc5_out = [nc.dram_tensor("cc5_out%d" % i, [4 * 128, 64], F32, kind="Internal").ap() for i in range(DEPTH)]
    cc_in = [nc.dram_tensor("cc_in%d" % i, [128, CCW], F32, kind="Internal").ap() for i in range(2 * DEPTH)]
    cc_out = [nc.dram_tensor("cc_out%d" % i, [4 * 128, CCW], F32, kind="Internal").ap() for i in range(2 * DEPTH)]

    _n = [0]
    dbg_list = []

    def dbg(name, ap, slot, shape, dtype=F32):
        if not DEBUG:
            return
        t = nc.dram_tensor("dbg_" + name, list(shape), dtype, kind="ExternalOutput").ap()
        P.dma("sp", lambda e: e.dma_start(out=t, in_=ap), reads=[slot], writes=[])

    def sb(shape, dtype, name=None):
        _n[0] += 1
        name = "sb_" + (name or "t%d" % _n[0])
        t = st.enter_context(nc.sbuf_tensor(name, list(shape), dtype))
        return TT(t, Slot(name))

    banks = [TT(st.enter_context(nc.psum_tensor("ps%d" % i, [128, 512], F32)), Slot("ps%d" % i)) for i in range(8)]
    _pb = [0]

    def psum():
        b = banks[_pb[0] % 6]
        _pb[0] += 1
        return b

    wslots = [sb([128, WSLOT], BF16, "wslot%d" % i) for i in range(N_WSLOT)]
    _ws = [0]

    def wload(src_ap, a, b, slot=None):
        if slot is None:
            w = wslots[_ws[0] % N_WSLOT]
            _ws[0] += 1
        else:
            w = wslots[slot]
        view = bass.AP(w.t, 0, [[WSLOT, 128], [b, a], [1, b]])
        P.dma("pool", lambda e, v=view, s=src_ap: e.dma_start(out=v, in_=s), writes=[w.s])
        return view, w.s

    arena_t = st.enter_context(nc.sbuf_tensor("arena", [128, ARENA_B], BF16))
    ar = {"off": 0, "live": []}

    def aalloc(free_shape, dtype, name):
        n = 1
        for v in free_shape:
            n *= v
        nb = n * (1 if dtype == BF16 else 2)
        nb = (nb + 1) // 2 * 2
        off = ar["off"]
        assert off + nb <= ARENA_B, ("arena overflow", name, off, nb)
        ar["off"] = off + nb
        v = arena_t[:, off:off + nb]
        if dtype != BF16:
            v = v.bitcast(dtype)
        if len(free_shape) > 1:
            names = "abcdefg"[:len(free_shape)]
            kw = {names[i]: free_shape[i] for i in range(1, len(free_shape))}
            v = v.rearrange("p (%s) -> p %s" % (" ".join(names), " ".join(names)), **kw)
        s = Slot(name)
        ar["live"].append(s)
        return TT(v, s)

    fence_t = sb([128, 2], F32, "fence")

    def aphase(new_names_hint=None):
        old = ar["live"]
        ar["live"] = []
        ar["off"] = 0
        ar["pending"] = old

    def afence():
        old = ar.get("pending", [])
        new = list(ar["live"])
        P.dve(lambda e: e.memset(fence_t[:], 0.0), reads=[], writes=old + new + [fence_t.s])
        ar["pending"] = []

    cst_f = sb([128, 512], F32, "cst_f")
    P.dma("sp", lambda e: e.dma_start(out=cst_f[:], in_=cst), writes=[cst_f.s])
    ident_f = cst_f[:, 0:128]
    ident_b = sb([128, 128], BF16, "ident_b")
    P.dve(lambda e: e.tensor_copy(out=ident_b[:], in_=cst_f[:, 0:128]), reads=[cst_f.s], writes=[ident_b.s])
    ones_b = sb([128, 128], BF16, "ones_b")
    P.dve(lambda e: e.memset(ones_b[:], 1.0), writes=[ones_b.s])
    zeros_f = sb([128, 128], F32, "zeros_f")
    P.dve(lambda e: e.memset(zeros_f[:], 0.0), writes=[zeros_f.s])
    epsc = sb([128, 2], F32, "epsc")
    P.dve(lambda e: e.memset(epsc[:, 0:1], float(D * EPS)), writes=[epsc.s])
    P.dve(lambda e: e.memset(epsc[:, 1:2], float(DK * EPS)), reads=[epsc.s], writes=[epsc.s])
    lnc = sb([128, 1], F32, "lnc")
    P.dve(lambda e: e.memset(lnc[:], float(math.log(DK ** -0.5))), writes=[lnc.s])

    xT = sb([128, KD, T], F32, "xT")
    xs = [[Slot("xT%d_%d" % (k, h)) for h in range(2)] for k in range(KD)]
    hT = sb([128, KD, T], BF16, "hT")
    hs = [[Slot("hT%d_%d" % (k, h)) for h in range(2)] for k in range(KD)]
    mixT = sb([128, KD, T], BF16, "mixT")
    mixs = [[Slot("mix%d_%d" % (k, h)) for h in range(2)] for k in range(KD)]

    def sin_turns(out, u, ki, kf, ap=lambda t: t[:]):
        P.dve(lambda e: e.tensor_copy(out=ap(ki), in_=ap(u)), reads=[u.s], writes=[ki.s])
        P.dve(lambda e: e.tensor_copy(out=ap(kf), in_=ap(ki)), reads=[ki.s], writes=[kf.s])
        P.dve(lambda e: e.tensor_tensor(out=ap(kf), in0=ap(u), in1=ap(kf), op=ALU.subtract), reads=[u.s, kf.s], writes=[kf.s])
        P.act(lambda e: e.activation(out=ap(out), in_=ap(kf), func=AF.Sin, scale=2 * math.pi), reads=[kf.s], writes=[out.s])

    aphase()
    xtok = [aalloc([D], F32, "xtok%d" % i) for i in range(NT)]
    posarg = aalloc([512], F32, "posarg")
    cst2 = aalloc([1024], F32, "cst2")
    posk_i = aalloc([512], I32, "posk_i")
    posk_f = aalloc([512], F32, "posk_f")
    afence()
    P.dma("sp", lambda e: e.dma_start(out=cst2[:], in_=cst2_in), writes=[cst2.s])
    for tt in range(NT):
        P.dma("sp", lambda e, tt=tt: e.dma_start(out=xtok[tt][:], in_=x_in[tt * 128:(tt + 1) * 128, :]),
              writes=[xtok[tt].s])
    for k in range(KD):
        for h in range(2):
            b = psum()
            for j in range(4):
                tt = h * 4 + j
                P.pe(lambda e, b=b, j=j, tt=tt, k=k: e.transpose(b[:, j * 128:(j + 1) * 128],
                                                                   xtok[tt][:, k * 128:(k + 1) * 128], ident_f),
                     reads=[xtok[tt].s, cst_f.s], writes=[b.s])
            P.act(lambda e, b=b, k=k, h=h: e.activation(out=xT[:, k, h * 512:(h + 1) * 512], in_=b[:], func=AF.Copy),
                  reads=[b.s], writes=[xs[k][h]])

    posi = aalloc_late = None
    for k in range(KD):
        blk = k // 2
        pos_src = cst2[:, 0:512] if blk < 2 else cst2[:, 512:1024]
        om = cst_f[:, 388 + (k % 2):389 + (k % 2)]
        P.dve(lambda e, pos_src=pos_src, om=om: e.tensor_scalar(
            out=posarg[:], in0=pos_src, scalar1=om, scalar2=1.0 / (2 * math.pi), op0=ALU.mult, op1=ALU.mult),
            reads=[cst2.s, cst_f.s], writes=[posarg.s])
        if blk % 2 == 1:
            P.dve(lambda e: e.tensor_scalar(out=posarg[:], in0=posarg[:], scalar1=0.25, scalar2=None, op0=ALU.add),
                  reads=[posarg.s], writes=[posarg.s])
        sin_turns(posarg, posarg, posk_i, posk_f)
        P.dve(lambda e, k=k: e.tensor_tensor(out=xT[:, k, 512:1024], in0=xT[:, k, 512:1024], in1=posarg[:], op=ALU.add),
              reads=[posarg.s, xs[k][1]], writes=[xs[k][1]])

    condf = sb([128, KD, 2], F32, "condf")
    condb = sb([128, KD, 2], BF16, "condb")
    for j in range(2):
        P.dma("sp", lambda e, j=j: e.dma_start(out=condf[:, :, j], in_=cond_in[j].rearrange("(k p) -> p k", p=128)),
              writes=[condf.s])
    P.act(lambda e: e.activation(out=condb[:], in_=condf[:], func=AF.Silu), reads=[condf.s], writes=[condb.s])

    nmix = sb([128, DEPTH, KD], F32, "nmix")
    nffn = sb([128, DEPTH, KD], F32, "nffn")
    nfin = sb([128, KD], F32, "nfin")
    bmod = sb([128, DEPTH, 48], F32, "bmod")
    P.dma("sp", lambda e: e.dma_start(out=nmix[:], in_=norm_mix.rearrange("l (k p) -> p l k", p=128)), writes=[nmix.s])
    P.dma("sp", lambda e: e.dma_start(out=nffn[:], in_=norm_ffn.rearrange("l (k p) -> p l k", p=128)), writes=[nffn.s])
    P.dma("sp", lambda e: e.dma_start(out=nfin[:], in_=norm_final.rearrange("(k p) -> p k", p=128)), writes=[nfin.s])
    P.dma("sp", lambda e: e.dma_start(out=bmod[:], in_=b_mod.rearrange("l (k p) -> p l k", p=128)), writes=[bmod.s])

    lbl = sb([128, 2, DEPTH, NH], F32, "lbl")
    for d in range(2):
        for l in range(DEPTH):
            P.dma("sp", lambda e, d=d, l=l: e.dma_start(out=lbl[:, d, l, :], in_=hg_lb[d, l].rearrange("(h p) -> p h", p=128)),
                  writes=[lbl.s])
    lb = sb([128, DEPTH, 2, NH], F32, "lb")
    oml = sb([128, DEPTH, 2, NH], F32, "oml")
    noml = sb([128, DEPTH, 2, NH], F32, "noml")
    P.dve(lambda e: e.memset(lb[:], 0.0), writes=[lb.s])
    P.dve(lambda e: e.tensor_tensor(out=lb[:, 1], in0=lbl[:, :, 1, :], in1=lbl[:, :, 0, :], op=ALU.subtract),
          reads=[lbl.s, lb.s], writes=[lb.s])
    P.act(lambda e: e.activation(out=lb[:, 1], in_=lb[:, 1], func=AF.Sigmoid), reads=[lb.s], writes=[lb.s])
    P.dve(lambda e: e.tensor_scalar(out=oml[:], in0=lb[:], scalar1=-1.0, scalar2=1.0, op0=ALU.mult, op1=ALU.add),
          reads=[lb.s], writes=[oml.s])
    P.dve(lambda e: e.tensor_scalar(out=noml[:], in0=lb[:], scalar1=1.0, scalar2=-1.0, op0=ALU.mult, op1=ALU.add),
          reads=[lb.s], writes=[noml.s])
    gn = sb([128, DEPTH], F32, "gn")
    P.dma("sp", lambda e: e.dma_start(out=gn[:], in_=hg_norm.rearrange("l p -> p l")), writes=[gn.s])
    P.dve(lambda e: e.tensor_scalar(out=gn[:], in0=gn[:], scalar1=float(math.sqrt(DK)), scalar2=None, op0=ALU.mult),
          reads=[gn.s], writes=[gn.s])

    par_all = sb([128, DEPTH, 2, 5, 16], F32, "par_all")
    for l_ in range(DEPTH):
        for d_ in range(2):
            P.dma("sp", lambda e, l_=l_, d_=d_: e.dma_start(
                out=par_all[:, l_, d_, 0, :], in_=s5_lam_re[l_, d_].rearrange("(gp g2) p -> (g2 p) gp", g2=2)), writes=[par_all.s])
            P.dma("sp", lambda e, l_=l_, d_=d_: e.dma_start(
                out=par_all[:, l_, d_, 1, :], in_=s5_lam_im[l_, d_].rearrange("(gp g2) p -> (g2 p) gp", g2=2)), writes=[par_all.s])
            for g2_ in range(2):
                P.dma("sp", lambda e, l_=l_, d_=d_, g2_=g2_: e.dma_start(
                    out=par_all[64 * g2_:64 * g2_ + 64, l_, d_, 2, :],
                    in_=s5_log_dt[l_, d_].rearrange("(gp g2) -> g2 gp", g2=2)[g2_].partition_broadcast(64)), writes=[par_all.s])
    mod = sb([128, DEPTH, 48, 2], F32, "mod")
    coef = sb([128, DEPTH, 2, KD, 2], F32, "coef")

    def compute_mod(l):
        bk = psum()
        for cc in range(12):
            wv, wsl = wload(w_mod[l][:, cc * 512:(cc + 1) * 512].rearrange("(k p) c -> p k c", p=128), KD, 512)
            for j in range(4):
                ft = cc * 4 + j
                for k in range(KD):
                    P.pe(lambda e, wv=wv, j=j, k=k, ft=ft, bk=bk: e.matmul(
                        bk[:, ft * 2:ft * 2 + 2], wv[:, k, j * 128:(j + 1) * 128], condb[:, k, :],
                        start=(k == 0), stop=(k == KD - 1)),
                        reads=[wsl, condb.s], writes=[bk.s])
        P.dve(lambda e, bk=bk, l=l: e.tensor_tensor(
            out=mod[:, l], in0=bk[:, 0:96].rearrange("p (f c) -> p f c", c=2),
            in1=bmod[:, l].unsqueeze(2).broadcast_to([128, 48, 2]), op=ALU.add),
            reads=[bk.s, bmod.s], writes=[mod.s])
        for n, (gt, base) in enumerate(((nmix, 8), (nffn, 32))):
            P.dve(lambda e, n=n, base=base, l=l: e.tensor_scalar(
                out=coef[:, l, n], in0=mod[:, l, base:base + 8, :], scalar1=1.0, scalar2=32.0,
                op0=ALU.add, op1=ALU.mult), reads=[mod.s], writes=[coef.s])
            P.dve(lambda e, n=n, gt=gt, l=l: e.tensor_tensor(
                out=coef[:, l, n], in0=coef[:, l, n], in1=gt[:, l].unsqueeze(2).broadcast_to([128, KD, 2]),
                op=ALU.mult), reads=[coef.s, gt.s], writes=[coef.s])

    sq = [sb([128, 512], BF16, "sq%d" % i) for i in range(2)]
    rstd = sb([128, 512], F32, "rstd")
    tmpn = [sb([128, 512], F32, "tmpn%d" % i) for i in range(2)]

    def rms_stats(h):
        bk = psum()
        for k in range(KD):
            s = sq[k % 2]
            P.act(lambda e, k=k, h=h, s=s: e.activation(out=s[:], in_=xT[:, k, h * 512:(h + 1) * 512], func=AF.Square),
                  reads=[xs[k][h]], writes=[s.s])
            P.pe(lambda e, k=k, bk=bk, s=s: e.matmul(bk[:], ones_b[:], s[:], start=(k == 0), stop=(k == KD - 1)),
                 reads=[s.s, ones_b.s], writes=[bk.s])
        P.act(lambda e, bk=bk: e.activation(out=rstd[:], in_=bk[:], func=AF.Ln, bias=epsc[:, 0:1]),
              reads=[bk.s, epsc.s], writes=[rstd.s])
        P.act(lambda e: e.activation(out=rstd[:], in_=rstd[:], func=AF.Exp, scale=-0.5), reads=[rstd.s], writes=[rstd.s])

    def norm_mod(l, n, shift_base):
        for h in range(2):
            rms_stats(h)
            for k in range(KD):
                tm = tmpn[k % 2]
                P.dve(lambda e, k=k, h=h, tm=tm: e.tensor_tensor(out=tm[:], in0=xT[:, k, h * 512:(h + 1) * 512],
                                                                 in1=rstd[:], op=ALU.mult),
                      reads=[xs[k][h], rstd.s], writes=[tm.s])
                P.act(lambda e, k=k, h=h, l=l, n=n, tm=tm: e.activation(
                    out=hT[:, k, h * 512:(h + 1) * 512], in_=tm[:], func=AF.Identity,
                    bias=mod[:, l, shift_base + k, h:h + 1], scale=coef[:, l, n, k, h:h + 1]),
                    reads=[tm.s, mod.s, coef.s], writes=[hs[k][h]])

    def residual_proj(wdram, l, gate_base, srcT, src_slots, nk):
        per = WSLOT // 256
        for c4 in range(4):
            wv = []
            for k0 in range(0, nk, per):
                kk = min(per, nk - k0)
                v, s = wload(wdram[k0 * 128:(k0 + kk) * 128, c4 * 256:(c4 + 1) * 256].rearrange("(k p) c -> p k c", p=128),
                             kk, 256)
                wv.append((k0, kk, v, s))
            for j in range(2):
                dt_ = c4 * 2 + j
                for h in range(2):
                    bk = psum()
                    for (k0, kk, v, s) in wv:
                        for k in range(kk):
                            kg = k0 + k
                            P.pe(lambda e, v=v, k=k, j=j, kg=kg, h=h, bk=bk: e.matmul(
                                bk[:], v[:, k, j * 128:(j + 1) * 128], srcT[:, kg, h * 512:(h + 1) * 512],
                                start=(kg == 0), stop=(kg == nk - 1)),
                                reads=[s, src_slots[kg][h]], writes=[bk.s])
                    P.dve(lambda e, bk=bk, dt_=dt_, h=h, l=l: e.scalar_tensor_tensor(
                        out=xT[:, dt_, h * 512:(h + 1) * 512], in0=bk[:], scalar=mod[:, l, gate_base + dt_, h:h + 1],
                        in1=xT[:, dt_, h * 512:(h + 1) * 512], op0=ALU.mult, op1=ALU.add),
                        reads=[bk.s, mod.s, xs[dt_][h]], writes=[xs[dt_][h]])


    def ffn(l):
        aphase()
        h1 = aalloc([NF, T], BF16, "h1")
        sgate = [aalloc([512], F32, "sgate%d" % i) for i in range(2)]
        h1s = [[Slot("h1_%d_%d" % (f, h)) for h in range(2)] for f in range(NF)]
        ar["live"].extend([s for row in h1s for s in row])
        afence()
        it = 0
        for c in range(6):
            ncol = 512 if c < 5 else 256
            vg, sg_ = wload(w_gate[l][:, c * 512:c * 512 + ncol].rearrange("(k p) c -> p k c", p=128), KD, ncol)
            vu, su_ = wload(w_up[l][:, c * 512:c * 512 + ncol].rearrange("(k p) c -> p k c", p=128), KD, ncol)
            for j in range(ncol // 128):
                f = c * 4 + j
                for h in range(2):
                    bg = psum()
                    bu = psum()
                    for k in range(KD):
                        P.pe(lambda e, vg=vg, k=k, j=j, h=h, bg=bg: e.matmul(
                            bg[:], vg[:, k, j * 128:(j + 1) * 128], hT[:, k, h * 512:(h + 1) * 512],
                            start=(k == 0), stop=(k == KD - 1)), reads=[sg_, hs[k][h]], writes=[bg.s])
                    for k in range(KD):
                        P.pe(lambda e, vu=vu, k=k, j=j, h=h, bu=bu: e.matmul(
                            bu[:], vu[:, k, j * 128:(j + 1) * 128], hT[:, k, h * 512:(h + 1) * 512],
                            start=(k == 0), stop=(k == KD - 1)), reads=[su_, hs[k][h]], writes=[bu.s])
                    sgt = sgate[it % 2]
                    it += 1
                    P.act(lambda e, bg=bg, sgt=sgt: e.activation(out=sgt[:], in_=bg[:], func=AF.Silu),
                          reads=[bg.s], writes=[sgt.s])
                    P.dve(lambda e, bu=bu, f=f, h=h, sgt=sgt: e.tensor_tensor(
                        out=h1[:, f, h * 512:(h + 1) * 512], in0=sgt[:], in1=bu[:], op=ALU.mult),
                        reads=[sgt.s, bu.s], writes=[h1s[f][h]])
        residual_proj(w_down[l], l, 40, h1, h1s, NF)

    def proj_feat(wv, wsl, c0, h):
        bk = psum()
        for k in range(KD):
            P.pe(lambda e, k=k, bk=bk: e.matmul(bk[:], wv[:, k, c0:c0 + 128], hT[:, k, h * 512:(h + 1) * 512],
                                                start=(k == 0), stop=(k == KD - 1)),
                 reads=[wsl, hs[k][h]], writes=[bk.s])
        return bk

    def hgrn(l):
        aphase()
        qk = [[[aalloc([T], BF16, "qk%d%d%d" % (h, d, w)) for w in range(2)] for d in range(2)] for h in range(NH)]
        kendT = [[aalloc([NT, DK], BF16, "kendT%d%d" % (h, d)) for d in range(2)] for h in range(NH)]
        V = [aalloc([HG_W], BF16, "V%d" % tt) for tt in range(NT)]
        gch = aalloc([NH * 2, NCH], F32, "gch")
        gch_s = [[Slot("gch%d%d" % (h, d)) for d in range(2)] for h in range(NH)]
        ar["live"].extend([s for row in gch_s for s in row])
        R1 = ar["off"]
        qs = [aalloc([512], F32, "qs%d" % hf) for hf in range(2)]
        rmask = aalloc([512], F32, "rmask")
        tmp = [[aalloc([512], F32, "gt%d_%d" % (i, j)) for j in range(5)] for i in range(2)]
        kend_t = [aalloc([512], BF16, "kend%d" % i) for i in range(2)]
        R1_end = ar["off"]
        afence()
        P.dve(lambda e: e.memset(rmask[:], 1.0), writes=[rmask.s])
        P.dve(lambda e: e.memset(rmask[:, 0:512:CH], 0.0), reads=[rmask.s], writes=[rmask.s])

        wv_iv, ws_iv = None, None

        def load_in(c):
            return wload(w_in[l][:, c * 512:(c + 1) * 512].rearrange("(k p) c -> p k c", p=128), KD, 512)

        wq, wqs = load_in(0)
        wf = [None, None]
        wf[0] = load_in(1)
        wf[1] = load_in(2)
        it = 0
        for h in range(NH):
            for hf in range(2):
                bk = proj_feat(wq, wqs, h * 128, hf)
                P.act(lambda e, bk=bk, hf=hf: e.activation(out=qs[hf][:], in_=bk[:], func=AF.Silu),
                      reads=[bk.s], writes=[qs[hf].s])
            for d in range(2):
                for hf in range(2):
                    t_sig, t_a, t_b, t_c, t_e = tmp[it % 2]
                    ke = kend_t[it % 2]
                    it += 1
                    bk = proj_feat(wf[d][0], wf[d][1], h * 128, hf)
                    lb_ = lb[:, l, d, h:h + 1]
                    oml_ = oml[:, l, d, h:h + 1]
                    noml_ = noml[:, l, d, h:h + 1]
                    P.act(lambda e, bk=bk, t_sig=t_sig: e.activation(out=t_sig[:], in_=bk[:], func=AF.Sigmoid),
                          reads=[bk.s], writes=[t_sig.s])
                    P.dve(lambda e, t_sig=t_sig, t_a=t_a, lb_=lb_, oml_=oml_: e.tensor_scalar(
                        out=t_a[:], in0=t_sig[:], scalar1=oml_, scalar2=lb_, op0=ALU.mult, op1=ALU.add),
                        reads=[t_sig.s, lb.s, oml.s], writes=[t_a.s])
                    P.act(lambda e, t_a=t_a: e.activation(out=t_a[:], in_=t_a[:], func=AF.Ln), reads=[t_a.s], writes=[t_a.s])
                    P.dve(lambda e, t_a=t_a, t_b=t_b: e.tensor_tensor_scan(
                        out=t_b[:], data0=rmask[:], data1=t_a[:], initial=0.0, op0=ALU.mult, op1=ALU.add),
                        reads=[t_a.s, rmask.s], writes=[t_b.s])
                    P.dve(lambda e, t_sig=t_sig, noml_=noml_, oml_=oml_: e.tensor_scalar(
                        out=t_sig[:], in0=t_sig[:], scalar1=noml_, scalar2=oml_, op0=ALU.mult, op1=ALU.add),
                        reads=[t_sig.s, noml.s, oml.s], writes=[t_sig.s])
                    P.act(lambda e, t_b=t_b, h=h, d=d, hf=hf: e.activation(
                        out=gch[:, h * 2 + d, hf * (512 // CH):(hf + 1) * (512 // CH)], in_=t_b[:, CH - 1:512:CH], func=AF.Exp),
                        reads=[t_b.s], writes=[gch_s[h][d]])
                    tb3 = t_b[:].rearrange("p (c j) -> p c j", j=CH)
                    tc3 = t_c[:].rearrange("p (c j) -> p c j", j=CH)
                    ta3 = t_a[:].rearrange("p (c j) -> p c j", j=CH)
                    tot_b = tb3[:, :, CH - 1:CH].broadcast_to([128, 512 // CH, CH])
                    if d == 0:
                        P.dve(lambda e, tc3=tc3, tb3=tb3, tot_b=tot_b: e.tensor_tensor(
                            out=tc3, in0=tb3, in1=tot_b, op=ALU.subtract), reads=[t_b.s], writes=[t_c.s])
                    else:
                        P.dve(lambda e, tc3=tc3, ta3=ta3, tb3=tb3: e.tensor_tensor(
                            out=tc3, in0=ta3, in1=tb3, op=ALU.subtract), reads=[t_a.s, t_b.s], writes=[t_c.s])
                        P.dve(lambda e, tc3=tc3, tb3=tb3, tot_b=tot_b, ta3=ta3: e.tensor_tensor(
                            out=ta3, in0=tc3, in1=tot_b, op=ALU.add), reads=[t_c.s, t_b.s], writes=[t_a.s])
                    beta = t_b if d == 0 else t_a
                    P.act(lambda e, beta=beta, t_e=t_e: e.activation(out=t_e[:], in_=beta[:], func=AF.Exp, bias=lnc[:, 0:1]),
                          reads=[beta.s, lnc.s], writes=[t_e.s])
                    P.dve(lambda e, t_e=t_e, h=h, d=d, hf=hf: e.tensor_tensor(
                        out=qk[h][d][0][:, hf * 512:(hf + 1) * 512], in0=qs[hf][:], in1=t_e[:], op=ALU.mult),
                        reads=[t_e.s, qs[hf].s], writes=[qk[h][d][0].s])
                    P.dve(lambda e, beta=beta, t_e=t_e: e.tensor_scalar(out=t_e[:], in0=beta[:], scalar1=-75.0, scalar2=None, op0=ALU.max),
                          reads=[beta.s], writes=[t_e.s])
                    P.act(lambda e, t_e=t_e: e.activation(out=t_e[:], in_=t_e[:], func=AF.Exp, scale=-1.0),
                          reads=[t_e.s], writes=[t_e.s])
                    P.dve(lambda e, t_e=t_e, t_sig=t_sig, h=h, d=d, hf=hf: e.tensor_tensor(
                        out=qk[h][d][1][:, hf * 512:(hf + 1) * 512], in0=t_sig[:], in1=t_e[:], op=ALU.mult),
                        reads=[t_e.s, t_sig.s], writes=[qk[h][d][1].s])
                    P.act(lambda e, t_c=t_c, t_e=t_e: e.activation(out=t_e[:], in_=t_c[:], func=AF.Exp, scale=-1.0),
                          reads=[t_c.s], writes=[t_e.s])
                    P.dve(lambda e, t_e=t_e, t_sig=t_sig, ke=ke: e.tensor_tensor(
                        out=ke[:], in0=t_sig[:], in1=t_e[:], op=ALU.mult), reads=[t_e.s, t_sig.s], writes=[ke.s])
                    bk2 = psum()
                    for j in range(4):
                        P.pe(lambda e, bk2=bk2, j=j, ke=ke: e.matmul(bk2[:, j * 128:(j + 1) * 128], ke[:, j * 128:(j + 1) * 128],
                                                                     ident_b[:], start=True, stop=True),
                             reads=[ke.s, ident_b.s], writes=[bk2.s])
                    P.act(lambda e, bk2=bk2, h=h, d=d, hf=hf: e.activation(
                        out=kendT[h][d][:, hf * 4:(hf + 1) * 4, :], in_=bk2[:].rearrange("p (j k) -> p j k", k=128), func=AF.Copy),
                        reads=[bk2.s], writes=[kendT[h][d].s])
        wiv, wivs = load_in(3)
        for tt in range(NT):
            bk = psum()
            hf = tt // 4
            for k in range(KD):
                P.pe(lambda e, k=k, bk=bk, tt=tt: e.matmul(bk[:], hT[:, k, tt * 128:(tt + 1) * 128], wiv[:, k, :],
                                                            start=(k == 0), stop=(k == KD - 1)),
                     reads=[wivs, hs[k][hf]], writes=[bk.s])
            P.act(lambda e, bk=bk, tt=tt: e.activation(out=V[tt][:], in_=bk[:], func=AF.Copy), reads=[bk.s], writes=[V[tt].s])

        old_tmp = [t.s for grp in tmp for t in grp] + [k.s for k in kend_t] + [q.s for q in qs] + [rmask.s]
        ar["off"] = R1
        S = [aalloc([DK], F32, "S%d" % i) for i in range(2)]
        Sent = aalloc([NCH, DK], BF16, "Sent")
        o_t = aalloc([512], F32, "o_t")
        on_t = aalloc([512], F32, "on_t")
        sg_t = aalloc([512], F32, "sg_t")
        sq_t = aalloc([512], BF16, "sq_t")
        PT = [aalloc([128], BF16, "PT%d" % i) for i in range(2)]
        gat = aalloc([4, DK], F32, "gat")
        gatG = aalloc([4, 8], F32, "gatG")
        s0t = aalloc([DK], F32, "s0t")
        Pc = [aalloc([DK], F32, "Pc%d" % i) for i in range(2)]
        Sinit = aalloc([2 * NH, DK], F32, "Sinit")
        Sinit_s = [[Slot("Sinit%d%d" % (h, d)) for d in range(2)] for h in range(NH)]
        gtot = aalloc([NH * 2, NCH // 2], F32, "gtot")
        new2 = [t.s for t in S + PT + Pc] + [Sent.s, o_t.s, on_t.s, sg_t.s, sq_t.s, gat.s, gatG.s, s0t.s, Sinit.s, gtot.s] + \
               [s_ for row in Sinit_s for s_ in row]
        ar["live"].extend([s_ for row in Sinit_s for s_ in row])
        P.dve(lambda e: e.memset(fence_t[:], 0.0), reads=[], writes=old_tmp + new2 + [fence_t.s])

        CPT = 128 // CH

        def u_matmul(h, d, c):
            tt, p0 = c // CPT, (c % CPT) * CH
            bk = psum()
            P.pe(lambda e, bk=bk: e.matmul(bk[:, 0:128], kendT[h][d][p0:p0 + CH, tt, :],
                                           V[tt][p0:p0 + CH, h * 128:(h + 1) * 128],
                                           start=True, stop=True, tile_position=(p0, 0)),
                 reads=[kendT[h][d].s, V[tt].s], writes=[bk.s])
            return bk

        def scan_order(c0, c1, d):
            return list(range(c0, c1)) if d == 0 else list(range(c1 - 1, c0 - 1, -1))

        si = [0]
        SC0, SC1 = SEGS[2]

        ci = l * 2
        ccs_in, ccs_out = Slot("ccin"), Slot("ccout")
        for h in range(NH):
            for d in range(2):
                hd = h * 2 + d
                Sx = S[si[0] % 2]
                si[0] += 1
                order = scan_order(SC0, SC1, d)
                for i, c in enumerate(order):
                    bk = u_matmul(h, d, c)
                    if i == 0:
                        P.dve(lambda e, bk=bk, Sx=Sx: e.tensor_copy(out=Sx[:], in_=bk[:, 0:128]), reads=[bk.s], writes=[Sx.s])
                    else:
                        P.dve(lambda e, bk=bk, Sx=Sx, c=c, hd=hd: e.scalar_tensor_tensor(
                            out=Sx[:], in0=Sx[:], scalar=gch[:, hd, c:c + 1], in1=bk[:, 0:128],
                            op0=ALU.mult, op1=ALU.add), reads=[bk.s, Sx.s, gch_s[h][d]], writes=[Sx.s])
                P.dma("sp", lambda e, Sx=Sx, hd=hd: e.dma_start(out=cc_in[ci][:, hd * 128:(hd + 1) * 128], in_=Sx[:]),
                      reads=[Sx.s], writes=[ccs_in])
                P.dve(lambda e, hd=hd: e.tensor_tensor_scan(
                    out=gtot[:, hd, :], data0=gch[:, hd, SC0:SC1], data1=zeros_f[:, 0:SC1 - SC0], initial=1.0,
                    op0=ALU.mult, op1=ALU.add), reads=[gch_s[h][d], zeros_f.s], writes=[gtot.s])
        P.dma("sp", lambda e: e.dma_start(out=cc_in[ci][:, 1024:1032], in_=gtot[:, :, SC1 - SC0 - 1]),
              reads=[gtot.s], writes=[ccs_in])
        P.dma("sp", lambda e: e.dma_start(out=cc_in[ci][:, 1032:CCW], in_=zeros_f[:, 0:CCW - 1032]),
              reads=[zeros_f.s], writes=[ccs_in])
        P.dma("pool", lambda e: e.collective_compute("AllGather", ALU.bypass, replica_groups=[[0, 1, 2, 3], [4, 5, 6, 7]],
                                                     ins=[cc_in[ci]], outs=[cc_out[ci]]),
              reads=[ccs_in], writes=[ccs_out], inc=1)
        ccv = cc_out[ci].rearrange("(r p) c -> p r c", p=128)
        P.dma("sp", lambda e: e.dma_start(out=gatG[:], in_=ccv[:, :, 1024:1032]), reads=[ccs_out], writes=[gatG.s])
        for h in range(NH):
            for d in range(2):
                hd = h * 2 + d
                col = slice(hd * 128, (hd + 1) * 128)
                P.dma("sp", lambda e, col=col: e.dma_start(out=gat[:], in_=ccv[:, :, col]), reads=[ccs_out], writes=[gat.s])
                P.dma("sp", lambda e, d=d, h=h: e.dma_start(out=s0t[:], in_=st_hg[l, d, h]), writes=[s0t.s])
                dst = Sinit[:, hd, :]
                ranks = [0, 1, 2, 3] if d == 0 else [3, 2, 1, 0]
                prev, prev_s = s0t[:], s0t.s
                P.dve(lambda e, dst=dst, prev=prev, r=ranks[0]: e.tensor_scalar(
                    out=dst, in0=prev, scalar1=cst_f[:, 384 + r:385 + r], scalar2=None, op0=ALU.mult),
                    reads=[prev_s, cst_f.s], writes=[Sinit_s[h][d]])
                for i in range(3):
                    r = ranks[i]
                    nxt = Pc[i % 2]
                    P.dve(lambda e, nxt=nxt, prev=prev, r=r, hd=hd: e.scalar_tensor_tensor(
                        out=nxt[:], in0=prev, scalar=gatG[:, r, hd:hd + 1], in1=gat[:, r, :],
                        op0=ALU.mult, op1=ALU.add), reads=[prev_s, gat.s, gatG.s], writes=[nxt.s])
                    rn = ranks[i + 1]
                    P.dve(lambda e, nxt=nxt, dst=dst, rn=rn: e.scalar_tensor_tensor(
                        out=dst, in0=nxt[:], scalar=cst_f[:, 384 + rn:385 + rn], in1=dst, op0=ALU.mult, op1=ALU.add),
                        reads=[nxt.s, cst_f.s, Sinit_s[h][d]], writes=[Sinit_s[h][d]])
                    prev, prev_s = nxt[:], nxt.s

        wg_v, wg_s = load_in(4)
        it2 = 0
        for h in range(NH):
            bo = [banks[6], banks[7]]
            for d in range(2):
                hd = h * 2 + d
                for (c0, c1) in SEGS:
                    Sx = S[si[0] % 2]
                    si[0] += 1
                    order = scan_order(c0, c1, d)
                    is_sample = (c0 == SC0)
                    for i, c in enumerate(order):
                        if i == 0:
                            src = Sinit[:, hd, :] if is_sample else zeros_f[:]
                            src_s = Sinit_s[h][d] if is_sample else zeros_f.s
                        else:
                            src, src_s = Sx[:], Sx.s
                        P.act(lambda e, src=src, c=c: e.activation(out=Sent[:, c, :], in_=src, func=AF.Copy),
                              reads=[src_s], writes=[Sent.s])
                        last = (i == len(order) - 1)
                        if last and is_sample:
                            continue
                        bk = u_matmul(h, d, c)
                        if i == 0 and not is_sample:
                            P.dve(lambda e, bk=bk, Sx=Sx: e.tensor_copy(out=Sx[:], in_=bk[:, 0:128]), reads=[bk.s], writes=[Sx.s])
                        else:
                            P.dve(lambda e, bk=bk, Sx=Sx, src=src, c=c, hd=hd: e.scalar_tensor_tensor(
                                out=Sx[:], in0=src, scalar=gch[:, hd, c:c + 1], in1=bk[:, 0:128],
                                op0=ALU.mult, op1=ALU.add), reads=[bk.s, src_s, gch_s[h][d]], writes=[Sx.s])
                    if not is_sample:
                        seq = 0 if c0 == 0 else 1
                        P.dma("sp", lambda e, Sx=Sx, seq=seq, d=d, h=h: e.dma_start(out=ns_hg[seq, l, d, h], in_=Sx[:]),
                              reads=[Sx.s], writes=[])
                for hf in range(2):
                    for j in range(4):
                        tt = hf * 4 + j
                        tok = slice(tt * 128, (tt + 1) * 128)
                        bs = psum()
                        P.pe(lambda e, bs=bs, tok=tok, d=d, h=h: e.matmul(bs[:, 0:128], qk[h][d][1][:, tok], qk[h][d][0][:, tok],
                                                                    start=True, stop=True),
                             reads=[qk[h][d][0].s, qk[h][d][1].s], writes=[bs.s])
                        pt = PT[it2 % 2]
                        it2 += 1
                        P.dve(lambda e, bs=bs, pt=pt, d=d: e.tensor_tensor(
                            out=pt[:], in0=bs[:, 0:128], in1=cst_f[:, 128 + d * 128:256 + d * 128], op=ALU.mult),
                            reads=[bs.s, cst_f.s], writes=[pt.s])
                        oc = slice(j * 128, (j + 1) * 128)
                        P.pe(lambda e, pt=pt, tt=tt, oc=oc, hf=hf, d=d, j=j, h=h: e.matmul(
                            bo[hf][:, oc], V[tt][:, h * 128:(h + 1) * 128], pt[:], start=(d == 0 and j == 0), stop=False),
                            reads=[V[tt].s, pt.s], writes=[bo[hf].s])
                        for sub in range(CPT):
                            c = tt * CPT + sub
                            cs = slice(j * 128 + sub * CH, j * 128 + (sub + 1) * CH)
                            ts = slice(tt * 128 + sub * CH, tt * 128 + (sub + 1) * CH)
                            P.pe(lambda e, c=c, cs=cs, ts=ts, hf=hf, d=d, h=h: e.matmul(
                                bo[hf][:, cs], Sent[:, c, :], qk[h][d][0][:, ts], start=False, stop=(d == 1)),
                                reads=[Sent.s, qk[h][d][0].s], writes=[bo[hf].s])
            for hf in range(2):
                P.act(lambda e, hf=hf: e.activation(out=o_t[:], in_=bo[hf][:], func=AF.Copy), reads=[bo[hf].s], writes=[o_t.s])
                P.act(lambda e, hf=hf: e.activation(out=sq_t[:], in_=bo[hf][:], func=AF.Square), reads=[bo[hf].s], writes=[sq_t.s])
                if l == 0 and h == 0 and hf == 0:
                    dbg("o", o_t[:], o_t.s, [128, 512])
                    dbg("qf", qk[0][0][0][:], qk[0][0][0].s, [128, T], BF16)
                    dbg("kf", qk[0][0][1][:], qk[0][0][1].s, [128, T], BF16)
                    dbg("qb", qk[0][1][0][:], qk[0][1][0].s, [128, T], BF16)
                    dbg("kb", qk[0][1][1][:], qk[0][1][1].s, [128, T], BF16)
                br = psum()
                P.pe(lambda e, br=br: e.matmul(br[:], ones_b[:], sq_t[:], start=True, stop=True),
                     reads=[sq_t.s, ones_b.s], writes=[br.s])
                P.act(lambda e, br=br: e.activation(out=on_t[:], in_=br[:], func=AF.Ln, bias=epsc[:, 1:2]),
                      reads=[br.s, epsc.s], writes=[on_t.s])
                P.act(lambda e: e.activation(out=on_t[:], in_=on_t[:], func=AF.Exp, scale=-0.5), reads=[on_t.s], writes=[on_t.s])
                P.dve(lambda e: e.tensor_tensor(out=on_t[:], in0=o_t[:], in1=on_t[:], op=ALU.mult),
                      reads=[o_t.s, on_t.s], writes=[on_t.s])
                bg = proj_feat(wg_v, wg_s, h * 128, hf)
                P.act(lambda e, bg=bg: e.activation(out=sg_t[:], in_=bg[:], func=AF.Silu), reads=[bg.s], writes=[sg_t.s])
                P.dve(lambda e, hf=hf, h=h: e.scalar_tensor_tensor(
                    out=mixT[:, h, hf * 512:(hf + 1) * 512], in0=on_t[:], scalar=gn[:, l:l + 1], in1=sg_t[:],
                    op0=ALU.mult, op1=ALU.mult), reads=[on_t.s, sg_t.s, gn.s], writes=[mixs[h][hf]])

    def TTop(out, in0, in1, op, reads, writes):
        return P.dve(lambda e: e.tensor_tensor(out=out, in0=in0, in1=in1, op=op), reads=reads, writes=writes)

    def TSop(out, in0, s1, s2, op0, op1, reads, writes):
        if s2 is None:
            return P.dve(lambda e: e.tensor_scalar(out=out, in0=in0, scalar1=s1, scalar2=None, op0=op0), reads=reads, writes=writes)
        return P.dve(lambda e: e.tensor_scalar(out=out, in0=in0, scalar1=s1, scalar2=s2, op0=op0, op1=op1), reads=reads, writes=writes)

    def STTop(out, in0, scalar, in1, op0, op1, reads, writes):
        return P.dve(lambda e: e.scalar_tensor_tensor(out=out, in0=in0, scalar=scalar, in1=in1, op0=op0, op1=op1),
                     reads=reads, writes=writes)

    def ACTop(out, in_, func, reads, writes, bias=None, scale=None):
        kw = {}
        if bias is not None:
            kw["bias"] = bias
        if scale is not None:
            kw["scale"] = scale
        return P.act(lambda e: e.activation(out=out, in_=in_, func=func, **kw), reads=reads, writes=writes)

    def MM(out, lhsT, rhs, start, stop, reads, writes, tp=None):
        if tp is None:
            return P.pe(lambda e: e.matmul(out, lhsT, rhs, start=start, stop=stop), reads=reads, writes=writes)
        return P.pe(lambda e: e.matmul(out, lhsT, rhs, start=start, stop=stop, tile_position=tp), reads=reads, writes=writes)

    def CPY(out, in_, reads, writes):
        return P.dve(lambda e: e.tensor_copy(out=out, in_=in_), reads=reads, writes=writes)

    def MSET(out, val, reads, writes):
        return P.dve(lambda e: e.memset(out, val), reads=reads, writes=writes)

    def SDMA(out, in_, reads, writes):
        return P.dma("sp", lambda e: e.dma_start(out=out, in_=in_), reads=reads, writes=writes)

    NCK = 128
    SEG8 = [(0, 32), (32, 64), (64, 128)]
    TWO_PI = 2.0 * math.pi

    def s5(l):
        aphase()
        c3 = aalloc([1024], F32, "c3")
        asel = aalloc([8, 240], BF16, "asel")
        UT = aalloc([NG, NCK], BF16, "UT")
        Hb = aalloc([2, 2, 16, NCK], BF16, "Hb")
        par = TT(par_all[:, l], par_all.s)
        tab = aalloc([3, 16, 65], F32, "tab")
        hin = aalloc([2, 16, 2], F32, "hin")
        sloc = aalloc([2, 2, 16], F32, "sloc")
        hent = aalloc([2, 2, 16], F32, "hent")
        fst = aalloc([2, 2, 16, 2], F32, "fst")
        dsk = aalloc([NG], F32, "dsk")
        gat5 = aalloc([4, 2, 2, 16], F32, "gat5")
        sm = [aalloc([16], F32, "sm%d" % i) for i in range(8)]
        W0 = ar["off"]
        Bt = aalloc([NG, 2, 64], BF16, "Bt")
        bb = aalloc([2, 2, 16, 16], F32, "bb")
        craw = aalloc([2, 2, 16, 16], F32, "craw")
        pw = aalloc([2, 2, 16, 17], F32, "pw")
        R2 = ar["off"]
        uT = aalloc([4, T], BF16, "uT")
        cnat = aalloc([16, 64], F32, "cnat")
        prs = [aalloc([16, 17], F32, "prs%d" % i) for i in range(3)]
        pri = aalloc([16, 17], I32, "pri")
        afence()
        CtS = [wslots[1], wslots[2]]
        CtV = [bass.AP(w.t, 0, [[WSLOT, 128], [512, 8], [256, 2], [128, 2], [1, 128]]) for w in CtS]
        DtS = wslots[3]
        DtV = bass.AP(DtS.t, 0, [[WSLOT, 128], [128, NG], [1, 128]])

        def Ct_(gp):
            return CtV[gp // 8], gp % 8, CtS[gp // 8].s

        def _stop(k):
            if STAGE["s5_stop"] <= k:
                for kk in range(4, 8):
                    for hh in range(2):
                        MSET(mixT[:, kk, hh * 512:(hh + 1) * 512], 0.0, [], [mixs[kk][hh]])
                return True
            return False

        SDMA(c3[:], cst3_in, [], [c3.s])
        for g8 in range(8):
            TSop(asel[:, g8, :], c3[:, 0:240], c3[:, 240 + g8:241 + g8], None, ALU.mult, None, [c3.s], [asel.s])
        EV = c3[:, 608:625]
        K8 = c3[:, 640:705]
        R_even = c3[:, 480:544]
        R_odd = c3[:, 544:608]
        M5 = c3[:, 768:1024]
        wu, wus = wload(w_in[l][:, 2560:3072].rearrange("(k p) c -> p k c", p=128), KD, 512, slot=0)
        for ct in range(4):
            for hf in range(2):
                bk = proj_feat(wu, wus, ct * 128, hf)
                ACTop(uT[:, ct, hf * 512:(hf + 1) * 512], bk[:], AF.Copy, [bk.s], [uT.s])
        for g0 in range(0, NG, 4):
            bk = psum()
            for gi in range(4):
                g = g0 + gi
                ct, g8 = g // 8, g % 8
                for s_ in range(8):
                    MM(bk[:, gi * 128:(gi + 1) * 128], asel[:, g8, 112 - 16 * s_:240 - 16 * s_],
                       uT[:, ct, s_:T:8], (gi == 0 and s_ == 0), (s_ == 7), [asel.s, uT.s], [bk.s])
            ACTop(UT[:, g0:g0 + 4, :], bk[:].rearrange("p (g n) -> p g n", n=128), AF.Copy, [bk.s], [UT.s])

        for d in range(2):
            for ri, src in enumerate((s5_b_re, s5_b_im)):
                SDMA(bb[:, d, ri], src[l, d].rearrange("(gp g2) p c -> (g2 p) gp c", g2=2), [], [bb.s])
            SDMA(hin[:, d], bass.AP(st_s5.tensor, st_s5[l, d].offset, [[2, 128], [256, 16], [1, 2]]), [], [hin.s])
        for s_ in range(8):
            SDMA(dsk[16 * s_:16 * s_ + 16, :], s5_d[l].rearrange("g c -> c g"), [], [dsk.s])
        for d in range(2):
            for ri, src in enumerate((s5_c_re, s5_c_im)):
                x0 = (d * 2 + ri) * 4
                SDMA(cnat[:, x0:x0 + 4, :], src[l, d].rearrange("(ct g8) c p -> (g8 c) ct p", g8=8), [], [cnat.s])
        for d in range(2):
            for ri in range(2):
                bk = psum()
                for ct in range(4):
                    x = (d * 2 + ri) * 4 + ct
                    MM(bk[0:64, ct * 64:(ct + 1) * 64], cnat[:, x, :], R_even, True, True, [cnat.s, c3.s], [bk.s], tp=(0, 0))
                    MM(bk[64:128, ct * 64:(ct + 1) * 64], cnat[:, x, :], R_odd, True, True, [cnat.s, c3.s], [bk.s], tp=(0, 64))
                ACTop(craw[:, d, ri].rearrange("p a b -> p (a b)"), bk[:, 0:256], AF.Copy, [bk.s], [craw.s])

        if _stop(1):
            return
        for d in range(2):
            lr, li, dt_, a_, th_ = (par[:, d, i, :] for i in range(5))
            TSop(lr, lr, -1e-4, None, ALU.min, None, [par.s], [par.s])
            ACTop(dt_, dt_, AF.Exp, [par.s], [par.s])
            TTop(a_, lr, dt_, ALU.mult, [par.s], [par.s])
            TTop(th_, li, dt_, ALU.mult, [par.s], [par.s])
            TSop(th_, th_, 1.0 / TWO_PI, None, ALU.mult, None, [par.s], [par.s])

        def powers(out_r, out_i, out_m, a_ap, th_ap, evals, ng_, ne, tr, ti_, tk_i, tk_f, rs, ws):
            sh = [128, ng_, ne]
            ev_b = evals.unsqueeze(1).broadcast_to(sh)
            TTop(tr, th_ap.unsqueeze(2).broadcast_to(sh), ev_b, ALU.mult, rs + ws, ws)
            TSop(ti_, tr, 0.25, None, ALU.add, None, ws, ws)
            for (dst, src) in ((out_i, tr), (out_r, ti_)):
                CPY(tk_i, src, ws, ws)
                CPY(tk_f, tk_i, ws, ws)
                TTop(tk_f, src, tk_f, ALU.subtract, ws, ws)
                ACTop(dst, tk_f, AF.Sin, ws, ws, scale=TWO_PI)
            TTop(tr, a_ap.unsqueeze(2).broadcast_to(sh), ev_b, ALU.mult, rs + ws, ws)
            ACTop(out_m, tr, AF.Exp, ws, ws)

        for d in range(2):
            ws = [pw.s, pri.s] + [p_.s for p_ in prs]
            tkf_ = cnat[:].rearrange("p a b -> p (a b)")[:, 0:272].rearrange("p (a b) -> p a b", b=17)
            powers(pw[:, d, 0], pw[:, d, 1], prs[2][:], par[:, d, 3, :], par[:, d, 4, :], EV, 16, 17, prs[0][:], prs[1][:],
                   pri[:], tkf_, [par.s, c3.s, craw.s], ws + [cnat.s])
            TTop(pw[:, d, 0], pw[:, d, 0], prs[2][:], ALU.mult, ws, ws)
            TTop(pw[:, d, 1], pw[:, d, 1], prs[2][:], ALU.mult, ws, ws)

        for d in range(2):
            lr, li = par[:, d, 0, :], par[:, d, 1, :]
            abr, abi = pw[:, d, 0, :, 9], pw[:, d, 1, :, 9]
            nr, den, zr, zi, t1, t2 = (sm[i][:] for i in range(6))
            ws = [s_.s for s_ in sm]
            rs = [par.s, pw.s] + ws
            TSop(nr, abr, -1.0, None, ALU.add, None, rs, ws)
            TTop(t1, lr, lr, ALU.mult, rs, ws)
            TTop(t2, li, li, ALU.mult, rs, ws)
            TTop(den, t1, t2, ALU.add, rs, ws)
            P.dve(lambda e, den=den: e.reciprocal(out=den, in_=den), reads=rs, writes=ws)
            TTop(t1, nr, lr, ALU.mult, rs, ws)
            TTop(t2, abi, li, ALU.mult, rs, ws)
            TTop(zr, t1, t2, ALU.add, rs, ws)
            TTop(zr, zr, den, ALU.mult, rs, ws)
            TTop(t1, abi, lr, ALU.mult, rs, ws)
            TTop(t2, nr, li, ALU.mult, rs, ws)
            TTop(zi, t1, t2, ALU.subtract, rs, ws)
            TTop(zi, zi, den, ALU.mult, rs, ws)
            zrb = zr.unsqueeze(2).broadcast_to([128, 16, 16])
            zib = zi.unsqueeze(2).broadcast_to([128, 16, 16])
            cf = cnat[:].rearrange("p a b -> p (a b)")
            t3 = cf[:, 0:256].rearrange("p (a b) -> p a b", b=16)
            t4 = cf[:, 256:512].rearrange("p (a b) -> p a b", b=16)
            t5 = cf[:, 512:768].rearrange("p (a b) -> p a b", b=16)
            br_, bi_ = bb[:, d, 0], bb[:, d, 1]
            rs2 = rs + [bb.s, cnat.s, craw.s]
            ws2 = [bb.s, cnat.s]
            TTop(t3, br_, zrb, ALU.mult, rs2, ws2)
            TTop(t4, bi_, zib, ALU.mult, rs2, ws2)
            TTop(t5, br_, zib, ALU.mult, rs2, ws2)
            TTop(t3, t3, t4, ALU.subtract, rs2, ws2)
            TTop(t4, bi_, zrb, ALU.mult, rs2, ws2)
            TTop(bi_, t4, t5, ALU.add, rs2, ws2)
            CPY(br_, t3, rs2, ws2)

        if l == 0:
            dbg("par", par[:].rearrange("p a b c -> p (a b c)"), par.s, [128, 160])
            dbg("bb", bb[:].rearrange("p a b c d -> p (a b c d)"), bb.s, [128, 1024])
            dbg("pw", pw[:].rearrange("p a b c d -> p (a b c d)"), pw.s, [128, 1088])
            dbg("craw", craw[:].rearrange("p a b c d -> p (a b c d)"), craw.s, [128, 1024])
        def lifted(dst_r, dst_i, coef_r, coef_i, d, e_idx, conj_sign, gp0, ws):
            sh = [128, 4, 8, 16]
            pr = pw[:, d, 0, gp0:gp0 + 4, e_idx].unsqueeze(3).broadcast_to(sh)
            pi_ = pw[:, d, 1, gp0:gp0 + 4, e_idx].unsqueeze(3).broadcast_to(sh)
            cr = coef_r[:, gp0:gp0 + 4, :].unsqueeze(2).broadcast_to(sh)
            ci = coef_i[:, gp0:gp0 + 4, :].unsqueeze(2).broadcast_to(sh)
            t1 = LA[:].rearrange("p a (j c) -> p a j c", c=16)
            t2 = LB[:].rearrange("p a (j c) -> p a j c", c=16)
            rs = [pw.s, bb.s, craw.s, LA.s, LB.s]
            TTop(t1, cr, pr, ALU.mult, rs, [LA.s])
            TTop(t2, ci, pi_, ALU.mult, rs, [LB.s])
            TTop(dst_r.rearrange("p a (j c) -> p a j c", c=16), t1, t2, ALU.subtract, rs, ws)
            TTop(t1, cr, pi_, ALU.mult, rs, [LA.s])
            TTop(t2, ci, pr, ALU.mult, rs, [LB.s])
            if conj_sign > 0:
                TTop(dst_i.rearrange("p a (j c) -> p a j c", c=16), t1, t2, ALU.add, rs, ws)
            else:
                STTop(dst_i.rearrange("p a (j c) -> p a j c", c=16), t1, -1.0, t2, ALU.mult, ALU.subtract, rs, ws)

        E_B = [slice(15, 7, -1), slice(8, 16)]
        E_C = [slice(9, 17), slice(16, 8, -1)]
        E_N = [slice(7, None, -1), slice(0, 8)]

        if _stop(4):
            return
        old_r2 = [uT.s, cnat.s, pri.s] + [p_.s for p_ in prs]
        ar["off"] = R2
        XR = aalloc([4, NCK], F32, "XR")
        XI = aalloc([4, NCK], F32, "XI")
        A1 = aalloc([4, NCK], F32, "A1")
        B2 = aalloc([4, NCK], F32, "B2")
        C2 = aalloc([4, NCK], F32, "C2")
        RC = aalloc([4, NCK], F32, "RC")
        LA = aalloc([4, 128], F32, "LA2")
        LB = aalloc([4, 128], F32, "LB2")
        mnat = [aalloc([4, 128], BF16, "mnat2_%d" % i) for i in range(2)]
        new_r2 = [XR.s, XI.s, A1.s, B2.s, C2.s, RC.s, LA.s, LB.s, mnat[0].s, mnat[1].s]
        tsc = [A1, B2, C2]
        tsi = RC
        P.dve(lambda e: e.memset(fence_t[:], 0.0), reads=[], writes=old_r2 + new_r2 + [fence_t.s])

        def tables(d, tsc, tsi):
            ws = [tab.s, tsi.s] + [t_.s for t_ in tsc]
            for q in range(4):
                g_ = slice(q * 4, q * 4 + 4)
                powers(tab[:, 0, g_, :], tab[:, 1, g_, :], tab[:, 2, g_, :], par[:, d, 3, g_], par[:, d, 4, g_], K8, 4, 65,
                       tsc[0][:, :, 0:65], tsc[1][:, :, 0:65], tsi[:, :, 0:65].bitcast(I32), tsc[2][:, :, 0:65], [par.s, c3.s], ws)

        def seg_views(buf, gsl, n0, n1, d, shift):
            if d == 0:
                if shift == 0:
                    return buf[:, gsl, n0:n1]
                return buf[:, gsl, n0 + 1:n1] if shift > 0 else buf[:, gsl, n0:n1 - 1]
            lo = None if n0 == 0 else n0 - 1
            if shift == 0:
                return buf[:, gsl, n1 - 1:lo:-1]
            if shift > 0:
                return buf[:, gsl, n1 - 2:lo:-1]
            return buf[:, gsl, n1 - 1:n0:-1]

        for d in range(2):
            tables(d, tsc, tsi)
            if l == 0 and d == 0:
                dbg("tab", tab[:].rearrange("p a b c -> p (a b c)"), tab.s, [128, 3 * 16 * 65])
            if _stop(4.2):
                return
            for q in range(4):
                gp0 = q * 4
                tsl = slice(gp0, gp0 + 4)
                lifted(mnat[0][:], mnat[1][:], bb[:, d, 0], bb[:, d, 1], d, E_B[d], +1, gp0, [mnat[0].s, mnat[1].s])
                for ri in range(2):
                    bk = psum()
                    for gl in range(4):
                        MM(bk[:, gl * 128:(gl + 1) * 128], mnat[ri][:, gl, :], ident_b[:], True, True,
                           [mnat[ri].s, ident_b.s], [bk.s])
                    ACTop(Bt[:, 2 * gp0:2 * gp0 + 8, ri, :], bk[:].rearrange("p (g q) -> p g q", q=64), AF.Copy, [bk.s], [Bt.s])
                if _stop(4.3):
                    return
                for gl in range(4):
                    gp = gp0 + gl
                    bk = psum()
                    for g2 in range(2):
                        g = 2 * gp + g2
                        for ri in range(2):
                            MM(bk[64 * g2:64 * g2 + 64, ri * 128:(ri + 1) * 128], Bt[:, g, ri, :], UT[:, g, :], True, True,
                               [Bt.s, UT.s], [bk.s], tp=(0, 64 * g2))
                    ACTop(XR[:, gl, :], bk[:, 0:128], AF.Copy, [bk.s], [XR.s])
                    ACTop(XI[:, gl, :], bk[:, 128:256], AF.Copy, [bk.s], [XI.s])
                if l == 0 and d == 0:
                    dbg("XR%d" % q, XR[:].rearrange("p a b -> p (a b)"), XR.s, [128, 512])
                    dbg("Bt%d" % q, Bt[:, 2 * gp0:2 * gp0 + 8].rearrange("p a b c -> p (a b c)"), Bt.s, [128, 1024], BF16)
                    dbg("UT%d" % q, UT[:, 2 * gp0:2 * gp0 + 8].rearrange("p a b -> p (a b)"), UT.s, [128, 1024], BF16)
                if _stop(4.4):
                    return
                CPY(RC[:], tab[:, 2, tsl, 1:2].broadcast_to([128, 4, NCK]), [tab.s], [RC.s])
                for (n0, n1) in SEG8:
                    first = n0 if d == 0 else n1 - 1
                    MSET(RC[:, :, first:first + 1], 0.0, [RC.s], [RC.s])
                gsl = slice(0, 4)
                for (n0, n1) in SEG8:
                    L = n1 - n0
                    xr, xi = seg_views(XR, gsl, n0, n1, d, 0), seg_views(XI, gsl, n0, n1, d, 0)
                    a1, b2, c2 = seg_views(A1, gsl, n0, n1, d, 0), seg_views(B2, gsl, n0, n1, d, 0), seg_views(C2, gsl, n0, n1, d, 0)
                    cs_, sn_ = tab[:, 0, tsl, 1:L + 1], tab[:, 1, tsl, 1:L + 1]
                    rs = [XR.s, XI.s, tab.s, A1.s, B2.s, C2.s]
                    TTop(a1, xr, cs_, ALU.mult, rs, [A1.s])
                    TTop(c2, xi, sn_, ALU.mult, rs, [C2.s])
                    TTop(a1, a1, c2, ALU.add, rs, [A1.s])
                    TTop(b2, xi, cs_, ALU.mult, rs, [B2.s])
                    TTop(c2, xr, sn_, ALU.mult, rs, [C2.s])
                    TTop(b2, b2, c2, ALU.subtract, rs, [B2.s])

                if _stop(4.5):
                    return

                def fl(t_):
                    v = t_[:].rearrange("p a b -> p (a b)")
                    return v if d == 0 else v[:, ::-1]
                o_r, o_i, i_r, i_i, cf_ = fl(XR), fl(XI), fl(A1), fl(B2), fl(RC)
                P.dve(lambda e, o_r=o_r, i_r=i_r, cf_=cf_: e.tensor_tensor_scan(out=o_r, data0=cf_, data1=i_r, initial=0.0,
                                                                                  op0=ALU.mult, op1=ALU.add),
                      reads=[RC.s, A1.s], writes=[XR.s])
                P.dve(lambda e, o_i=o_i, i_i=i_i, cf_=cf_: e.tensor_tensor_scan(out=o_i, data0=cf_, data1=i_i, initial=0.0,
                                                                                  op0=ALU.mult, op1=ALU.add),
                      reads=[RC.s, B2.s], writes=[XI.s])
                if _stop(4.6):
                    return
                for si_, (n0, n1) in enumerate(SEG8):
                    L = n1 - n0
                    gr, gi_ = seg_views(XR, gsl, n0, n1, d, -1), seg_views(XI, gsl, n0, n1, d, -1)
                    a1, b2 = seg_views(A1, gsl, n0, n1, d, 1), seg_views(B2, gsl, n0, n1, d, 1)
                    hr = seg_views(Hb[:, d, 0], tsl, n0, n1, d, 1)
                    hi = seg_views(Hb[:, d, 1], tsl, n0, n1, d, 1)
                    cs_, sn_ = tab[:, 0, tsl, 1:L], tab[:, 1, tsl, 1:L]
                    rs = [XR.s, XI.s, tab.s, A1.s, B2.s]
                    TTop(a1, gr, cs_, ALU.mult, rs, [A1.s])
                    TTop(b2, gi_, sn_, ALU.mult, rs, [B2.s])
                    TTop(hr, a1, b2, ALU.subtract, rs, [Hb.s])
                    TTop(a1, gr, sn_, ALU.mult, rs, [A1.s])
                    TTop(b2, gi_, cs_, ALU.mult, rs, [B2.s])
                    TTop(hi, a1, b2, ALU.add, rs, [Hb.s])
                    first = n0 if d == 0 else n1 - 1
                    MSET(Hb[:, d, :, tsl, first:first + 1], 0.0, [Hb.s], [Hb.s])
                    last = n1 - 1 if d == 0 else n0
                    glr, gli = XR[:, :, last], XI[:, :, last]
                    cL, sL = tab[:, 0, tsl, L], tab[:, 1, tsl, L]
                    t1, t2 = sm[6][:, 0:4], sm[7][:, 0:4]
                    if si_ < 2:
                        dr, di = fst[:, si_, d, tsl, 0], fst[:, si_, d, tsl, 1]
                        dsl = fst.s
                    else:
                        dr, di = sloc[:, d, 0, tsl], sloc[:, d, 1, tsl]
                        dsl = sloc.s
                    rs = [XR.s, XI.s, tab.s, sm[6].s, sm[7].s, dsl]
                    TTop(t1, glr, cL, ALU.mult, rs, [sm[6].s])
                    TTop(t2, gli, sL, ALU.mult, rs, [sm[7].s])
                    TTop(dr, t1, t2, ALU.subtract, rs, [dsl])
                    TTop(t1, glr, sL, ALU.mult, rs, [sm[6].s])
                    TTop(t2, gli, cL, ALU.mult, rs, [sm[7].s])
                    TTop(di, t1, t2, ALU.add, rs, [dsl])
        if _stop(4.8):
            return
        for seq in range(2):
            for d in range(2):
                SDMA(bass.AP(ns_s5.tensor, ns_s5[seq, l, d].offset, [[2, 128], [256, 16], [1, 2]]), fst[:, seq, d], [fst.s], [])

        if _stop(5):
            return
        ccs_in, ccs_out = Slot("cc5in"), Slot("cc5out")
        SDMA(cc5_in[l], sloc[:].rearrange("p a b c -> p (a b c)"), [sloc.s], [ccs_in])
        P.dma("pool", lambda e: e.collective_compute("AllGather", ALU.bypass, replica_groups=[[0, 1, 2, 3], [4, 5, 6, 7]],
                                                     ins=[cc5_in[l]], outs=[cc5_out[l]]),
              reads=[ccs_in], writes=[ccs_out], inc=1)
        SDMA(gat5[:].rearrange("p r a b c -> p r (a b c)"), cc5_out[l].rearrange("(r p) c -> p r c", p=128), [ccs_out], [gat5.s])

        old_r2 = new_r2
        ar["off"] = R2
        LA = aalloc([4, 128], F32, "LA")
        LB = aalloc([4, 128], F32, "LB")
        mnat = [aalloc([4, 128], BF16, "mnat%d" % i) for i in range(2)]
        Dacc = aalloc([8, 128], F32, "Dacc")
        new_r2 = [LA.s, LB.s, mnat[0].s, mnat[1].s, Dacc.s]
        P.dve(lambda e: e.memset(fence_t[:], 0.0), reads=[], writes=old_r2 + new_r2 + [fence_t.s])

        for d in range(2):
            for q in range(4):
                gp0 = q * 4
                cv, g8_, cs_ = Ct_(gp0)
                lifted(cv[:, g8_:g8_ + 4, d, 0, :], cv[:, g8_:g8_ + 4, d, 1, :], craw[:, d, 0], craw[:, d, 1], d, E_C[d], -1, gp0, [cs_])
        for q in range(4):
            gp0 = q * 4
            for d in range(2):
                lifted(mnat[0][:], mnat[1][:], bb[:, d, 0], bb[:, d, 1], d, E_N[d], +1, gp0, [mnat[0].s, mnat[1].s])
                for gi in range(8):
                    gl, g2 = gi // 2, gi % 2
                    gp = gp0 + gl
                    cv, g8_, cs_ = Ct_(gp)
                    bk = psum()
                    for ri in range(2):
                        MM(bk[:, 0:128], mnat[ri][64 * g2:64 * g2 + 64, gl, :], cv[64 * g2:64 * g2 + 64, g8_, d, ri, :],
                           (ri == 0), (ri == 1), [mnat[ri].s, cs_], [bk.s])
                    msk = M5[:, d * 128:(d + 1) * 128]
                    if d == 0:
                        TTop(Dacc[:, gi, :], bk[:, 0:128], msk, ALU.mult, [bk.s, c3.s], [Dacc.s])
                    else:
                        tmpv = LA[:, gl, :] if g2 == 0 else LB[:, gl, :]
                        tmps = LA.s if g2 == 0 else LB.s
                        TTop(tmpv, bk[:, 0:128], msk, ALU.mult, [bk.s, c3.s, mnat[0].s, mnat[1].s], [tmps])
                        TTop(Dacc[:, gi, :], Dacc[:, gi, :], tmpv, ALU.add, [tmps, Dacc.s], [Dacc.s])
                        g = 2 * gp + g2
                        STTop(DtV[:, g, :], ident_f, dsk[:, g:g + 1], Dacc[:, gi, :], ALU.mult, ALU.add,
                              [cst_f.s, dsk.s, Dacc.s], [DtS.s])

        old_w0 = [Bt.s, bb.s, craw.s, pw.s] + new_r2 + [XR.s, XI.s, A1.s, B2.s, C2.s, RC.s]
        ar["off"] = W0
        DH = aalloc([16, 2, 2, 64], BF16, "DH")
        W1 = ar["off"]
        tsc2 = [aalloc([4, 128], F32, "tscb%d" % i) for i in range(3)]
        tsi2 = aalloc([4, 128], F32, "tsib")
        TRt = aalloc([16, 64], F32, "TRt")
        TIt = aalloc([16, 64], F32, "TIt")
        U1 = aalloc([16, 64], F32, "U1")
        U2 = aalloc([16, 64], F32, "U2")
        pc = [aalloc([16], F32, "pc%d" % i) for i in range(6)]
        new_w0 = [DH.s, tsi2.s, TRt.s, TIt.s, U1.s, U2.s] + [t_.s for t_ in tsc2] + [p_.s for p_ in pc]
        P.dve(lambda e: e.memset(fence_t[:], 0.0), reads=[], writes=old_w0 + new_w0 + [fence_t.s])

        def cmul(dr, di, ar_, ai_, br_, bi_, t1, t2, rs, ws):
            TTop(t1, ar_, br_, ALU.mult, rs, ws)
            TTop(t2, ai_, bi_, ALU.mult, rs, ws)
            TTop(dr, t1, t2, ALU.subtract, rs, ws)
            TTop(t1, ar_, bi_, ALU.mult, rs, ws)
            TTop(t2, ai_, br_, ALU.mult, rs, ws)
            TTop(di, t1, t2, ALU.add, rs, ws)

        for d in range(2):
            tables(d, tsc2, tsi2)
            atr, ati, cr_, ci_, t1, t2 = (p_[:] for p_ in pc)
            ws = [p_.s for p_ in pc] + [sm[0].s, sm[1].s]
            rs = [tab.s, gat5.s, hin.s, hent.s, cst_f.s] + ws
            TTop(atr, tab[:, 0, :, 64], tab[:, 2, :, 64], ALU.mult, rs, ws)
            TTop(ati, tab[:, 1, :, 64], tab[:, 2, :, 64], ALU.mult, rs, ws)
            ranks = [0, 1, 2, 3] if d == 0 else [3, 2, 1, 0]
            CPY(cr_, hin[:, d, :, 0], rs, ws)
            CPY(ci_, hin[:, d, :, 1], rs, ws)
            TSop(hent[:, d, 0], cr_, cst_f[:, 384 + ranks[0]:385 + ranks[0]], None, ALU.mult, None, rs, [hent.s])
            TSop(hent[:, d, 1], ci_, cst_f[:, 384 + ranks[0]:385 + ranks[0]], None, ALU.mult, None, rs, [hent.s])
            for i in range(3):
                r = ranks[i]
                nr_, ni_ = sm[0][:], sm[1][:]
                cmul(nr_, ni_, atr, ati, cr_, ci_, t1, t2, rs, ws)
                TTop(cr_, nr_, gat5[:, r, d, 0, :], ALU.add, rs, ws)
                TTop(ci_, ni_, gat5[:, r, d, 1, :], ALU.add, rs, ws)
                rn = ranks[i + 1]
                STTop(hent[:, d, 0], cr_, cst_f[:, 384 + rn:385 + rn], hent[:, d, 0], ALU.mult, ALU.add, rs, [hent.s])
                STTop(hent[:, d, 1], ci_, cst_f[:, 384 + rn:385 + rn], hent[:, d, 1], ALU.mult, ALU.add, rs, [hent.s])
            rs = [tab.s, hent.s, TRt.s, TIt.s, U1.s, U2.s]
            TTop(TRt[:], tab[:, 0, :, 0:64], tab[:, 2, :, 0:64], ALU.mult, rs, [TRt.s])
            TTop(TIt[:], tab[:, 1, :, 0:64], tab[:, 2, :, 0:64], ALU.mult, rs, [TIt.s])
            her = hent[:, d, 0].unsqueeze(2).broadcast_to([128, 16, 64])
            hei = hent[:, d, 1].unsqueeze(2).broadcast_to([128, 16, 64])
            dhr = DH[:, :, d, 0, :] if d == 0 else DH[:, :, d, 0, ::-1]
            dhi = DH[:, :, d, 1, :] if d == 0 else DH[:, :, d, 1, ::-1]
            TTop(U1[:], TRt[:], her, ALU.mult, rs, [U1.s])
            TTop(U2[:], TIt[:], hei, ALU.mult, rs, [U2.s])
            TTop(dhr, U1[:], U2[:], ALU.subtract, rs, [DH.s])
            TTop(U1[:], TRt[:], hei, ALU.mult, rs, [U1.s])
            TTop(U2[:], TIt[:], her, ALU.mult, rs, [U2.s])
            TTop(dhi, U1[:], U2[:], ALU.add, rs, [DH.s])

        if _stop(6):
            return
        old_w1 = new_w0[1:]
        ar["off"] = W1
        YA = aalloc([NG, NCK], BF16, "YA")
        yT = aalloc([4, T], BF16, "yT5")
        gl_t = [aalloc([512], F32, "gl%d" % i) for i in range(4)]
        new_w1 = [YA.s, yT.s] + [g_.s for g_ in gl_t]
        P.dve(lambda e: e.memset(fence_t[:], 0.0), reads=[], writes=old_w1 + new_w1 + [fence_t.s])

        for g0 in range(0, NG, 4):
            bk = psum()
            for gi in range(4):
                g = g0 + gi
                gp, g2 = g // 2, g % 2
                cv, g8_, cs_ = Ct_(gp)
                cols = slice(gi * 128, (gi + 1) * 128)
                MM(bk[:, cols], DtV[:, g, :], UT[:, g, :], (gi == 0), False, [DtS.s, UT.s], [bk.s])
                for d in range(2):
                    for ri in range(2):
                        MM(bk[:, cols], cv[64 * g2:64 * g2 + 64, g8_, d, ri, :], Hb[64 * g2:64 * g2 + 64, d, ri, gp, :], False, False,
                           [cs_, Hb.s], [bk.s])
                for d in range(2):
                    for ri in range(2):
                        MM(bk[:, gi * 128 + 64:(gi + 1) * 128], cv[64 * g2:64 * g2 + 64, g8_, d, ri, :],
                           DH[64 * g2:64 * g2 + 64, gp, d, ri, :], False, (d == 1 and ri == 1), [cs_, DH.s], [bk.s])
            xs_, sq_, u_, sg_ = gl_t
            ACTop(xs_[:], bk[:], AF.Copy, [bk.s], [xs_.s])
            ACTop(sq_[:], bk[:], AF.Square, [bk.s], [sq_.s])
            TSop(sq_[:], sq_[:], 0.044715, 1.0, ALU.mult, ALU.add, [sq_.s], [sq_.s])
            TTop(u_[:], sq_[:], xs_[:], ALU.mult, [sq_.s, xs_.s], [u_.s])
            ACTop(sg_[:], u_[:], AF.Sigmoid, [u_.s], [sg_.s], scale=2.0 * math.sqrt(2.0 / math.pi))
            TTop(YA[:, g0:g0 + 4, :].rearrange("p g n -> p (g n)"), xs_[:], sg_[:], ALU.mult, [xs_.s, sg_.s], [YA.s])

        for ct in range(4):
            for t0 in range(0, 8, 4):
                bk = psum()
                for ti in range(4):
                    t_ = t0 + ti
                    for g8 in range(8):
                        g = ct * 8 + g8
                        MM(bk[:, ti * 128:(ti + 1) * 128], asel[:, t_, 112 - 16 * g8:240 - 16 * g8],
                           YA[:, g, :], (ti == 0 and g8 == 0), (g8 == 7), [asel.s, YA.s], [bk.s])
                ACTop(yT[:, ct, :].rearrange("p (n t) -> p t n", t=8)[:, t0:t0 + 4, :],
                      bk[:].rearrange("p (t n) -> p t n", n=128), AF.Copy, [bk.s], [yT.s])

        wgl, wgls = wload(s5_w_glu[l].rearrange("(k p) c -> p k c", p=128), 4, 512, slot=0)
        for c2 in range(4):
            for hf in range(2):
                bk = psum()
                for ct in range(4):
                    MM(bk[:], wgl[:, ct, c2 * 128:(c2 + 1) * 128], yT[:, ct, hf * 512:(hf + 1) * 512], (ct == 0), (ct == 3),
                       [wgls, yT.s], [bk.s])
                sgl = gl_t[(c2 * 2 + hf) % 2]
                ACTop(sgl[:], bk[:], AF.Sigmoid, [bk.s], [sgl.s])
                TTop(mixT[:, 4 + c2, hf * 512:(hf + 1) * 512], yT[:, c2, hf * 512:(hf + 1) * 512], sgl[:], ALU.mult,
                     [yT.s, sgl.s], [mixs[4 + c2][hf]])


    for l in range(DEPTH):
        compute_mod(l)
        norm_mod(l, 0, 0)
        if STAGE["s5"]:
            s5(l)
        else:
            for k in range(4, 8):
                for h in range(2):
                    P.dve(lambda e, k=k, h=h: e.memset(mixT[:, k, h * 512:(h + 1) * 512], 0.0), writes=[mixs[k][h]])
        if STAGE["hg"]:
            hgrn(l)
        else:
            for k in range(0, 4):
                for h in range(2):
                    P.dve(lambda e, k=k, h=h: e.memset(mixT[:, k, h * 512:(h + 1) * 512], 0.0), writes=[mixs[k][h]])
        residual_proj(w_out[l], l, 16, mixT, mixs, KD)
        norm_mod(l, 1, 24)
        ffn(l)

    nfin32 = sb([128, KD], F32, "nfin32")
    P.dve(lambda e: e.tensor_scalar(out=nfin32[:], in0=nfin[:], scalar1=32.0, scalar2=None, op0=ALU.mult),
          reads=[nfin.s], writes=[nfin32.s])
    aphase()
    ytok = [aalloc([D], F32, "ytok%d" % j) for j in range(4)]
    yT = aalloc([512], F32, "yT")
    afence()
    for h in range(2):
        rms_stats(h)
        for k in range(KD):
            P.dve(lambda e, k=k, h=h: e.scalar_tensor_tensor(
                out=yT[:], in0=xT[:, k, h * 512:(h + 1) * 512], scalar=nfin32[:, k:k + 1], in1=rstd[:],
                op0=ALU.mult, op1=ALU.mult), reads=[xs[k][h], nfin32.s, rstd.s], writes=[yT.s])
            bk = psum()
            for j in range(4):
                P.pe(lambda e, bk=bk, j=j: e.transpose(bk[:, j * 128:(j + 1) * 128], yT[:, j * 128:(j + 1) * 128], ident_f),
                     reads=[yT.s, cst_f.s], writes=[bk.s])
            for j in range(4):
                P.act(lambda e, bk=bk, j=j, k=k: e.activation(out=ytok[j][:, k * 128:(k + 1) * 128],
                                                               in_=bk[:, j * 128:(j + 1) * 128], func=AF.Copy),
                      reads=[bk.s], writes=[ytok[j].s])
        for j in range(4):
            tt = h * 4 + j
            P.dma("sp", lambda e, j=j, tt=tt: e.dma_start(out=y_out[tt * 128:(tt + 1) * 128, :], in_=ytok[j][:]),
                  reads=[ytok[j].s], writes=[])

    with nc.allow_non_contiguous_dma(reason="small strided parameter loads"):
        P.emit(st)
    st.close()
    return nc


def _consts(core):
    q = core % 4
    c = np.zeros((128, 512), np.float32)
    c[:, 0:128] = np.eye(128, dtype=np.float32)
    j = np.arange(128)[:, None]
    i = np.arange(128)[None, :]
    same = (j // CH) == (i // CH)
    c[:, 128:256] = (same & (j <= i)).astype(np.float32)
    c[:, 256:384] = (same & (j >= i)).astype(np.float32)
    c[:, 384 + q] = 1.0
    nf = D // 4
    p = np.arange(128, dtype=np.float32)
    for par in range(2):
        kf = par * 128 + p
        c[:, 388 + par] = (1.0 / (np.float32(10000.0) ** (kf / np.float32(nf)))).astype(np.float32)
    t = q * 512 + np.arange(512)
    c2 = np.zeros((128, 1024), np.float32)
    c2[:, 0:512] = (t // 64).astype(np.float32)[None, :]
    c2[:, 512:1024] = (t % 64).astype(np.float32)[None, :]
    c3 = np.zeros((128, 1024), np.float32)
    for p_ in range(128):
        c3[p_, 112 + p_ % 16] = 1.0
        c3[p_, 240 + p_ // 16] = 1.0
    for g4 in range(4):
        for cc in range(16):
            c3[(2 * g4) * 16 + cc, 480 + g4 * 16 + cc] = 1.0
            c3[(2 * g4 + 1) * 16 + cc, 544 + g4 * 16 + cc] = 1.0
    c3[:, 608:625] = np.arange(-8, 9, dtype=np.float32)[None, :]
    c3[:, 640:705] = (8.0 * np.arange(65, dtype=np.float32))[None, :]
    sI = (np.arange(128) // 16)[:, None]
    tI = (np.arange(128) // 16)[None, :]
    c3[:, 768:896] = (sI <= tI).astype(np.float32)
    c3[:, 896:1024] = (sI >= tI).astype(np.float32)
    return c, c2, c3


_NC_CACHE = {}


def kernel(**inp):
    inp = {k: np.asarray(v) for k, v in inp.items()}
    if "nc" not in _NC_CACHE:
        _NC_CACHE["nc"] = build_program()
    nc = _NC_CACHE["nc"]
    xp = inp["x_prompt"]
    xsm = inp["x_sample"]
    in_maps = []
    shared = {k: np.ascontiguousarray(inp[k], dtype=np.float32) for k in
              ("w_mod", "b_mod", "norm_mix", "norm_ffn", "norm_final", "w_in", "w_out", "w_gate", "w_up", "w_down",
               "hg_lb_logits", "hg_norm", "s5_lam_re", "s5_lam_im", "s5_log_dt", "s5_b_re", "s5_b_im", "s5_c_re", "s5_c_im",
               "s5_d", "s5_w_glu")}
    for core in range(8):
        b, q = core // 4, core % 4
        x = np.concatenate([xp[2 * core], xp[2 * core + 1], xsm[b, q * 512:(q + 1) * 512]], axis=0)
        cond = np.stack([inp["c_ctx"], inp["c"][b]], axis=0)
        c1, c2, c3 = _consts(core)
        m = dict(shared)
        m.update({"x": np.ascontiguousarray(x, dtype=np.float32), "cond": np.ascontiguousarray(cond, dtype=np.float32),
                  "st_hg": np.ascontiguousarray(inp["state_hgrn"][b], dtype=np.float32), "cst": c1, "cst2": c2, "cst3": c3,
                  "st_s5": np.ascontiguousarray(inp["state_s5"][b], dtype=np.float32)})
        in_maps.append(m)
    res = run_bass_kernel_spmd(nc, in_maps, core_ids=list(range(8)))
    outs = res.results
    _DBG["outs"] = outs
    y_prompt = np.zeros_like(xp)
    y_sample = np.zeros_like(xsm)
    ns_hg = np.zeros((16, DEPTH, 2, NH, DK, DK), np.float32)
    ns_s5 = np.zeros((16, DEPTH, 2, NG, SP, 2), np.float32)
    for core in range(8):
        b, q = core // 4, core % 4
        y = outs[core]["y"]
        y_prompt[2 * core] = y[0:256]
        y_prompt[2 * core + 1] = y[256:512]
        y_sample[b, q * 512:(q + 1) * 512] = y[512:1024]
        ns_hg[2 * core:2 * core + 2] = outs[core]["ns_hg"]
        ns_s5[2 * core:2 * core + 2] = outs[core]["ns_s5"]
    return (y_prompt, y_sample, ns_hg, ns_s5)
```

```python
import math
from contextlib import ExitStack

import numpy as np
import concourse.bass as bass
import concourse.mybir as mybir
from concourse.bass_utils import run_bass_kernel_spmd

F32 = mybir.dt.float32
BF16 = mybir.dt.bfloat16
AF = mybir.ActivationFunctionType
ALU = mybir.AluOpType

ENGS = ("pe", "act", "dve", "pool", "sp")
SAME_ENGINE_RAW_DIST = 2


class Slot:
    __slots__ = ("name", "w", "r", "al")

    def __init__(self, name):
        self.name = name
        self.w = None
        self.r = []
        self.al = [self]


def alias(*slots):
    grp = []
    for s in slots:
        for a in s.al:
            if a not in grp:
                grp.append(a)
    for s in grp:
        s.al = grp


class Op:
    __slots__ = ("eng", "fn", "deps", "raw", "dma", "idx", "milestone", "mcount", "dsem", "dval", "inc", "eidx")


class Prog:
    def __init__(self, nc, n_dma_sems=8, sync_same_engine=True):
        self.nc = nc
        self.ops = []
        self.n_dma_sems = n_dma_sems
        self.sync_same = sync_same_engine

    def add(self, eng, fn, reads=(), writes=(), dma=False, inc=16):
        op = Op()
        op.eng, op.fn, op.dma, op.inc = eng, fn, dma, inc
        op.deps = set()
        op.raw = set()
        op.milestone = False
        op.mcount = 0
        op.dsem = None
        op.dval = 0
        op.idx = len(self.ops)
        for s0 in reads:
            for s in s0.al:
                if s.w is not None:
                    op.deps.add(s.w)
                    op.raw.add(s.w)
        for s0 in writes:
            for s in s0.al:
                if s.w is not None:
                    op.deps.add(s.w)
                op.deps.update(s.r)
        for s in reads:
            s.r.append(op.idx)
        for s in writes:
            s.w = op.idx
            s.r = []
        op.deps.discard(op.idx)
        self.ops.append(op)
        return op

    def pe(self, fn, reads=(), writes=()):
        return self.add("pe", fn, reads, writes)

    def act(self, fn, reads=(), writes=()):
        return self.add("act", fn, reads, writes)

    def dve(self, fn, reads=(), writes=()):
        return self.add("dve", fn, reads, writes)

    def dma(self, eng, fn, reads=(), writes=(), inc=16):
        return self.add(eng, fn, reads, writes, dma=True, inc=inc)

    def emit(self, stack):
        nc = self.nc
        ops = self.ops
        ecount = {e: 0 for e in ENGS}
        for op in ops:
            op.eidx = ecount[op.eng]
            ecount[op.eng] += 1

        def needs_sync(op, dop):
            if dop.dma or op.dma or dop.eng != op.eng:
                return True
            if dop.eng == "pe" or not self.sync_same:
                return False
            return (dop.idx in op.raw) and (op.eidx - dop.eidx < SAME_ENGINE_RAW_DIST)

        self.needs_sync = needs_sync
        for op in ops:
            for d in op.deps:
                dop = ops[d]
                if dop.dma:
                    continue
                if not needs_sync(op, dop):
                    continue
                dop.milestone = True
        cnt = {e: 0 for e in ENGS}
        for op in ops:
            if not op.dma and op.milestone:
                cnt[op.eng] += 1
            op.mcount = cnt[op.eng]
        esem = {e: stack.enter_context(nc.semaphore("s_" + e)) for e in ENGS}
        dsems = {e: None for e in ENGS}
        dcount = {}
        dn = {e: 0 for e in ENGS}
        for op in ops:
            if op.dma:
                if dsems[op.eng] is None:
                    dsems[op.eng] = [stack.enter_context(nc.semaphore("d_%s_%d" % (op.eng, i)))
                                     for i in range(self.n_dma_sems)]
                k = dn[op.eng]
                dn[op.eng] += 1
                op.dsem = (op.eng, k % self.n_dma_sems)
                dcount[op.dsem] = dcount.get(op.dsem, 0) + op.inc
                op.dval = dcount[op.dsem]
        per = {e: [o for o in ops if o.eng == e] for e in ENGS}
        block = stack.enter_context(nc.Block())
        sync_same = self.sync_same

        def make(e):
            def body(eng):
                waited = {}

                def wait(key, sem, val):
                    if waited.get(key, 0) >= val:
                        return
                    waited[key] = val
                    eng.wait_ge(sem, val)

                for op in per[e]:
                    for d in sorted(op.deps):
                        dop = ops[d]
                        if dop.dma:
                            wait(("d",) + dop.dsem, dsems[dop.dsem[0]][dop.dsem[1]], dop.dval)
                        else:
                            if not self.needs_sync(op, dop):
                                continue
                            wait(("e", dop.eng), esem[dop.eng], dop.mcount)
                    if op.dma:
                        prev = op.dval - op.inc
                        if prev > 0:
                            wait(("d",) + op.dsem, dsems[op.dsem[0]][op.dsem[1]], prev)
                        ins = op.fn(eng)
                        ins.then_inc(dsems[op.dsem[0]][op.dsem[1]], op.inc)
                    else:
                        ins = op.fn(eng)
                        if op.milestone:
                            ins.then_inc(esem[e], 1)
                if dsems[e] is not None:
                    for i, s in enumerate(dsems[e]):
                        v = dcount.get((e, i), 0)
                        if v:
                            wait(("d", e, i), s, v)
            return body

        block.tensor(make("pe"))
        block.scalar(make("act"))
        block.vector(make("dve"))
        block.gpsimd(make("pool"))
        block.sync(make("sp"))


D = 1024
KD = 8
T = 1024
NT = 8
DEPTH = 2
HG_W = 512
NH = 4
DK = 128
S5_W = 512
NG = 32
SP = 64
IN_W = 3072
DFF = 2816
NF = 22
EPS = 1e-6
CH = 32
NCH = T // CH
SEGS = [(0, 8), (8, 16), (16, 32)]
WSLOT = 4096
N_WSLOT = 4
CCW = 1096
ARENA_B = 90 * 1024 // 2

I32 = mybir.dt.int32
STAGE = {"hg": True, "s5": True, "s5_stop": 99}
DEBUG = False
_DBG = {}


class TT:
    def __init__(self, t, slot):
        self.t = t
        self.s = slot

    def __getitem__(self, k):
        return self.t[k]


def build_program():
    nc = bass.Bass("TRN2", target_bir_lowering=False)
    st = ExitStack()
    P = Prog(nc)

    def din(name, shape):
        return nc.dram_tensor(name, list(shape), F32, kind="ExternalInput").ap()

    def dout(name, shape):
        return nc.dram_tensor(name, list(shape), F32, kind="ExternalOutput").ap()

    x_in = din("x", [T, D])
    cond_in = din("cond", [2, D])
    w_mod = din("w_mod", [DEPTH, D, 6 * D])
    b_mod = din("b_mod", [DEPTH, 6 * D])
    norm_mix = din("norm_mix", [DEPTH, D])
    norm_ffn = din("norm_ffn", [DEPTH, D])
    norm_final = din("norm_final", [D])
    w_in = din("w_in", [DEPTH, D, IN_W])
    w_out = din("w_out", [DEPTH, D, D])
    w_gate = din("w_gate", [DEPTH, D, DFF])
    w_up = din("w_up", [DEPTH, D, DFF])
    w_down = din("w_down", [DEPTH, DFF, D])
    hg_lb = din("hg_lb_logits", [2, DEPTH, HG_W])
    hg_norm = din("hg_norm", [DEPTH, DK])
    st_hg = din("st_hg", [DEPTH, 2, NH, DK, DK])
    cst = din("cst", [128, 512])
    cst2_in = din("cst2", [128, 1024])
    cst3_in = din("cst3", [128, 1024])
    s5_lam_re = din("s5_lam_re", [DEPTH, 2, NG, SP])
    s5_lam_im = din("s5_lam_im", [DEPTH, 2, NG, SP])
    s5_log_dt = din("s5_log_dt", [DEPTH, 2, NG])
    s5_b_re = din("s5_b_re", [DEPTH, 2, NG, SP, 16])
    s5_b_im = din("s5_b_im", [DEPTH, 2, NG, SP, 16])
    s5_c_re = din("s5_c_re", [DEPTH, 2, NG, 16, SP])
    s5_c_im = din("s5_c_im", [DEPTH, 2, NG, 16, SP])
    s5_d = din("s5_d", [DEPTH, NG, 16])
    s5_w_glu = din("s5_w_glu", [DEPTH, S5_W, S5_W])
    st_s5 = din("st_s5", [DEPTH, 2, NG, SP, 2])
    y_out = dout("y", [T, D])
    ns_hg = dout("ns_hg", [2, DEPTH, 2, NH, DK, DK])
    ns_s5 = dout("ns_s5", [2, DEPTH, 2, NG, SP, 2])
    cc5_in = [nc.dram_tensor("cc5_in%d" % i, [128, 64], F32, kind="Internal").ap() for i in range(DEPTH)]
    cc5_out = [nc.dram_tensor("cc5_out%d" % i, [4 * 128, 64], F32, kind="Internal").ap() for i in range(DEPTH)]
    cc_in = [nc.dram_tensor("cc_in%d" % i, [128, CCW], F32, kind="Internal").ap() for i in range(2 * DEPTH)]
    cc_out = [nc.dram_tensor("cc_out%d" % i, [4 * 128, CCW], F32, kind="Internal").ap() for i in range(2 * DEPTH)]

    _n = [0]
    dbg_list = []

    def dbg(name, ap, slot, shape, dtype=F32):
        if not DEBUG:
            return
        t = nc.dram_tensor("dbg_" + name, list(shape), dtype, kind="ExternalOutput").ap()
        P.dma("sp", lambda e: e.dma_start(out=t, in_=ap), reads=[slot], writes=[])

    def sb(shape, dtype, name=None):
        _n[0] += 1
        name = "sb_" + (name or "t%d" % _n[0])
        t = st.enter_context(nc.sbuf_tensor(name, list(shape), dtype))
        return TT(t, Slot(name))

    banks = [TT(st.enter_context(nc.psum_tensor("ps%d" % i, [128, 512], F32)), Slot("ps%d" % i)) for i in range(8)]
    _pb = [0]

    def psum():
        b = banks[_pb[0] % 6]
        _pb[0] += 1
        return b

    wslots = [sb([128, WSLOT], BF16, "wslot%d" % i) for i in range(N_WSLOT)]
    _ws = [0]

    def wload(src_ap, a, b, slot=None):
        if slot is None:
            w = wslots[_ws[0] % N_WSLOT]
            _ws[0] += 1
        else:
            w = wslots[slot]
        view = bass.AP(w.t, 0, [[WSLOT, 128], [b, a], [1, b]])
        P.dma("pool", lambda e, v=view, s=src_ap: e.dma_start(out=v, in_=s), writes=[w.s])
        return view, w.s

    arena_t = st.enter_context(nc.sbuf_tensor("arena", [128, ARENA_B], BF16))
    ar = {"off": 0, "live": []}

    def aalloc(free_shape, dtype, name):
        n = 1
        for v in free_shape:
            n *= v
        nb = n * (1 if dtype == BF16 else 2)
        nb = (nb + 1) // 2 * 2
        off = ar["off"]
        assert off + nb <= ARENA_B, ("arena overflow", name, off, nb)
        ar["off"] = off + nb
        v = arena_t[:, off:off + nb]
        if dtype != BF16:
            v = v.bitcast(dtype)
        if len(free_shape) > 1:
            names = "abcdefg"[:len(free_shape)]
            kw = {names[i]: free_shape[i] for i in range(1, len(free_shape))}
            v = v.rearrange("p (%s) -> p %s" % (" ".join(names), " ".join(names)), **kw)
        s = Slot(name)
        ar["live"].append(s)
        return TT(v, s)

    fence_t = sb([128, 2], F32, "fence")

    def aphase(new_names_hint=None):
        old = ar["live"]
        ar["live"] = []
        ar["off"] = 0
        ar["pending"] = old

    def afence():
        old = ar.get("pending", [])
        new = list(ar["live"])
        P.dve(lambda e: e.memset(fence_t[:], 0.0), reads=[], writes=old + new + [fence_t.s])
        ar["pending"] = []

    cst_f = sb([128, 512], F32, "cst_f")
    P.dma("sp", lambda e: e.dma_start(out=cst_f[:], in_=cst), writes=[cst_f.s])
    ident_f = cst_f[:, 0:128]
    ident_b = sb([128, 128], BF16, "ident_b")
    P.dve(lambda e: e.tensor_copy(out=ident_b[:], in_=cst_f[:, 0:128]), reads=[cst_f.s], writes=[ident_b.s])
    ones_b = sb([128, 128], BF16, "ones_b")
    P.dve(lambda e: e.memset(ones_b[:], 1.0), writes=[ones_b.s])
    zeros_f = sb([128, 128], F32, "zeros_f")
    P.dve(lambda e: e.memset(zeros_f[:], 0.0), writes=[zeros_f.s])
    epsc = sb([128, 2], F32, "epsc")
    P.dve(lambda e: e.memset(epsc[:, 0:1], float(D * EPS)), writes=[epsc.s])
    P.dve(lambda e: e.memset(epsc[:, 1:2], float(DK * EPS)), reads=[epsc.s], writes=[epsc.s])
    lnc = sb([128, 1], F32, "lnc")
    P.dve(lambda e: e.memset(lnc[:], float(math.log(DK ** -0.5))), writes=[lnc.s])

    xT = sb([128, KD, T], F32, "xT")
    xs = [[Slot("xT%d_%d" % (k, h)) for h in range(2)] for k in range(KD)]
    hT = sb([128, KD, T], BF16, "hT")
    hs = [[Slot("hT%d_%d" % (k, h)) for h in range(2)] for k in range(KD)]
    mixT = sb([128, KD, T], BF16, "mixT")
    mixs = [[Slot("mix%d_%d" % (k, h)) for h in range(2)] for k in range(KD)]

    def sin_turns(out, u, ki, kf, ap=lambda t: t[:]):
        P.dve(lambda e: e.tensor_copy(out=ap(ki), in_=ap(u)), reads=[u.s], writes=[ki.s])
        P.dve(lambda e: e.tensor_copy(out=ap(kf), in_=ap(ki)), reads=[ki.s], writes=[kf.s])
        P.dve(lambda e: e.tensor_tensor(out=ap(kf), in0=ap(u), in1=ap(kf), op=ALU.subtract), reads=[u.s, kf.s], writes=[kf.s])
        P.act(lambda e: e.activation(out=ap(out), in_=ap(kf), func=AF.Sin, scale=2 * math.pi), reads=[kf.s], writes=[out.s])

    aphase()
    xtok = [aalloc([D], F32, "xtok%d" % i) for i in range(NT)]
    posarg = aalloc([512], F32, "posarg")
    cst2 = aalloc([1024], F32, "cst2")
    posk_i = aalloc([512], I32, "posk_i")
    posk_f = aalloc([512], F32, "posk_f")
    afence()
    P.dma("sp", lambda e: e.dma_start(out=cst2[:], in_=cst2_in), writes=[cst2.s])
    for tt in range(NT):
        P.dma("sp", lambda e, tt=tt: e.dma_start(out=xtok[tt][:], in_=x_in[tt * 128:(tt + 1) * 128, :]),
              writes=[xtok[tt].s])
    for k in range(KD):
        for h in range(2):
            b = psum()
            for j in range(4):
                tt = h * 4 + j
                P.pe(lambda e, b=b, j=j, tt=tt, k=k: e.transpose(b[:, j * 128:(j + 1) * 128],
                                                                   xtok[tt][:, k * 128:(k + 1) * 128], ident_f),
                     reads=[xtok[tt].s, cst_f.s], writes=[b.s])
            P.act(lambda e, b=b, k=k, h=h: e.activation(out=xT[:, k, h * 512:(h + 1) * 512], in_=b[:], func=AF.Copy),
                  reads=[b.s], writes=[xs[k][h]])

    posi = aalloc_late = None
    for k in range(KD):
        blk = k // 2
        pos_src = cst2[:, 0:512] if blk < 2 else cst2[:, 512:1024]
        om = cst_f[:, 388 + (k % 2):389 + (k % 2)]
        P.dve(lambda e, pos_src=pos_src, om=om: e.tensor_scalar(
            out=posarg[:], in0=pos_src, scalar1=om, scalar2=1.0 / (2 * math.pi), op0=ALU.mult, op1=ALU.mult),
            reads=[cst2.s, cst_f.s], writes=[posarg.s])
        if blk % 2 == 1:
            P.dve(lambda e: e.tensor_scalar(out=posarg[:], in0=posarg[:], scalar1=0.25, scalar2=None, op0=ALU.add),
                  reads=[posarg.s], writes=[posarg.s])
        sin_turns(posarg, posarg, posk_i, posk_f)
        P.dve(lambda e, k=k: e.tensor_tensor(out=xT[:, k, 512:1024], in0=xT[:, k, 512:1024], in1=posarg[:], op=ALU.add),
              reads=[posarg.s, xs[k][1]], writes=[xs[k][1]])

    condf = sb([128, KD, 2], F32, "condf")
    condb = sb([128, KD, 2], BF16, "condb")
    for j in range(2):
        P.dma("sp", lambda e, j=j: e.dma_start(out=condf[:, :, j], in_=cond_in[j].rearrange("(k p) -> p k", p=128)),
              writes=[condf.s])
    P.act(lambda e: e.activation(out=condb[:], in_=condf[:], func=AF.Silu), reads=[condf.s], writes=[condb.s])

    nmix = sb([128, DEPTH, KD], F32, "nmix")
    nffn = sb([128, DEPTH, KD], F32, "nffn")
    nfin = sb([128, KD], F32, "nfin")
    bmod = sb([128, DEPTH, 48], F32, "bmod")
    P.dma("sp", lambda e: e.dma_start(out=nmix[:], in_=norm_mix.rearrange("l (k p) -> p l k", p=128)), writes=[nmix.s])
    P.dma("sp", lambda e: e.dma_start(out=nffn[:], in_=norm_ffn.rearrange("l (k p) -> p l k", p=128)), writes=[nffn.s])
    P.dma("sp", lambda e: e.dma_start(out=nfin[:], in_=norm_final.rearrange("(k p) -> p k", p=128)), writes=[nfin.s])
    P.dma("sp", lambda e: e.dma_start(out=bmod[:], in_=b_mod.rearrange("l (k p) -> p l k", p=128)), writes=[bmod.s])

    lbl = sb([128, 2, DEPTH, NH], F32, "lbl")
    for d in range(2):
        for l in range(DEPTH):
            P.dma("sp", lambda e, d=d, l=l: e.dma_start(out=lbl[:, d, l, :], in_=hg_lb[d, l].rearrange("(h p) -> p h", p=128)),
                  writes=[lbl.s])
    lb = sb([128, DEPTH, 2, NH], F32, "lb")
    oml = sb([128, DEPTH, 2, NH], F32, "oml")
    noml = sb([128, DEPTH, 2, NH], F32, "noml")
    P.dve(lambda e: e.memset(lb[:], 0.0), writes=[lb.s])
    P.dve(lambda e: e.tensor_tensor(out=lb[:, 1], in0=lbl[:, :, 1, :], in1=lbl[:, :, 0, :], op=ALU.subtract),
          reads=[lbl.s, lb.s], writes=[lb.s])
    P.act(lambda e: e.activation(out=lb[:, 1], in_=lb[:, 1], func=AF.Sigmoid), reads=[lb.s], writes=[lb.s])
    P.dve(lambda e: e.tensor_scalar(out=oml[:], in0=lb[:], scalar1=-1.0, scalar2=1.0, op0=ALU.mult, op1=ALU.add),
          reads=[lb.s], writes=[oml.s])
    P.dve(lambda e: e.tensor_scalar(out=noml[:], in0=lb[:], scalar1=1.0, scalar2=-1.0, op0=ALU.mult, op1=ALU.add),
          reads=[lb.s], writes=[noml.s])
    gn = sb([128, DEPTH], F32, "gn")
    P.dma("sp", lambda e: e.dma_start(out=gn[:], in_=hg_norm.rearrange("l p -> p l")), writes=[gn.s])
    P.dve(lambda e: e.tensor_scalar(out=gn[:], in0=gn[:], scalar1=float(math.sqrt(DK)), scalar2=None, op0=ALU.mult),
          reads=[gn.s], writes=[gn.s])

    par_all = sb([128, DEPTH, 2, 5, 16], F32, "par_all")
    for l_ in range(DEPTH):
        for d_ in range(2):
            P.dma("sp", lambda e, l_=l_, d_=d_: e.dma_start(
                out=par_all[:, l_, d_, 0, :], in_=s5_lam_re[l_, d_].rearrange("(gp g2) p -> (g2 p) gp", g2=2)), writes=[par_all.s])
            P.dma("sp", lambda e, l_=l_, d_=d_: e.dma_start(
                out=par_all[:, l_, d_, 1, :], in_=s5_lam_im[l_, d_].rearrange("(gp g2) p -> (g2 p) gp", g2=2)), writes=[par_all.s])
            for g2_ in range(2):
                P.dma("sp", lambda e, l_=l_, d_=d_, g2_=g2_: e.dma_start(
                    out=par_all[64 * g2_:64 * g2_ + 64, l_, d_, 2, :],
                    in_=s5_log_dt[l_, d_].rearrange("(gp g2) -> g2 gp", g2=2)[g2_].partition_broadcast(64)), writes=[par_all.s])
    mod = sb([128, DEPTH, 48, 2], F32, "mod")
    coef = sb([128, DEPTH, 2, KD, 2], F32, "coef")

    mod_s = [[Slot("mod%d_%d" % (l_, p_)) for p_ in range(3)] for l_ in range(DEPTH)]
    coef_s = [[Slot("coef%d_%d" % (l_, n_)) for n_ in range(2)] for l_ in range(DEPTH)]

    def compute_mod(l, part, slots=None):
        bk = psum()
        for ci_, cc in enumerate(range(part * 4, part * 4 + 4)):
            wv, wsl = wload(w_mod[l][:, cc * 512:(cc + 1) * 512].rearrange("(k p) c -> p k c", p=128), KD, 512,
                            slot=None if slots is None else slots[ci_])
            for j in range(4):
                ft = cc * 4 + j - part * 16
                for k in range(KD):
                    P.pe(lambda e, wv=wv, j=j, k=k, ft=ft, bk=bk: e.matmul(
                        bk[:, ft * 2:ft * 2 + 2], wv[:, k, j * 128:(j + 1) * 128], condb[:, k, :],
                        start=(k == 0), stop=(k == KD - 1)),
                        reads=[wsl, condb.s], writes=[bk.s])
        t0_ = part * 16
        P.dve(lambda e, bk=bk: e.tensor_tensor(
            out=mod[:, l, t0_:t0_ + 16], in0=bk[:, 0:32].rearrange("p (f c) -> p f c", c=2),
            in1=bmod[:, l, t0_:t0_ + 16].unsqueeze(2).broadcast_to([128, 16, 2]), op=ALU.add),
            reads=[bk.s, bmod.s], writes=[mod_s[l][part]])
        for n, (gt, base, prt) in enumerate(((nmix, 8, 0), (nffn, 32, 2))):
            if prt != part:
                continue
            P.dve(lambda e, n=n, base=base: e.tensor_scalar(
                out=coef[:, l, n], in0=mod[:, l, base:base + 8, :], scalar1=1.0, scalar2=32.0,
                op0=ALU.add, op1=ALU.mult), reads=[mod_s[l][part]], writes=[coef_s[l][n]])
            P.dve(lambda e, n=n, gt=gt: e.tensor_tensor(
                out=coef[:, l, n], in0=coef[:, l, n], in1=gt[:, l].unsqueeze(2).broadcast_to([128, KD, 2]),
                op=ALU.mult), reads=[coef_s[l][n], gt.s], writes=[coef_s[l][n]])

    sq = [sb([128, 512], BF16, "sq%d" % i) for i in range(2)]
    rstd = sb([128, 512], F32, "rstd")
    tmpn = [sb([128, 512], F32, "tmpn%d" % i) for i in range(2)]

    def rms_stats(h):
        bk = psum()
        for k in range(KD):
            s = sq[k % 2]
            P.act(lambda e, k=k, h=h, s=s: e.activation(out=s[:], in_=xT[:, k, h * 512:(h + 1) * 512], func=AF.Square),
                  reads=[xs[k][h]], writes=[s.s])
            P.pe(lambda e, k=k, bk=bk, s=s: e.matmul(bk[:], ones_b[:], s[:], start=(k == 0), stop=(k == KD - 1)),
                 reads=[s.s, ones_b.s], writes=[bk.s])
        P.act(lambda e, bk=bk: e.activation(out=rstd[:], in_=bk[:], func=AF.Ln, bias=epsc[:, 0:1]),
              reads=[bk.s, epsc.s], writes=[rstd.s])
        P.act(lambda e: e.activation(out=rstd[:], in_=rstd[:], func=AF.Exp, scale=-0.5), reads=[rstd.s], writes=[rstd.s])

    def norm_mod(l, n, shift_base):
        for h in range(2):
            rms_stats(h)
            for k in range(KD):
                tm = tmpn[k % 2]
                P.dve(lambda e, k=k, h=h, tm=tm: e.tensor_tensor(out=tm[:], in0=xT[:, k, h * 512:(h + 1) * 512],
                                                                 in1=rstd[:], op=ALU.mult),
                      reads=[xs[k][h], rstd.s], writes=[tm.s])
                P.act(lambda e, k=k, h=h, l=l, n=n, tm=tm: e.activation(
                    out=hT[:, k, h * 512:(h + 1) * 512], in_=tm[:], func=AF.Identity,
                    bias=mod[:, l, shift_base + k, h:h + 1], scale=coef[:, l, n, k, h:h + 1]),
                    reads=[tm.s, mod_s[l][shift_base // 16], coef_s[l][n]], writes=[hs[k][h]])

    def residual_proj(wdram, l, gate_base, srcT, src_slots, nk):
        per = WSLOT // 256
        for c4 in range(4):
            wv = []
            for k0 in range(0, nk, per):
                kk = min(per, nk - k0)
                v, s = wload(wdram[k0 * 128:(k0 + kk) * 128, c4 * 256:(c4 + 1) * 256].rearrange("(k p) c -> p k c", p=128),
                             kk, 256)
                wv.append((k0, kk, v, s))
            for j in range(2):
                dt_ = c4 * 2 + j
                for h in range(2):
                    bk = psum()
                    for (k0, kk, v, s) in wv:
                        for k in range(kk):
                            kg = k0 + k
                            P.pe(lambda e, v=v, k=k, j=j, kg=kg, h=h, bk=bk: e.matmul(
                                bk[:], v[:, k, j * 128:(j + 1) * 128], srcT[:, kg, h * 512:(h + 1) * 512],
                                start=(kg == 0), stop=(kg == nk - 1)),
                                reads=[s, src_slots[kg][h]], writes=[bk.s])
                    P.dve(lambda e, bk=bk, dt_=dt_, h=h, l=l: e.scalar_tensor_tensor(
                        out=xT[:, dt_, h * 512:(h + 1) * 512], in0=bk[:], scalar=mod[:, l, gate_base + dt_, h:h + 1],
                        in1=xT[:, dt_, h * 512:(h + 1) * 512], op0=ALU.mult, op1=ALU.add),
                        reads=[bk.s, mod_s[l][gate_base // 16], xs[dt_][h]], writes=[xs[dt_][h]])


    def ffn(l):
        aphase()
        h1 = aalloc([NF, T], BF16, "h1")
        sgate = [aalloc([512], F32, "sgate%d" % i) for i in range(2)]
        h1s = [[Slot("h1_%d_%d" % (f, h)) for h in range(2)] for f in range(NF)]
        ar["live"].extend([s for row in h1s for s in row])
        afence()
        it = 0
        for c in range(6):
            ncol = 512 if c < 5 else 256
            vg, sg_ = wload(w_gate[l][:, c * 512:c * 512 + ncol].rearrange("(k p) c -> p k c", p=128), KD, ncol)
            vu, su_ = wload(w_up[l][:, c * 512:c * 512 + ncol].rearrange("(k p) c -> p k c", p=128), KD, ncol)
            for j in range(ncol // 128):
                f = c * 4 + j
                for h in range(2):
                    bg = psum()
                    bu = psum()
                    for k in range(KD):
                        P.pe(lambda e, vg=vg, k=k, j=j, h=h, bg=bg: e.matmul(
                            bg[:], vg[:, k, j * 128:(j + 1) * 128], hT[:, k, h * 512:(h + 1) * 512],
                            start=(k == 0), stop=(k == KD - 1)), reads=[sg_, hs[k][h]], writes=[bg.s])
                    for k in range(KD):
                        P.pe(lambda e, vu=vu, k=k, j=j, h=h, bu=bu: e.matmul(
                            bu[:], vu[:, k, j * 128:(j + 1) * 128], hT[:, k, h * 512:(h + 1) * 512],
                            start=(k == 0), stop=(k == KD - 1)), reads=[su_, hs[k][h]], writes=[bu.s])
                    sgt = sgate[it % 2]
                    it += 1
                    P.act(lambda e, bg=bg, sgt=sgt: e.activation(out=sgt[:], in_=bg[:], func=AF.Silu),
                          reads=[bg.s], writes=[sgt.s])
                    P.dve(lambda e, bu=bu, f=f, h=h, sgt=sgt: e.tensor_tensor(
                        out=h1[:, f, h * 512:(h + 1) * 512], in0=sgt[:], in1=bu[:], op=ALU.mult),
                        reads=[sgt.s, bu.s], writes=[h1s[f][h]])
        residual_proj(w_down[l], l, 40, h1, h1s, NF)

    def proj_feat(wv, wsl, c0, h):
        bk = psum()
        for k in range(KD):
            P.pe(lambda e, k=k, bk=bk: e.matmul(bk[:], wv[:, k, c0:c0 + 128], hT[:, k, h * 512:(h + 1) * 512],
                                                start=(k == 0), stop=(k == KD - 1)),
                 reads=[wsl, hs[k][h]], writes=[bk.s])
        return bk

    def hgrn(l):
        aphase()
        qk = [[[aalloc([T], BF16, "qk%d%d%d" % (h, d, w)) for w in range(2)] for d in range(2)] for h in range(NH)]
        kendT = [[aalloc([NT, DK], BF16, "kendT%d%d" % (h, d)) for d in range(2)] for h in range(NH)]
        V = [aalloc([HG_W], BF16, "V%d" % tt) for tt in range(NT)]
        gch = aalloc([NH * 2, NCH], F32, "gch")
        gch_s = [[Slot("gch%d%d" % (h, d)) for d in range(2)] for h in range(NH)]
        ar["live"].extend([s for row in gch_s for s in row])
        R1 = ar["off"]
        qs = [aalloc([512], F32, "qs%d" % hf) for hf in range(2)]
        rmask = aalloc([512], F32, "rmask")
        tmp = [[aalloc([512], F32, "gt%d_%d" % (i, j)) for j in range(5)] for i in range(2)]
        kend_t = [aalloc([512], BF16, "kend%d" % i) for i in range(2)]
        R1_end = ar["off"]
        afence()
        P.dve(lambda e: e.memset(rmask[:], 1.0), writes=[rmask.s])
        P.dve(lambda e: e.memset(rmask[:, 0:512:CH], 0.0), reads=[rmask.s], writes=[rmask.s])

        wv_iv, ws_iv = None, None

        def load_in(c):
            return wload(w_in[l][:, c * 512:(c + 1) * 512].rearrange("(k p) c -> p k c", p=128), KD, 512)

        wq, wqs = load_in(0)
        wf = [None, None]
        wf[0] = load_in(1)
        wf[1] = load_in(2)
        it = 0
        for h in range(NH):
            for hf in range(2):
                bk = proj_feat(wq, wqs, h * 128, hf)
                P.act(lambda e, bk=bk, hf=hf: e.activation(out=qs[hf][:], in_=bk[:], func=AF.Silu),
                      reads=[bk.s], writes=[qs[hf].s])
            for d in range(2):
                for hf in range(2):
                    t_sig, t_a, t_b, t_c, t_e = tmp[it % 2]
                    ke = kend_t[it % 2]
                    it += 1
                    bk = proj_feat(wf[d][0], wf[d][1], h * 128, hf)
                    lb_ = lb[:, l, d, h:h + 1]
                    oml_ = oml[:, l, d, h:h + 1]
                    noml_ = noml[:, l, d, h:h + 1]
                    P.act(lambda e, bk=bk, t_sig=t_sig: e.activation(out=t_sig[:], in_=bk[:], func=AF.Sigmoid),
                          reads=[bk.s], writes=[t_sig.s])
                    P.dve(lambda e, t_sig=t_sig, t_a=t_a, lb_=lb_, oml_=oml_: e.tensor_scalar(
                        out=t_a[:], in0=t_sig[:], scalar1=oml_, scalar2=lb_, op0=ALU.mult, op1=ALU.add),
                        reads=[t_sig.s, lb.s, oml.s], writes=[t_a.s])
                    P.act(lambda e, t_a=t_a: e.activation(out=t_a[:], in_=t_a[:], func=AF.Ln), reads=[t_a.s], writes=[t_a.s])
                    P.dve(lambda e, t_a=t_a, t_b=t_b: e.tensor_tensor_scan(
                        out=t_b[:], data0=rmask[:], data1=t_a[:], initial=0.0, op0=ALU.mult, op1=ALU.add),
                        reads=[t_a.s, rmask.s], writes=[t_b.s])
                    P.dve(lambda e, t_sig=t_sig, noml_=noml_, oml_=oml_: e.tensor_scalar(
                        out=t_sig[:], in0=t_sig[:], scalar1=noml_, scalar2=oml_, op0=ALU.mult, op1=ALU.add),
                        reads=[t_sig.s, noml.s, oml.s], writes=[t_sig.s])
                    P.act(lambda e, t_b=t_b, h=h, d=d, hf=hf: e.activation(
                        out=gch[:, h * 2 + d, hf * (512 // CH):(hf + 1) * (512 // CH)], in_=t_b[:, CH - 1:512:CH], func=AF.Exp),
                        reads=[t_b.s], writes=[gch_s[h][d]])
                    tb3 = t_b[:].rearrange("p (c j) -> p c j", j=CH)
                    tc3 = t_c[:].rearrange("p (c j) -> p c j", j=CH)
                    ta3 = t_a[:].rearrange("p (c j) -> p c j", j=CH)
                    tot_b = tb3[:, :, CH - 1:CH].broadcast_to([128, 512 // CH, CH])
                    if d == 0:
                        P.dve(lambda e, tc3=tc3, tb3=tb3, tot_b=tot_b: e.tensor_tensor(
                            out=tc3, in0=tb3, in1=tot_b, op=ALU.subtract), reads=[t_b.s], writes=[t_c.s])
                    else:
                        P.dve(lambda e, tc3=tc3, ta3=ta3, tb3=tb3: e.tensor_tensor(
                            out=tc3, in0=ta3, in1=tb3, op=ALU.subtract), reads=[t_a.s, t_b.s], writes=[t_c.s])
                        P.dve(lambda e, tc3=tc3, tb3=tb3, tot_b=tot_b, ta3=ta3: e.tensor_tensor(
                            out=ta3, in0=tc3, in1=tot_b, op=ALU.add), reads=[t_c.s, t_b.s], writes=[t_a.s])
                    beta = t_b if d == 0 else t_a
                    P.act(lambda e, beta=beta, t_e=t_e: e.activation(out=t_e[:], in_=beta[:], func=AF.Exp, bias=lnc[:, 0:1]),
                          reads=[beta.s, lnc.s], writes=[t_e.s])
                    P.dve(lambda e, t_e=t_e, h=h, d=d, hf=hf: e.tensor_tensor(
                        out=qk[h][d][0][:, hf * 512:(hf + 1) * 512], in0=qs[hf][:], in1=t_e[:], op=ALU.mult),
                        reads=[t_e.s, qs[hf].s], writes=[qk[h][d][0].s])
                    P.dve(lambda e, beta=beta, t_e=t_e: e.tensor_scalar(out=t_e[:], in0=beta[:], scalar1=-75.0, scalar2=None, op0=ALU.max),
                          reads=[beta.s], writes=[t_e.s])
                    P.act(lambda e, t_e=t_e: e.activation(out=t_e[:], in_=t_e[:], func=AF.Exp, scale=-1.0),
                          reads=[t_e.s], writes=[t_e.s])
                    P.dve(lambda e, t_e=t_e, t_sig=t_sig, h=h, d=d, hf=hf: e.tensor_tensor(
                        out=qk[h][d][1][:, hf * 512:(hf + 1) * 512], in0=t_sig[:], in1=t_e[:], op=ALU.mult),
                        reads=[t_e.s, t_sig.s], writes=[qk[h][d][1].s])
                    P.act(lambda e, t_c=t_c, t_e=t_e: e.activation(out=t_e[:], in_=t_c[:], func=AF.Exp, scale=-1.0),
                          reads=[t_c.s], writes=[t_e.s])
                    P.dve(lambda e, t_e=t_e, t_sig=t_sig, ke=ke: e.tensor_tensor(
                        out=ke[:], in0=t_sig[:], in1=t_e[:], op=ALU.mult), reads=[t_e.s, t_sig.s], writes=[ke.s])
                    bk2 = psum()
                    for j in range(4):
                        P.pe(lambda e, bk2=bk2, j=j, ke=ke: e.matmul(bk2[:, j * 128:(j + 1) * 128], ke[:, j * 128:(j + 1) * 128],
                                                                     ident_b[:], start=True, stop=True),
                             reads=[ke.s, ident_b.s], writes=[bk2.s])
                    P.act(lambda e, bk2=bk2, h=h, d=d, hf=hf: e.activation(
                        out=kendT[h][d][:, hf * 4:(hf + 1) * 4, :], in_=bk2[:].rearrange("p (j k) -> p j k", k=128), func=AF.Copy),
                        reads=[bk2.s], writes=[kendT[h][d].s])
        wiv, wivs = load_in(3)
        for tt in range(NT):
            bk = psum()
            hf = tt // 4
            for k in range(KD):
                P.pe(lambda e, k=k, bk=bk, tt=tt: e.matmul(bk[:], hT[:, k, tt * 128:(tt + 1) * 128], wiv[:, k, :],
                                                            start=(k == 0), stop=(k == KD - 1)),
                     reads=[wivs, hs[k][hf]], writes=[bk.s])
            P.act(lambda e, bk=bk, tt=tt: e.activation(out=V[tt][:], in_=bk[:], func=AF.Copy), reads=[bk.s], writes=[V[tt].s])

        old_tmp = [t.s for grp in tmp for t in grp] + [k.s for k in kend_t] + [q.s for q in qs] + [rmask.s]
        ar["off"] = R1
        S = [aalloc([DK], F32, "S%d" % i) for i in range(2)]
        Sent = aalloc([NCH, DK], BF16, "Sent")
        o_t = aalloc([512], F32, "o_t")
        on_t = aalloc([512], F32, "on_t")
        sg_t = aalloc([512], F32, "sg_t")
        sq_t = aalloc([512], BF16, "sq_t")
        PT = [aalloc([128], BF16, "PT%d" % i) for i in range(2)]
        gat = aalloc([4, DK], F32, "gat")
        gatG = aalloc([4, 8], F32, "gatG")
        s0t = aalloc([DK], F32, "s0t")
        Pc = [aalloc([DK], F32, "Pc%d" % i) for i in range(2)]
        Sinit = aalloc([2 * NH, DK], F32, "Sinit")
        Sinit_s = [[Slot("Sinit%d%d" % (h, d)) for d in range(2)] for h in range(NH)]
        gtot = aalloc([NH * 2, NCH // 2], F32, "gtot")
        new2 = [t.s for t in S + PT + Pc] + [Sent.s, o_t.s, on_t.s, sg_t.s, sq_t.s, gat.s, gatG.s, s0t.s, Sinit.s, gtot.s] + \
               [s_ for row in Sinit_s for s_ in row]
        ar["live"].extend([s_ for row in Sinit_s for s_ in row])
        P.dve(lambda e: e.memset(fence_t[:], 0.0), reads=[], writes=old_tmp + new2 + [fence_t.s])

        CPT = 128 // CH

        def u_matmul(h, d, c):
            tt, p0 = c // CPT, (c % CPT) * CH
            bk = psum()
            P.pe(lambda e, bk=bk: e.matmul(bk[:, 0:128], kendT[h][d][p0:p0 + CH, tt, :],
                                           V[tt][p0:p0 + CH, h * 128:(h + 1) * 128],
                                           start=True, stop=True, tile_position=(p0, 0)),
                 reads=[kendT[h][d].s, V[tt].s], writes=[bk.s])
            return bk

        def scan_order(c0, c1, d):
            return list(range(c0, c1)) if d == 0 else list(range(c1 - 1, c0 - 1, -1))

        si = [0]
        SC0, SC1 = SEGS[2]

        ci = l * 2
        ccs_in, ccs_out = Slot("ccin"), Slot("ccout")
        for h in range(NH):
            for d in range(2):
                hd = h * 2 + d
                Sx = S[si[0] % 2]
                si[0] += 1
                order = scan_order(SC0, SC1, d)
                for i, c in enumerate(order):
                    bk = u_matmul(h, d, c)
                    if i == 0:
                        P.dve(lambda e, bk=bk, Sx=Sx: e.tensor_copy(out=Sx[:], in_=bk[:, 0:128]), reads=[bk.s], writes=[Sx.s])
                    else:
                        P.dve(lambda e, bk=bk, Sx=Sx, c=c, hd=hd: e.scalar_tensor_tensor(
                            out=Sx[:], in0=Sx[:], scalar=gch[:, hd, c:c + 1], in1=bk[:, 0:128],
                            op0=ALU.mult, op1=ALU.add), reads=[bk.s, Sx.s, gch_s[h][d]], writes=[Sx.s])
                P.dma("sp", lambda e, Sx=Sx, hd=hd: e.dma_start(out=cc_in[ci][:, hd * 128:(hd + 1) * 128], in_=Sx[:]),
                      reads=[Sx.s], writes=[ccs_in])
                P.dve(lambda e, hd=hd: e.tensor_tensor_scan(
                    out=gtot[:, hd, :], data0=gch[:, hd, SC0:SC1], data1=zeros_f[:, 0:SC1 - SC0], initial=1.0,
                    op0=ALU.mult, op1=ALU.add), reads=[gch_s[h][d], zeros_f.s], writes=[gtot.s])
        P.dma("sp", lambda e: e.dma_start(out=cc_in[ci][:, 1024:1032], in_=gtot[:, :, SC1 - SC0 - 1]),
              reads=[gtot.s], writes=[ccs_in])
        P.dma("sp", lambda e: e.dma_start(out=cc_in[ci][:, 1032:CCW], in_=zeros_f[:, 0:CCW - 1032]),
              reads=[zeros_f.s], writes=[ccs_in])
        P.dma("pool", lambda e: e.collective_compute("AllGather", ALU.bypass, replica_groups=[[0, 1, 2, 3], [4, 5, 6, 7]],
                                                     ins=[cc_in[ci]], outs=[cc_out[ci]]),
              reads=[ccs_in], writes=[ccs_out], inc=1)
        ccv = cc_out[ci].rearrange("(r p) c -> p r c", p=128)
        P.dma("sp", lambda e: e.dma_start(out=gatG[:], in_=ccv[:, :, 1024:1032]), reads=[ccs_out], writes=[gatG.s])
        for h in range(NH):
            for d in range(2):
                hd = h * 2 + d
                col = slice(hd * 128, (hd + 1) * 128)
                P.dma("sp", lambda e, col=col: e.dma_start(out=gat[:], in_=ccv[:, :, col]), reads=[ccs_out], writes=[gat.s])
                P.dma("sp", lambda e, d=d, h=h: e.dma_start(out=s0t[:], in_=st_hg[l, d, h]), writes=[s0t.s])
                dst = Sinit[:, hd, :]
                ranks = [0, 1, 2, 3] if d == 0 else [3, 2, 1, 0]
                prev, prev_s = s0t[:], s0t.s
                P.dve(lambda e, dst=dst, prev=prev, r=ranks[0]: e.tensor_scalar(
                    out=dst, in0=prev, scalar1=cst_f[:, 384 + r:385 + r], scalar2=None, op0=ALU.mult),
                    reads=[prev_s, cst_f.s], writes=[Sinit_s[h][d]])
                for i in range(3):
                    r = ranks[i]
                    nxt = Pc[i % 2]
                    P.dve(lambda e, nxt=nxt, prev=prev, r=r, hd=hd: e.scalar_tensor_tensor(
                        out=nxt[:], in0=prev, scalar=gatG[:, r, hd:hd + 1], in1=gat[:, r, :],
                        op0=ALU.mult, op1=ALU.add), reads=[prev_s, gat.s, gatG.s], writes=[nxt.s])
                    rn = ranks[i + 1]
                    P.dve(lambda e, nxt=nxt, dst=dst, rn=rn: e.scalar_tensor_tensor(
                        out=dst, in0=nxt[:], scalar=cst_f[:, 384 + rn:385 + rn], in1=dst, op0=ALU.mult, op1=ALU.add),
                        reads=[nxt.s, cst_f.s, Sinit_s[h][d]], writes=[Sinit_s[h][d]])
                    prev, prev_s = nxt[:], nxt.s

        wg_v, wg_s = load_in(4)
        it2 = 0
        for h in range(NH):
            bo = [banks[6], banks[7]]
            for d in range(2):
                hd = h * 2 + d
                for (c0, c1) in SEGS:
                    Sx = S[si[0] % 2]
                    si[0] += 1
                    order = scan_order(c0, c1, d)
                    is_sample = (c0 == SC0)
                    for i, c in enumerate(order):
                        if i == 0:
                            src = Sinit[:, hd, :] if is_sample else zeros_f[:]
                            src_s = Sinit_s[h][d] if is_sample else zeros_f.s
                        else:
                            src, src_s = Sx[:], Sx.s
                        P.act(lambda e, src=src, c=c: e.activation(out=Sent[:, c, :], in_=src, func=AF.Copy),
                              reads=[src_s], writes=[Sent.s])
                        last = (i == len(order) - 1)
                        if last and is_sample:
                            continue
                        bk = u_matmul(h, d, c)
                        if i == 0 and not is_sample:
                            P.dve(lambda e, bk=bk, Sx=Sx: e.tensor_copy(out=Sx[:], in_=bk[:, 0:128]), reads=[bk.s], writes=[Sx.s])
                        else:
                            P.dve(lambda e, bk=bk, Sx=Sx, src=src, c=c, hd=hd: e.scalar_tensor_tensor(
                                out=Sx[:], in0=src, scalar=gch[:, hd, c:c + 1], in1=bk[:, 0:128],
                                op0=ALU.mult, op1=ALU.add), reads=[bk.s, src_s, gch_s[h][d]], writes=[Sx.s])
                    if not is_sample:
                        seq = 0 if c0 == 0 else 1
                        P.dma("sp", lambda e, Sx=Sx, seq=seq, d=d, h=h: e.dma_start(out=ns_hg[seq, l, d, h], in_=Sx[:]),
                              reads=[Sx.s], writes=[])
                for hf in range(2):
                    for j in range(4):
                        tt = hf * 4 + j
                        tok = slice(tt * 128, (tt + 1) * 128)
                        bs = psum()
                        P.pe(lambda e, bs=bs, tok=tok, d=d, h=h: e.matmul(bs[:, 0:128], qk[h][d][1][:, tok], qk[h][d][0][:, tok],
                                                                    start=True, stop=True),
                             reads=[qk[h][d][0].s, qk[h][d][1].s], writes=[bs.s])
                        pt = PT[it2 % 2]
                        it2 += 1
                        P.dve(lambda e, bs=bs, pt=pt, d=d: e.tensor_tensor(
                            out=pt[:], in0=bs[:, 0:128], in1=cst_f[:, 128 + d * 128:256 + d * 128], op=ALU.mult),
                            reads=[bs.s, cst_f.s], writes=[pt.s])
                        oc = slice(j * 128, (j + 1) * 128)
                        P.pe(lambda e, pt=pt, tt=tt, oc=oc, hf=hf, d=d, j=j, h=h: e.matmul(
                            bo[hf][:, oc], V[tt][:, h * 128:(h + 1) * 128], pt[:], start=(d == 0 and j == 0), stop=False),
                            reads=[V[tt].s, pt.s], writes=[bo[hf].s])
                        for sub in range(CPT):
                            c = tt * CPT + sub
                            cs = slice(j * 128 + sub * CH, j * 128 + (sub + 1) * CH)
                            ts = slice(tt * 128 + sub * CH, tt * 128 + (sub + 1) * CH)
                            P.pe(lambda e, c=c, cs=cs, ts=ts, hf=hf, d=d, h=h: e.matmul(
                                bo[hf][:, cs], Sent[:, c, :], qk[h][d][0][:, ts], start=False, stop=(d == 1)),
                                reads=[Sent.s, qk[h][d][0].s], writes=[bo[hf].s])
            for hf in range(2):
                P.act(lambda e, hf=hf: e.activation(out=o_t[:], in_=bo[hf][:], func=AF.Copy), reads=[bo[hf].s], writes=[o_t.s])
                P.act(lambda e, hf=hf: e.activation(out=sq_t[:], in_=bo[hf][:], func=AF.Square), reads=[bo[hf].s], writes=[sq_t.s])
                if l == 0 and h == 0 and hf == 0:
                    dbg("o", o_t[:], o_t.s, [128, 512])
                    dbg("qf", qk[0][0][0][:], qk[0][0][0].s, [128, T], BF16)
                    dbg("kf", qk[0][0][1][:], qk[0][0][1].s, [128, T], BF16)
                    dbg("qb", qk[0][1][0][:], qk[0][1][0].s, [128, T], BF16)
                    dbg("kb", qk[0][1][1][:], qk[0][1][1].s, [128, T], BF16)
                br = psum()
                P.pe(lambda e, br=br: e.matmul(br[:], ones_b[:], sq_t[:], start=True, stop=True),
                     reads=[sq_t.s, ones_b.s], writes=[br.s])
                P.act(lambda e, br=br: e.activation(out=on_t[:], in_=br[:], func=AF.Ln, bias=epsc[:, 1:2]),
                      reads=[br.s, epsc.s], writes=[on_t.s])
                P.act(lambda e: e.activation(out=on_t[:], in_=on_t[:], func=AF.Exp, scale=-0.5), reads=[on_t.s], writes=[on_t.s])
                P.dve(lambda e: e.tensor_tensor(out=on_t[:], in0=o_t[:], in1=on_t[:], op=ALU.mult),
                      reads=[o_t.s, on_t.s], writes=[on_t.s])
                bg = proj_feat(wg_v, wg_s, h * 128, hf)
                P.act(lambda e, bg=bg: e.activation(out=sg_t[:], in_=bg[:], func=AF.Silu), reads=[bg.s], writes=[sg_t.s])
                P.dve(lambda e, hf=hf, h=h: e.scalar_tensor_tensor(
                    out=mixT[:, h, hf * 512:(hf + 1) * 512], in0=on_t[:], scalar=gn[:, l:l + 1], in1=sg_t[:],
                    op0=ALU.mult, op1=ALU.mult), reads=[on_t.s, sg_t.s, gn.s], writes=[mixs[h][hf]])

    def TTop(out, in0, in1, op, reads, writes):
        return P.dve(lambda e: e.tensor_tensor(out=out, in0=in0, in1=in1, op=op), reads=reads, writes=writes)

    def TSop(out, in0, s1, s2, op0, op1, reads, writes):
        if s2 is None:
            return P.dve(lambda e: e.tensor_scalar(out=out, in0=in0, scalar1=s1, scalar2=None, op0=op0), reads=reads, writes=writes)
        return P.dve(lambda e: e.tensor_scalar(out=out, in0=in0, scalar1=s1, scalar2=s2, op0=op0, op1=op1), reads=reads, writes=writes)

    def STTop(out, in0, scalar, in1, op0, op1, reads, writes):
        return P.dve(lambda e: e.scalar_tensor_tensor(out=out, in0=in0, scalar=scalar, in1=in1, op0=op0, op1=op1),
                     reads=reads, writes=writes)

    def ACTop(out, in_, func, reads, writes, bias=None, scale=None):
        kw = {}
        if bias is not None:
            kw["bias"] = bias
        if scale is not None:
            kw["scale"] = scale
        return P.act(lambda e: e.activation(out=out, in_=in_, func=func, **kw), reads=reads, writes=writes)

    def MM(out, lhsT, rhs, start, stop, reads, writes, tp=None):
        if tp is None:
            return P.pe(lambda e: e.matmul(out, lhsT, rhs, start=start, stop=stop), reads=reads, writes=writes)
        return P.pe(lambda e: e.matmul(out, lhsT, rhs, start=start, stop=stop, tile_position=tp), reads=reads, writes=writes)

    def CPY(out, in_, reads, writes):
        return P.dve(lambda e: e.tensor_copy(out=out, in_=in_), reads=reads, writes=writes)

    def MSET(out, val, reads, writes):
        return P.dve(lambda e: e.memset(out, val), reads=reads, writes=writes)

    def SDMA(out, in_, reads, writes):
        return P.dma("sp", lambda e: e.dma_start(out=out, in_=in_), reads=reads, writes=writes)

    NCK = 128
    SEG8 = [(0, 32), (32, 64), (64, 128)]
    TWO_PI = 2.0 * math.pi

    def s5(l, after_u=None):
        aphase()
        c3 = aalloc([1024], F32, "c3")
        asel = aalloc([8, 240], BF16, "asel")
        UT = aalloc([NG, NCK], BF16, "UT")
        Hb = aalloc([2, 2, 16, NCK], BF16, "Hb")
        par = TT(par_all[:, l], par_all.s)
        tab = aalloc([3, 16, 65], F32, "tab")
        hin = aalloc([2, 16, 2], F32, "hin")
        sloc = aalloc([2, 2, 16], F32, "sloc")
        hent = aalloc([2, 2, 16], F32, "hent")
        fst = aalloc([2, 2, 16, 2], F32, "fst")
        dsk = aalloc([NG], F32, "dsk")
        gat5 = aalloc([4, 2, 2, 16], F32, "gat5")
        sm = [aalloc([16], F32, "sm%d" % i) for i in range(8)]
        W0 = ar["off"]
        Bt = aalloc([NG, 2, 64], BF16, "Bt")
        bb = aalloc([2, 2, 16, 16], F32, "bb")
        craw = aalloc([2, 2, 16, 16], F32, "craw")
        pw = aalloc([2, 2, 16, 17], F32, "pw")
        R2 = ar["off"]
        uT = aalloc([4, T], BF16, "uT")
        cnat = aalloc([16, 64], F32, "cnat")
        prs = [aalloc([16, 17], F32, "prs%d" % i) for i in range(3)]
        pri = aalloc([16, 17], I32, "pri")
        afence()
        CtS = [wslots[1], wslots[2]]
        CtV = [bass.AP(w.t, 0, [[WSLOT, 128], [512, 8], [256, 2], [128, 2], [1, 128]]) for w in CtS]
        DtS = wslots[3]
        DtV = bass.AP(DtS.t, 0, [[WSLOT, 128], [128, NG], [1, 128]])

        def Ct_(gp):
            return CtV[gp // 8], gp % 8, CtS[gp // 8].s

        def _stop(k):
            if STAGE["s5_stop"] <= k:
                for kk in range(4, 8):
                    for hh in range(2):
                        MSET(mixT[:, kk, hh * 512:(hh + 1) * 512], 0.0, [], [mixs[kk][hh]])
                return True
            return False

        SDMA(c3[:], cst3_in, [], [c3.s])
        for g8 in range(8):
            TSop(asel[:, g8, :], c3[:, 0:240], c3[:, 240 + g8:241 + g8], None, ALU.mult, None, [c3.s], [asel.s])
        EV = c3[:, 608:625]
        K8 = c3[:, 640:705]
        R_even = c3[:, 480:544]
        R_odd = c3[:, 544:608]
        M5 = c3[:, 768:1024]
        wu, wus = wload(w_in[l][:, 2560:3072].rearrange("(k p) c -> p k c", p=128), KD, 512, slot=0)
        for ct in range(4):
            for hf in range(2):
                bk = proj_feat(wu, wus, ct * 128, hf)
                ACTop(uT[:, ct, hf * 512:(hf + 1) * 512], bk[:], AF.Copy, [bk.s], [uT.s])
        for g0 in range(0, NG, 4):
            bk = psum()
            for gi in range(4):
                g = g0 + gi
                ct, g8 = g // 8, g % 8
                for s_ in range(8):
                    MM(bk[:, gi * 128:(gi + 1) * 128], asel[:, g8, 112 - 16 * s_:240 - 16 * s_],
                       uT[:, ct, s_:T:8], (gi == 0 and s_ == 0), (s_ == 7), [asel.s, uT.s], [bk.s])
            ACTop(UT[:, g0:g0 + 4, :], bk[:].rearrange("p (g n) -> p g n", n=128), AF.Copy, [bk.s], [UT.s])
        if after_u is not None:
            after_u()

        for d in range(2):
            for ri, src in enumerate((s5_b_re, s5_b_im)):
                SDMA(bb[:, d, ri], src[l, d].rearrange("(gp g2) p c -> (g2 p) gp c", g2=2), [], [bb.s])
            SDMA(hin[:, d], bass.AP(st_s5.tensor, st_s5[l, d].offset, [[2, 128], [256, 16], [1, 2]]), [], [hin.s])
        for s_ in range(8):
            SDMA(dsk[16 * s_:16 * s_ + 16, :], s5_d[l].rearrange("g c -> c g"), [], [dsk.s])
        for d in range(2):
            for ri, src in enumerate((s5_c_re, s5_c_im)):
                x0 = (d * 2 + ri) * 4
                SDMA(cnat[:, x0:x0 + 4, :], src[l, d].rearrange("(ct g8) c p -> (g8 c) ct p", g8=8), [], [cnat.s])
        for d in range(2):
            for ri in range(2):
                bk = psum()
                for ct in range(4):
                    x = (d * 2 + ri) * 4 + ct
                    MM(bk[0:64, ct * 64:(ct + 1) * 64], cnat[:, x, :], R_even, True, True, [cnat.s, c3.s], [bk.s], tp=(0, 0))
                    MM(bk[64:128, ct * 64:(ct + 1) * 64], cnat[:, x, :], R_odd, True, True, [cnat.s, c3.s], [bk.s], tp=(0, 64))
                ACTop(craw[:, d, ri].rearrange("p a b -> p (a b)"), bk[:, 0:256], AF.Copy, [bk.s], [craw.s])

        if _stop(1):
            return
        for d in range(2):
            lr, li, dt_, a_, th_ = (par[:, d, i, :] for i in range(5))
            TSop(lr, lr, -1e-4, None, ALU.min, None, [par.s], [par.s])
            ACTop(dt_, dt_, AF.Exp, [par.s], [par.s])
            TTop(a_, lr, dt_, ALU.mult, [par.s], [par.s])
            TTop(th_, li, dt_, ALU.mult, [par.s], [par.s])
            TSop(th_, th_, 1.0 / TWO_PI, None, ALU.mult, None, [par.s], [par.s])

        def powers(out_r, out_i, out_m, a_ap, th_ap, evals, ng_, ne, tr, ti_, tk_i, tk_f, rs, ws):
            sh = [128, ng_, ne]
            ev_b = evals.unsqueeze(1).broadcast_to(sh)
            TTop(tr, th_ap.unsqueeze(2).broadcast_to(sh), ev_b, ALU.mult, rs + ws, ws)
            TSop(ti_, tr, 0.25, None, ALU.add, None, ws, ws)
            for (dst, src) in ((out_i, tr), (out_r, ti_)):
                CPY(tk_i, src, ws, ws)
                CPY(tk_f, tk_i, ws, ws)
                TTop(tk_f, src, tk_f, ALU.subtract, ws, ws)
                ACTop(dst, tk_f, AF.Sin, ws, ws, scale=TWO_PI)
            TTop(tr, a_ap.unsqueeze(2).broadcast_to(sh), ev_b, ALU.mult, rs + ws, ws)
            ACTop(out_m, tr, AF.Exp, ws, ws)

        for d in range(2):
            ws = [pw.s, pri.s] + [p_.s for p_ in prs]
            tkf_ = cnat[:].rearrange("p a b -> p (a b)")[:, 0:272].rearrange("p (a b) -> p a b", b=17)
            powers(pw[:, d, 0], pw[:, d, 1], prs[2][:], par[:, d, 3, :], par[:, d, 4, :], EV, 16, 17, prs[0][:], prs[1][:],
                   pri[:], tkf_, [par.s, c3.s, craw.s], ws + [cnat.s])
            TTop(pw[:, d, 0], pw[:, d, 0], prs[2][:], ALU.mult, ws, ws)
            TTop(pw[:, d, 1], pw[:, d, 1], prs[2][:], ALU.mult, ws, ws)

        for d in range(2):
            lr, li = par[:, d, 0, :], par[:, d, 1, :]
            abr, abi = pw[:, d, 0, :, 9], pw[:, d, 1, :, 9]
            nr, den, zr, zi, t1, t2 = (sm[i][:] for i in range(6))
            ws = [s_.s for s_ in sm]
            rs = [par.s, pw.s] + ws
            TSop(nr, abr, -1.0, None, ALU.add, None, rs, ws)
            TTop(t1, lr, lr, ALU.mult, rs, ws)
            TTop(t2, li, li, ALU.mult, rs, ws)
            TTop(den, t1, t2, ALU.add, rs, ws)
            P.dve(lambda e, den=den: e.reciprocal(out=den, in_=den), reads=rs, writes=ws)
            TTop(t1, nr, lr, ALU.mult, rs, ws)
            TTop(t2, abi, li, ALU.mult, rs, ws)
            TTop(zr, t1, t2, ALU.add, rs, ws)
            TTop(zr, zr, den, ALU.mult, rs, ws)
            TTop(t1, abi, lr, ALU.mult, rs, ws)
            TTop(t2, nr, li, ALU.mult, rs, ws)
            TTop(zi, t1, t2, ALU.subtract, rs, ws)
            TTop(zi, zi, den, ALU.mult, rs, ws)
            zrb = zr.unsqueeze(2).broadcast_to([128, 16, 16])
            zib = zi.unsqueeze(2).broadcast_to([128, 16, 16])
            cf = cnat[:].rearrange("p a b -> p (a b)")
            t3 = cf[:, 0:256].rearrange("p (a b) -> p a b", b=16)
            t4 = cf[:, 256:512].rearrange("p (a b) -> p a b", b=16)
            t5 = cf[:, 512:768].rearrange("p (a b) -> p a b", b=16)
            br_, bi_ = bb[:, d, 0], bb[:, d, 1]
            rs2 = rs + [bb.s, cnat.s, craw.s]
            ws2 = [bb.s, cnat.s]
            TTop(t3, br_, zrb, ALU.mult, rs2, ws2)
            TTop(t4, bi_, zib, ALU.mult, rs2, ws2)
            TTop(t5, br_, zib, ALU.mult, rs2, ws2)
            TTop(t3, t3, t4, ALU.subtract, rs2, ws2)
            TTop(t4, bi_, zrb, ALU.mult, rs2, ws2)
            TTop(bi_, t4, t5, ALU.add, rs2, ws2)
            CPY(br_, t3, rs2, ws2)

        if l == 0:
            dbg("par", par[:].rearrange("p a b c -> p (a b c)"), par.s, [128, 160])
            dbg("bb", bb[:].rearrange("p a b c d -> p (a b c d)"), bb.s, [128, 1024])
            dbg("pw", pw[:].rearrange("p a b c d -> p (a b c d)"), pw.s, [128, 1088])
            dbg("craw", craw[:].rearrange("p a b c d -> p (a b c d)"), craw.s, [128, 1024])
        def lifted(dst_r, dst_i, coef_r, coef_i, d, e_idx, conj_sign, gp0, ws):
            sh = [128, 4, 8, 16]
            pr = pw[:, d, 0, gp0:gp0 + 4, e_idx].unsqueeze(3).broadcast_to(sh)
            pi_ = pw[:, d, 1, gp0:gp0 + 4, e_idx].unsqueeze(3).broadcast_to(sh)
            cr = coef_r[:, gp0:gp0 + 4, :].unsqueeze(2).broadcast_to(sh)
            ci = coef_i[:, gp0:gp0 + 4, :].unsqueeze(2).broadcast_to(sh)
            t1 = LA[:].rearrange("p a (j c) -> p a j c", c=16)
            t2 = LB[:].rearrange("p a (j c) -> p a j c", c=16)
            rs = [pw.s, bb.s, craw.s, LA.s, LB.s]
            TTop(t1, cr, pr, ALU.mult, rs, [LA.s])
            TTop(t2, ci, pi_, ALU.mult, rs, [LB.s])
            TTop(dst_r.rearrange("p a (j c) -> p a j c", c=16), t1, t2, ALU.subtract, rs, ws)
            TTop(t1, cr, pi_, ALU.mult, rs, [LA.s])
            TTop(t2, ci, pr, ALU.mult, rs, [LB.s])
            if conj_sign > 0:
                TTop(dst_i.rearrange("p a (j c) -> p a j c", c=16), t1, t2, ALU.add, rs, ws)
            else:
                STTop(dst_i.rearrange("p a (j c) -> p a j c", c=16), t1, -1.0, t2, ALU.mult, ALU.subtract, rs, ws)

        E_B = [slice(15, 7, -1), slice(8, 16)]
        E_C = [slice(9, 17), slice(16, 8, -1)]
        E_N = [slice(7, None, -1), slice(0, 8)]

        if _stop(4):
            return
        old_r2 = [uT.s, cnat.s, pri.s] + [p_.s for p_ in prs]
        ar["off"] = R2
        XR = aalloc([4, NCK], F32, "XR")
        XI = aalloc([4, NCK], F32, "XI")
        A1 = aalloc([4, NCK], F32, "A1")
        B2 = aalloc([4, NCK], F32, "B2")
        C2 = aalloc([4, NCK], F32, "C2")
        RC = aalloc([4, NCK], F32, "RC")
        LA = aalloc([4, 128], F32, "LA2")
        LB = aalloc([4, 128], F32, "LB2")
        mnat = [aalloc([4, 128], BF16, "mnat2_%d" % i) for i in range(2)]
        new_r2 = [XR.s, XI.s, A1.s, B2.s, C2.s, RC.s, LA.s, LB.s, mnat[0].s, mnat[1].s]
        tsc = [A1, B2, C2]
        tsi = RC
        P.dve(lambda e: e.memset(fence_t[:], 0.0), reads=[], writes=old_r2 + new_r2 + [fence_t.s])

        def tables(d, tsc, tsi):
            ws = [tab.s, tsi.s] + [t_.s for t_ in tsc]
            for q in range(4):
                g_ = slice(q * 4, q * 4 + 4)
                powers(tab[:, 0, g_, :], tab[:, 1, g_, :], tab[:, 2, g_, :], par[:, d, 3, g_], par[:, d, 4, g_], K8, 4, 65,
                       tsc[0][:, :, 0:65], tsc[1][:, :, 0:65], tsi[:, :, 0:65].bitcast(I32), tsc[2][:, :, 0:65], [par.s, c3.s], ws)

        def seg_views(buf, gsl, n0, n1, d, shift):
            if d == 0:
                if shift == 0:
                    return buf[:, gsl, n0:n1]
                return buf[:, gsl, n0 + 1:n1] if shift > 0 else buf[:, gsl, n0:n1 - 1]
            lo = None if n0 == 0 else n0 - 1
            if shift == 0:
                return buf[:, gsl, n1 - 1:lo:-1]
            if shift > 0:
                return buf[:, gsl, n1 - 2:lo:-1]
            return buf[:, gsl, n1 - 1:n0:-1]

        for d in range(2):
            tables(d, tsc, tsi)
            if l == 0 and d == 0:
                dbg("tab", tab[:].rearrange("p a b c -> p (a b c)"), tab.s, [128, 3 * 16 * 65])
            if _stop(4.2):
                return
            for q in range(4):
                gp0 = q * 4
                tsl = slice(gp0, gp0 + 4)
                lifted(mnat[0][:], mnat[1][:], bb[:, d, 0], bb[:, d, 1], d, E_B[d], +1, gp0, [mnat[0].s, mnat[1].s])
                for ri in range(2):
                    bk = psum()
                    for gl in range(4):
                        MM(bk[:, gl * 128:(gl + 1) * 128], mnat[ri][:, gl, :], ident_b[:], True, True,
                           [mnat[ri].s, ident_b.s], [bk.s])
                    ACTop(Bt[:, 2 * gp0:2 * gp0 + 8, ri, :], bk[:].rearrange("p (g q) -> p g q", q=64), AF.Copy, [bk.s], [Bt.s])
                if _stop(4.3):
                    return
                for gl in range(4):
                    gp = gp0 + gl
                    bk = psum()
                    for g2 in range(2):
                        g = 2 * gp + g2
                        for ri in range(2):
                            MM(bk[64 * g2:64 * g2 + 64, ri * 128:(ri + 1) * 128], Bt[:, g, ri, :], UT[:, g, :], True, True,
                               [Bt.s, UT.s], [bk.s], tp=(0, 64 * g2))
                    ACTop(XR[:, gl, :], bk[:, 0:128], AF.Copy, [bk.s], [XR.s])
                    ACTop(XI[:, gl, :], bk[:, 128:256], AF.Copy, [bk.s], [XI.s])
                if l == 0 and d == 0:
                    dbg("XR%d" % q, XR[:].rearrange("p a b -> p (a b)"), XR.s, [128, 512])
                    dbg("Bt%d" % q, Bt[:, 2 * gp0:2 * gp0 + 8].rearrange("p a b c -> p (a b c)"), Bt.s, [128, 1024], BF16)
                    dbg("UT%d" % q, UT[:, 2 * gp0:2 * gp0 + 8].rearrange("p a b -> p (a b)"), UT.s, [128, 1024], BF16)
                if _stop(4.4):
                    return
                CPY(RC[:], tab[:, 2, tsl, 1:2].broadcast_to([128, 4, NCK]), [tab.s], [RC.s])
                for (n0, n1) in SEG8:
                    first = n0 if d == 0 else n1 - 1
                    MSET(RC[:, :, first:first + 1], 0.0, [RC.s], [RC.s])
                gsl = slice(0, 4)
                for (n0, n1) in SEG8:
                    L = n1 - n0
                    xr, xi = seg_views(XR, gsl, n0, n1, d, 0), seg_views(XI, gsl, n0, n1, d, 0)
                    a1, b2, c2 = seg_views(A1, gsl, n0, n1, d, 0), seg_views(B2, gsl, n0, n1, d, 0), seg_views(C2, gsl, n0, n1, d, 0)
                    cs_, sn_ = tab[:, 0, tsl, 1:L + 1], tab[:, 1, tsl, 1:L + 1]
                    rs = [XR.s, XI.s, tab.s, A1.s, B2.s, C2.s]
                    TTop(a1, xr, cs_, ALU.mult, rs, [A1.s])
                    TTop(c2, xi, sn_, ALU.mult, rs, [C2.s])
                    TTop(a1, a1, c2, ALU.add, rs, [A1.s])
                    TTop(b2, xi, cs_, ALU.mult, rs, [B2.s])
                    TTop(c2, xr, sn_, ALU.mult, rs, [C2.s])
                    TTop(b2, b2, c2, ALU.subtract, rs, [B2.s])

                if _stop(4.5):
                    return

                def fl(t_):
                    v = t_[:].rearrange("p a b -> p (a b)")
                    return v if d == 0 else v[:, ::-1]
                o_r, o_i, i_r, i_i, cf_ = fl(XR), fl(XI), fl(A1), fl(B2), fl(RC)
                P.dve(lambda e, o_r=o_r, i_r=i_r, cf_=cf_: e.tensor_tensor_scan(out=o_r, data0=cf_, data1=i_r, initial=0.0,
                                                                                  op0=ALU.mult, op1=ALU.add),
                      reads=[RC.s, A1.s], writes=[XR.s])
                P.dve(lambda e, o_i=o_i, i_i=i_i, cf_=cf_: e.tensor_tensor_scan(out=o_i, data0=cf_, data1=i_i, initial=0.0,
                                                                                  op0=ALU.mult, op1=ALU.add),
                      reads=[RC.s, B2.s], writes=[XI.s])
                if _stop(4.6):
                    return
                for si_, (n0, n1) in enumerate(SEG8):
                    L = n1 - n0
                    gr, gi_ = seg_views(XR, gsl, n0, n1, d, -1), seg_views(XI, gsl, n0, n1, d, -1)
                    a1, b2 = seg_views(A1, gsl, n0, n1, d, 1), seg_views(B2, gsl, n0, n1, d, 1)
                    hr = seg_views(Hb[:, d, 0], tsl, n0, n1, d, 1)
                    hi = seg_views(Hb[:, d, 1], tsl, n0, n1, d, 1)
                    cs_, sn_ = tab[:, 0, tsl, 1:L], tab[:, 1, tsl, 1:L]
                    rs = [XR.s, XI.s, tab.s, A1.s, B2.s]
                    TTop(a1, gr, cs_, ALU.mult, rs, [A1.s])
                    TTop(b2, gi_, sn_, ALU.mult, rs, [B2.s])
                    TTop(hr, a1, b2, ALU.subtract, rs, [Hb.s])
                    TTop(a1, gr, sn_, ALU.mult, rs, [A1.s])
                    TTop(b2, gi_, cs_, ALU.mult, rs, [B2.s])
                    TTop(hi, a1, b2, ALU.add, rs, [Hb.s])
                    first = n0 if d == 0 else n1 - 1
                    MSET(Hb[:, d, :, tsl, first:first + 1], 0.0, [Hb.s], [Hb.s])
                    last = n1 - 1 if d == 0 else n0
                    glr, gli = XR[:, :, last], XI[:, :, last]
                    cL, sL = tab[:, 0, tsl, L], tab[:, 1, tsl, L]
                    t1, t2 = sm[6][:, 0:4], sm[7][:, 0:4]
                    if si_ < 2:
                        dr, di = fst[:, si_, d, tsl, 0], fst[:, si_, d, tsl, 1]
                        dsl = fst.s
                    else:
                        dr, di = sloc[:, d, 0, tsl], sloc[:, d, 1, tsl]
                        dsl = sloc.s
                    rs = [XR.s, XI.s, tab.s, sm[6].s, sm[7].s, dsl]
                    TTop(t1, glr, cL, ALU.mult, rs, [sm[6].s])
                    TTop(t2, gli, sL, ALU.mult, rs, [sm[7].s])
                    TTop(dr, t1, t2, ALU.subtract, rs, [dsl])
                    TTop(t1, glr, sL, ALU.mult, rs, [sm[6].s])
                    TTop(t2, gli, cL, ALU.mult, rs, [sm[7].s])
                    TTop(di, t1, t2, ALU.add, rs, [dsl])
        if _stop(4.8):
            return
        for seq in range(2):
            for d in range(2):
                SDMA(bass.AP(ns_s5.tensor, ns_s5[seq, l, d].offset, [[2, 128], [256, 16], [1, 2]]), fst[:, seq, d], [fst.s], [])

        if _stop(5):
            return
        ccs_in, ccs_out = Slot("cc5in"), Slot("cc5out")
        SDMA(cc5_in[l], sloc[:].rearrange("p a b c -> p (a b c)"), [sloc.s], [ccs_in])
        P.dma("pool", lambda e: e.collective_compute("AllGather", ALU.bypass, replica_groups=[[0, 1, 2, 3], [4, 5, 6, 7]],
                                                     ins=[cc5_in[l]], outs=[cc5_out[l]]),
              reads=[ccs_in], writes=[ccs_out], inc=1)
        SDMA(gat5[:].rearrange("p r a b c -> p r (a b c)"), cc5_out[l].rearrange("(r p) c -> p r c", p=128), [ccs_out], [gat5.s])

        old_r2 = new_r2
        ar["off"] = R2
        LA = aalloc([4, 128], F32, "LA")
        LB = aalloc([4, 128], F32, "LB")
        mnat = [aalloc([4, 128], BF16, "mnat%d" % i) for i in range(2)]
        Dacc = aalloc([8, 128], F32, "Dacc")
        new_r2 = [LA.s, LB.s, mnat[0].s, mnat[1].s, Dacc.s]
        P.dve(lambda e: e.memset(fence_t[:], 0.0), reads=[], writes=old_r2 + new_r2 + [fence_t.s])

        for d in range(2):
            for q in range(4):
                gp0 = q * 4
                cv, g8_, cs_ = Ct_(gp0)
                lifted(cv[:, g8_:g8_ + 4, d, 0, :], cv[:, g8_:g8_ + 4, d, 1, :], craw[:, d, 0], craw[:, d, 1], d, E_C[d], -1, gp0, [cs_])
        for q in range(4):
            gp0 = q * 4
            for d in range(2):
                lifted(mnat[0][:], mnat[1][:], bb[:, d, 0], bb[:, d, 1], d, E_N[d], +1, gp0, [mnat[0].s, mnat[1].s])
                for gi in range(8):
                    gl, g2 = gi // 2, gi % 2
                    gp = gp0 + gl
                    cv, g8_, cs_ = Ct_(gp)
                    bk = psum()
                    for ri in range(2):
                        MM(bk[:, 0:128], mnat[ri][64 * g2:64 * g2 + 64, gl, :], cv[64 * g2:64 * g2 + 64, g8_, d, ri, :],
                           (ri == 0), (ri == 1), [mnat[ri].s, cs_], [bk.s])
                    msk = M5[:, d * 128:(d + 1) * 128]
                    if d == 0:
                        TTop(Dacc[:, gi, :], bk[:, 0:128], msk, ALU.mult, [bk.s, c3.s], [Dacc.s])
                    else:
                        tmpv = LA[:, gl, :] if g2 == 0 else LB[:, gl, :]
                        tmps = LA.s if g2 == 0 else LB.s
                        TTop(tmpv, bk[:, 0:128], msk, ALU.mult, [bk.s, c3.s, mnat[0].s, mnat[1].s], [tmps])
                        TTop(Dacc[:, gi, :], Dacc[:, gi, :], tmpv, ALU.add, [tmps, Dacc.s], [Dacc.s])
                        g = 2 * gp + g2
                        STTop(DtV[:, g, :], ident_f, dsk[:, g:g + 1], Dacc[:, gi, :], ALU.mult, ALU.add,
                              [cst_f.s, dsk.s, Dacc.s], [DtS.s])

        old_w0 = [Bt.s, bb.s, craw.s, pw.s] + new_r2 + [XR.s, XI.s, A1.s, B2.s, C2.s, RC.s]
        ar["off"] = W0
        DH = aalloc([16, 2, 2, 64], BF16, "DH")
        W1 = ar["off"]
        tsc2 = [aalloc([4, 128], F32, "tscb%d" % i) for i in range(3)]
        tsi2 = aalloc([4, 128], F32, "tsib")
        TRt = aalloc([16, 64], F32, "TRt")
        TIt = aalloc([16, 64], F32, "TIt")
        U1 = aalloc([16, 64], F32, "U1")
        U2 = aalloc([16, 64], F32, "U2")
        pc = [aalloc([16], F32, "pc%d" % i) for i in range(6)]
        new_w0 = [DH.s, tsi2.s, TRt.s, TIt.s, U1.s, U2.s] + [t_.s for t_ in tsc2] + [p_.s for p_ in pc]
        P.dve(lambda e: e.memset(fence_t[:], 0.0), reads=[], writes=old_w0 + new_w0 + [fence_t.s])

        def cmul(dr, di, ar_, ai_, br_, bi_, t1, t2, rs, ws):
            TTop(t1, ar_, br_, ALU.mult, rs, ws)
            TTop(t2, ai_, bi_, ALU.mult, rs, ws)
            TTop(dr, t1, t2, ALU.subtract, rs, ws)
            TTop(t1, ar_, bi_, ALU.mult, rs, ws)
            TTop(t2, ai_, br_, ALU.mult, rs, ws)
            TTop(di, t1, t2, ALU.add, rs, ws)

        for d in range(2):
            tables(d, tsc2, tsi2)
            atr, ati, cr_, ci_, t1, t2 = (p_[:] for p_ in pc)
            ws = [p_.s for p_ in pc] + [sm[0].s, sm[1].s]
            rs = [tab.s, gat5.s, hin.s, hent.s, cst_f.s] + ws
            TTop(atr, tab[:, 0, :, 64], tab[:, 2, :, 64], ALU.mult, rs, ws)
            TTop(ati, tab[:, 1, :, 64], tab[:, 2, :, 64], ALU.mult, rs, ws)
            ranks = [0, 1, 2, 3] if d == 0 else [3, 2, 1, 0]
            CPY(cr_, hin[:, d, :, 0], rs, ws)
            CPY(ci_, hin[:, d, :, 1], rs, ws)
            TSop(hent[:, d, 0], cr_, cst_f[:, 384 + ranks[0]:385 + ranks[0]], None, ALU.mult, None, rs, [hent.s])
            TSop(hent[:, d, 1], ci_, cst_f[:, 384 + ranks[0]:385 + ranks[0]], None, ALU.mult, None, rs, [hent.s])
            for i in range(3):
                r = ranks[i]
                nr_, ni_ = sm[0][:], sm[1][:]
                cmul(nr_, ni_, atr, ati, cr_, ci_, t1, t2, rs, ws)
                TTop(cr_, nr_, gat5[:, r, d, 0, :], ALU.add, rs, ws)
                TTop(ci_, ni_, gat5[:, r, d, 1, :], ALU.add, rs, ws)
                rn = ranks[i + 1]
                STTop(hent[:, d, 0], cr_, cst_f[:, 384 + rn:385 + rn], hent[:, d, 0], ALU.mult, ALU.add, rs, [hent.s])
                STTop(hent[:, d, 1], ci_, cst_f[:, 384 + rn:385 + rn], hent[:, d, 1], ALU.mult, ALU.add, rs, [hent.s])
            rs = [tab.s, hent.s, TRt.s, TIt.s, U1.s, U2.s]
            TTop(TRt[:], tab[:, 0, :, 0:64], tab[:, 2, :, 0:64], ALU.mult, rs, [TRt.s])
            TTop(TIt[:], tab[:, 1, :, 0:64], tab[:, 2, :, 0:64], ALU.mult, rs, [TIt.s])
            her = hent[:, d, 0].unsqueeze(2).broadcast_to([128, 16, 64])
            hei = hent[:, d, 1].unsqueeze(2).broadcast_to([128, 16, 64])
            dhr = DH[:, :, d, 0, :] if d == 0 else DH[:, :, d, 0, ::-1]
            dhi = DH[:, :, d, 1, :] if d == 0 else DH[:, :, d, 1, ::-1]
            TTop(U1[:], TRt[:], her, ALU.mult, rs, [U1.s])
            TTop(U2[:], TIt[:], hei, ALU.mult, rs, [U2.s])
            TTop(dhr, U1[:], U2[:], ALU.subtract, rs, [DH.s])
            TTop(U1[:], TRt[:], hei, ALU.mult, rs, [U1.s])
            TTop(U2[:], TIt[:], her, ALU.mult, rs, [U2.s])
            TTop(dhi, U1[:], U2[:], ALU.add, rs, [DH.s])

        if _stop(6):
            return
        old_w1 = new_w0[1:]
        ar["off"] = W1
        YA = aalloc([NG, NCK], BF16, "YA")
        yT = aalloc([4, T], BF16, "yT5")
        gl_t = [aalloc([512], F32, "gl%d" % i) for i in range(4)]
        new_w1 = [YA.s, yT.s] + [g_.s for g_ in gl_t]
        P.dve(lambda e: e.memset(fence_t[:], 0.0), reads=[], writes=old_w1 + new_w1 + [fence_t.s])

        for g0 in range(0, NG, 4):
            bk = psum()
            for gi in range(4):
                g = g0 + gi
                gp, g2 = g // 2, g % 2
                cv, g8_, cs_ = Ct_(gp)
                cols = slice(gi * 128, (gi + 1) * 128)
                MM(bk[:, cols], DtV[:, g, :], UT[:, g, :], (gi == 0), False, [DtS.s, UT.s], [bk.s])
                for d in range(2):
                    for ri in range(2):
                        MM(bk[:, cols], cv[64 * g2:64 * g2 + 64, g8_, d, ri, :], Hb[64 * g2:64 * g2 + 64, d, ri, gp, :], False, False,
                           [cs_, Hb.s], [bk.s])
                for d in range(2):
                    for ri in range(2):
                        MM(bk[:, gi * 128 + 64:(gi + 1) * 128], cv[64 * g2:64 * g2 + 64, g8_, d, ri, :],
                           DH[64 * g2:64 * g2 + 64, gp, d, ri, :], False, (d == 1 and ri == 1), [cs_, DH.s], [bk.s])
            xs_, sq_, u_, sg_ = gl_t
            ACTop(xs_[:], bk[:], AF.Copy, [bk.s], [xs_.s])
            ACTop(sq_[:], bk[:], AF.Square, [bk.s], [sq_.s])
            TSop(sq_[:], sq_[:], 0.044715, 1.0, ALU.mult, ALU.add, [sq_.s], [sq_.s])
            TTop(u_[:], sq_[:], xs_[:], ALU.mult, [sq_.s, xs_.s], [u_.s])
            ACTop(sg_[:], u_[:], AF.Sigmoid, [u_.s], [sg_.s], scale=2.0 * math.sqrt(2.0 / math.pi))
            TTop(YA[:, g0:g0 + 4, :].rearrange("p g n -> p (g n)"), xs_[:], sg_[:], ALU.mult, [xs_.s, sg_.s], [YA.s])

        for ct in range(4):
            for t0 in range(0, 8, 4):
                bk = psum()
                for ti in range(4):
                    t_ = t0 + ti
                    for g8 in range(8):
                        g = ct * 8 + g8
                        MM(bk[:, ti * 128:(ti + 1) * 128], asel[:, t_, 112 - 16 * g8:240 - 16 * g8],
                           YA[:, g, :], (ti == 0 and g8 == 0), (g8 == 7), [asel.s, YA.s], [bk.s])
                ACTop(yT[:, ct, :].rearrange("p (n t) -> p t n", t=8)[:, t0:t0 + 4, :],
                      bk[:].rearrange("p (t n) -> p t n", n=128), AF.Copy, [bk.s], [yT.s])

        wgl, wgls = wload(s5_w_glu[l].rearrange("(k p) c -> p k c", p=128), 4, 512, slot=0)
        for c2 in range(4):
            for hf in range(2):
                bk = psum()
                for ct in range(4):
                    MM(bk[:], wgl[:, ct, c2 * 128:(c2 + 1) * 128], yT[:, ct, hf * 512:(hf + 1) * 512], (ct == 0), (ct == 3),
                       [wgls, yT.s], [bk.s])
                sgl = gl_t[(c2 * 2 + hf) % 2]
                ACTop(sgl[:], bk[:], AF.Sigmoid, [bk.s], [sgl.s])
                TTop(mixT[:, 4 + c2, hf * 512:(hf + 1) * 512], yT[:, c2, hf * 512:(hf + 1) * 512], sgl[:], ALU.mult,
                     [yT.s, sgl.s], [mixs[4 + c2][hf]])


    for l in range(DEPTH):
        compute_mod(l, 0)

        def mod_rest(l=l):
            compute_mod(l, 1, slots=[1, 2, 3, 1])
            compute_mod(l, 2, slots=[2, 3, 1, 2])
        norm_mod(l, 0, 0)
        if STAGE["s5"]:
            s5(l, after_u=mod_rest)
        else:
            mod_rest()
            for k in range(4, 8):
                for h in range(2):
                    P.dve(lambda e, k=k, h=h: e.memset(mixT[:, k, h * 512:(h + 1) * 512], 0.0), writes=[mixs[k][h]])
        if STAGE["hg"]:
            hgrn(l)
        else:
            for k in range(0, 4):
                for h in range(2):
                    P.dve(lambda e, k=k, h=h: e.memset(mixT[:, k, h * 512:(h + 1) * 512], 0.0), writes=[mixs[k][h]])
        residual_proj(w_out[l], l, 16, mixT, mixs, KD)
        norm_mod(l, 1, 24)
        ffn(l)

    nfin32 = sb([128, KD], F32, "nfin32")
    P.dve(lambda e: e.tensor_scalar(out=nfin32[:], in0=nfin[:], scalar1=32.0, scalar2=None, op0=ALU.mult),
          reads=[nfin.s], writes=[nfin32.s])
    aphase()
    ytok = [aalloc([D], F32, "ytok%d" % j) for j in range(4)]
    yT = aalloc([512], F32, "yT")
    afence()
    for h in range(2):
        rms_stats(h)
        for k in range(KD):
            P.dve(lambda e, k=k, h=h: e.scalar_tensor_tensor(
                out=yT[:], in0=xT[:, k, h * 512:(h + 1) * 512], scalar=nfin32[:, k:k + 1], in1=rstd[:],
                op0=ALU.mult, op1=ALU.mult), reads=[xs[k][h], nfin32.s, rstd.s], writes=[yT.s])
            bk = psum()
            for j in range(4):
                P.pe(lambda e, bk=bk, j=j: e.transpose(bk[:, j * 128:(j + 1) * 128], yT[:, j * 128:(j + 1) * 128], ident_f),
                     reads=[yT.s, cst_f.s], writes=[bk.s])
            for j in range(4):
                P.act(lambda e, bk=bk, j=j, k=k: e.activation(out=ytok[j][:, k * 128:(k + 1) * 128],
                                                               in_=bk[:, j * 128:(j + 1) * 128], func=AF.Copy),
                      reads=[bk.s], writes=[ytok[j].s])
        for j in range(4):
            tt = h * 4 + j
            P.dma("sp", lambda e, j=j, tt=tt: e.dma_start(out=y_out[tt * 128:(tt + 1) * 128, :], in_=ytok[j][:]),
                  reads=[ytok[j].s], writes=[])

    with nc.allow_non_contiguous_dma(reason="small strided parameter loads"):
        P.emit(st)
    st.close()
    return nc


def _consts(core):
    q = core % 4
    c = np.zeros((128, 512), np.float32)
    c[:, 0:128] = np.eye(128, dtype=np.float32)
    j = np.arange(128)[:, None]
    i = np.arange(128)[None, :]
    same = (j // CH) == (i // CH)
    c[:, 128:256] = (same & (j <= i)).astype(np.float32)
    c[:, 256:384] = (same & (j >= i)).astype(np.float32)
    c[:, 384 + q] = 1.0
    nf = D // 4
    p = np.arange(128, dtype=np.float32)
    for par in range(2):
        kf = par * 128 + p
        c[:, 388 + par] = (1.0 / (np.float32(10000.0) ** (kf / np.float32(nf)))).astype(np.float32)
    t = q * 512 + np.arange(512)
    c2 = np.zeros((128, 1024), np.float32)
    c2[:, 0:512] = (t // 64).astype(np.float32)[None, :]
    c2[:, 512:1024] = (t % 64).astype(np.float32)[None, :]
    c3 = np.zeros((128, 1024), np.float32)
    for p_ in range(128):
        c3[p_, 112 + p_ % 16] = 1.0
        c3[p_, 240 + p_ // 16] = 1.0
    for g4 in range(4):
        for cc in range(16):
            c3[(2 * g4) * 16 + cc, 480 + g4 * 16 + cc] = 1.0
            c3[(2 * g4 + 1) * 16 + cc, 544 + g4 * 16 + cc] = 1.0
    c3[:, 608:625] = np.arange(-8, 9, dtype=np.float32)[None, :]
    c3[:, 640:705] = (8.0 * np.arange(65, dtype=np.float32))[None, :]
    sI = (np.arange(128) // 16)[:, None]
    tI = (np.arange(128) // 16)[None, :]
    c3[:, 768:896] = (sI <= tI).astype(np.float32)
    c3[:, 896:1024] = (sI >= tI).astype(np.float32)
    return c, c2, c3


_NC_CACHE = {}


def kernel(**inp):
    inp = {k: np.asarray(v) for k, v in inp.items()}
    if "nc" not in _NC_CACHE:
        _NC_CACHE["nc"] = build_program()
    nc = _NC_CACHE["nc"]
    xp = inp["x_prompt"]
    xsm = inp["x_sample"]
    in_maps = []
    shared = {k: np.ascontiguousarray(inp[k], dtype=np.float32) for k in
              ("w_mod", "b_mod", "norm_mix", "norm_ffn", "norm_final", "w_in", "w_out", "w_gate", "w_up", "w_down",
               "hg_lb_logits", "hg_norm", "s5_lam_re", "s5_lam_im", "s5_log_dt", "s5_b_re", "s5_b_im", "s5_c_re", "s5_c_im",
               "s5_d", "s5_w_glu")}
    for core in range(8):
        b, q = core // 4, core % 4
        x = np.concatenate([xp[2 * core], xp[2 * core + 1], xsm[b, q * 512:(q + 1) * 512]], axis=0)
        cond = np.stack([inp["c_ctx"], inp["c"][b]], axis=0)
        c1, c2, c3 = _consts(core)
        m = dict(shared)
        m.update({"x": np.ascontiguousarray(x, dtype=np.float32), "cond": np.ascontiguousarray(cond, dtype=np.float32),
                  "st_hg": np.ascontiguousarray(inp["state_hgrn"][b], dtype=np.float32), "cst": c1, "cst2": c2, "cst3": c3,
                  "st_s5": np.ascontiguousarray(inp["state_s5"][b], dtype=np.float32)})
        in_maps.append(m)
    res = run_bass_kernel_spmd(nc, in_maps, core_ids=list(range(8)))
    outs = res.results
    _DBG["outs"] = outs
    y_prompt = np.zeros_like(xp)
    y_sample = np.zeros_like(xsm)
    ns_hg = np.zeros((16, DEPTH, 2, NH, DK, DK), np.float32)
    ns_s5 = np.zeros((16, DEPTH, 2, NG, SP, 2), np.float32)
    for core in range(8):
        b, q = core // 4, core % 4
        y = outs[core]["y"]
        y_prompt[2 * core] = y[0:256]
        y_prompt[2 * core + 1] = y[256:512]
        y_sample[b, q * 512:(q + 1) * 512] = y[512:1024]
        ns_hg[2 * core:2 * core + 2] = outs[core]["ns_hg"]
        ns_s5[2 * core:2 * core + 2] = outs[core]["ns_s5"]
    return (y_prompt, y_sample, ns_hg, ns_s5)
```

```python
import math
from contextlib import ExitStack

import numpy as np
import concourse.bass as bass
import concourse.mybir as mybir
from concourse.bass_utils import run_bass_kernel_spmd

F32 = mybir.dt.float32
BF16 = mybir.dt.bfloat16
AF = mybir.ActivationFunctionType
ALU = mybir.AluOpType

ENGS = ("pe", "act", "dve", "pool", "sp")
SAME_ENGINE_RAW_DIST = 2


class Slot:
    __slots__ = ("name", "w", "r", "al")

    def __init__(self, name):
        self.name = name
        self.w = None
        self.r = []
        self.al = [self]


def alias(*slots):
    grp = []
    for s in slots:
        for a in s.al:
            if a not in grp:
                grp.append(a)
    for s in grp:
        s.al = grp


class Op:
    __slots__ = ("eng", "fn", "deps", "raw", "dma", "idx", "milestone", "mcount", "dsem", "dval", "inc", "eidx")


class Prog:
    def __init__(self, nc, n_dma_sems=8, sync_same_engine=True):
        self.nc = nc
        self.ops = []
        self.n_dma_sems = n_dma_sems
        self.sync_same = sync_same_engine

    def add(self, eng, fn, reads=(), writes=(), dma=False, inc=16):
        op = Op()
        op.eng, op.fn, op.dma, op.inc = eng, fn, dma, inc
        op.deps = set()
        op.raw = set()
        op.milestone = False
        op.mcount = 0
        op.dsem = None
        op.dval = 0
        op.idx = len(self.ops)
        for s0 in reads:
            for s in s0.al:
                if s.w is not None:
                    op.deps.add(s.w)
                    op.raw.add(s.w)
        for s0 in writes:
            for s in s0.al:
                if s.w is not None:
                    op.deps.add(s.w)
                op.deps.update(s.r)
        for s in reads:
            s.r.append(op.idx)
        for s in writes:
            s.w = op.idx
            s.r = []
        op.deps.discard(op.idx)
        self.ops.append(op)
        return op

    def pe(self, fn, reads=(), writes=()):
        return self.add("pe", fn, reads, writes)

    def act(self, fn, reads=(), writes=()):
        return self.add("act", fn, reads, writes)

    def dve(self, fn, reads=(), writes=()):
        return self.add("dve", fn, reads, writes)

    def dma(self, eng, fn, reads=(), writes=(), inc=16):
        return self.add(eng, fn, reads, writes, dma=True, inc=inc)

    def emit(self, stack):
        nc = self.nc
        ops = self.ops
        ecount = {e: 0 for e in ENGS}
        for op in ops:
            op.eidx = ecount[op.eng]
            ecount[op.eng] += 1

        def needs_sync(op, dop):
            if dop.dma or op.dma or dop.eng != op.eng:
                return True
            if dop.eng == "pe" or not self.sync_same:
                return False
            return (dop.idx in op.raw) and (op.eidx - dop.eidx < SAME_ENGINE_RAW_DIST)

        self.needs_sync = needs_sync
        for op in ops:
            for d in op.deps:
                dop = ops[d]
                if dop.dma:
                    continue
                if not needs_sync(op, dop):
                    continue
                dop.milestone = True
        cnt = {e: 0 for e in ENGS}
        for op in ops:
            if not op.dma and op.milestone:
                cnt[op.eng] += 1
            op.mcount = cnt[op.eng]
        esem = {e: stack.enter_context(nc.semaphore("s_" + e)) for e in ENGS}
        dsems = {e: None for e in ENGS}
        dcount = {}
        dn = {e: 0 for e in ENGS}
        for op in ops:
            if op.dma:
                if dsems[op.eng] is None:
                    dsems[op.eng] = [stack.enter_context(nc.semaphore("d_%s_%d" % (op.eng, i)))
                                     for i in range(self.n_dma_sems)]
                k = dn[op.eng]
                dn[op.eng] += 1
                op.dsem = (op.eng, k % self.n_dma_sems)
                dcount[op.dsem] = dcount.get(op.dsem, 0) + op.inc
                op.dval = dcount[op.dsem]
        per = {e: [o for o in ops if o.eng == e] for e in ENGS}
        block = stack.enter_context(nc.Block())
        sync_same = self.sync_same

        def make(e):
            def body(eng):
                waited = {}

                def wait(key, sem, val):
                    if waited.get(key, 0) >= val:
                        return
                    waited[key] = val
                    eng.wait_ge(sem, val)

                for op in per[e]:
                    for d in sorted(op.deps):
                        dop = ops[d]
                        if dop.dma:
                            wait(("d",) + dop.dsem, dsems[dop.dsem[0]][dop.dsem[1]], dop.dval)
                        else:
                            if not self.needs_sync(op, dop):
                                continue
                            wait(("e", dop.eng), esem[dop.eng], dop.mcount)
                    if op.dma:
                        prev = op.dval - op.inc
                        if prev > 0:
                            wait(("d",) + op.dsem, dsems[op.dsem[0]][op.dsem[1]], prev)
                        ins = op.fn(eng)
                        ins.then_inc(dsems[op.dsem[0]][op.dsem[1]], op.inc)
                    else:
                        ins = op.fn(eng)
                        if op.milestone:
                            ins.then_inc(esem[e], 1)
                if dsems[e] is not None:
                    for i, s in enumerate(dsems[e]):
                        v = dcount.get((e, i), 0)
                        if v:
                            wait(("d", e, i), s, v)
            return body

        block.tensor(make("pe"))
        block.scalar(make("act"))
        block.vector(make("dve"))
        block.gpsimd(make("pool"))
        block.sync(make("sp"))


D = 1024
KD = 8
T = 1024
NT = 8
DEPTH = 2
HG_W = 512
NH = 4
DK = 128
S5_W = 512
NG = 32
SP = 64
IN_W = 3072
DFF = 2816
NF = 22
EPS = 1e-6
CH = 32
NCH = T // CH
SEGS = [(0, 8), (8, 16), (16, 32)]
WSLOT = 4096
N_WSLOT = 4
CCW = 1096
ARENA_B = 90 * 1024 // 2

I32 = mybir.dt.int32
STAGE = {"hg": True, "s5": True, "s5_stop": 99}
DEBUG = False
_DBG = {}


class TT:
    def __init__(self, t, slot):
        self.t = t
        self.s = slot

    def __getitem__(self, k):
        return self.t[k]


def build_program():
    nc = bass.Bass("TRN2", target_bir_lowering=False)
    st = ExitStack()
    P = Prog(nc)

    def din(name, shape):
        return nc.dram_tensor(name, list(shape), F32, kind="ExternalInput").ap()

    def dout(name, shape):
        return nc.dram_tensor(name, list(shape), F32, kind="ExternalOutput").ap()

    x_in = din("x", [T, D])
    cond_in = din("cond", [2, D])
    w_mod = din("w_mod", [DEPTH, D, 6 * D])
    b_mod = din("b_mod", [DEPTH, 6 * D])
    norm_mix = din("norm_mix", [DEPTH, D])
    norm_ffn = din("norm_ffn", [DEPTH, D])
    norm_final = din("norm_final", [D])
    w_in = din("w_in", [DEPTH, D, IN_W])
    w_out = din("w_out", [DEPTH, D, D])
    w_gate = din("w_gate", [DEPTH, D, DFF])
    w_up = din("w_up", [DEPTH, D, DFF])
    w_down = din("w_down", [DEPTH, DFF, D])
    hg_lb = din("hg_lb_logits", [2, DEPTH, HG_W])
    hg_norm = din("hg_norm", [DEPTH, DK])
    st_hg = din("st_hg", [DEPTH, 2, NH, DK, DK])
    cst = din("cst", [128, 512])
    cst2_in = din("cst2", [128, 1024])
    cst3_in = din("cst3", [128, 1024])
    s5_lam_re = din("s5_lam_re", [DEPTH, 2, NG, SP])
    s5_lam_im = din("s5_lam_im", [DEPTH, 2, NG, SP])
    s5_log_dt = din("s5_log_dt", [DEPTH, 2, NG])
    s5_b_re = din("s5_b_re", [DEPTH, 2, NG, SP, 16])
    s5_b_im = din("s5_b_im", [DEPTH, 2, NG, SP, 16])
    s5_c_re = din("s5_c_re", [DEPTH, 2, NG, 16, SP])
    s5_c_im = din("s5_c_im", [DEPTH, 2, NG, 16, SP])
    s5_d = din("s5_d", [DEPTH, NG, 16])
    s5_w_glu = din("s5_w_glu", [DEPTH, S5_W, S5_W])
    st_s5 = din("st_s5", [DEPTH, 2, NG, SP, 2])
    y_out = dout("y", [T, D])
    ns_hg = dout("ns_hg", [2, DEPTH, 2, NH, DK, DK])
    ns_s5 = dout("ns_s5", [2, DEPTH, 2, NG, SP, 2])
    cc5_in = [nc.dram_tensor("cc5_in%d" % i, [128, 64], F32, kind="Internal").ap() for i in range(DEPTH)]
    cc5_out = [nc.dram_tensor("cc5_out%d" % i, [4 * 128, 64], F32, kind="Internal").ap() for i in range(DEPTH)]
    cc_in = [nc.dram_tensor("cc_in%d" % i, [128, CCW], F32, kind="Internal").ap() for i in range(2 * DEPTH)]
    cc_out = [nc.dram_tensor("cc_out%d" % i, [4 * 128, CCW], F32, kind="Internal").ap() for i in range(2 * DEPTH)]

    _n = [0]
    dbg_list = []

    def dbg(name, ap, slot, shape, dtype=F32):
        if not DEBUG:
            return
        t = nc.dram_tensor("dbg_" + name, list(shape), dtype, kind="ExternalOutput").ap()
        P.dma("sp", lambda e: e.dma_start(out=t, in_=ap), reads=[slot], writes=[])

    def sb(shape, dtype, name=None):
        _n[0] += 1
        name = "sb_" + (name or "t%d" % _n[0])
        t = st.enter_context(nc.sbuf_tensor(name, list(shape), dtype))
        return TT(t, Slot(name))

    banks = [TT(st.enter_context(nc.psum_tensor("ps%d" % i, [128, 512], F32)), Slot("ps%d" % i)) for i in range(8)]
    _pb = [0]

    def psum():
        b = banks[_pb[0] % 6]
        _pb[0] += 1
        return b

    wslots = [sb([128, WSLOT], BF16, "wslot%d" % i) for i in range(N_WSLOT)]
    _ws = [0]

    def wload(src_ap, a, b, slot=None):
        if slot is None:
            w = wslots[_ws[0] % N_WSLOT]
            _ws[0] += 1
        else:
            w = wslots[slot]
        view = bass.AP(w.t, 0, [[WSLOT, 128], [b, a], [1, b]])
        P.dma("pool", lambda e, v=view, s=src_ap: e.dma_start(out=v, in_=s), writes=[w.s])
        return view, w.s

    arena_t = st.enter_context(nc.sbuf_tensor("arena", [128, ARENA_B], BF16))
    ar = {"off": 0, "live": []}

    def aalloc(free_shape, dtype, name):
        n = 1
        for v in free_shape:
            n *= v
        nb = n * (1 if dtype == BF16 else 2)
        nb = (nb + 1) // 2 * 2
        off = ar["off"]
        assert off + nb <= ARENA_B, ("arena overflow", name, off, nb)
        ar["off"] = off + nb
        v = arena_t[:, off:off + nb]
        if dtype != BF16:
            v = v.bitcast(dtype)
        if len(free_shape) > 1:
            names = "abcdefg"[:len(free_shape)]
            kw = {names[i]: free_shape[i] for i in range(1, len(free_shape))}
            v = v.rearrange("p (%s) -> p %s" % (" ".join(names), " ".join(names)), **kw)
        s = Slot(name)
        ar["live"].append(s)
        return TT(v, s)

    fence_t = sb([128, 2], F32, "fence")

    def aphase(new_names_hint=None):
        old = ar["live"]
        ar["live"] = []
        ar["off"] = 0
        ar["pending"] = old

    def afence():
        old = ar.get("pending", [])
        new = list(ar["live"])
        P.dve(lambda e: e.memset(fence_t[:], 0.0), reads=[], writes=old + new + [fence_t.s])
        ar["pending"] = []

    cst_f = sb([128, 512], F32, "cst_f")
    P.dma("sp", lambda e: e.dma_start(out=cst_f[:], in_=cst), writes=[cst_f.s])
    ident_f = cst_f[:, 0:128]
    ident_b = sb([128, 128], BF16, "ident_b")
    P.dve(lambda e: e.tensor_copy(out=ident_b[:], in_=cst_f[:, 0:128]), reads=[cst_f.s], writes=[ident_b.s])
    ones_b = sb([128, 128], BF16, "ones_b")
    P.dve(lambda e: e.memset(ones_b[:], 1.0), writes=[ones_b.s])
    zeros_f = sb([128, 128], F32, "zeros_f")
    P.dve(lambda e: e.memset(zeros_f[:], 0.0), writes=[zeros_f.s])
    epsc = sb([128, 2], F32, "epsc")
    P.dve(lambda e: e.memset(epsc[:, 0:1], float(D * EPS)), writes=[epsc.s])
    P.dve(lambda e: e.memset(epsc[:, 1:2], float(DK * EPS)), reads=[epsc.s], writes=[epsc.s])
    lnc = sb([128, 1], F32, "lnc")
    P.dve(lambda e: e.memset(lnc[:], float(math.log(DK ** -0.5))), writes=[lnc.s])

    xT = sb([128, KD, T], F32, "xT")
    xs = [[Slot("xT%d_%d" % (k, h)) for h in range(2)] for k in range(KD)]
    hT = sb([128, KD, T], BF16, "hT")
    hs = [[Slot("hT%d_%d" % (k, h)) for h in range(2)] for k in range(KD)]
    mixT = sb([128, KD, T], BF16, "mixT")
    mixs = [[Slot("mix%d_%d" % (k, h)) for h in range(2)] for k in range(KD)]

    def sin_turns(out, u, ki, kf, ap=lambda t: t[:]):
        P.dve(lambda e: e.tensor_copy(out=ap(ki), in_=ap(u)), reads=[u.s], writes=[ki.s])
        P.dve(lambda e: e.tensor_copy(out=ap(kf), in_=ap(ki)), reads=[ki.s], writes=[kf.s])
        P.dve(lambda e: e.tensor_tensor(out=ap(kf), in0=ap(u), in1=ap(kf), op=ALU.subtract), reads=[u.s, kf.s], writes=[kf.s])
        P.act(lambda e: e.activation(out=ap(out), in_=ap(kf), func=AF.Sin, scale=2 * math.pi), reads=[kf.s], writes=[out.s])

    aphase()
    xtok = [aalloc([D], F32, "xtok%d" % i) for i in range(NT)]
    posarg = aalloc([512], F32, "posarg")
    cst2 = aalloc([1024], F32, "cst2")
    posk_i = aalloc([512], I32, "posk_i")
    posk_f = aalloc([512], F32, "posk_f")
    afence()
    P.dma("sp", lambda e: e.dma_start(out=cst2[:], in_=cst2_in), writes=[cst2.s])
    for tt in range(NT):
        P.dma("sp", lambda e, tt=tt: e.dma_start(out=xtok[tt][:], in_=x_in[tt * 128:(tt + 1) * 128, :]),
              writes=[xtok[tt].s])
    for k in range(KD):
        for h in range(2):
            b = psum()
            for j in range(4):
                tt = h * 4 + j
                P.pe(lambda e, b=b, j=j, tt=tt, k=k: e.transpose(b[:, j * 128:(j + 1) * 128],
                                                                   xtok[tt][:, k * 128:(k + 1) * 128], ident_f),
                     reads=[xtok[tt].s, cst_f.s], writes=[b.s])
            P.act(lambda e, b=b, k=k, h=h: e.activation(out=xT[:, k, h * 512:(h + 1) * 512], in_=b[:], func=AF.Copy),
                  reads=[b.s], writes=[xs[k][h]])

    posi = aalloc_late = None
    for k in range(KD):
        blk = k // 2
        pos_src = cst2[:, 0:512] if blk < 2 else cst2[:, 512:1024]
        om = cst_f[:, 388 + (k % 2):389 + (k % 2)]
        P.dve(lambda e, pos_src=pos_src, om=om: e.tensor_scalar(
            out=posarg[:], in0=pos_src, scalar1=om, scalar2=1.0 / (2 * math.pi), op0=ALU.mult, op1=ALU.mult),
            reads=[cst2.s, cst_f.s], writes=[posarg.s])
        if blk % 2 == 1:
            P.dve(lambda e: e.tensor_scalar(out=posarg[:], in0=posarg[:], scalar1=0.25, scalar2=None, op0=ALU.add),
                  reads=[posarg.s], writes=[posarg.s])
        sin_turns(posarg, posarg, posk_i, posk_f)
        P.dve(lambda e, k=k: e.tensor_tensor(out=xT[:, k, 512:1024], in0=xT[:, k, 512:1024], in1=posarg[:], op=ALU.add),
              reads=[posarg.s, xs[k][1]], writes=[xs[k][1]])

    condf = sb([128, KD, 2], F32, "condf")
    condb = sb([128, KD, 2], BF16, "condb")
    for j in range(2):
        P.dma("sp", lambda e, j=j: e.dma_start(out=condf[:, :, j], in_=cond_in[j].rearrange("(k p) -> p k", p=128)),
              writes=[condf.s])
    P.act(lambda e: e.activation(out=condb[:], in_=condf[:], func=AF.Silu), reads=[condf.s], writes=[condb.s])

    nmix = sb([128, DEPTH, KD], F32, "nmix")
    nffn = sb([128, DEPTH, KD], F32, "nffn")
    nfin = sb([128, KD], F32, "nfin")
    bmod = sb([128, DEPTH, 48], F32, "bmod")
    P.dma("sp", lambda e: e.dma_start(out=nmix[:], in_=norm_mix.rearrange("l (k p) -> p l k", p=128)), writes=[nmix.s])
    P.dma("sp", lambda e: e.dma_start(out=nffn[:], in_=norm_ffn.rearrange("l (k p) -> p l k", p=128)), writes=[nffn.s])
    P.dma("sp", lambda e: e.dma_start(out=nfin[:], in_=norm_final.rearrange("(k p) -> p k", p=128)), writes=[nfin.s])
    P.dma("sp", lambda e: e.dma_start(out=bmod[:], in_=b_mod.rearrange("l (k p) -> p l k", p=128)), writes=[bmod.s])

    lbl = sb([128, 2, DEPTH, NH], F32, "lbl")
    for d in range(2):
        for l in range(DEPTH):
            P.dma("sp", lambda e, d=d, l=l: e.dma_start(out=lbl[:, d, l, :], in_=hg_lb[d, l].rearrange("(h p) -> p h", p=128)),
                  writes=[lbl.s])
    lb = sb([128, DEPTH, 2, NH], F32, "lb")
    oml = sb([128, DEPTH, 2, NH], F32, "oml")
    noml = sb([128, DEPTH, 2, NH], F32, "noml")
    P.dve(lambda e: e.memset(lb[:], 0.0), writes=[lb.s])
    P.dve(lambda e: e.tensor_tensor(out=lb[:, 1], in0=lbl[:, :, 1, :], in1=lbl[:, :, 0, :], op=ALU.subtract),
          reads=[lbl.s, lb.s], writes=[lb.s])
    P.act(lambda e: e.activation(out=lb[:, 1], in_=lb[:, 1], func=AF.Sigmoid), reads=[lb.s], writes=[lb.s])
    P.dve(lambda e: e.tensor_scalar(out=oml[:], in0=lb[:], scalar1=-1.0, scalar2=1.0, op0=ALU.mult, op1=ALU.add),
          reads=[lb.s], writes=[oml.s])
    P.dve(lambda e: e.tensor_scalar(out=noml[:], in0=lb[:], scalar1=1.0, scalar2=-1.0, op0=ALU.mult, op1=ALU.add),
          reads=[lb.s], writes=[noml.s])
    gn = sb([128, DEPTH], F32, "gn")
    P.dma("sp", lambda e: e.dma_start(out=gn[:], in_=hg_norm.rearrange("l p -> p l")), writes=[gn.s])
    P.dve(lambda e: e.tensor_scalar(out=gn[:], in0=gn[:], scalar1=float(math.sqrt(DK)), scalar2=None, op0=ALU.mult),
          reads=[gn.s], writes=[gn.s])

    par_all = sb([128, DEPTH, 2, 5, 16], F32, "par_all")
    for l_ in range(DEPTH):
        for d_ in range(2):
            P.dma("sp", lambda e, l_=l_, d_=d_: e.dma_start(
                out=par_all[:, l_, d_, 0, :], in_=s5_lam_re[l_, d_].rearrange("(gp g2) p -> (g2 p) gp", g2=2)), writes=[par_all.s])
            P.dma("sp", lambda e, l_=l_, d_=d_: e.dma_start(
                out=par_all[:, l_, d_, 1, :], in_=s5_lam_im[l_, d_].rearrange("(gp g2) p -> (g2 p) gp", g2=2)), writes=[par_all.s])
            for g2_ in range(2):
                P.dma("sp", lambda e, l_=l_, d_=d_, g2_=g2_: e.dma_start(
                    out=par_all[64 * g2_:64 * g2_ + 64, l_, d_, 2, :],
                    in_=s5_log_dt[l_, d_].rearrange("(gp g2) -> g2 gp", g2=2)[g2_].partition_broadcast(64)), writes=[par_all.s])
    mod = sb([128, DEPTH, 48, 2], F32, "mod")
    coef = sb([128, DEPTH, 2, KD, 2], F32, "coef")

    mod_s = [[Slot("mod%d_%d" % (l_, p_)) for p_ in range(3)] for l_ in range(DEPTH)]
    coef_s = [[Slot("coef%d_%d" % (l_, n_)) for n_ in range(2)] for l_ in range(DEPTH)]

    def compute_mod(l, part, slots=None):
        bk = psum()
        for ci_, cc in enumerate(range(part * 4, part * 4 + 4)):
            wv, wsl = wload(w_mod[l][:, cc * 512:(cc + 1) * 512].rearrange("(k p) c -> p k c", p=128), KD, 512,
                            slot=None if slots is None else slots[ci_])
            for j in range(4):
                ft = cc * 4 + j - part * 16
                for k in range(KD):
                    P.pe(lambda e, wv=wv, j=j, k=k, ft=ft, bk=bk: e.matmul(
                        bk[:, ft * 2:ft * 2 + 2], wv[:, k, j * 128:(j + 1) * 128], condb[:, k, :],
                        start=(k == 0), stop=(k == KD - 1)),
                        reads=[wsl, condb.s], writes=[bk.s])
        t0_ = part * 16
        P.dve(lambda e, bk=bk: e.tensor_tensor(
            out=mod[:, l, t0_:t0_ + 16], in0=bk[:, 0:32].rearrange("p (f c) -> p f c", c=2),
            in1=bmod[:, l, t0_:t0_ + 16].unsqueeze(2).broadcast_to([128, 16, 2]), op=ALU.add),
            reads=[bk.s, bmod.s], writes=[mod_s[l][part]])
        for n, (gt, base, prt) in enumerate(((nmix, 8, 0), (nffn, 32, 2))):
            if prt != part:
                continue
            P.dve(lambda e, n=n, base=base: e.tensor_scalar(
                out=coef[:, l, n], in0=mod[:, l, base:base + 8, :], scalar1=1.0, scalar2=32.0,
                op0=ALU.add, op1=ALU.mult), reads=[mod_s[l][part]], writes=[coef_s[l][n]])
            P.dve(lambda e, n=n, gt=gt: e.tensor_tensor(
                out=coef[:, l, n], in0=coef[:, l, n], in1=gt[:, l].unsqueeze(2).broadcast_to([128, KD, 2]),
                op=ALU.mult), reads=[coef_s[l][n], gt.s], writes=[coef_s[l][n]])

    sq = [sb([128, 512], BF16, "sq%d" % i) for i in range(2)]
    rstd = sb([128, 512], F32, "rstd")
    tmpn = [sb([128, 512], F32, "tmpn%d" % i) for i in range(2)]

    def rms_stats(h):
        bk = psum()
        for k in range(KD):
            s = sq[k % 2]
            P.act(lambda e, k=k, h=h, s=s: e.activation(out=s[:], in_=xT[:, k, h * 512:(h + 1) * 512], func=AF.Square),
                  reads=[xs[k][h]], writes=[s.s])
            P.pe(lambda e, k=k, bk=bk, s=s: e.matmul(bk[:], ones_b[:], s[:], start=(k == 0), stop=(k == KD - 1)),
                 reads=[s.s, ones_b.s], writes=[bk.s])
        P.act(lambda e, bk=bk: e.activation(out=rstd[:], in_=bk[:], func=AF.Ln, bias=epsc[:, 0:1]),
              reads=[bk.s, epsc.s], writes=[rstd.s])
        P.act(lambda e: e.activation(out=rstd[:], in_=rstd[:], func=AF.Exp, scale=-0.5), reads=[rstd.s], writes=[rstd.s])

    def norm_mod(l, n, shift_base):
        for h in range(2):
            rms_stats(h)
            for k in range(KD):
                tm = tmpn[k % 2]
                P.dve(lambda e, k=k, h=h, tm=tm: e.tensor_tensor(out=tm[:], in0=xT[:, k, h * 512:(h + 1) * 512],
                                                                 in1=rstd[:], op=ALU.mult),
                      reads=[xs[k][h], rstd.s], writes=[tm.s])
                P.act(lambda e, k=k, h=h, l=l, n=n, tm=tm: e.activation(
                    out=hT[:, k, h * 512:(h + 1) * 512], in_=tm[:], func=AF.Identity,
                    bias=mod[:, l, shift_base + k, h:h + 1], scale=coef[:, l, n, k, h:h + 1]),
                    reads=[tm.s, mod_s[l][shift_base // 16], coef_s[l][n]], writes=[hs[k][h]])

    def residual_proj(wdram, l, gate_base, srcT, src_slots, nk):
        per = WSLOT // 256
        for c4 in range(4):
            wv = []
            for k0 in range(0, nk, per):
                kk = min(per, nk - k0)
                v, s = wload(wdram[k0 * 128:(k0 + kk) * 128, c4 * 256:(c4 + 1) * 256].rearrange("(k p) c -> p k c", p=128),
                             kk, 256)
                wv.append((k0, kk, v, s))
            for j in range(2):
                dt_ = c4 * 2 + j
                for h in range(2):
                    bk = psum()
                    for (k0, kk, v, s) in wv:
                        for k in range(kk):
                            kg = k0 + k
                            P.pe(lambda e, v=v, k=k, j=j, kg=kg, h=h, bk=bk: e.matmul(
                                bk[:], v[:, k, j * 128:(j + 1) * 128], srcT[:, kg, h * 512:(h + 1) * 512],
                                start=(kg == 0), stop=(kg == nk - 1)),
                                reads=[s, src_slots[kg][h]], writes=[bk.s])
                    P.dve(lambda e, bk=bk, dt_=dt_, h=h, l=l: e.scalar_tensor_tensor(
                        out=xT[:, dt_, h * 512:(h + 1) * 512], in0=bk[:], scalar=mod[:, l, gate_base + dt_, h:h + 1],
                        in1=xT[:, dt_, h * 512:(h + 1) * 512], op0=ALU.mult, op1=ALU.add),
                        reads=[bk.s, mod_s[l][gate_base // 16], xs[dt_][h]], writes=[xs[dt_][h]])


    def ffn(l):
        aphase()
        h1 = aalloc([NF, T], BF16, "h1")
        sgate = [aalloc([512], F32, "sgate%d" % i) for i in range(2)]
        h1s = [[Slot("h1_%d_%d" % (f, h)) for h in range(2)] for f in range(NF)]
        ar["live"].extend([s for row in h1s for s in row])
        afence()
        it = 0
        for c in range(6):
            ncol = 512 if c < 5 else 256
            vg, sg_ = wload(w_gate[l][:, c * 512:c * 512 + ncol].rearrange("(k p) c -> p k c", p=128), KD, ncol)
            vu, su_ = wload(w_up[l][:, c * 512:c * 512 + ncol].rearrange("(k p) c -> p k c", p=128), KD, ncol)
            for j in range(ncol // 128):
                f = c * 4 + j
                for h in range(2):
                    bg = psum()
                    bu = psum()
                    for k in range(KD):
                        P.pe(lambda e, vg=vg, k=k, j=j, h=h, bg=bg: e.matmul(
                            bg[:], vg[:, k, j * 128:(j + 1) * 128], hT[:, k, h * 512:(h + 1) * 512],
                            start=(k == 0), stop=(k == KD - 1)), reads=[sg_, hs[k][h]], writes=[bg.s])
                    for k in range(KD):
                        P.pe(lambda e, vu=vu, k=k, j=j, h=h, bu=bu: e.matmul(
                            bu[:], vu[:, k, j * 128:(j + 1) * 128], hT[:, k, h * 512:(h + 1) * 512],
                            start=(k == 0), stop=(k == KD - 1)), reads=[su_, hs[k][h]], writes=[bu.s])
                    sgt = sgate[it % 2]
                    it += 1
                    P.act(lambda e, bg=bg, sgt=sgt: e.activation(out=sgt[:], in_=bg[:], func=AF.Silu),
                          reads=[bg.s], writes=[sgt.s])
                    P.dve(lambda e, bu=bu, f=f, h=h, sgt=sgt: e.tensor_tensor(
                        out=h1[:, f, h * 512:(h + 1) * 512], in0=sgt[:], in1=bu[:], op=ALU.mult),
                        reads=[sgt.s, bu.s], writes=[h1s[f][h]])
        residual_proj(w_down[l], l, 40, h1, h1s, NF)

    def proj_feat(wv, wsl, c0, h):
        bk = psum()
        for k in range(KD):
            P.pe(lambda e, k=k, bk=bk: e.matmul(bk[:], wv[:, k, c0:c0 + 128], hT[:, k, h * 512:(h + 1) * 512],
                                                start=(k == 0), stop=(k == KD - 1)),
                 reads=[wsl, hs[k][h]], writes=[bk.s])
        return bk

    def hgrn(l):
        aphase()
        qk = [[[aalloc([T], BF16, "qk%d%d%d" % (h, d, w)) for w in range(2)] for d in range(2)] for h in range(NH)]
        kendT = [[aalloc([NT, DK], BF16, "kendT%d%d" % (h, d)) for d in range(2)] for h in range(NH)]
        V = [aalloc([HG_W], BF16, "V%d" % tt) for tt in range(NT)]
        gch = aalloc([NH * 2, NCH], F32, "gch")
        gch_s = [[Slot("gch%d%d" % (h, d)) for d in range(2)] for h in range(NH)]
        ar["live"].extend([s for row in gch_s for s in row])
        R1 = ar["off"]
        qs = [aalloc([512], F32, "qs%d" % hf) for hf in range(2)]
        rmask = aalloc([512], F32, "rmask")
        tmp = [[aalloc([512], F32, "gt%d_%d" % (i, j)) for j in range(5)] for i in range(2)]
        kend_t = [aalloc([512], BF16, "kend%d" % i) for i in range(2)]
        R1_end = ar["off"]
        afence()
        P.dve(lambda e: e.memset(rmask[:], 1.0), writes=[rmask.s])
        P.dve(lambda e: e.memset(rmask[:, 0:512:CH], 0.0), reads=[rmask.s], writes=[rmask.s])

        wv_iv, ws_iv = None, None

        def load_in(c):
            return wload(w_in[l][:, c * 512:(c + 1) * 512].rearrange("(k p) c -> p k c", p=128), KD, 512)

        wq, wqs = load_in(0)
        wf = [None, None]
        wf[0] = load_in(1)
        wf[1] = load_in(2)
        def gate_chain(h, d, hf, tset, ke):
            t_sig, t_a, t_b, t_c, t_e = tset
            bk = proj_feat(wf[d][0], wf[d][1], h * 128, hf)
            lb_ = lb[:, l, d, h:h + 1]
            oml_ = oml[:, l, d, h:h + 1]
            noml_ = noml[:, l, d, h:h + 1]
            P.act(lambda e, bk=bk, t_sig=t_sig: e.activation(out=t_sig[:], in_=bk[:], func=AF.Sigmoid),
                  reads=[bk.s], writes=[t_sig.s])
            yield
            P.dve(lambda e, t_sig=t_sig, t_a=t_a, lb_=lb_, oml_=oml_: e.tensor_scalar(
                out=t_a[:], in0=t_sig[:], scalar1=oml_, scalar2=lb_, op0=ALU.mult, op1=ALU.add),
                reads=[t_sig.s, lb.s, oml.s], writes=[t_a.s])
            yield
            P.act(lambda e, t_a=t_a: e.activation(out=t_a[:], in_=t_a[:], func=AF.Ln), reads=[t_a.s], writes=[t_a.s])
            yield
            P.dve(lambda e, t_a=t_a, t_b=t_b: e.tensor_tensor_scan(
                out=t_b[:], data0=rmask[:], data1=t_a[:], initial=0.0, op0=ALU.mult, op1=ALU.add),
                reads=[t_a.s, rmask.s], writes=[t_b.s])
            yield
            P.dve(lambda e, t_sig=t_sig, noml_=noml_, oml_=oml_: e.tensor_scalar(
                out=t_sig[:], in0=t_sig[:], scalar1=noml_, scalar2=oml_, op0=ALU.mult, op1=ALU.add),
                reads=[t_sig.s, noml.s, oml.s], writes=[t_sig.s])
            yield
            P.act(lambda e, t_b=t_b, h=h, d=d, hf=hf: e.activation(
                out=gch[:, h * 2 + d, hf * (512 // CH):(hf + 1) * (512 // CH)], in_=t_b[:, CH - 1:512:CH], func=AF.Exp),
                reads=[t_b.s], writes=[gch_s[h][d]])
            tb3 = t_b[:].rearrange("p (c j) -> p c j", j=CH)
            tc3 = t_c[:].rearrange("p (c j) -> p c j", j=CH)
            ta3 = t_a[:].rearrange("p (c j) -> p c j", j=CH)
            tot_b = tb3[:, :, CH - 1:CH].broadcast_to([128, 512 // CH, CH])
            if d == 0:
                P.dve(lambda e, tc3=tc3, tb3=tb3, tot_b=tot_b: e.tensor_tensor(
                    out=tc3, in0=tb3, in1=tot_b, op=ALU.subtract), reads=[t_b.s], writes=[t_c.s])
            else:
                P.dve(lambda e, tc3=tc3, ta3=ta3, tb3=tb3: e.tensor_tensor(
                    out=tc3, in0=ta3, in1=tb3, op=ALU.subtract), reads=[t_a.s, t_b.s], writes=[t_c.s])
                P.dve(lambda e, tc3=tc3, tb3=tb3, tot_b=tot_b, ta3=ta3: e.tensor_tensor(
                    out=ta3, in0=tc3, in1=tot_b, op=ALU.add), reads=[t_c.s, t_b.s], writes=[t_a.s])
            beta = t_b if d == 0 else t_a
            yield
            P.act(lambda e, beta=beta, t_e=t_e: e.activation(out=t_e[:], in_=beta[:], func=AF.Exp, bias=lnc[:, 0:1]),
                  reads=[beta.s, lnc.s], writes=[t_e.s])
            yield
            P.dve(lambda e, t_e=t_e, h=h, d=d, hf=hf: e.tensor_tensor(
                out=qk[h][d][0][:, hf * 512:(hf + 1) * 512], in0=qs[hf][:], in1=t_e[:], op=ALU.mult),
                reads=[t_e.s, qs[hf].s], writes=[qk[h][d][0].s])
            yield
            P.dve(lambda e, beta=beta, t_e=t_e: e.tensor_scalar(out=t_e[:], in0=beta[:], scalar1=-75.0, scalar2=None, op0=ALU.max),
                  reads=[beta.s], writes=[t_e.s])
            yield
            P.act(lambda e, t_e=t_e: e.activation(out=t_e[:], in_=t_e[:], func=AF.Exp, scale=-1.0),
                  reads=[t_e.s], writes=[t_e.s])
            yield
            P.dve(lambda e, t_e=t_e, t_sig=t_sig, h=h, d=d, hf=hf: e.tensor_tensor(
                out=qk[h][d][1][:, hf * 512:(hf + 1) * 512], in0=t_sig[:], in1=t_e[:], op=ALU.mult),
                reads=[t_e.s, t_sig.s], writes=[qk[h][d][1].s])
            yield
            P.act(lambda e, t_c=t_c, t_e=t_e: e.activation(out=t_e[:], in_=t_c[:], func=AF.Exp, scale=-1.0),
                  reads=[t_c.s], writes=[t_e.s])
            yield
            P.dve(lambda e, t_e=t_e, t_sig=t_sig, ke=ke: e.tensor_tensor(
                out=ke[:], in0=t_sig[:], in1=t_e[:], op=ALU.mult), reads=[t_e.s, t_sig.s], writes=[ke.s])
            yield
            bk2 = psum()
            yield
            for j in range(4):
                P.pe(lambda e, bk2=bk2, j=j, ke=ke: e.matmul(bk2[:, j * 128:(j + 1) * 128], ke[:, j * 128:(j + 1) * 128],
                                                             ident_b[:], start=True, stop=True),
                     reads=[ke.s, ident_b.s], writes=[bk2.s])
            yield
            P.act(lambda e, bk2=bk2, h=h, d=d, hf=hf: e.activation(
                out=kendT[h][d][:, hf * 4:(hf + 1) * 4, :], in_=bk2[:].rearrange("p (j k) -> p j k", k=128), func=AF.Copy),
                reads=[bk2.s], writes=[kendT[h][d].s])
            yield

        def interleave(gens):
            gens = list(gens)
            while gens:
                for g_ in list(gens):
                    try:
                        next(g_)
                    except StopIteration:
                        gens.remove(g_)

        for h in range(NH):
            for hf in range(2):
                bk = proj_feat(wq, wqs, h * 128, hf)
                P.act(lambda e, bk=bk, hf=hf: e.activation(out=qs[hf][:], in_=bk[:], func=AF.Silu),
                      reads=[bk.s], writes=[qs[hf].s])
            for d in range(2):
                interleave([gate_chain(h, d, 0, tmp[0], kend_t[0]), gate_chain(h, d, 1, tmp[1], kend_t[1])])
        wiv, wivs = load_in(3)
        for tt in range(NT):
            bk = psum()
            hf = tt // 4
            for k in range(KD):
                P.pe(lambda e, k=k, bk=bk, tt=tt: e.matmul(bk[:], hT[:, k, tt * 128:(tt + 1) * 128], wiv[:, k, :],
                                                            start=(k == 0), stop=(k == KD - 1)),
                     reads=[wivs, hs[k][hf]], writes=[bk.s])
            P.act(lambda e, bk=bk, tt=tt: e.activation(out=V[tt][:], in_=bk[:], func=AF.Copy), reads=[bk.s], writes=[V[tt].s])

        old_tmp = [t.s for grp in tmp for t in grp] + [k.s for k in kend_t] + [q.s for q in qs] + [rmask.s]
        ar["off"] = R1
        S = [aalloc([DK], F32, "S%d" % i) for i in range(2)]
        Sent = aalloc([NCH, DK], BF16, "Sent")
        o_t = aalloc([512], F32, "o_t")
        on_t = aalloc([512], F32, "on_t")
        sg_t = aalloc([512], F32, "sg_t")
        sq_t = aalloc([512], BF16, "sq_t")
        PT = [aalloc([128], BF16, "PT%d" % i) for i in range(2)]
        gat = aalloc([4, DK], F32, "gat")
        gatG = aalloc([4, 8], F32, "gatG")
        s0t = aalloc([DK], F32, "s0t")
        Pc = [aalloc([DK], F32, "Pc%d" % i) for i in range(2)]
        Sinit = aalloc([2 * NH, DK], F32, "Sinit")
        Sinit_s = [[Slot("Sinit%d%d" % (h, d)) for d in range(2)] for h in range(NH)]
        gtot = aalloc([NH * 2, NCH // 2], F32, "gtot")
        new2 = [t.s for t in S + PT + Pc] + [Sent.s, o_t.s, on_t.s, sg_t.s, sq_t.s, gat.s, gatG.s, s0t.s, Sinit.s, gtot.s] + \
               [s_ for row in Sinit_s for s_ in row]
        ar["live"].extend([s_ for row in Sinit_s for s_ in row])
        P.dve(lambda e: e.memset(fence_t[:], 0.0), reads=[], writes=old_tmp + new2 + [fence_t.s])

        CPT = 128 // CH

        def u_matmul(h, d, c):
            tt, p0 = c // CPT, (c % CPT) * CH
            bk = psum()
            P.pe(lambda e, bk=bk: e.matmul(bk[:, 0:128], kendT[h][d][p0:p0 + CH, tt, :],
                                           V[tt][p0:p0 + CH, h * 128:(h + 1) * 128],
                                           start=True, stop=True, tile_position=(p0, 0)),
                 reads=[kendT[h][d].s, V[tt].s], writes=[bk.s])
            return bk

        def scan_order(c0, c1, d):
            return list(range(c0, c1)) if d == 0 else list(range(c1 - 1, c0 - 1, -1))

        si = [0]
        SC0, SC1 = SEGS[2]

        ci = l * 2
        ccs_in, ccs_out = Slot("ccin"), Slot("ccout")
        for h in range(NH):
            for d in range(2):
                hd = h * 2 + d
                Sx = S[si[0] % 2]
                si[0] += 1
                order = scan_order(SC0, SC1, d)
                for i, c in enumerate(order):
                    bk = u_matmul(h, d, c)
                    if i == 0:
                        P.dve(lambda e, bk=bk, Sx=Sx: e.tensor_copy(out=Sx[:], in_=bk[:, 0:128]), reads=[bk.s], writes=[Sx.s])
                    else:
                        P.dve(lambda e, bk=bk, Sx=Sx, c=c, hd=hd: e.scalar_tensor_tensor(
                            out=Sx[:], in0=Sx[:], scalar=gch[:, hd, c:c + 1], in1=bk[:, 0:128],
                            op0=ALU.mult, op1=ALU.add), reads=[bk.s, Sx.s, gch_s[h][d]], writes=[Sx.s])
                P.dma("sp", lambda e, Sx=Sx, hd=hd: e.dma_start(out=cc_in[ci][:, hd * 128:(hd + 1) * 128], in_=Sx[:]),
                      reads=[Sx.s], writes=[ccs_in])
                P.dve(lambda e, hd=hd: e.tensor_tensor_scan(
                    out=gtot[:, hd, :], data0=gch[:, hd, SC0:SC1], data1=zeros_f[:, 0:SC1 - SC0], initial=1.0,
                    op0=ALU.mult, op1=ALU.add), reads=[gch_s[h][d], zeros_f.s], writes=[gtot.s])
        P.dma("sp", lambda e: e.dma_start(out=cc_in[ci][:, 1024:1032], in_=gtot[:, :, SC1 - SC0 - 1]),
              reads=[gtot.s], writes=[ccs_in])
        P.dma("sp", lambda e: e.dma_start(out=cc_in[ci][:, 1032:CCW], in_=zeros_f[:, 0:CCW - 1032]),
              reads=[zeros_f.s], writes=[ccs_in])
        P.dma("pool", lambda e: e.collective_compute("AllGather", ALU.bypass, replica_groups=[[0, 1, 2, 3], [4, 5, 6, 7]],
                                                     ins=[cc_in[ci]], outs=[cc_out[ci]]),
              reads=[ccs_in], writes=[ccs_out], inc=1)
        ccv = cc_out[ci].rearrange("(r p) c -> p r c", p=128)
        P.dma("sp", lambda e: e.dma_start(out=gatG[:], in_=ccv[:, :, 1024:1032]), reads=[ccs_out], writes=[gatG.s])
        for h in range(NH):
            for d in range(2):
                hd = h * 2 + d
                col = slice(hd * 128, (hd + 1) * 128)
                P.dma("sp", lambda e, col=col: e.dma_start(out=gat[:], in_=ccv[:, :, col]), reads=[ccs_out], writes=[gat.s])
                P.dma("sp", lambda e, d=d, h=h: e.dma_start(out=s0t[:], in_=st_hg[l, d, h]), writes=[s0t.s])
                dst = Sinit[:, hd, :]
                ranks = [0, 1, 2, 3] if d == 0 else [3, 2, 1, 0]
                prev, prev_s = s0t[:], s0t.s
                P.dve(lambda e, dst=dst, prev=prev, r=ranks[0]: e.tensor_scalar(
                    out=dst, in0=prev, scalar1=cst_f[:, 384 + r:385 + r], scalar2=None, op0=ALU.mult),
                    reads=[prev_s, cst_f.s], writes=[Sinit_s[h][d]])
                for i in range(3):
                    r = ranks[i]
                    nxt = Pc[i % 2]
                    P.dve(lambda e, nxt=nxt, prev=prev, r=r, hd=hd: e.scalar_tensor_tensor(
                        out=nxt[:], in0=prev, scalar=gatG[:, r, hd:hd + 1], in1=gat[:, r, :],
                        op0=ALU.mult, op1=ALU.add), reads=[prev_s, gat.s, gatG.s], writes=[nxt.s])
                    rn = ranks[i + 1]
                    P.dve(lambda e, nxt=nxt, dst=dst, rn=rn: e.scalar_tensor_tensor(
                        out=dst, in0=nxt[:], scalar=cst_f[:, 384 + rn:385 + rn], in1=dst, op0=ALU.mult, op1=ALU.add),
                        reads=[nxt.s, cst_f.s, Sinit_s[h][d]], writes=[Sinit_s[h][d]])
                    prev, prev_s = nxt[:], nxt.s

        wg_v, wg_s = load_in(4)
        it2 = 0
        for h in range(NH):
            bo = [banks[6], banks[7]]
            for d in range(2):
                hd = h * 2 + d
                for (c0, c1) in SEGS:
                    Sx = S[si[0] % 2]
                    si[0] += 1
                    order = scan_order(c0, c1, d)
                    is_sample = (c0 == SC0)
                    for i, c in enumerate(order):
                        if i == 0:
                            src = Sinit[:, hd, :] if is_sample else zeros_f[:]
                            src_s = Sinit_s[h][d] if is_sample else zeros_f.s
                        else:
                            src, src_s = Sx[:], Sx.s
                        P.act(lambda e, src=src, c=c: e.activation(out=Sent[:, c, :], in_=src, func=AF.Copy),
                              reads=[src_s], writes=[Sent.s])
                        last = (i == len(order) - 1)
                        if last and is_sample:
                            continue
                        bk = u_matmul(h, d, c)
                        if i == 0 and not is_sample:
                            P.dve(lambda e, bk=bk, Sx=Sx: e.tensor_copy(out=Sx[:], in_=bk[:, 0:128]), reads=[bk.s], writes=[Sx.s])
                        else:
                            P.dve(lambda e, bk=bk, Sx=Sx, src=src, c=c, hd=hd: e.scalar_tensor_tensor(
                                out=Sx[:], in0=src, scalar=gch[:, hd, c:c + 1], in1=bk[:, 0:128],
                                op0=ALU.mult, op1=ALU.add), reads=[bk.s, src_s, gch_s[h][d]], writes=[Sx.s])
                    if not is_sample:
                        seq = 0 if c0 == 0 else 1
                        P.dma("sp", lambda e, Sx=Sx, seq=seq, d=d, h=h: e.dma_start(out=ns_hg[seq, l, d, h], in_=Sx[:]),
                              reads=[Sx.s], writes=[])
                for hf in range(2):
                    for j in range(4):
                        tt = hf * 4 + j
                        tok = slice(tt * 128, (tt + 1) * 128)
                        bs = psum()
                        P.pe(lambda e, bs=bs, tok=tok, d=d, h=h: e.matmul(bs[:, 0:128], qk[h][d][1][:, tok], qk[h][d][0][:, tok],
                                                                    start=True, stop=True),
                             reads=[qk[h][d][0].s, qk[h][d][1].s], writes=[bs.s])
                        pt = PT[it2 % 2]
                        it2 += 1
                        P.dve(lambda e, bs=bs, pt=pt, d=d: e.tensor_tensor(
                            out=pt[:], in0=bs[:, 0:128], in1=cst_f[:, 128 + d * 128:256 + d * 128], op=ALU.mult),
                            reads=[bs.s, cst_f.s], writes=[pt.s])
                        oc = slice(j * 128, (j + 1) * 128)
                        P.pe(lambda e, pt=pt, tt=tt, oc=oc, hf=hf, d=d, j=j, h=h: e.matmul(
                            bo[hf][:, oc], V[tt][:, h * 128:(h + 1) * 128], pt[:], start=(d == 0 and j == 0), stop=False),
                            reads=[V[tt].s, pt.s], writes=[bo[hf].s])
                        for sub in range(CPT):
                            c = tt * CPT + sub
                            cs = slice(j * 128 + sub * CH, j * 128 + (sub + 1) * CH)
                            ts = slice(tt * 128 + sub * CH, tt * 128 + (sub + 1) * CH)
                            P.pe(lambda e, c=c, cs=cs, ts=ts, hf=hf, d=d, h=h: e.matmul(
                                bo[hf][:, cs], Sent[:, c, :], qk[h][d][0][:, ts], start=False, stop=(d == 1)),
                                reads=[Sent.s, qk[h][d][0].s], writes=[bo[hf].s])
            for hf in range(2):
                P.act(lambda e, hf=hf: e.activation(out=o_t[:], in_=bo[hf][:], func=AF.Copy), reads=[bo[hf].s], writes=[o_t.s])
                P.act(lambda e, hf=hf: e.activation(out=sq_t[:], in_=bo[hf][:], func=AF.Square), reads=[bo[hf].s], writes=[sq_t.s])
                if l == 0 and h == 0 and hf == 0:
                    dbg("o", o_t[:], o_t.s, [128, 512])
                    dbg("qf", qk[0][0][0][:], qk[0][0][0].s, [128, T], BF16)
                    dbg("kf", qk[0][0][1][:], qk[0][0][1].s, [128, T], BF16)
                    dbg("qb", qk[0][1][0][:], qk[0][1][0].s, [128, T], BF16)
                    dbg("kb", qk[0][1][1][:], qk[0][1][1].s, [128, T], BF16)
                br = psum()
                P.pe(lambda e, br=br: e.matmul(br[:], ones_b[:], sq_t[:], start=True, stop=True),
                     reads=[sq_t.s, ones_b.s], writes=[br.s])
                P.act(lambda e, br=br: e.activation(out=on_t[:], in_=br[:], func=AF.Ln, bias=epsc[:, 1:2]),
                      reads=[br.s, epsc.s], writes=[on_t.s])
                P.act(lambda e: e.activation(out=on_t[:], in_=on_t[:], func=AF.Exp, scale=-0.5), reads=[on_t.s], writes=[on_t.s])
                P.dve(lambda e: e.tensor_tensor(out=on_t[:], in0=o_t[:], in1=on_t[:], op=ALU.mult),
                      reads=[o_t.s, on_t.s], writes=[on_t.s])
                bg = proj_feat(wg_v, wg_s, h * 128, hf)
                P.act(lambda e, bg=bg: e.activation(out=sg_t[:], in_=bg[:], func=AF.Silu), reads=[bg.s], writes=[sg_t.s])
                P.dve(lambda e, hf=hf, h=h: e.scalar_tensor_tensor(
                    out=mixT[:, h, hf * 512:(hf + 1) * 512], in0=on_t[:], scalar=gn[:, l:l + 1], in1=sg_t[:],
                    op0=ALU.mult, op1=ALU.mult), reads=[on_t.s, sg_t.s, gn.s], writes=[mixs[h][hf]])

    def TTop(out, in0, in1, op, reads, writes):
        return P.dve(lambda e: e.tensor_tensor(out=out, in0=in0, in1=in1, op=op), reads=reads, writes=writes)

    def TSop(out, in0, s1, s2, op0, op1, reads, writes):
        if s2 is None:
            return P.dve(lambda e: e.tensor_scalar(out=out, in0=in0, scalar1=s1, scalar2=None, op0=op0), reads=reads, writes=writes)
        return P.dve(lambda e: e.tensor_scalar(out=out, in0=in0, scalar1=s1, scalar2=s2, op0=op0, op1=op1), reads=reads, writes=writes)

    def STTop(out, in0, scalar, in1, op0, op1, reads, writes):
        return P.dve(lambda e: e.scalar_tensor_tensor(out=out, in0=in0, scalar=scalar, in1=in1, op0=op0, op1=op1),
                     reads=reads, writes=writes)

    def ACTop(out, in_, func, reads, writes, bias=None, scale=None):
        kw = {}
        if bias is not None:
            kw["bias"] = bias
        if scale is not None:
            kw["scale"] = scale
        return P.act(lambda e: e.activation(out=out, in_=in_, func=func, **kw), reads=reads, writes=writes)

    def MM(out, lhsT, rhs, start, stop, reads, writes, tp=None):
        if tp is None:
            return P.pe(lambda e: e.matmul(out, lhsT, rhs, start=start, stop=stop), reads=reads, writes=writes)
        return P.pe(lambda e: e.matmul(out, lhsT, rhs, start=start, stop=stop, tile_position=tp), reads=reads, writes=writes)

    def CPY(out, in_, reads, writes):
        return P.dve(lambda e: e.tensor_copy(out=out, in_=in_), reads=reads, writes=writes)

    def MSET(out, val, reads, writes):
        return P.dve(lambda e: e.memset(out, val), reads=reads, writes=writes)

    def SDMA(out, in_, reads, writes):
        return P.dma("sp", lambda e: e.dma_start(out=out, in_=in_), reads=reads, writes=writes)

    NCK = 128
    SEG8 = [(0, 32), (32, 64), (64, 128)]
    TWO_PI = 2.0 * math.pi

    def s5(l, after_u=None):
        aphase()
        c3 = aalloc([1024], F32, "c3")
        asel = aalloc([8, 240], BF16, "asel")
        UT = aalloc([NG, NCK], BF16, "UT")
        Hb = aalloc([2, 2, 16, NCK], BF16, "Hb")
        par = TT(par_all[:, l], par_all.s)
        tab = aalloc([3, 16, 65], F32, "tab")
        hin = aalloc([2, 16, 2], F32, "hin")
        sloc = aalloc([2, 2, 16], F32, "sloc")
        hent = aalloc([2, 2, 16], F32, "hent")
        fst = aalloc([2, 2, 16, 2], F32, "fst")
        dsk = aalloc([NG], F32, "dsk")
        gat5 = aalloc([4, 2, 2, 16], F32, "gat5")
        sm = [aalloc([16], F32, "sm%d" % i) for i in range(8)]
        W0 = ar["off"]
        Bt = aalloc([NG, 2, 64], BF16, "Bt")
        bb = aalloc([2, 2, 16, 16], F32, "bb")
        craw = aalloc([2, 2, 16, 16], F32, "craw")
        pw = aalloc([2, 2, 16, 17], F32, "pw")
        R2 = ar["off"]
        uT = aalloc([4, T], BF16, "uT")
        cnat = aalloc([16, 64], F32, "cnat")
        prs = [aalloc([16, 17], F32, "prs%d" % i) for i in range(3)]
        pri = aalloc([16, 17], I32, "pri")
        afence()
        CtS = [wslots[1], wslots[2]]
        CtV = [bass.AP(w.t, 0, [[WSLOT, 128], [512, 8], [256, 2], [128, 2], [1, 128]]) for w in CtS]
        DtS = wslots[3]
        DtV = bass.AP(DtS.t, 0, [[WSLOT, 128], [128, NG], [1, 128]])

        def Ct_(gp):
            return CtV[gp // 8], gp % 8, CtS[gp // 8].s

        def _stop(k):
            if STAGE["s5_stop"] <= k:
                for kk in range(4, 8):
                    for hh in range(2):
                        MSET(mixT[:, kk, hh * 512:(hh + 1) * 512], 0.0, [], [mixs[kk][hh]])
                return True
            return False

        SDMA(c3[:], cst3_in, [], [c3.s])
        for g8 in range(8):
            TSop(asel[:, g8, :], c3[:, 0:240], c3[:, 240 + g8:241 + g8], None, ALU.mult, None, [c3.s], [asel.s])
        EV = c3[:, 608:625]
        K8 = c3[:, 640:705]
        R_even = c3[:, 480:544]
        R_odd = c3[:, 544:608]
        M5 = c3[:, 768:1024]
        wu, wus = wload(w_in[l][:, 2560:3072].rearrange("(k p) c -> p k c", p=128), KD, 512, slot=0)
        for ct in range(4):
            for hf in range(2):
                bk = proj_feat(wu, wus, ct * 128, hf)
                ACTop(uT[:, ct, hf * 512:(hf + 1) * 512], bk[:], AF.Copy, [bk.s], [uT.s])
        for g0 in range(0, NG, 4):
            bk = psum()
            for gi in range(4):
                g = g0 + gi
                ct, g8 = g // 8, g % 8
                for s_ in range(8):
                    MM(bk[:, gi * 128:(gi + 1) * 128], asel[:, g8, 112 - 16 * s_:240 - 16 * s_],
                       uT[:, ct, s_:T:8], (gi == 0 and s_ == 0), (s_ == 7), [asel.s, uT.s], [bk.s])
            ACTop(UT[:, g0:g0 + 4, :], bk[:].rearrange("p (g n) -> p g n", n=128), AF.Copy, [bk.s], [UT.s])
        if after_u is not None:
            after_u()

        for d in range(2):
            for ri, src in enumerate((s5_b_re, s5_b_im)):
                SDMA(bb[:, d, ri], src[l, d].rearrange("(gp g2) p c -> (g2 p) gp c", g2=2), [], [bb.s])
            SDMA(hin[:, d], bass.AP(st_s5.tensor, st_s5[l, d].offset, [[2, 128], [256, 16], [1, 2]]), [], [hin.s])
        for s_ in range(8):
            SDMA(dsk[16 * s_:16 * s_ + 16, :], s5_d[l].rearrange("g c -> c g"), [], [dsk.s])
        for d in range(2):
            for ri, src in enumerate((s5_c_re, s5_c_im)):
                x0 = (d * 2 + ri) * 4
                SDMA(cnat[:, x0:x0 + 4, :], src[l, d].rearrange("(ct g8) c p -> (g8 c) ct p", g8=8), [], [cnat.s])
        for d in range(2):
            for ri in range(2):
                bk = psum()
                for ct in range(4):
                    x = (d * 2 + ri) * 4 + ct
                    MM(bk[0:64, ct * 64:(ct + 1) * 64], cnat[:, x, :], R_even, True, True, [cnat.s, c3.s], [bk.s], tp=(0, 0))
                    MM(bk[64:128, ct * 64:(ct + 1) * 64], cnat[:, x, :], R_odd, True, True, [cnat.s, c3.s], [bk.s], tp=(0, 64))
                ACTop(craw[:, d, ri].rearrange("p a b -> p (a b)"), bk[:, 0:256], AF.Copy, [bk.s], [craw.s])

        if _stop(1):
            return
        for d in range(2):
            lr, li, dt_, a_, th_ = (par[:, d, i, :] for i in range(5))
            TSop(lr, lr, -1e-4, None, ALU.min, None, [par.s], [par.s])
            ACTop(dt_, dt_, AF.Exp, [par.s], [par.s])
            TTop(a_, lr, dt_, ALU.mult, [par.s], [par.s])
            TTop(th_, li, dt_, ALU.mult, [par.s], [par.s])
            TSop(th_, th_, 1.0 / TWO_PI, None, ALU.mult, None, [par.s], [par.s])

        def powers(out_r, out_i, out_m, a_ap, th_ap, evals, ng_, ne, tr, ti_, tk_i, tk_f, rs, ws):
            sh = [128, ng_, ne]
            ev_b = evals.unsqueeze(1).broadcast_to(sh)
            TTop(tr, th_ap.unsqueeze(2).broadcast_to(sh), ev_b, ALU.mult, rs + ws, ws)
            TSop(ti_, tr, 0.25, None, ALU.add, None, ws, ws)
            for (dst, src) in ((out_i, tr), (out_r, ti_)):
                CPY(tk_i, src, ws, ws)
                CPY(tk_f, tk_i, ws, ws)
                TTop(tk_f, src, tk_f, ALU.subtract, ws, ws)
                ACTop(dst, tk_f, AF.Sin, ws, ws, scale=TWO_PI)
            TTop(tr, a_ap.unsqueeze(2).broadcast_to(sh), ev_b, ALU.mult, rs + ws, ws)
            ACTop(out_m, tr, AF.Exp, ws, ws)

        for d in range(2):
            ws = [pw.s, pri.s] + [p_.s for p_ in prs]
            tkf_ = cnat[:].rearrange("p a b -> p (a b)")[:, 0:272].rearrange("p (a b) -> p a b", b=17)
            powers(pw[:, d, 0], pw[:, d, 1], prs[2][:], par[:, d, 3, :], par[:, d, 4, :], EV, 16, 17, prs[0][:], prs[1][:],
                   pri[:], tkf_, [par.s, c3.s, craw.s], ws + [cnat.s])
            TTop(pw[:, d, 0], pw[:, d, 0], prs[2][:], ALU.mult, ws, ws)
            TTop(pw[:, d, 1], pw[:, d, 1], prs[2][:], ALU.mult, ws, ws)

        for d in range(2):
            lr, li = par[:, d, 0, :], par[:, d, 1, :]
            abr, abi = pw[:, d, 0, :, 9], pw[:, d, 1, :, 9]
            nr, den, zr, zi, t1, t2 = (sm[i][:] for i in range(6))
            ws = [s_.s for s_ in sm]
            rs = [par.s, pw.s] + ws
            TSop(nr, abr, -1.0, None, ALU.add, None, rs, ws)
            TTop(t1, lr, lr, ALU.mult, rs, ws)
            TTop(t2, li, li, ALU.mult, rs, ws)
            TTop(den, t1, t2, ALU.add, rs, ws)
            P.dve(lambda e, den=den: e.reciprocal(out=den, in_=den), reads=rs, writes=ws)
            TTop(t1, nr, lr, ALU.mult, rs, ws)
            TTop(t2, abi, li, ALU.mult, rs, ws)
            TTop(zr, t1, t2, ALU.add, rs, ws)
            TTop(zr, zr, den, ALU.mult, rs, ws)
            TTop(t1, abi, lr, ALU.mult, rs, ws)
            TTop(t2, nr, li, ALU.mult, rs, ws)
            TTop(zi, t1, t2, ALU.subtract, rs, ws)
            TTop(zi, zi, den, ALU.mult, rs, ws)
            zrb = zr.unsqueeze(2).broadcast_to([128, 16, 16])
            zib = zi.unsqueeze(2).broadcast_to([128, 16, 16])
            cf = cnat[:].rearrange("p a b -> p (a b)")
            t3 = cf[:, 0:256].rearrange("p (a b) -> p a b", b=16)
            t4 = cf[:, 256:512].rearrange("p (a b) -> p a b", b=16)
            t5 = cf[:, 512:768].rearrange("p (a b) -> p a b", b=16)
            br_, bi_ = bb[:, d, 0], bb[:, d, 1]
            rs2 = rs + [bb.s, cnat.s, craw.s]
            ws2 = [bb.s, cnat.s]
            TTop(t3, br_, zrb, ALU.mult, rs2, ws2)
            TTop(t4, bi_, zib, ALU.mult, rs2, ws2)
            TTop(t5, br_, zib, ALU.mult, rs2, ws2)
            TTop(t3, t3, t4, ALU.subtract, rs2, ws2)
            TTop(t4, bi_, zrb, ALU.mult, rs2, ws2)
            TTop(bi_, t4, t5, ALU.add, rs2, ws2)
            CPY(br_, t3, rs2, ws2)

        if l == 0:
            dbg("par", par[:].rearrange("p a b c -> p (a b c)"), par.s, [128, 160])
            dbg("bb", bb[:].rearrange("p a b c d -> p (a b c d)"), bb.s, [128, 1024])
            dbg("pw", pw[:].rearrange("p a b c d -> p (a b c d)"), pw.s, [128, 1088])
            dbg("craw", craw[:].rearrange("p a b c d -> p (a b c d)"), craw.s, [128, 1024])
        def lifted(dst_r, dst_i, coef_r, coef_i, d, e_idx, conj_sign, gp0, ws):
            sh = [128, 4, 8, 16]
            pr = pw[:, d, 0, gp0:gp0 + 4, e_idx].unsqueeze(3).broadcast_to(sh)
            pi_ = pw[:, d, 1, gp0:gp0 + 4, e_idx].unsqueeze(3).broadcast_to(sh)
            cr = coef_r[:, gp0:gp0 + 4, :].unsqueeze(2).broadcast_to(sh)
            ci = coef_i[:, gp0:gp0 + 4, :].unsqueeze(2).broadcast_to(sh)
            t1 = LA[:].rearrange("p a (j c) -> p a j c", c=16)
            t2 = LB[:].rearrange("p a (j c) -> p a j c", c=16)
            rs = [pw.s, bb.s, craw.s, LA.s, LB.s]
            TTop(t1, cr, pr, ALU.mult, rs, [LA.s])
            TTop(t2, ci, pi_, ALU.mult, rs, [LB.s])
            TTop(dst_r.rearrange("p a (j c) -> p a j c", c=16), t1, t2, ALU.subtract, rs, ws)
            TTop(t1, cr, pi_, ALU.mult, rs, [LA.s])
            TTop(t2, ci, pr, ALU.mult, rs, [LB.s])
            if conj_sign > 0:
                TTop(dst_i.rearrange("p a (j c) -> p a j c", c=16), t1, t2, ALU.add, rs, ws)
            else:
                STTop(dst_i.rearrange("p a (j c) -> p a j c", c=16), t1, -1.0, t2, ALU.mult, ALU.subtract, rs, ws)

        E_B = [slice(15, 7, -1), slice(8, 16)]
        E_C = [slice(9, 17), slice(16, 8, -1)]
        E_N = [slice(7, None, -1), slice(0, 8)]

        if _stop(4):
            return
        old_r2 = [uT.s, cnat.s, pri.s] + [p_.s for p_ in prs]
        ar["off"] = R2
        XR = aalloc([4, NCK], F32, "XR")
        XI = aalloc([4, NCK], F32, "XI")
        A1 = aalloc([4, NCK], F32, "A1")
        B2 = aalloc([4, NCK], F32, "B2")
        C2 = aalloc([4, NCK], F32, "C2")
        RC = aalloc([4, NCK], F32, "RC")
        LA = aalloc([4, 128], F32, "LA2")
        LB = aalloc([4, 128], F32, "LB2")
        mnat = [aalloc([4, 128], BF16, "mnat2_%d" % i) for i in range(2)]
        new_r2 = [XR.s, XI.s, A1.s, B2.s, C2.s, RC.s, LA.s, LB.s, mnat[0].s, mnat[1].s]
        tsc = [A1, B2, C2]
        tsi = RC
        P.dve(lambda e: e.memset(fence_t[:], 0.0), reads=[], writes=old_r2 + new_r2 + [fence_t.s])

        def tables(d, tsc, tsi):
            ws = [tab.s, tsi.s] + [t_.s for t_ in tsc]
            for q in range(4):
                g_ = slice(q * 4, q * 4 + 4)
                powers(tab[:, 0, g_, :], tab[:, 1, g_, :], tab[:, 2, g_, :], par[:, d, 3, g_], par[:, d, 4, g_], K8, 4, 65,
                       tsc[0][:, :, 0:65], tsc[1][:, :, 0:65], tsi[:, :, 0:65].bitcast(I32), tsc[2][:, :, 0:65], [par.s, c3.s], ws)

        def seg_views(buf, gsl, n0, n1, d, shift):
            if d == 0:
                if shift == 0:
                    return buf[:, gsl, n0:n1]
                return buf[:, gsl, n0 + 1:n1] if shift > 0 else buf[:, gsl, n0:n1 - 1]
            lo = None if n0 == 0 else n0 - 1
            if shift == 0:
                return buf[:, gsl, n1 - 1:lo:-1]
            if shift > 0:
                return buf[:, gsl, n1 - 2:lo:-1]
            return buf[:, gsl, n1 - 1:n0:-1]

        for d in range(2):
            tables(d, tsc, tsi)
            if l == 0 and d == 0:
                dbg("tab", tab[:].rearrange("p a b c -> p (a b c)"), tab.s, [128, 3 * 16 * 65])
            if _stop(4.2):
                return
            for q in range(4):
                gp0 = q * 4
                tsl = slice(gp0, gp0 + 4)
                lifted(mnat[0][:], mnat[1][:], bb[:, d, 0], bb[:, d, 1], d, E_B[d], +1, gp0, [mnat[0].s, mnat[1].s])
                for ri in range(2):
                    bk = psum()
                    for gl in range(4):
                        MM(bk[:, gl * 128:(gl + 1) * 128], mnat[ri][:, gl, :], ident_b[:], True, True,
                           [mnat[ri].s, ident_b.s], [bk.s])
                    ACTop(Bt[:, 2 * gp0:2 * gp0 + 8, ri, :], bk[:].rearrange("p (g q) -> p g q", q=64), AF.Copy, [bk.s], [Bt.s])
                if _stop(4.3):
                    return
                for gl in range(4):
                    gp = gp0 + gl
                    bk = psum()
                    for g2 in range(2):
                        g = 2 * gp + g2
                        for ri in range(2):
                            MM(bk[64 * g2:64 * g2 + 64, ri * 128:(ri + 1) * 128], Bt[:, g, ri, :], UT[:, g, :], True, True,
                               [Bt.s, UT.s], [bk.s], tp=(0, 64 * g2))
                    ACTop(XR[:, gl, :], bk[:, 0:128], AF.Copy, [bk.s], [XR.s])
                    ACTop(XI[:, gl, :], bk[:, 128:256], AF.Copy, [bk.s], [XI.s])
                if l == 0 and d == 0:
                    dbg("XR%d" % q, XR[:].rearrange("p a b -> p (a b)"), XR.s, [128, 512])
                    dbg("Bt%d" % q, Bt[:, 2 * gp0:2 * gp0 + 8].rearrange("p a b c -> p (a b c)"), Bt.s, [128, 1024], BF16)
                    dbg("UT%d" % q, UT[:, 2 * gp0:2 * gp0 + 8].rearrange("p a b -> p (a b)"), UT.s, [128, 1024], BF16)
                if _stop(4.4):
                    return
                CPY(RC[:], tab[:, 2, tsl, 1:2].broadcast_to([128, 4, NCK]), [tab.s], [RC.s])
                for (n0, n1) in SEG8:
                    first = n0 if d == 0 else n1 - 1
                    MSET(RC[:, :, first:first + 1], 0.0, [RC.s], [RC.s])
                gsl = slice(0, 4)
                for (n0, n1) in SEG8:
                    L = n1 - n0
                    xr, xi = seg_views(XR, gsl, n0, n1, d, 0), seg_views(XI, gsl, n0, n1, d, 0)
                    a1, b2, c2 = seg_views(A1, gsl, n0, n1, d, 0), seg_views(B2, gsl, n0, n1, d, 0), seg_views(C2, gsl, n0, n1, d, 0)
                    cs_, sn_ = tab[:, 0, tsl, 1:L + 1], tab[:, 1, tsl, 1:L + 1]
                    rs = [XR.s, XI.s, tab.s, A1.s, B2.s, C2.s]
                    TTop(a1, xr, cs_, ALU.mult, rs, [A1.s])
                    TTop(c2, xi, sn_, ALU.mult, rs, [C2.s])
                    TTop(a1, a1, c2, ALU.add, rs, [A1.s])
                    TTop(b2, xi, cs_, ALU.mult, rs, [B2.s])
                    TTop(c2, xr, sn_, ALU.mult, rs, [C2.s])
                    TTop(b2, b2, c2, ALU.subtract, rs, [B2.s])

                if _stop(4.5):
                    return

                def fl(t_):
                    v = t_[:].rearrange("p a b -> p (a b)")
                    return v if d == 0 else v[:, ::-1]
                o_r, o_i, i_r, i_i, cf_ = fl(XR), fl(XI), fl(A1), fl(B2), fl(RC)
                P.dve(lambda e, o_r=o_r, i_r=i_r, cf_=cf_: e.tensor_tensor_scan(out=o_r, data0=cf_, data1=i_r, initial=0.0,
                                                                                  op0=ALU.mult, op1=ALU.add),
                      reads=[RC.s, A1.s], writes=[XR.s])
                P.dve(lambda e, o_i=o_i, i_i=i_i, cf_=cf_: e.tensor_tensor_scan(out=o_i, data0=cf_, data1=i_i, initial=0.0,
                                                                                  op0=ALU.mult, op1=ALU.add),
                      reads=[RC.s, B2.s], writes=[XI.s])
                if _stop(4.6):
                    return
                for si_, (n0, n1) in enumerate(SEG8):
                    L = n1 - n0
                    gr, gi_ = seg_views(XR, gsl, n0, n1, d, -1), seg_views(XI, gsl, n0, n1, d, -1)
                    a1, b2 = seg_views(A1, gsl, n0, n1, d, 1), seg_views(B2, gsl, n0, n1, d, 1)
                    hr = seg_views(Hb[:, d, 0], tsl, n0, n1, d, 1)
                    hi = seg_views(Hb[:, d, 1], tsl, n0, n1, d, 1)
                    cs_, sn_ = tab[:, 0, tsl, 1:L], tab[:, 1, tsl, 1:L]
                    rs = [XR.s, XI.s, tab.s, A1.s, B2.s]
                    TTop(a1, gr, cs_, ALU.mult, rs, [A1.s])
                    TTop(b2, gi_, sn_, ALU.mult, rs, [B2.s])
                    TTop(hr, a1, b2, ALU.subtract, rs, [Hb.s])
                    TTop(a1, gr, sn_, ALU.mult, rs, [A1.s])
                    TTop(b2, gi_, cs_, ALU.mult, rs, [B2.s])
                    TTop(hi, a1, b2, ALU.add, rs, [Hb.s])
                    first = n0 if d == 0 else n1 - 1
                    MSET(Hb[:, d, :, tsl, first:first + 1], 0.0, [Hb.s], [Hb.s])
                    last = n1 - 1 if d == 0 else n0
                    glr, gli = XR[:, :, last], XI[:, :, last]
                    cL, sL = tab[:, 0, tsl, L], tab[:, 1, tsl, L]
                    t1, t2 = sm[6][:, 0:4], sm[7][:, 0:4]
                    if si_ < 2:
                        dr, di = fst[:, si_, d, tsl, 0], fst[:, si_, d, tsl, 1]
                        dsl = fst.s
                    else:
                        dr, di = sloc[:, d, 0, tsl], sloc[:, d, 1, tsl]
                        dsl = sloc.s
                    rs = [XR.s, XI.s, tab.s, sm[6].s, sm[7].s, dsl]
                    TTop(t1, glr, cL, ALU.mult, rs, [sm[6].s])
                    TTop(t2, gli, sL, ALU.mult, rs, [sm[7].s])
                    TTop(dr, t1, t2, ALU.subtract, rs, [dsl])
                    TTop(t1, glr, sL, ALU.mult, rs, [sm[6].s])
                    TTop(t2, gli, cL, ALU.mult, rs, [sm[7].s])
                    TTop(di, t1, t2, ALU.add, rs, [dsl])
        if _stop(4.8):
            return
        for seq in range(2):
            for d in range(2):
                SDMA(bass.AP(ns_s5.tensor, ns_s5[seq, l, d].offset, [[2, 128], [256, 16], [1, 2]]), fst[:, seq, d], [fst.s], [])

        if _stop(5):
            return
        ccs_in, ccs_out = Slot("cc5in"), Slot("cc5out")
        SDMA(cc5_in[l], sloc[:].rearrange("p a b c -> p (a b c)"), [sloc.s], [ccs_in])
        P.dma("pool", lambda e: e.collective_compute("AllGather", ALU.bypass, replica_groups=[[0, 1, 2, 3], [4, 5, 6, 7]],
                                                     ins=[cc5_in[l]], outs=[cc5_out[l]]),
              reads=[ccs_in], writes=[ccs_out], inc=1)
        SDMA(gat5[:].rearrange("p r a b c -> p r (a b c)"), cc5_out[l].rearrange("(r p) c -> p r c", p=128), [ccs_out], [gat5.s])

        old_r2 = new_r2
        ar["off"] = R2
        LA = aalloc([4, 128], F32, "LA")
        LB = aalloc([4, 128], F32, "LB")
        mnat = [aalloc([4, 128], BF16, "mnat%d" % i) for i in range(2)]
        Dacc = aalloc([8, 128], F32, "Dacc")
        new_r2 = [LA.s, LB.s, mnat[0].s, mnat[1].s, Dacc.s]
        P.dve(lambda e: e.memset(fence_t[:], 0.0), reads=[], writes=old_r2 + new_r2 + [fence_t.s])

        for d in range(2):
            for q in range(4):
                gp0 = q * 4
                cv, g8_, cs_ = Ct_(gp0)
                lifted(cv[:, g8_:g8_ + 4, d, 0, :], cv[:, g8_:g8_ + 4, d, 1, :], craw[:, d, 0], craw[:, d, 1], d, E_C[d], -1, gp0, [cs_])
        for q in range(4):
            gp0 = q * 4
            for d in range(2):
                lifted(mnat[0][:], mnat[1][:], bb[:, d, 0], bb[:, d, 1], d, E_N[d], +1, gp0, [mnat[0].s, mnat[1].s])
                for gi in range(8):
                    gl, g2 = gi // 2, gi % 2
                    gp = gp0 + gl
                    cv, g8_, cs_ = Ct_(gp)
                    bk = psum()
                    for ri in range(2):
                        MM(bk[:, 0:128], mnat[ri][64 * g2:64 * g2 + 64, gl, :], cv[64 * g2:64 * g2 + 64, g8_, d, ri, :],
                           (ri == 0), (ri == 1), [mnat[ri].s, cs_], [bk.s])
                    msk = M5[:, d * 128:(d + 1) * 128]
                    if d == 0:
                        TTop(Dacc[:, gi, :], bk[:, 0:128], msk, ALU.mult, [bk.s, c3.s], [Dacc.s])
                    else:
                        tmpv = LA[:, gl, :] if g2 == 0 else LB[:, gl, :]
                        tmps = LA.s if g2 == 0 else LB.s
                        TTop(tmpv, bk[:, 0:128], msk, ALU.mult, [bk.s, c3.s, mnat[0].s, mnat[1].s], [tmps])
                        TTop(Dacc[:, gi, :], Dacc[:, gi, :], tmpv, ALU.add, [tmps, Dacc.s], [Dacc.s])
                        g = 2 * gp + g2
                        STTop(DtV[:, g, :], ident_f, dsk[:, g:g + 1], Dacc[:, gi, :], ALU.mult, ALU.add,
                              [cst_f.s, dsk.s, Dacc.s], [DtS.s])

        old_w0 = [Bt.s, bb.s, craw.s, pw.s] + new_r2 + [XR.s, XI.s, A1.s, B2.s, C2.s, RC.s]
        ar["off"] = W0
        DH = aalloc([16, 2, 2, 64], BF16, "DH")
        W1 = ar["off"]
        tsc2 = [aalloc([4, 128], F32, "tscb%d" % i) for i in range(3)]
        tsi2 = aalloc([4, 128], F32, "tsib")
        TRt = aalloc([16, 64], F32, "TRt")
        TIt = aalloc([16, 64], F32, "TIt")
        U1 = aalloc([16, 64], F32, "U1")
        U2 = aalloc([16, 64], F32, "U2")
        pc = [aalloc([16], F32, "pc%d" % i) for i in range(6)]
        new_w0 = [DH.s, tsi2.s, TRt.s, TIt.s, U1.s, U2.s] + [t_.s for t_ in tsc2] + [p_.s for p_ in pc]
        P.dve(lambda e: e.memset(fence_t[:], 0.0), reads=[], writes=old_w0 + new_w0 + [fence_t.s])

        def cmul(dr, di, ar_, ai_, br_, bi_, t1, t2, rs, ws):
            TTop(t1, ar_, br_, ALU.mult, rs, ws)
            TTop(t2, ai_, bi_, ALU.mult, rs, ws)
            TTop(dr, t1, t2, ALU.subtract, rs, ws)
            TTop(t1, ar_, bi_, ALU.mult, rs, ws)
            TTop(t2, ai_, br_, ALU.mult, rs, ws)
            TTop(di, t1, t2, ALU.add, rs, ws)

        for d in range(2):
            tables(d, tsc2, tsi2)
            atr, ati, cr_, ci_, t1, t2 = (p_[:] for p_ in pc)
            ws = [p_.s for p_ in pc] + [sm[0].s, sm[1].s]
            rs = [tab.s, gat5.s, hin.s, hent.s, cst_f.s] + ws
            TTop(atr, tab[:, 0, :, 64], tab[:, 2, :, 64], ALU.mult, rs, ws)
            TTop(ati, tab[:, 1, :, 64], tab[:, 2, :, 64], ALU.mult, rs, ws)
            ranks = [0, 1, 2, 3] if d == 0 else [3, 2, 1, 0]
            CPY(cr_, hin[:, d, :, 0], rs, ws)
            CPY(ci_, hin[:, d, :, 1], rs, ws)
            TSop(hent[:, d, 0], cr_, cst_f[:, 384 + ranks[0]:385 + ranks[0]], None, ALU.mult, None, rs, [hent.s])
            TSop(hent[:, d, 1], ci_, cst_f[:, 384 + ranks[0]:385 + ranks[0]], None, ALU.mult, None, rs, [hent.s])
            for i in range(3):
                r = ranks[i]
                nr_, ni_ = sm[0][:], sm[1][:]
                cmul(nr_, ni_, atr, ati, cr_, ci_, t1, t2, rs, ws)
                TTop(cr_, nr_, gat5[:, r, d, 0, :], ALU.add, rs, ws)
                TTop(ci_, ni_, gat5[:, r, d, 1, :], ALU.add, rs, ws)
                rn = ranks[i + 1]
                STTop(hent[:, d, 0], cr_, cst_f[:, 384 + rn:385 + rn], hent[:, d, 0], ALU.mult, ALU.add, rs, [hent.s])
                STTop(hent[:, d, 1], ci_, cst_f[:, 384 + rn:385 + rn], hent[:, d, 1], ALU.mult, ALU.add, rs, [hent.s])
            rs = [tab.s, hent.s, TRt.s, TIt.s, U1.s, U2.s]
            TTop(TRt[:], tab[:, 0, :, 0:64], tab[:, 2, :, 0:64], ALU.mult, rs, [TRt.s])
            TTop(TIt[:], tab[:, 1, :, 0:64], tab[:, 2, :, 0:64], ALU.mult, rs, [TIt.s])
            her = hent[:, d, 0].unsqueeze(2).broadcast_to([128, 16, 64])
            hei = hent[:, d, 1].unsqueeze(2).broadcast_to([128, 16, 64])
            dhr = DH[:, :, d, 0, :] if d == 0 else DH[:, :, d, 0, ::-1]
            dhi = DH[:, :, d, 1, :] if d == 0 else DH[:, :, d, 1, ::-1]
            TTop(U1[:], TRt[:], her, ALU.mult, rs, [U1.s])
            TTop(U2[:], TIt[:], hei, ALU.mult, rs, [U2.s])
            TTop(dhr, U1[:], U2[:], ALU.subtract, rs, [DH.s])
            TTop(U1[:], TRt[:], hei, ALU.mult, rs, [U1.s])
            TTop(U2[:], TIt[:], her, ALU.mult, rs, [U2.s])
            TTop(dhi, U1[:], U2[:], ALU.add, rs, [DH.s])

        if _stop(6):
            return
        old_w1 = new_w0[1:]
        ar["off"] = W1
        YA = aalloc([NG, NCK], BF16, "YA")
        yT = aalloc([4, T], BF16, "yT5")
        gl_t = [aalloc([512], F32, "gl%d" % i) for i in range(4)]
        new_w1 = [YA.s, yT.s] + [g_.s for g_ in gl_t]
        P.dve(lambda e: e.memset(fence_t[:], 0.0), reads=[], writes=old_w1 + new_w1 + [fence_t.s])

        for g0 in range(0, NG, 4):
            bk = psum()
            for gi in range(4):
                g = g0 + gi
                gp, g2 = g // 2, g % 2
                cv, g8_, cs_ = Ct_(gp)
                cols = slice(gi * 128, (gi + 1) * 128)
                MM(bk[:, cols], DtV[:, g, :], UT[:, g, :], (gi == 0), False, [DtS.s, UT.s], [bk.s])
                for d in range(2):
                    for ri in range(2):
                        MM(bk[:, cols], cv[64 * g2:64 * g2 + 64, g8_, d, ri, :], Hb[64 * g2:64 * g2 + 64, d, ri, gp, :], False, False,
                           [cs_, Hb.s], [bk.s])
                for d in range(2):
                    for ri in range(2):
                        MM(bk[:, gi * 128 + 64:(gi + 1) * 128], cv[64 * g2:64 * g2 + 64, g8_, d, ri, :],
                           DH[64 * g2:64 * g2 + 64, gp, d, ri, :], False, (d == 1 and ri == 1), [cs_, DH.s], [bk.s])
            xs_, sq_, u_, sg_ = gl_t
            ACTop(xs_[:], bk[:], AF.Copy, [bk.s], [xs_.s])
            ACTop(sq_[:], bk[:], AF.Square, [bk.s], [sq_.s])
            TSop(sq_[:], sq_[:], 0.044715, 1.0, ALU.mult, ALU.add, [sq_.s], [sq_.s])
            TTop(u_[:], sq_[:], xs_[:], ALU.mult, [sq_.s, xs_.s], [u_.s])
            ACTop(sg_[:], u_[:], AF.Sigmoid, [u_.s], [sg_.s], scale=2.0 * math.sqrt(2.0 / math.pi))
            TTop(YA[:, g0:g0 + 4, :].rearrange("p g n -> p (g n)"), xs_[:], sg_[:], ALU.mult, [xs_.s, sg_.s], [YA.s])

        for ct in range(4):
            for t0 in range(0, 8, 4):
                bk = psum()
                for ti in range(4):
                    t_ = t0 + ti
                    for g8 in range(8):
                        g = ct * 8 + g8
                        MM(bk[:, ti * 128:(ti + 1) * 128], asel[:, t_, 112 - 16 * g8:240 - 16 * g8],
                           YA[:, g, :], (ti == 0 and g8 == 0), (g8 == 7), [asel.s, YA.s], [bk.s])
                ACTop(yT[:, ct, :].rearrange("p (n t) -> p t n", t=8)[:, t0:t0 + 4, :],
                      bk[:].rearrange("p (t n) -> p t n", n=128), AF.Copy, [bk.s], [yT.s])

        wgl, wgls = wload(s5_w_glu[l].rearrange("(k p) c -> p k c", p=128), 4, 512, slot=0)
        for c2 in range(4):
            for hf in range(2):
                bk = psum()
                for ct in range(4):
                    MM(bk[:], wgl[:, ct, c2 * 128:(c2 + 1) * 128], yT[:, ct, hf * 512:(hf + 1) * 512], (ct == 0), (ct == 3),
                       [wgls, yT.s], [bk.s])
                sgl = gl_t[(c2 * 2 + hf) % 2]
                ACTop(sgl[:], bk[:], AF.Sigmoid, [bk.s], [sgl.s])
                TTop(mixT[:, 4 + c2, hf * 512:(hf + 1) * 512], yT[:, c2, hf * 512:(hf + 1) * 512], sgl[:], ALU.mult,
                     [yT.s, sgl.s], [mixs[4 + c2][hf]])


    for l in range(DEPTH):
        compute_mod(l, 0)

        def mod_rest(l=l):
            compute_mod(l, 1, slots=[1, 2, 3, 1])
            compute_mod(l, 2, slots=[2, 3, 1, 2])
        norm_mod(l, 0, 0)
        if STAGE["s5"]:
            s5(l, after_u=mod_rest)
        else:
            mod_rest()
            for k in range(4, 8):
                for h in range(2):
                    P.dve(lambda e, k=k, h=h: e.memset(mixT[:, k, h * 512:(h + 1) * 512], 0.0), writes=[mixs[k][h]])
        if STAGE["hg"]:
            hgrn(l)
        else:
            for k in range(0, 4):
                for h in range(2):
                    P.dve(lambda e, k=k, h=h: e.memset(mixT[:, k, h * 512:(h + 1) * 512], 0.0), writes=[mixs[k][h]])
        residual_proj(w_out[l], l, 16, mixT, mixs, KD)
        norm_mod(l, 1, 24)
        ffn(l)

    nfin32 = sb([128, KD], F32, "nfin32")
    P.dve(lambda e: e.tensor_scalar(out=nfin32[:], in0=nfin[:], scalar1=32.0, scalar2=None, op0=ALU.mult),
          reads=[nfin.s], writes=[nfin32.s])
    aphase()
    ytok = [aalloc([D], F32, "ytok%d" % j) for j in range(4)]
    yT = aalloc([512], F32, "yT")
    afence()
    for h in range(2):
        rms_stats(h)
        for k in range(KD):
            P.dve(lambda e, k=k, h=h: e.scalar_tensor_tensor(
                out=yT[:], in0=xT[:, k, h * 512:(h + 1) * 512], scalar=nfin32[:, k:k + 1], in1=rstd[:],
                op0=ALU.mult, op1=ALU.mult), reads=[xs[k][h], nfin32.s, rstd.s], writes=[yT.s])
            bk = psum()
            for j in range(4):
                P.pe(lambda e, bk=bk, j=j: e.transpose(bk[:, j * 128:(j + 1) * 128], yT[:, j * 128:(j + 1) * 128], ident_f),
                     reads=[yT.s, cst_f.s], writes=[bk.s])
            for j in range(4):
                P.act(lambda e, bk=bk, j=j, k=k: e.activation(out=ytok[j][:, k * 128:(k + 1) * 128],
                                                               in_=bk[:, j * 128:(j + 1) * 128], func=AF.Copy),
                      reads=[bk.s], writes=[ytok[j].s])
        for j in range(4):
            tt = h * 4 + j
            P.dma("sp", lambda e, j=j, tt=tt: e.dma_start(out=y_out[tt * 128:(tt + 1) * 128, :], in_=ytok[j][:]),
                  reads=[ytok[j].s], writes=[])

    with nc.allow_non_contiguous_dma(reason="small strided parameter loads"):
        P.emit(st)
    st.close()
    return nc


def _consts(core):
    q = core % 4
    c = np.zeros((128, 512), np.float32)
    c[:, 0:128] = np.eye(128, dtype=np.float32)
    j = np.arange(128)[:, None]
    i = np.arange(128)[None, :]
    same = (j // CH) == (i // CH)
    c[:, 128:256] = (same & (j <= i)).astype(np.float32)
    c[:, 256:384] = (same & (j >= i)).astype(np.float32)
    c[:, 384 + q] = 1.0
    nf = D // 4
    p = np.arange(128, dtype=np.float32)
    for par in range(2):
        kf = par * 128 + p
        c[:, 388 + par] = (1.0 / (np.float32(10000.0) ** (kf / np.float32(nf)))).astype(np.float32)
    t = q * 512 + np.arange(512)
    c2 = np.zeros((128, 1024), np.float32)
    c2[:, 0:512] = (t // 64).astype(np.float32)[None, :]
    c2[:, 512:1024] = (t % 64).astype(np.float32)[None, :]
    c3 = np.zeros((128, 1024), np.float32)
    for p_ in range(128):
        c3[p_, 112 + p_ % 16] = 1.0
        c3[p_, 240 + p_ // 16] = 1.0
    for g4 in range(4):
        for cc in range(16):
            c3[(2 * g4) * 16 + cc, 480 + g4 * 16 + cc] = 1.0
            c3[(2 * g4 + 1) * 16 + cc, 544 + g4 * 16 + cc] = 1.0
    c3[:, 608:625] = np.arange(-8, 9, dtype=np.float32)[None, :]
    c3[:, 640:705] = (8.0 * np.arange(65, dtype=np.float32))[None, :]
    sI = (np.arange(128) // 16)[:, None]
    tI = (np.arange(128) // 16)[None, :]
    c3[:, 768:896] = (sI <= tI).astype(np.float32)
    c3[:, 896:1024] = (sI >= tI).astype(np.float32)
    return c, c2, c3


_NC_CACHE = {}


def kernel(**inp):
    inp = {k: np.asarray(v) for k, v in inp.items()}
    if "nc" not in _NC_CACHE:
        _NC_CACHE["nc"] = build_program()
    nc = _NC_CACHE["nc"]
    xp = inp["x_prompt"]
    xsm = inp["x_sample"]
    in_maps = []
    shared = {k: np.ascontiguousarray(inp[k], dtype=np.float32) for k in
              ("w_mod", "b_mod", "norm_mix", "norm_ffn", "norm_final", "w_in", "w_out", "w_gate", "w_up", "w_down",
               "hg_lb_logits", "hg_norm", "s5_lam_re", "s5_lam_im", "s5_log_dt", "s5_b_re", "s5_b_im", "s5_c_re", "s5_c_im",
               "s5_d", "s5_w_glu")}
    for core in range(8):
        b, q = core // 4, core % 4
        x = np.concatenate([xp[2 * core], xp[2 * core + 1], xsm[b, q * 512:(q + 1) * 512]], axis=0)
        cond = np.stack([inp["c_ctx"], inp["c"][b]], axis=0)
        c1, c2, c3 = _consts(core)
        m = dict(shared)
        m.update({"x": np.ascontiguousarray(x, dtype=np.float32), "cond": np.ascontiguousarray(cond, dtype=np.float32),
                  "st_hg": np.ascontiguousarray(inp["state_hgrn"][b], dtype=np.float32), "cst": c1, "cst2": c2, "cst3": c3,
                  "st_s5": np.ascontiguousarray(inp["state_s5"][b], dtype=np.float32)})
        in_maps.append(m)
    res = run_bass_kernel_spmd(nc, in_maps, core_ids=list(range(8)))
    outs = res.results
    _DBG["outs"] = outs
    y_prompt = np.zeros_like(xp)
    y_sample = np.zeros_like(xsm)
    ns_hg = np.zeros((16, DEPTH, 2, NH, DK, DK), np.float32)
    ns_s5 = np.zeros((16, DEPTH, 2, NG, SP, 2), np.float32)
    for core in range(8):
        b, q = core // 4, core % 4
        y = outs[core]["y"]
        y_prompt[2 * core] = y[0:256]
        y_prompt[2 * core + 1] = y[256:512]
        y_sample[b, q * 512:(q + 1) * 512] = y[512:1024]
        ns_hg[2 * core:2 * core + 2] = outs[core]["ns_hg"]
        ns_s5[2 * core:2 * core + 2] = outs[core]["ns_s5"]
    return (y_prompt, y_sample, ns_hg, ns_s5)
```

```python
import math
from contextlib import ExitStack

import numpy as np
import concourse.bass as bass
import concourse.mybir as mybir
from concourse.bass_utils import run_bass_kernel_spmd

F32 = mybir.dt.float32
BF16 = mybir.dt.bfloat16
AF = mybir.ActivationFunctionType
ALU = mybir.AluOpType

ENGS = ("pe", "act", "dve", "pool", "sp")
SAME_ENGINE_RAW_DIST = 2


class Slot:
    __slots__ = ("name", "w", "r", "al")

    def __init__(self, name):
        self.name = name
        self.w = None
        self.r = []
        self.al = [self]


def alias(*slots):
    grp = []
    for s in slots:
        for a in s.al:
            if a not in grp:
                grp.append(a)
    for s in grp:
        s.al = grp


class Op:
    __slots__ = ("eng", "fn", "deps", "raw", "dma", "idx", "milestone", "mcount", "dsem", "dval", "inc", "eidx")


class Prog:
    def __init__(self, nc, n_dma_sems=8, sync_same_engine=True):
        self.nc = nc
        self.ops = []
        self.n_dma_sems = n_dma_sems
        self.sync_same = sync_same_engine

    def add(self, eng, fn, reads=(), writes=(), dma=False, inc=16):
        op = Op()
        op.eng, op.fn, op.dma, op.inc = eng, fn, dma, inc
        op.deps = set()
        op.raw = set()
        op.milestone = False
        op.mcount = 0
        op.dsem = None
        op.dval = 0
        op.idx = len(self.ops)
        for s0 in reads:
            for s in s0.al:
                if s.w is not None:
                    op.deps.add(s.w)
                    op.raw.add(s.w)
        for s0 in writes:
            for s in s0.al:
                if s.w is not None:
                    op.deps.add(s.w)
                op.deps.update(s.r)
        for s in reads:
            s.r.append(op.idx)
        for s in writes:
            s.w = op.idx
            s.r = []
        op.deps.discard(op.idx)
        self.ops.append(op)
        return op

    def pe(self, fn, reads=(), writes=()):
        return self.add("pe", fn, reads, writes)

    def act(self, fn, reads=(), writes=()):
        return self.add("act", fn, reads, writes)

    def dve(self, fn, reads=(), writes=()):
        return self.add("dve", fn, reads, writes)

    def dma(self, eng, fn, reads=(), writes=(), inc=16):
        return self.add(eng, fn, reads, writes, dma=True, inc=inc)

    def emit(self, stack):
        nc = self.nc
        ops = self.ops
        ecount = {e: 0 for e in ENGS}
        for op in ops:
            op.eidx = ecount[op.eng]
            ecount[op.eng] += 1

        def needs_sync(op, dop):
            if dop.dma or op.dma or dop.eng != op.eng:
                return True
            if dop.eng == "pe" or not self.sync_same:
                return False
            return (dop.idx in op.raw) and (op.eidx - dop.eidx < SAME_ENGINE_RAW_DIST)

        self.needs_sync = needs_sync
        for op in ops:
            for d in op.deps:
                dop = ops[d]
                if dop.dma:
                    continue
                if not needs_sync(op, dop):
                    continue
                dop.milestone = True
        cnt = {e: 0 for e in ENGS}
        for op in ops:
            if not op.dma and op.milestone:
                cnt[op.eng] += 1
            op.mcount = cnt[op.eng]
        esem = {e: stack.enter_context(nc.semaphore("s_" + e)) for e in ENGS}
        dsems = {e: None for e in ENGS}
        dcount = {}
        dn = {e: 0 for e in ENGS}
        for op in ops:
            if op.dma:
                if dsems[op.eng] is None:
                    dsems[op.eng] = [stack.enter_context(nc.semaphore("d_%s_%d" % (op.eng, i)))
                                     for i in range(self.n_dma_sems)]
                k = dn[op.eng]
                dn[op.eng] += 1
                op.dsem = (op.eng, k % self.n_dma_sems)
                dcount[op.dsem] = dcount.get(op.dsem, 0) + op.inc
                op.dval = dcount[op.dsem]
        per = {e: [o for o in ops if o.eng == e] for e in ENGS}
        block = stack.enter_context(nc.Block())
        sync_same = self.sync_same

        def make(e):
            def body(eng):
                waited = {}

                def wait(key, sem, val):
                    if waited.get(key, 0) >= val:
                        return
                    waited[key] = val
                    eng.wait_ge(sem, val)

                for op in per[e]:
                    for d in sorted(op.deps):
                        dop = ops[d]
                        if dop.dma:
                            wait(("d",) + dop.dsem, dsems[dop.dsem[0]][dop.dsem[1]], dop.dval)
                        else:
                            if not self.needs_sync(op, dop):
                                continue
                            wait(("e", dop.eng), esem[dop.eng], dop.mcount)
                    if op.dma:
                        prev = op.dval - op.inc
                        if prev > 0:
                            wait(("d",) + op.dsem, dsems[op.dsem[0]][op.dsem[1]], prev)
                        ins = op.fn(eng)
                        ins.then_inc(dsems[op.dsem[0]][op.dsem[1]], op.inc)
                    else:
                        ins = op.fn(eng)
                        if op.milestone:
                            ins.then_inc(esem[e], 1)
                if dsems[e] is not None:
                    for i, s in enumerate(dsems[e]):
                        v = dcount.get((e, i), 0)
                        if v:
                            wait(("d", e, i), s, v)
            return body

        block.tensor(make("pe"))
        block.scalar(make("act"))
        block.vector(make("dve"))
        block.gpsimd(make("pool"))
        block.sync(make("sp"))


D = 1024
KD = 8
T = 1024
NT = 8
DEPTH = 2
HG_W = 512
NH = 4
DK = 128
S5_W = 512
NG = 32
SP = 64
IN_W = 3072
DFF = 2816
NF = 22
EPS = 1e-6
CH = 32
NCH = T // CH
SEGS = [(0, 8), (8, 16), (16, 32)]
WSLOT = 4096
N_WSLOT = 4
CCW = 1096
ARENA_B = 91 * 1024 // 2

I32 = mybir.dt.int32
STAGE = {"hg": True, "s5": True, "s5_stop": 99}
DEBUG = False
_DBG = {}


class TT:
    def __init__(self, t, slot):
        self.t = t
        self.s = slot

    def __getitem__(self, k):
        return self.t[k]


def build_program():
    nc = bass.Bass("TRN2", target_bir_lowering=False)
    st = ExitStack()
    P = Prog(nc)

    def din(name, shape):
        return nc.dram_tensor(name, list(shape), F32, kind="ExternalInput").ap()

    def dout(name, shape):
        return nc.dram_tensor(name, list(shape), F32, kind="ExternalOutput").ap()

    x_in = din("x", [T, D])
    cond_in = din("cond", [2, D])
    w_mod = din("w_mod", [DEPTH, D, 6 * D])
    b_mod = din("b_mod", [DEPTH, 6 * D])
    norm_mix = din("norm_mix", [DEPTH, D])
    norm_ffn = din("norm_ffn", [DEPTH, D])
    norm_final = din("norm_final", [D])
    w_in = din("w_in", [DEPTH, D, IN_W])
    w_out = din("w_out", [DEPTH, D, D])
    w_gate = din("w_gate", [DEPTH, D, DFF])
    w_up = din("w_up", [DEPTH, D, DFF])
    w_down = din("w_down", [DEPTH, DFF, D])
    hg_lb = din("hg_lb_logits", [2, DEPTH, HG_W])
    hg_norm = din("hg_norm", [DEPTH, DK])
    st_hg = din("st_hg", [DEPTH, 2, NH, DK, DK])
    cst = din("cst", [128, 512])
    cst2_in = din("cst2", [128, 1024])
    cst3_in = din("cst3", [128, 1024])
    s5_lam_re = din("s5_lam_re", [DEPTH, 2, NG, SP])
    s5_lam_im = din("s5_lam_im", [DEPTH, 2, NG, SP])
    s5_log_dt = din("s5_log_dt", [DEPTH, 2, NG])
    s5_b_re = din("s5_b_re", [DEPTH, 2, NG, SP, 16])
    s5_b_im = din("s5_b_im", [DEPTH, 2, NG, SP, 16])
    s5_c_re = din("s5_c_re", [DEPTH, 2, NG, 16, SP])
    s5_c_im = din("s5_c_im", [DEPTH, 2, NG, 16, SP])
    s5_d = din("s5_d", [DEPTH, NG, 16])
    s5_w_glu = din("s5_w_glu", [DEPTH, S5_W, S5_W])
    st_s5 = din("st_s5", [DEPTH, 2, NG, SP, 2])
    y_out = dout("y", [T, D])
    ns_hg = dout("ns_hg", [2, DEPTH, 2, NH, DK, DK])
    ns_s5 = dout("ns_s5", [2, DEPTH, 2, NG, SP, 2])
    cc5_in = [nc.dram_tensor("cc5_in%d" % i, [128, 64], F32, kind="Internal").ap() for i in range(DEPTH)]
    cc5_out = [nc.dram_tensor("cc5_out%d" % i, [4 * 128, 64], F32, kind="Internal").ap() for i in range(DEPTH)]
    cc_in = [nc.dram_tensor("cc_in%d" % i, [128, CCW], F32, kind="Internal").ap() for i in range(2 * DEPTH)]
    cc_out = [nc.dram_tensor("cc_out%d" % i, [4 * 128, CCW], F32, kind="Internal").ap() for i in range(2 * DEPTH)]

    _n = [0]
    dbg_list = []

    def dbg(name, ap, slot, shape, dtype=F32):
        if not DEBUG:
            return
        t = nc.dram_tensor("dbg_" + name, list(shape), dtype, kind="ExternalOutput").ap()
        P.dma("sp", lambda e: e.dma_start(out=t, in_=ap), reads=[slot], writes=[])

    def sb(shape, dtype, name=None):
        _n[0] += 1
        name = "sb_" + (name or "t%d" % _n[0])
        t = st.enter_context(nc.sbuf_tensor(name, list(shape), dtype))
        return TT(t, Slot(name))

    banks = [TT(st.enter_context(nc.psum_tensor("ps%d" % i, [128, 512], F32)), Slot("ps%d" % i)) for i in range(8)]
    _pb = [0]

    def psum():
        b = banks[_pb[0] % 6]
        _pb[0] += 1
        return b

    wslots = [sb([128, WSLOT], BF16, "wslot%d" % i) for i in range(N_WSLOT)]
    _ws = [0]

    def wload(src_ap, a, b, slot=None):
        if slot is None:
            w = wslots[_ws[0] % N_WSLOT]
            _ws[0] += 1
        else:
            w = wslots[slot]
        view = bass.AP(w.t, 0, [[WSLOT, 128], [b, a], [1, b]])
        P.dma("pool", lambda e, v=view, s=src_ap: e.dma_start(out=v, in_=s), writes=[w.s])
        return view, w.s

    arena_t = st.enter_context(nc.sbuf_tensor("arena", [128, ARENA_B], BF16))
    ar = {"off": 0, "live": []}

    def aalloc(free_shape, dtype, name):
        n = 1
        for v in free_shape:
            n *= v
        nb = n * (1 if dtype == BF16 else 2)
        nb = (nb + 1) // 2 * 2
        off = ar["off"]
        assert off + nb <= ARENA_B, ("arena overflow", name, off, nb)
        ar["off"] = off + nb
        v = arena_t[:, off:off + nb]
        if dtype != BF16:
            v = v.bitcast(dtype)
        if len(free_shape) > 1:
            names = "abcdefg"[:len(free_shape)]
            kw = {names[i]: free_shape[i] for i in range(1, len(free_shape))}
            v = v.rearrange("p (%s) -> p %s" % (" ".join(names), " ".join(names)), **kw)
        s = Slot(name)
        ar["live"].append(s)
        return TT(v, s)

    fence_t = sb([128, 2], F32, "fence")

    def aphase(new_names_hint=None):
        old = ar["live"]
        ar["live"] = []
        ar["off"] = 0
        ar["pending"] = old

    def afence():
        old = ar.get("pending", [])
        new = list(ar["live"])
        P.dve(lambda e: e.memset(fence_t[:], 0.0), reads=[], writes=old + new + [fence_t.s])
        ar["pending"] = []

    cst_f = sb([128, 512], F32, "cst_f")
    P.dma("sp", lambda e: e.dma_start(out=cst_f[:], in_=cst), writes=[cst_f.s])
    ident_f = cst_f[:, 0:128]
    ident_b = sb([128, 128], BF16, "ident_b")
    P.dve(lambda e: e.tensor_copy(out=ident_b[:], in_=cst_f[:, 0:128]), reads=[cst_f.s], writes=[ident_b.s])
    ones_b = sb([128, 128], BF16, "ones_b")
    P.dve(lambda e: e.memset(ones_b[:], 1.0), writes=[ones_b.s])
    zeros_f = sb([128, 128], F32, "zeros_f")
    P.dve(lambda e: e.memset(zeros_f[:], 0.0), writes=[zeros_f.s])
    epsc = sb([128, 2], F32, "epsc")
    P.dve(lambda e: e.memset(epsc[:, 0:1], float(D * EPS)), writes=[epsc.s])
    P.dve(lambda e: e.memset(epsc[:, 1:2], float(DK * EPS)), reads=[epsc.s], writes=[epsc.s])
    lnc = sb([128, 1], F32, "lnc")
    P.dve(lambda e: e.memset(lnc[:], float(math.log(DK ** -0.5))), writes=[lnc.s])

    xT = sb([128, KD, T], F32, "xT")
    xs = [[Slot("xT%d_%d" % (k, h)) for h in range(2)] for k in range(KD)]
    hT = sb([128, KD, T], BF16, "hT")
    hs = [[Slot("hT%d_%d" % (k, h)) for h in range(2)] for k in range(KD)]
    mixT = sb([128, KD, T], BF16, "mixT")
    mixs = [[Slot("mix%d_%d" % (k, h)) for h in range(2)] for k in range(KD)]

    def sin_turns(out, u, ki, kf, ap=lambda t: t[:]):
        P.dve(lambda e: e.tensor_copy(out=ap(ki), in_=ap(u)), reads=[u.s], writes=[ki.s])
        P.dve(lambda e: e.tensor_copy(out=ap(kf), in_=ap(ki)), reads=[ki.s], writes=[kf.s])
        P.dve(lambda e: e.tensor_tensor(out=ap(kf), in0=ap(u), in1=ap(kf), op=ALU.subtract), reads=[u.s, kf.s], writes=[kf.s])
        P.act(lambda e: e.activation(out=ap(out), in_=ap(kf), func=AF.Sin, scale=2 * math.pi), reads=[kf.s], writes=[out.s])

    aphase()
    xtok = [aalloc([D], F32, "xtok%d" % i) for i in range(NT)]
    posarg = aalloc([512], F32, "posarg")
    cst2 = aalloc([1024], F32, "cst2")
    posk_i = aalloc([512], I32, "posk_i")
    posk_f = aalloc([512], F32, "posk_f")
    afence()
    P.dma("sp", lambda e: e.dma_start(out=cst2[:], in_=cst2_in), writes=[cst2.s])
    for tt in range(NT):
        P.dma("sp", lambda e, tt=tt: e.dma_start(out=xtok[tt][:], in_=x_in[tt * 128:(tt + 1) * 128, :]),
              writes=[xtok[tt].s])
    for k in range(KD):
        for h in range(2):
            b = psum()
            for j in range(4):
                tt = h * 4 + j
                P.pe(lambda e, b=b, j=j, tt=tt, k=k: e.transpose(b[:, j * 128:(j + 1) * 128],
                                                                   xtok[tt][:, k * 128:(k + 1) * 128], ident_f),
                     reads=[xtok[tt].s, cst_f.s], writes=[b.s])
            P.act(lambda e, b=b, k=k, h=h: e.activation(out=xT[:, k, h * 512:(h + 1) * 512], in_=b[:], func=AF.Copy),
                  reads=[b.s], writes=[xs[k][h]])

    posi = aalloc_late = None
    for k in range(KD):
        blk = k // 2
        pos_src = cst2[:, 0:512] if blk < 2 else cst2[:, 512:1024]
        om = cst_f[:, 388 + (k % 2):389 + (k % 2)]
        P.dve(lambda e, pos_src=pos_src, om=om: e.tensor_scalar(
            out=posarg[:], in0=pos_src, scalar1=om, scalar2=1.0 / (2 * math.pi), op0=ALU.mult, op1=ALU.mult),
            reads=[cst2.s, cst_f.s], writes=[posarg.s])
        if blk % 2 == 1:
            P.dve(lambda e: e.tensor_scalar(out=posarg[:], in0=posarg[:], scalar1=0.25, scalar2=None, op0=ALU.add),
                  reads=[posarg.s], writes=[posarg.s])
        sin_turns(posarg, posarg, posk_i, posk_f)
        P.dve(lambda e, k=k: e.tensor_tensor(out=xT[:, k, 512:1024], in0=xT[:, k, 512:1024], in1=posarg[:], op=ALU.add),
              reads=[posarg.s, xs[k][1]], writes=[xs[k][1]])

    condf = sb([128, KD, 2], F32, "condf")
    condb = sb([128, KD, 2], BF16, "condb")
    for j in range(2):
        P.dma("sp", lambda e, j=j: e.dma_start(out=condf[:, :, j], in_=cond_in[j].rearrange("(k p) -> p k", p=128)),
              writes=[condf.s])
    P.act(lambda e: e.activation(out=condb[:], in_=condf[:], func=AF.Silu), reads=[condf.s], writes=[condb.s])

    nmix = sb([128, DEPTH, KD], F32, "nmix")
    nffn = sb([128, DEPTH, KD], F32, "nffn")
    nfin = sb([128, KD], F32, "nfin")
    bmod = sb([128, DEPTH, 48], F32, "bmod")
    P.dma("sp", lambda e: e.dma_start(out=nmix[:], in_=norm_mix.rearrange("l (k p) -> p l k", p=128)), writes=[nmix.s])
    P.dma("sp", lambda e: e.dma_start(out=nffn[:], in_=norm_ffn.rearrange("l (k p) -> p l k", p=128)), writes=[nffn.s])
    P.dma("sp", lambda e: e.dma_start(out=nfin[:], in_=norm_final.rearrange("(k p) -> p k", p=128)), writes=[nfin.s])
    P.dma("sp", lambda e: e.dma_start(out=bmod[:], in_=b_mod.rearrange("l (k p) -> p l k", p=128)), writes=[bmod.s])

    lbl = sb([128, 2, DEPTH, NH], F32, "lbl")
    for d in range(2):
        for l in range(DEPTH):
            P.dma("sp", lambda e, d=d, l=l: e.dma_start(out=lbl[:, d, l, :], in_=hg_lb[d, l].rearrange("(h p) -> p h", p=128)),
                  writes=[lbl.s])
    lb = sb([128, DEPTH, 2, NH], F32, "lb")
    oml = sb([128, DEPTH, 2, NH], F32, "oml")
    noml = sb([128, DEPTH, 2, NH], F32, "noml")
    P.dve(lambda e: e.memset(lb[:], 0.0), writes=[lb.s])
    P.dve(lambda e: e.tensor_tensor(out=lb[:, 1], in0=lbl[:, :, 1, :], in1=lbl[:, :, 0, :], op=ALU.subtract),
          reads=[lbl.s, lb.s], writes=[lb.s])
    P.act(lambda e: e.activation(out=lb[:, 1], in_=lb[:, 1], func=AF.Sigmoid), reads=[lb.s], writes=[lb.s])
    P.dve(lambda e: e.tensor_scalar(out=oml[:], in0=lb[:], scalar1=-1.0, scalar2=1.0, op0=ALU.mult, op1=ALU.add),
          reads=[lb.s], writes=[oml.s])
    P.dve(lambda e: e.tensor_scalar(out=noml[:], in0=lb[:], scalar1=1.0, scalar2=-1.0, op0=ALU.mult, op1=ALU.add),
          reads=[lb.s], writes=[noml.s])
    gn = sb([128, DEPTH], F32, "gn")
    P.dma("sp", lambda e: e.dma_start(out=gn[:], in_=hg_norm.rearrange("l p -> p l")), writes=[gn.s])
    P.dve(lambda e: e.tensor_scalar(out=gn[:], in0=gn[:], scalar1=float(math.sqrt(DK)), scalar2=None, op0=ALU.mult),
          reads=[gn.s], writes=[gn.s])

    par_all = sb([128, DEPTH, 2, 5, 16], F32, "par_all")
    for l_ in range(DEPTH):
        for d_ in range(2):
            P.dma("sp", lambda e, l_=l_, d_=d_: e.dma_start(
                out=par_all[:, l_, d_, 0, :], in_=s5_lam_re[l_, d_].rearrange("(gp g2) p -> (g2 p) gp", g2=2)), writes=[par_all.s])
            P.dma("sp", lambda e, l_=l_, d_=d_: e.dma_start(
                out=par_all[:, l_, d_, 1, :], in_=s5_lam_im[l_, d_].rearrange("(gp g2) p -> (g2 p) gp", g2=2)), writes=[par_all.s])
            for g2_ in range(2):
                P.dma("sp", lambda e, l_=l_, d_=d_, g2_=g2_: e.dma_start(
                    out=par_all[64 * g2_:64 * g2_ + 64, l_, d_, 2, :],
                    in_=s5_log_dt[l_, d_].rearrange("(gp g2) -> g2 gp", g2=2)[g2_].partition_broadcast(64)), writes=[par_all.s])
    mod = sb([128, DEPTH, 48, 2], F32, "mod")
    coef = sb([128, DEPTH, 2, KD, 2], F32, "coef")

    mod_s = [[Slot("mod%d_%d" % (l_, p_)) for p_ in range(3)] for l_ in range(DEPTH)]
    coef_s = [[Slot("coef%d_%d" % (l_, n_)) for n_ in range(2)] for l_ in range(DEPTH)]

    def compute_mod(l, part, slots=None):
        bk = psum()
        for ci_, cc in enumerate(range(part * 4, part * 4 + 4)):
            wv, wsl = wload(w_mod[l][:, cc * 512:(cc + 1) * 512].rearrange("(k p) c -> p k c", p=128), KD, 512,
                            slot=None if slots is None else slots[ci_])
            for j in range(4):
                ft = cc * 4 + j - part * 16
                for k in range(KD):
                    P.pe(lambda e, wv=wv, j=j, k=k, ft=ft, bk=bk: e.matmul(
                        bk[:, ft * 2:ft * 2 + 2], wv[:, k, j * 128:(j + 1) * 128], condb[:, k, :],
                        start=(k == 0), stop=(k == KD - 1)),
                        reads=[wsl, condb.s], writes=[bk.s])
        t0_ = part * 16
        P.dve(lambda e, bk=bk: e.tensor_tensor(
            out=mod[:, l, t0_:t0_ + 16], in0=bk[:, 0:32].rearrange("p (f c) -> p f c", c=2),
            in1=bmod[:, l, t0_:t0_ + 16].unsqueeze(2).broadcast_to([128, 16, 2]), op=ALU.add),
            reads=[bk.s, bmod.s], writes=[mod_s[l][part]])
        for n, (gt, base, prt) in enumerate(((nmix, 8, 0), (nffn, 32, 2))):
            if prt != part:
                continue
            P.dve(lambda e, n=n, base=base: e.tensor_scalar(
                out=coef[:, l, n], in0=mod[:, l, base:base + 8, :], scalar1=1.0, scalar2=32.0,
                op0=ALU.add, op1=ALU.mult), reads=[mod_s[l][part]], writes=[coef_s[l][n]])
            P.dve(lambda e, n=n, gt=gt: e.tensor_tensor(
                out=coef[:, l, n], in0=coef[:, l, n], in1=gt[:, l].unsqueeze(2).broadcast_to([128, KD, 2]),
                op=ALU.mult), reads=[coef_s[l][n], gt.s], writes=[coef_s[l][n]])

    sq = [sb([128, 512], BF16, "sq%d" % i) for i in range(2)]
    rstd = sb([128, 512], F32, "rstd")
    tmpn = [sb([128, 512], F32, "tmpn%d" % i) for i in range(2)]

    def rms_stats(h):
        bk = psum()
        for k in range(KD):
            s = sq[k % 2]
            P.act(lambda e, k=k, h=h, s=s: e.activation(out=s[:], in_=xT[:, k, h * 512:(h + 1) * 512], func=AF.Square),
                  reads=[xs[k][h]], writes=[s.s])
            P.pe(lambda e, k=k, bk=bk, s=s: e.matmul(bk[:], ones_b[:], s[:], start=(k == 0), stop=(k == KD - 1)),
                 reads=[s.s, ones_b.s], writes=[bk.s])
        P.act(lambda e, bk=bk: e.activation(out=rstd[:], in_=bk[:], func=AF.Ln, bias=epsc[:, 0:1]),
              reads=[bk.s, epsc.s], writes=[rstd.s])
        P.act(lambda e: e.activation(out=rstd[:], in_=rstd[:], func=AF.Exp, scale=-0.5), reads=[rstd.s], writes=[rstd.s])

    def norm_mod(l, n, shift_base):
        for h in range(2):
            rms_stats(h)
            for k in range(KD):
                tm = tmpn[k % 2]
                P.dve(lambda e, k=k, h=h, tm=tm: e.tensor_tensor(out=tm[:], in0=xT[:, k, h * 512:(h + 1) * 512],
                                                                 in1=rstd[:], op=ALU.mult),
                      reads=[xs[k][h], rstd.s], writes=[tm.s])
                P.act(lambda e, k=k, h=h, l=l, n=n, tm=tm: e.activation(
                    out=hT[:, k, h * 512:(h + 1) * 512], in_=tm[:], func=AF.Identity,
                    bias=mod[:, l, shift_base + k, h:h + 1], scale=coef[:, l, n, k, h:h + 1]),
                    reads=[tm.s, mod_s[l][shift_base // 16], coef_s[l][n]], writes=[hs[k][h]])

    def residual_proj(wdram, l, gate_base, srcT, src_slots, nk):
        per = WSLOT // 256
        for c4 in range(4):
            wv = []
            for k0 in range(0, nk, per):
                kk = min(per, nk - k0)
                v, s = wload(wdram[k0 * 128:(k0 + kk) * 128, c4 * 256:(c4 + 1) * 256].rearrange("(k p) c -> p k c", p=128),
                             kk, 256)
                wv.append((k0, kk, v, s))
            for j in range(2):
                dt_ = c4 * 2 + j
                for h in range(2):
                    bk = psum()
                    for (k0, kk, v, s) in wv:
                        for k in range(kk):
                            kg = k0 + k
                            P.pe(lambda e, v=v, k=k, j=j, kg=kg, h=h, bk=bk: e.matmul(
                                bk[:], v[:, k, j * 128:(j + 1) * 128], srcT[:, kg, h * 512:(h + 1) * 512],
                                start=(kg == 0), stop=(kg == nk - 1)),
                                reads=[s, src_slots[kg][h]], writes=[bk.s])
                    P.dve(lambda e, bk=bk, dt_=dt_, h=h, l=l: e.scalar_tensor_tensor(
                        out=xT[:, dt_, h * 512:(h + 1) * 512], in0=bk[:], scalar=mod[:, l, gate_base + dt_, h:h + 1],
                        in1=xT[:, dt_, h * 512:(h + 1) * 512], op0=ALU.mult, op1=ALU.add),
                        reads=[bk.s, mod_s[l][gate_base // 16], xs[dt_][h]], writes=[xs[dt_][h]])


    def ffn(l):
        aphase()
        h1 = aalloc([NF, T], BF16, "h1")
        sgate = [aalloc([512], F32, "sgate%d" % i) for i in range(2)]
        h1s = [[Slot("h1_%d_%d" % (f, h)) for h in range(2)] for f in range(NF)]
        ar["live"].extend([s for row in h1s for s in row])
        afence()
        it = 0
        for c in range(6):
            ncol = 512 if c < 5 else 256
            vg, sg_ = wload(w_gate[l][:, c * 512:c * 512 + ncol].rearrange("(k p) c -> p k c", p=128), KD, ncol)
            vu, su_ = wload(w_up[l][:, c * 512:c * 512 + ncol].rearrange("(k p) c -> p k c", p=128), KD, ncol)
            for j in range(ncol // 128):
                f = c * 4 + j
                for h in range(2):
                    bg = psum()
                    bu = psum()
                    for k in range(KD):
                        P.pe(lambda e, vg=vg, k=k, j=j, h=h, bg=bg: e.matmul(
                            bg[:], vg[:, k, j * 128:(j + 1) * 128], hT[:, k, h * 512:(h + 1) * 512],
                            start=(k == 0), stop=(k == KD - 1)), reads=[sg_, hs[k][h]], writes=[bg.s])
                    for k in range(KD):
                        P.pe(lambda e, vu=vu, k=k, j=j, h=h, bu=bu: e.matmul(
                            bu[:], vu[:, k, j * 128:(j + 1) * 128], hT[:, k, h * 512:(h + 1) * 512],
                            start=(k == 0), stop=(k == KD - 1)), reads=[su_, hs[k][h]], writes=[bu.s])
                    sgt = sgate[it % 2]
                    it += 1
                    P.act(lambda e, bg=bg, sgt=sgt: e.activation(out=sgt[:], in_=bg[:], func=AF.Silu),
                          reads=[bg.s], writes=[sgt.s])
                    P.dve(lambda e, bu=bu, f=f, h=h, sgt=sgt: e.tensor_tensor(
                        out=h1[:, f, h * 512:(h + 1) * 512], in0=sgt[:], in1=bu[:], op=ALU.mult),
                        reads=[sgt.s, bu.s], writes=[h1s[f][h]])
        residual_proj(w_down[l], l, 40, h1, h1s, NF)

    def proj_feat(wv, wsl, c0, h):
        bk = psum()
        for k in range(KD):
            P.pe(lambda e, k=k, bk=bk: e.matmul(bk[:], wv[:, k, c0:c0 + 128], hT[:, k, h * 512:(h + 1) * 512],
                                                start=(k == 0), stop=(k == KD - 1)),
                 reads=[wsl, hs[k][h]], writes=[bk.s])
        return bk

    def hgrn(l):
        aphase()
        qk = [[[aalloc([T], BF16, "qk%d%d%d" % (h, d, w)) for w in range(2)] for d in range(2)] for h in range(NH)]
        kendT = [[aalloc([NT, DK], BF16, "kendT%d%d" % (h, d)) for d in range(2)] for h in range(NH)]
        V = [aalloc([HG_W], BF16, "V%d" % tt) for tt in range(NT)]
        gch = aalloc([NH * 2, NCH], F32, "gch")
        gch_s = [[Slot("gch%d%d" % (h, d)) for d in range(2)] for h in range(NH)]
        ar["live"].extend([s for row in gch_s for s in row])
        R1 = ar["off"]
        qs = [aalloc([512], F32, "qs%d" % hf) for hf in range(2)]
        rmask = aalloc([512], F32, "rmask")
        tmp = [[aalloc([512], F32, "gt%d_%d" % (i, j)) for j in range(5)] for i in range(2)]
        kend_t = [aalloc([512], BF16, "kend%d" % i) for i in range(2)]
        R1_end = ar["off"]
        afence()
        P.dve(lambda e: e.memset(rmask[:], 1.0), writes=[rmask.s])
        P.dve(lambda e: e.memset(rmask[:, 0:512:CH], 0.0), reads=[rmask.s], writes=[rmask.s])

        wv_iv, ws_iv = None, None

        def load_in(c):
            return wload(w_in[l][:, c * 512:(c + 1) * 512].rearrange("(k p) c -> p k c", p=128), KD, 512)

        wq, wqs = load_in(0)
        wf = [None, None]
        wf[0] = load_in(1)
        wf[1] = load_in(2)
        def gate_chain(h, d, hf, tset, ke):
            t_sig, t_a, t_b, t_c, t_e = tset
            bk = proj_feat(wf[d][0], wf[d][1], h * 128, hf)
            lb_ = lb[:, l, d, h:h + 1]
            oml_ = oml[:, l, d, h:h + 1]
            noml_ = noml[:, l, d, h:h + 1]
            P.act(lambda e, bk=bk, t_sig=t_sig: e.activation(out=t_sig[:], in_=bk[:], func=AF.Sigmoid),
                  reads=[bk.s], writes=[t_sig.s])
            yield
            P.dve(lambda e, t_sig=t_sig, t_a=t_a, lb_=lb_, oml_=oml_: e.tensor_scalar(
                out=t_a[:], in0=t_sig[:], scalar1=oml_, scalar2=lb_, op0=ALU.mult, op1=ALU.add),
                reads=[t_sig.s, lb.s, oml.s], writes=[t_a.s])
            yield
            P.act(lambda e, t_a=t_a: e.activation(out=t_a[:], in_=t_a[:], func=AF.Ln), reads=[t_a.s], writes=[t_a.s])
            yield
            P.dve(lambda e, t_a=t_a, t_b=t_b: e.tensor_tensor_scan(
                out=t_b[:], data0=rmask[:], data1=t_a[:], initial=0.0, op0=ALU.mult, op1=ALU.add),
                reads=[t_a.s, rmask.s], writes=[t_b.s])
            yield
            P.dve(lambda e, t_sig=t_sig, noml_=noml_, oml_=oml_: e.tensor_scalar(
                out=t_sig[:], in0=t_sig[:], scalar1=noml_, scalar2=oml_, op0=ALU.mult, op1=ALU.add),
                reads=[t_sig.s, noml.s, oml.s], writes=[t_sig.s])
            yield
            P.act(lambda e, t_b=t_b, h=h, d=d, hf=hf: e.activation(
                out=gch[:, h * 2 + d, hf * (512 // CH):(hf + 1) * (512 // CH)], in_=t_b[:, CH - 1:512:CH], func=AF.Exp),
                reads=[t_b.s], writes=[gch_s[h][d]])
            tb3 = t_b[:].rearrange("p (c j) -> p c j", j=CH)
            tc3 = t_c[:].rearrange("p (c j) -> p c j", j=CH)
            ta3 = t_a[:].rearrange("p (c j) -> p c j", j=CH)
            tot_b = tb3[:, :, CH - 1:CH].broadcast_to([128, 512 // CH, CH])
            if d == 0:
                P.dve(lambda e, tc3=tc3, tb3=tb3, tot_b=tot_b: e.tensor_tensor(
                    out=tc3, in0=tb3, in1=tot_b, op=ALU.subtract), reads=[t_b.s], writes=[t_c.s])
            else:
                P.dve(lambda e, tc3=tc3, ta3=ta3, tb3=tb3: e.tensor_tensor(
                    out=tc3, in0=ta3, in1=tb3, op=ALU.subtract), reads=[t_a.s, t_b.s], writes=[t_c.s])
                P.dve(lambda e, tc3=tc3, tb3=tb3, tot_b=tot_b, ta3=ta3: e.tensor_tensor(
                    out=ta3, in0=tc3, in1=tot_b, op=ALU.add), reads=[t_c.s, t_b.s], writes=[t_a.s])
            beta = t_b if d == 0 else t_a
            yield
            P.act(lambda e, beta=beta, t_e=t_e: e.activation(out=t_e[:], in_=beta[:], func=AF.Exp, bias=lnc[:, 0:1]),
                  reads=[beta.s, lnc.s], writes=[t_e.s])
            yield
            P.dve(lambda e, t_e=t_e, h=h, d=d, hf=hf: e.tensor_tensor(
                out=qk[h][d][0][:, hf * 512:(hf + 1) * 512], in0=qs[hf][:], in1=t_e[:], op=ALU.mult),
                reads=[t_e.s, qs[hf].s], writes=[qk[h][d][0].s])
            yield
            P.dve(lambda e, beta=beta, t_e=t_e: e.tensor_scalar(out=t_e[:], in0=beta[:], scalar1=-75.0, scalar2=None, op0=ALU.max),
                  reads=[beta.s], writes=[t_e.s])
            yield
            P.act(lambda e, t_e=t_e: e.activation(out=t_e[:], in_=t_e[:], func=AF.Exp, scale=-1.0),
                  reads=[t_e.s], writes=[t_e.s])
            yield
            P.dve(lambda e, t_e=t_e, t_sig=t_sig, h=h, d=d, hf=hf: e.tensor_tensor(
                out=qk[h][d][1][:, hf * 512:(hf + 1) * 512], in0=t_sig[:], in1=t_e[:], op=ALU.mult),
                reads=[t_e.s, t_sig.s], writes=[qk[h][d][1].s])
            yield
            P.act(lambda e, t_c=t_c, t_e=t_e: e.activation(out=t_e[:], in_=t_c[:], func=AF.Exp, scale=-1.0),
                  reads=[t_c.s], writes=[t_e.s])
            yield
            P.dve(lambda e, t_e=t_e, t_sig=t_sig, ke=ke: e.tensor_tensor(
                out=ke[:], in0=t_sig[:], in1=t_e[:], op=ALU.mult), reads=[t_e.s, t_sig.s], writes=[ke.s])
            yield
            bk2 = psum()
            yield
            for j in range(4):
                P.pe(lambda e, bk2=bk2, j=j, ke=ke: e.matmul(bk2[:, j * 128:(j + 1) * 128], ke[:, j * 128:(j + 1) * 128],
                                                             ident_b[:], start=True, stop=True),
                     reads=[ke.s, ident_b.s], writes=[bk2.s])
            yield
            P.act(lambda e, bk2=bk2, h=h, d=d, hf=hf: e.activation(
                out=kendT[h][d][:, hf * 4:(hf + 1) * 4, :], in_=bk2[:].rearrange("p (j k) -> p j k", k=128), func=AF.Copy),
                reads=[bk2.s], writes=[kendT[h][d].s])
            yield

        def interleave(gens):
            gens = list(gens)
            while gens:
                for g_ in list(gens):
                    try:
                        next(g_)
                    except StopIteration:
                        gens.remove(g_)

        for h in range(NH):
            for hf in range(2):
                bk = proj_feat(wq, wqs, h * 128, hf)
                P.act(lambda e, bk=bk, hf=hf: e.activation(out=qs[hf][:], in_=bk[:], func=AF.Silu),
                      reads=[bk.s], writes=[qs[hf].s])
            for d in range(2):
                interleave([gate_chain(h, d, 0, tmp[0], kend_t[0]), gate_chain(h, d, 1, tmp[1], kend_t[1])])
        wiv, wivs = load_in(3)
        for tt in range(NT):
            bk = psum()
            hf = tt // 4
            for k in range(KD):
                P.pe(lambda e, k=k, bk=bk, tt=tt: e.matmul(bk[:], hT[:, k, tt * 128:(tt + 1) * 128], wiv[:, k, :],
                                                            start=(k == 0), stop=(k == KD - 1)),
                     reads=[wivs, hs[k][hf]], writes=[bk.s])
            P.act(lambda e, bk=bk, tt=tt: e.activation(out=V[tt][:], in_=bk[:], func=AF.Copy), reads=[bk.s], writes=[V[tt].s])

        old_tmp = [t.s for grp in tmp for t in grp] + [k.s for k in kend_t] + [q.s for q in qs] + [rmask.s]
        ar["off"] = R1
        S = [aalloc([DK], F32, "S%d" % i) for i in range(3)]
        Sent = aalloc([NCH, DK], BF16, "Sent")
        o_t = aalloc([512], F32, "o_t")
        on_t = aalloc([512], F32, "on_t")
        sg_t = aalloc([512], F32, "sg_t")
        sq_t = aalloc([512], BF16, "sq_t")
        PT = [aalloc([128], BF16, "PT%d" % i) for i in range(2)]
        gat = aalloc([4, DK], F32, "gat")
        gatG = aalloc([4, 8], F32, "gatG")
        s0t = aalloc([DK], F32, "s0t")
        Pc = [aalloc([DK], F32, "Pc%d" % i) for i in range(2)]
        Sinit = aalloc([2 * NH, DK], F32, "Sinit")
        Sinit_s = [[Slot("Sinit%d%d" % (h, d)) for d in range(2)] for h in range(NH)]
        gtot = aalloc([NH * 2, NCH // 2], F32, "gtot")
        new2 = [t.s for t in S + PT + Pc] + [Sent.s, o_t.s, on_t.s, sg_t.s, sq_t.s, gat.s, gatG.s, s0t.s, Sinit.s, gtot.s] + \
               [s_ for row in Sinit_s for s_ in row]
        ar["live"].extend([s_ for row in Sinit_s for s_ in row])
        P.dve(lambda e: e.memset(fence_t[:], 0.0), reads=[], writes=old_tmp + new2 + [fence_t.s])

        CPT = 128 // CH

        def u_matmul(h, d, c):
            tt, p0 = c // CPT, (c % CPT) * CH
            bk = psum()
            P.pe(lambda e, bk=bk: e.matmul(bk[:, 0:128], kendT[h][d][p0:p0 + CH, tt, :],
                                           V[tt][p0:p0 + CH, h * 128:(h + 1) * 128],
                                           start=True, stop=True, tile_position=(p0, 0)),
                 reads=[kendT[h][d].s, V[tt].s], writes=[bk.s])
            return bk

        def scan_order(c0, c1, d):
            return list(range(c0, c1)) if d == 0 else list(range(c1 - 1, c0 - 1, -1))

        si = [0]
        SC0, SC1 = SEGS[2]

        ci = l * 2
        ccs_in, ccs_out = Slot("ccin"), Slot("ccout")
        def rec1_chain(h, d, Sx):
            hd = h * 2 + d
            order = scan_order(SC0, SC1, d)
            for i, c in enumerate(order):
                bk = u_matmul(h, d, c)
                yield
                if i == 0:
                    P.dve(lambda e, bk=bk, Sx=Sx: e.tensor_copy(out=Sx[:], in_=bk[:, 0:128]), reads=[bk.s], writes=[Sx.s])
                else:
                    P.dve(lambda e, bk=bk, Sx=Sx, c=c, hd=hd: e.scalar_tensor_tensor(
                        out=Sx[:], in0=Sx[:], scalar=gch[:, hd, c:c + 1], in1=bk[:, 0:128],
                        op0=ALU.mult, op1=ALU.add), reads=[bk.s, Sx.s, gch_s[h][d]], writes=[Sx.s])
                yield
            P.dma("sp", lambda e, Sx=Sx, hd=hd: e.dma_start(out=cc_in[ci][:, hd * 128:(hd + 1) * 128], in_=Sx[:]),
                  reads=[Sx.s], writes=[ccs_in])
            P.dve(lambda e, hd=hd: e.tensor_tensor_scan(
                out=gtot[:, hd, :], data0=gch[:, hd, SC0:SC1], data1=zeros_f[:, 0:SC1 - SC0], initial=1.0,
                op0=ALU.mult, op1=ALU.add), reads=[gch_s[h][d], zeros_f.s], writes=[gtot.s])
            yield

        for h in range(NH):
            interleave([rec1_chain(h, 0, S[0]), rec1_chain(h, 1, S[1])])
        P.dma("sp", lambda e: e.dma_start(out=cc_in[ci][:, 1024:1032], in_=gtot[:, :, SC1 - SC0 - 1]),
              reads=[gtot.s], writes=[ccs_in])
        P.dma("sp", lambda e: e.dma_start(out=cc_in[ci][:, 1032:CCW], in_=zeros_f[:, 0:CCW - 1032]),
              reads=[zeros_f.s], writes=[ccs_in])
        P.dma("pool", lambda e: e.collective_compute("AllGather", ALU.bypass, replica_groups=[[0, 1, 2, 3], [4, 5, 6, 7]],
                                                     ins=[cc_in[ci]], outs=[cc_out[ci]]),
              reads=[ccs_in], writes=[ccs_out], inc=1)
        ccv = cc_out[ci].rearrange("(r p) c -> p r c", p=128)
        P.dma("sp", lambda e: e.dma_start(out=gatG[:], in_=ccv[:, :, 1024:1032]), reads=[ccs_out], writes=[gatG.s])
        for h in range(NH):
            for d in range(2):
                hd = h * 2 + d
                col = slice(hd * 128, (hd + 1) * 128)
                P.dma("sp", lambda e, col=col: e.dma_start(out=gat[:], in_=ccv[:, :, col]), reads=[ccs_out], writes=[gat.s])
                P.dma("sp", lambda e, d=d, h=h: e.dma_start(out=s0t[:], in_=st_hg[l, d, h]), writes=[s0t.s])
                dst = Sinit[:, hd, :]
                ranks = [0, 1, 2, 3] if d == 0 else [3, 2, 1, 0]
                prev, prev_s = s0t[:], s0t.s
                P.dve(lambda e, dst=dst, prev=prev, r=ranks[0]: e.tensor_scalar(
                    out=dst, in0=prev, scalar1=cst_f[:, 384 + r:385 + r], scalar2=None, op0=ALU.mult),
                    reads=[prev_s, cst_f.s], writes=[Sinit_s[h][d]])
                for i in range(3):
                    r = ranks[i]
                    nxt = Pc[i % 2]
                    P.dve(lambda e, nxt=nxt, prev=prev, r=r, hd=hd: e.scalar_tensor_tensor(
                        out=nxt[:], in0=prev, scalar=gatG[:, r, hd:hd + 1], in1=gat[:, r, :],
                        op0=ALU.mult, op1=ALU.add), reads=[prev_s, gat.s, gatG.s], writes=[nxt.s])
                    rn = ranks[i + 1]
                    P.dve(lambda e, nxt=nxt, dst=dst, rn=rn: e.scalar_tensor_tensor(
                        out=dst, in0=nxt[:], scalar=cst_f[:, 384 + rn:385 + rn], in1=dst, op0=ALU.mult, op1=ALU.add),
                        reads=[nxt.s, cst_f.s, Sinit_s[h][d]], writes=[Sinit_s[h][d]])
                    prev, prev_s = nxt[:], nxt.s

        wg_v, wg_s = load_in(4)
        it2 = 0
        for h in range(NH):
            bo = [banks[6], banks[7]]
            for d in range(2):
                hd = h * 2 + d
                def seg_chain(h, d, c0, c1, Sx):
                    hd = h * 2 + d
                    order = scan_order(c0, c1, d)
                    is_sample = (c0 == SC0)
                    for i, c in enumerate(order):
                        if i == 0:
                            src = Sinit[:, hd, :] if is_sample else zeros_f[:]
                            src_s = Sinit_s[h][d] if is_sample else zeros_f.s
                        else:
                            src, src_s = Sx[:], Sx.s
                        P.act(lambda e, src=src, c=c: e.activation(out=Sent[:, c, :], in_=src, func=AF.Copy),
                              reads=[src_s], writes=[Sent.s])
                        yield
                        last = (i == len(order) - 1)
                        if last and is_sample:
                            continue
                        bk = u_matmul(h, d, c)
                        yield
                        if i == 0 and not is_sample:
                            P.dve(lambda e, bk=bk, Sx=Sx: e.tensor_copy(out=Sx[:], in_=bk[:, 0:128]), reads=[bk.s], writes=[Sx.s])
                        else:
                            P.dve(lambda e, bk=bk, Sx=Sx, src=src, c=c, hd=hd: e.scalar_tensor_tensor(
                                out=Sx[:], in0=src, scalar=gch[:, hd, c:c + 1], in1=bk[:, 0:128],
                                op0=ALU.mult, op1=ALU.add), reads=[bk.s, src_s, gch_s[h][d]], writes=[Sx.s])
                        yield
                    if not is_sample:
                        seq = 0 if c0 == 0 else 1
                        P.dma("sp", lambda e, Sx=Sx, seq=seq, d=d, h=h: e.dma_start(out=ns_hg[seq, l, d, h], in_=Sx[:]),
                              reads=[Sx.s], writes=[])
                    yield

                interleave([seg_chain(h, d, SEGS[i_][0], SEGS[i_][1], S[i_]) for i_ in range(3)])
                for hf in range(2):
                    for j in range(4):
                        tt = hf * 4 + j
                        tok = slice(tt * 128, (tt + 1) * 128)
                        bs = psum()
                        P.pe(lambda e, bs=bs, tok=tok, d=d, h=h: e.matmul(bs[:, 0:128], qk[h][d][1][:, tok], qk[h][d][0][:, tok],
                                                                    start=True, stop=True),
                             reads=[qk[h][d][0].s, qk[h][d][1].s], writes=[bs.s])
                        pt = PT[it2 % 2]
                        it2 += 1
                        P.dve(lambda e, bs=bs, pt=pt, d=d: e.tensor_tensor(
                            out=pt[:], in0=bs[:, 0:128], in1=cst_f[:, 128 + d * 128:256 + d * 128], op=ALU.mult),
                            reads=[bs.s, cst_f.s], writes=[pt.s])
                        oc = slice(j * 128, (j + 1) * 128)
                        P.pe(lambda e, pt=pt, tt=tt, oc=oc, hf=hf, d=d, j=j, h=h: e.matmul(
                            bo[hf][:, oc], V[tt][:, h * 128:(h + 1) * 128], pt[:], start=(d == 0 and j == 0), stop=False),
                            reads=[V[tt].s, pt.s], writes=[bo[hf].s])
                        for sub in range(CPT):
                            c = tt * CPT + sub
                            cs = slice(j * 128 + sub * CH, j * 128 + (sub + 1) * CH)
                            ts = slice(tt * 128 + sub * CH, tt * 128 + (sub + 1) * CH)
                            P.pe(lambda e, c=c, cs=cs, ts=ts, hf=hf, d=d, h=h: e.matmul(
                                bo[hf][:, cs], Sent[:, c, :], qk[h][d][0][:, ts], start=False, stop=(d == 1)),
                                reads=[Sent.s, qk[h][d][0].s], writes=[bo[hf].s])
            for hf in range(2):
                P.act(lambda e, hf=hf: e.activation(out=o_t[:], in_=bo[hf][:], func=AF.Copy), reads=[bo[hf].s], writes=[o_t.s])
                P.act(lambda e, hf=hf: e.activation(out=sq_t[:], in_=bo[hf][:], func=AF.Square), reads=[bo[hf].s], writes=[sq_t.s])
                if l == 0 and h == 0 and hf == 0:
                    dbg("o", o_t[:], o_t.s, [128, 512])
                    dbg("qf", qk[0][0][0][:], qk[0][0][0].s, [128, T], BF16)
                    dbg("kf", qk[0][0][1][:], qk[0][0][1].s, [128, T], BF16)
                    dbg("qb", qk[0][1][0][:], qk[0][1][0].s, [128, T], BF16)
                    dbg("kb", qk[0][1][1][:], qk[0][1][1].s, [128, T], BF16)
                br = psum()
                P.pe(lambda e, br=br: e.matmul(br[:], ones_b[:], sq_t[:], start=True, stop=True),
                     reads=[sq_t.s, ones_b.s], writes=[br.s])
                P.act(lambda e, br=br: e.activation(out=on_t[:], in_=br[:], func=AF.Ln, bias=epsc[:, 1:2]),
                      reads=[br.s, epsc.s], writes=[on_t.s])
                P.act(lambda e: e.activation(out=on_t[:], in_=on_t[:], func=AF.Exp, scale=-0.5), reads=[on_t.s], writes=[on_t.s])
                P.dve(lambda e: e.tensor_tensor(out=on_t[:], in0=o_t[:], in1=on_t[:], op=ALU.mult),
                      reads=[o_t.s, on_t.s], writes=[on_t.s])
                bg = proj_feat(wg_v, wg_s, h * 128, hf)
                P.act(lambda e, bg=bg: e.activation(out=sg_t[:], in_=bg[:], func=AF.Silu), reads=[bg.s], writes=[sg_t.s])
                P.dve(lambda e, hf=hf, h=h: e.scalar_tensor_tensor(
                    out=mixT[:, h, hf * 512:(hf + 1) * 512], in0=on_t[:], scalar=gn[:, l:l + 1], in1=sg_t[:],
                    op0=ALU.mult, op1=ALU.mult), reads=[on_t.s, sg_t.s, gn.s], writes=[mixs[h][hf]])

    def TTop(out, in0, in1, op, reads, writes):
        return P.dve(lambda e: e.tensor_tensor(out=out, in0=in0, in1=in1, op=op), reads=reads, writes=writes)

    def TSop(out, in0, s1, s2, op0, op1, reads, writes):
        if s2 is None:
            return P.dve(lambda e: e.tensor_scalar(out=out, in0=in0, scalar1=s1, scalar2=None, op0=op0), reads=reads, writes=writes)
        return P.dve(lambda e: e.tensor_scalar(out=out, in0=in0, scalar1=s1, scalar2=s2, op0=op0, op1=op1), reads=reads, writes=writes)

    def STTop(out, in0, scalar, in1, op0, op1, reads, writes):
        return P.dve(lambda e: e.scalar_tensor_tensor(out=out, in0=in0, scalar=scalar, in1=in1, op0=op0, op1=op1),
                     reads=reads, writes=writes)

    def ACTop(out, in_, func, reads, writes, bias=None, scale=None):
        kw = {}
        if bias is not None:
            kw["bias"] = bias
        if scale is not None:
            kw["scale"] = scale
        return P.act(lambda e: e.activation(out=out, in_=in_, func=func, **kw), reads=reads, writes=writes)

    def MM(out, lhsT, rhs, start, stop, reads, writes, tp=None):
        if tp is None:
            return P.pe(lambda e: e.matmul(out, lhsT, rhs, start=start, stop=stop), reads=reads, writes=writes)
        return P.pe(lambda e: e.matmul(out, lhsT, rhs, start=start, stop=stop, tile_position=tp), reads=reads, writes=writes)

    def CPY(out, in_, reads, writes):
        return P.dve(lambda e: e.tensor_copy(out=out, in_=in_), reads=reads, writes=writes)

    def MSET(out, val, reads, writes):
        return P.dve(lambda e: e.memset(out, val), reads=reads, writes=writes)

    def SDMA(out, in_, reads, writes):
        return P.dma("sp", lambda e: e.dma_start(out=out, in_=in_), reads=reads, writes=writes)

    NCK = 128
    SEG8 = [(0, 32), (32, 64), (64, 128)]
    TWO_PI = 2.0 * math.pi

    def s5(l, after_u=None):
        aphase()
        c3 = aalloc([1024], F32, "c3")
        asel = aalloc([8, 240], BF16, "asel")
        UT = aalloc([NG, NCK], BF16, "UT")
        Hb = aalloc([2, 2, 16, NCK], BF16, "Hb")
        par = TT(par_all[:, l], par_all.s)
        tab = aalloc([3, 16, 65], F32, "tab")
        hin = aalloc([2, 16, 2], F32, "hin")
        sloc = aalloc([2, 2, 16], F32, "sloc")
        hent = aalloc([2, 2, 16], F32, "hent")
        fst = aalloc([2, 2, 16, 2], F32, "fst")
        dsk = aalloc([NG], F32, "dsk")
        gat5 = aalloc([4, 2, 2, 16], F32, "gat5")
        sm = [aalloc([16], F32, "sm%d" % i) for i in range(8)]
        W0 = ar["off"]
        Bt = aalloc([NG, 2, 64], BF16, "Bt")
        bb = aalloc([2, 2, 16, 16], F32, "bb")
        craw = aalloc([2, 2, 16, 16], F32, "craw")
        pw = aalloc([2, 2, 16, 17], F32, "pw")
        R2 = ar["off"]
        uT = aalloc([4, T], BF16, "uT")
        cnat = aalloc([16, 64], F32, "cnat")
        prs = [aalloc([16, 17], F32, "prs%d" % i) for i in range(3)]
        pri = aalloc([16, 17], I32, "pri")
        afence()
        CtS = [wslots[1], wslots[2]]
        CtV = [bass.AP(w.t, 0, [[WSLOT, 128], [512, 8], [256, 2], [128, 2], [1, 128]]) for w in CtS]
        DtS = wslots[3]
        DtV = bass.AP(DtS.t, 0, [[WSLOT, 128], [128, NG], [1, 128]])

        def Ct_(gp):
            return CtV[gp // 8], gp % 8, CtS[gp // 8].s

        def _stop(k):
            if STAGE["s5_stop"] <= k:
                for kk in range(4, 8):
                    for hh in range(2):
                        MSET(mixT[:, kk, hh * 512:(hh + 1) * 512], 0.0, [], [mixs[kk][hh]])
                return True
            return False

        SDMA(c3[:], cst3_in, [], [c3.s])
        for g8 in range(8):
            TSop(asel[:, g8, :], c3[:, 0:240], c3[:, 240 + g8:241 + g8], None, ALU.mult, None, [c3.s], [asel.s])
        EV = c3[:, 608:625]
        K8 = c3[:, 640:705]
        R_even = c3[:, 480:544]
        R_odd = c3[:, 544:608]
        M5 = c3[:, 768:1024]
        wu, wus = wload(w_in[l][:, 2560:3072].rearrange("(k p) c -> p k c", p=128), KD, 512, slot=0)
        for ct in range(4):
            for hf in range(2):
                bk = proj_feat(wu, wus, ct * 128, hf)
                ACTop(uT[:, ct, hf * 512:(hf + 1) * 512], bk[:], AF.Copy, [bk.s], [uT.s])
        for g0 in range(0, NG, 4):
            bk = psum()
            for gi in range(4):
                g = g0 + gi
                ct, g8 = g // 8, g % 8
                for s_ in range(8):
                    MM(bk[:, gi * 128:(gi + 1) * 128], asel[:, g8, 112 - 16 * s_:240 - 16 * s_],
                       uT[:, ct, s_:T:8], (gi == 0 and s_ == 0), (s_ == 7), [asel.s, uT.s], [bk.s])
            ACTop(UT[:, g0:g0 + 4, :], bk[:].rearrange("p (g n) -> p g n", n=128), AF.Copy, [bk.s], [UT.s])
        if after_u is not None:
            after_u()

        for d in range(2):
            for ri, src in enumerate((s5_b_re, s5_b_im)):
                SDMA(bb[:, d, ri], src[l, d].rearrange("(gp g2) p c -> (g2 p) gp c", g2=2), [], [bb.s])
            SDMA(hin[:, d], bass.AP(st_s5.tensor, st_s5[l, d].offset, [[2, 128], [256, 16], [1, 2]]), [], [hin.s])
        for s_ in range(8):
            SDMA(dsk[16 * s_:16 * s_ + 16, :], s5_d[l].rearrange("g c -> c g"), [], [dsk.s])
        for d in range(2):
            for ri, src in enumerate((s5_c_re, s5_c_im)):
                x0 = (d * 2 + ri) * 4
                SDMA(cnat[:, x0:x0 + 4, :], src[l, d].rearrange("(ct g8) c p -> (g8 c) ct p", g8=8), [], [cnat.s])
        for d in range(2):
            for ri in range(2):
                bk = psum()
                for ct in range(4):
                    x = (d * 2 + ri) * 4 + ct
                    MM(bk[0:64, ct * 64:(ct + 1) * 64], cnat[:, x, :], R_even, True, True, [cnat.s, c3.s], [bk.s], tp=(0, 0))
                    MM(bk[64:128, ct * 64:(ct + 1) * 64], cnat[:, x, :], R_odd, True, True, [cnat.s, c3.s], [bk.s], tp=(0, 64))
                ACTop(craw[:, d, ri].rearrange("p a b -> p (a b)"), bk[:, 0:256], AF.Copy, [bk.s], [craw.s])

        if _stop(1):
            return
        for d in range(2):
            lr, li, dt_, a_, th_ = (par[:, d, i, :] for i in range(5))
            TSop(lr, lr, -1e-4, None, ALU.min, None, [par.s], [par.s])
            ACTop(dt_, dt_, AF.Exp, [par.s], [par.s])
            TTop(a_, lr, dt_, ALU.mult, [par.s], [par.s])
            TTop(th_, li, dt_, ALU.mult, [par.s], [par.s])
            TSop(th_, th_, 1.0 / TWO_PI, None, ALU.mult, None, [par.s], [par.s])

        def powers(out_r, out_i, out_m, a_ap, th_ap, evals, ng_, ne, tr, ti_, tk_i, tk_f, rs, ws):
            sh = [128, ng_, ne]
            ev_b = evals.unsqueeze(1).broadcast_to(sh)
            TTop(tr, th_ap.unsqueeze(2).broadcast_to(sh), ev_b, ALU.mult, rs + ws, ws)
            TSop(ti_, tr, 0.25, None, ALU.add, None, ws, ws)
            for (dst, src) in ((out_i, tr), (out_r, ti_)):
                CPY(tk_i, src, ws, ws)
                CPY(tk_f, tk_i, ws, ws)
                TTop(tk_f, src, tk_f, ALU.subtract, ws, ws)
                ACTop(dst, tk_f, AF.Sin, ws, ws, scale=TWO_PI)
            TTop(tr, a_ap.unsqueeze(2).broadcast_to(sh), ev_b, ALU.mult, rs + ws, ws)
            ACTop(out_m, tr, AF.Exp, ws, ws)

        for d in range(2):
            ws = [pw.s, pri.s] + [p_.s for p_ in prs]
            tkf_ = cnat[:].rearrange("p a b -> p (a b)")[:, 0:272].rearrange("p (a b) -> p a b", b=17)
            powers(pw[:, d, 0], pw[:, d, 1], prs[2][:], par[:, d, 3, :], par[:, d, 4, :], EV, 16, 17, prs[0][:], prs[1][:],
                   pri[:], tkf_, [par.s, c3.s, craw.s], ws + [cnat.s])
            TTop(pw[:, d, 0], pw[:, d, 0], prs[2][:], ALU.mult, ws, ws)
            TTop(pw[:, d, 1], pw[:, d, 1], prs[2][:], ALU.mult, ws, ws)

        for d in range(2):
            lr, li = par[:, d, 0, :], par[:, d, 1, :]
            abr, abi = pw[:, d, 0, :, 9], pw[:, d, 1, :, 9]
            nr, den, zr, zi, t1, t2 = (sm[i][:] for i in range(6))
            ws = [s_.s for s_ in sm]
            rs = [par.s, pw.s] + ws
            TSop(nr, abr, -1.0, None, ALU.add, None, rs, ws)
            TTop(t1, lr, lr, ALU.mult, rs, ws)
            TTop(t2, li, li, ALU.mult, rs, ws)
            TTop(den, t1, t2, ALU.add, rs, ws)
            P.dve(lambda e, den=den: e.reciprocal(out=den, in_=den), reads=rs, writes=ws)
            TTop(t1, nr, lr, ALU.mult, rs, ws)
            TTop(t2, abi, li, ALU.mult, rs, ws)
            TTop(zr, t1, t2, ALU.add, rs, ws)
            TTop(zr, zr, den, ALU.mult, rs, ws)
            TTop(t1, abi, lr, ALU.mult, rs, ws)
            TTop(t2, nr, li, ALU.mult, rs, ws)
            TTop(zi, t1, t2, ALU.subtract, rs, ws)
            TTop(zi, zi, den, ALU.mult, rs, ws)
            zrb = zr.unsqueeze(2).broadcast_to([128, 16, 16])
            zib = zi.unsqueeze(2).broadcast_to([128, 16, 16])
            cf = cnat[:].rearrange("p a b -> p (a b)")
            t3 = cf[:, 0:256].rearrange("p (a b) -> p a b", b=16)
            t4 = cf[:, 256:512].rearrange("p (a b) -> p a b", b=16)
            t5 = cf[:, 512:768].rearrange("p (a b) -> p a b", b=16)
            br_, bi_ = bb[:, d, 0], bb[:, d, 1]
            rs2 = rs + [bb.s, cnat.s, craw.s]
            ws2 = [bb.s, cnat.s]
            TTop(t3, br_, zrb, ALU.mult, rs2, ws2)
            TTop(t4, bi_, zib, ALU.mult, rs2, ws2)
            TTop(t5, br_, zib, ALU.mult, rs2, ws2)
            TTop(t3, t3, t4, ALU.subtract, rs2, ws2)
            TTop(t4, bi_, zrb, ALU.mult, rs2, ws2)
            TTop(bi_, t4, t5, ALU.add, rs2, ws2)
            CPY(br_, t3, rs2, ws2)

        if l == 0:
            dbg("par", par[:].rearrange("p a b c -> p (a b c)"), par.s, [128, 160])
            dbg("bb", bb[:].rearrange("p a b c d -> p (a b c d)"), bb.s, [128, 1024])
            dbg("pw", pw[:].rearrange("p a b c d -> p (a b c d)"), pw.s, [128, 1088])
            dbg("craw", craw[:].rearrange("p a b c d -> p (a b c d)"), craw.s, [128, 1024])
        def lifted(dst_r, dst_i, coef_r, coef_i, d, e_idx, conj_sign, gp0, ws):
            sh = [128, 4, 8, 16]
            pr = pw[:, d, 0, gp0:gp0 + 4, e_idx].unsqueeze(3).broadcast_to(sh)
            pi_ = pw[:, d, 1, gp0:gp0 + 4, e_idx].unsqueeze(3).broadcast_to(sh)
            cr = coef_r[:, gp0:gp0 + 4, :].unsqueeze(2).broadcast_to(sh)
            ci = coef_i[:, gp0:gp0 + 4, :].unsqueeze(2).broadcast_to(sh)
            t1 = LA[:].rearrange("p a (j c) -> p a j c", c=16)
            t2 = LB[:].rearrange("p a (j c) -> p a j c", c=16)
            rs = [pw.s, bb.s, craw.s, LA.s, LB.s]
            TTop(t1, cr, pr, ALU.mult, rs, [LA.s])
            TTop(t2, ci, pi_, ALU.mult, rs, [LB.s])
            TTop(dst_r.rearrange("p a (j c) -> p a j c", c=16), t1, t2, ALU.subtract, rs, ws)
            TTop(t1, cr, pi_, ALU.mult, rs, [LA.s])
            TTop(t2, ci, pr, ALU.mult, rs, [LB.s])
            if conj_sign > 0:
                TTop(dst_i.rearrange("p a (j c) -> p a j c", c=16), t1, t2, ALU.add, rs, ws)
            else:
                STTop(dst_i.rearrange("p a (j c) -> p a j c", c=16), t1, -1.0, t2, ALU.mult, ALU.subtract, rs, ws)

        E_B = [slice(15, 7, -1), slice(8, 16)]
        E_C = [slice(9, 17), slice(16, 8, -1)]
        E_N = [slice(7, None, -1), slice(0, 8)]

        if _stop(4):
            return
        old_r2 = [uT.s, cnat.s, pri.s] + [p_.s for p_ in prs]
        ar["off"] = R2
        XR = aalloc([4, NCK], F32, "XR")
        XI = aalloc([4, NCK], F32, "XI")
        A1 = aalloc([4, NCK], F32, "A1")
        B2 = aalloc([4, NCK], F32, "B2")
        C2 = aalloc([4, NCK], F32, "C2")
        RC = aalloc([4, NCK], F32, "RC")
        LA = aalloc([4, 128], F32, "LA2")
        LB = aalloc([4, 128], F32, "LB2")
        mnat = [aalloc([4, 128], BF16, "mnat2_%d" % i) for i in range(2)]
        new_r2 = [XR.s, XI.s, A1.s, B2.s, C2.s, RC.s, LA.s, LB.s, mnat[0].s, mnat[1].s]
        tsc = [A1, B2, C2]
        tsi = RC
        P.dve(lambda e: e.memset(fence_t[:], 0.0), reads=[], writes=old_r2 + new_r2 + [fence_t.s])

        def tables(d, tsc, tsi):
            ws = [tab.s, tsi.s] + [t_.s for t_ in tsc]
            for q in range(4):
                g_ = slice(q * 4, q * 4 + 4)
                powers(tab[:, 0, g_, :], tab[:, 1, g_, :], tab[:, 2, g_, :], par[:, d, 3, g_], par[:, d, 4, g_], K8, 4, 65,
                       tsc[0][:, :, 0:65], tsc[1][:, :, 0:65], tsi[:, :, 0:65].bitcast(I32), tsc[2][:, :, 0:65], [par.s, c3.s], ws)

        def seg_views(buf, gsl, n0, n1, d, shift):
            if d == 0:
                if shift == 0:
                    return buf[:, gsl, n0:n1]
                return buf[:, gsl, n0 + 1:n1] if shift > 0 else buf[:, gsl, n0:n1 - 1]
            lo = None if n0 == 0 else n0 - 1
            if shift == 0:
                return buf[:, gsl, n1 - 1:lo:-1]
            if shift > 0:
                return buf[:, gsl, n1 - 2:lo:-1]
            return buf[:, gsl, n1 - 1:n0:-1]

        for d in range(2):
            tables(d, tsc, tsi)
            if l == 0 and d == 0:
                dbg("tab", tab[:].rearrange("p a b c -> p (a b c)"), tab.s, [128, 3 * 16 * 65])
            if _stop(4.2):
                return
            for q in range(4):
                gp0 = q * 4
                tsl = slice(gp0, gp0 + 4)
                lifted(mnat[0][:], mnat[1][:], bb[:, d, 0], bb[:, d, 1], d, E_B[d], +1, gp0, [mnat[0].s, mnat[1].s])
                for ri in range(2):
                    bk = psum()
                    for gl in range(4):
                        MM(bk[:, gl * 128:(gl + 1) * 128], mnat[ri][:, gl, :], ident_b[:], True, True,
                           [mnat[ri].s, ident_b.s], [bk.s])
                    ACTop(Bt[:, 2 * gp0:2 * gp0 + 8, ri, :], bk[:].rearrange("p (g q) -> p g q", q=64), AF.Copy, [bk.s], [Bt.s])
                if _stop(4.3):
                    return
                for gl in range(4):
                    gp = gp0 + gl
                    bk = psum()
                    for g2 in range(2):
                        g = 2 * gp + g2
                        for ri in range(2):
                            MM(bk[64 * g2:64 * g2 + 64, ri * 128:(ri + 1) * 128], Bt[:, g, ri, :], UT[:, g, :], True, True,
                               [Bt.s, UT.s], [bk.s], tp=(0, 64 * g2))
                    ACTop(XR[:, gl, :], bk[:, 0:128], AF.Copy, [bk.s], [XR.s])
                    ACTop(XI[:, gl, :], bk[:, 128:256], AF.Copy, [bk.s], [XI.s])
                if l == 0 and d == 0:
                    dbg("XR%d" % q, XR[:].rearrange("p a b -> p (a b)"), XR.s, [128, 512])
                    dbg("Bt%d" % q, Bt[:, 2 * gp0:2 * gp0 + 8].rearrange("p a b c -> p (a b c)"), Bt.s, [128, 1024], BF16)
                    dbg("UT%d" % q, UT[:, 2 * gp0:2 * gp0 + 8].rearrange("p a b -> p (a b)"), UT.s, [128, 1024], BF16)
                if _stop(4.4):
                    return
                CPY(RC[:], tab[:, 2, tsl, 1:2].broadcast_to([128, 4, NCK]), [tab.s], [RC.s])
                for (n0, n1) in SEG8:
                    first = n0 if d == 0 else n1 - 1
                    MSET(RC[:, :, first:first + 1], 0.0, [RC.s], [RC.s])
                gsl = slice(0, 4)
                for (n0, n1) in SEG8:
                    L = n1 - n0
                    xr, xi = seg_views(XR, gsl, n0, n1, d, 0), seg_views(XI, gsl, n0, n1, d, 0)
                    a1, b2, c2 = seg_views(A1, gsl, n0, n1, d, 0), seg_views(B2, gsl, n0, n1, d, 0), seg_views(C2, gsl, n0, n1, d, 0)
                    cs_, sn_ = tab[:, 0, tsl, 1:L + 1], tab[:, 1, tsl, 1:L + 1]
                    rs = [XR.s, XI.s, tab.s, A1.s, B2.s, C2.s]
                    TTop(a1, xr, cs_, ALU.mult, rs, [A1.s])
                    TTop(c2, xi, sn_, ALU.mult, rs, [C2.s])
                    TTop(a1, a1, c2, ALU.add, rs, [A1.s])
                    TTop(b2, xi, cs_, ALU.mult, rs, [B2.s])
                    TTop(c2, xr, sn_, ALU.mult, rs, [C2.s])
                    TTop(b2, b2, c2, ALU.subtract, rs, [B2.s])

                if _stop(4.5):
                    return

                def fl(t_):
                    v = t_[:].rearrange("p a b -> p (a b)")
                    return v if d == 0 else v[:, ::-1]
                o_r, o_i, i_r, i_i, cf_ = fl(XR), fl(XI), fl(A1), fl(B2), fl(RC)
                P.dve(lambda e, o_r=o_r, i_r=i_r, cf_=cf_: e.tensor_tensor_scan(out=o_r, data0=cf_, data1=i_r, initial=0.0,
                                                                                  op0=ALU.mult, op1=ALU.add),
                      reads=[RC.s, A1.s], writes=[XR.s])
                P.dve(lambda e, o_i=o_i, i_i=i_i, cf_=cf_: e.tensor_tensor_scan(out=o_i, data0=cf_, data1=i_i, initial=0.0,
                                                                                  op0=ALU.mult, op1=ALU.add),
                      reads=[RC.s, B2.s], writes=[XI.s])
                if _stop(4.6):
                    return
                for si_, (n0, n1) in enumerate(SEG8):
                    L = n1 - n0
                    gr, gi_ = seg_views(XR, gsl, n0, n1, d, -1), seg_views(XI, gsl, n0, n1, d, -1)
                    a1, b2 = seg_views(A1, gsl, n0, n1, d, 1), seg_views(B2, gsl, n0, n1, d, 1)
                    hr = seg_views(Hb[:, d, 0], tsl, n0, n1, d, 1)
                    hi = seg_views(Hb[:, d, 1], tsl, n0, n1, d, 1)
                    cs_, sn_ = tab[:, 0, tsl, 1:L], tab[:, 1, tsl, 1:L]
                    rs = [XR.s, XI.s, tab.s, A1.s, B2.s]
                    TTop(a1, gr, cs_, ALU.mult, rs, [A1.s])
                    TTop(b2, gi_, sn_, ALU.mult, rs, [B2.s])
                    TTop(hr, a1, b2, ALU.subtract, rs, [Hb.s])
                    TTop(a1, gr, sn_, ALU.mult, rs, [A1.s])
                    TTop(b2, gi_, cs_, ALU.mult, rs, [B2.s])
                    TTop(hi, a1, b2, ALU.add, rs, [Hb.s])
                    first = n0 if d == 0 else n1 - 1
                    MSET(Hb[:, d, :, tsl, first:first + 1], 0.0, [Hb.s], [Hb.s])
                    last = n1 - 1 if d == 0 else n0
                    glr, gli = XR[:, :, last], XI[:, :, last]
                    cL, sL = tab[:, 0, tsl, L], tab[:, 1, tsl, L]
                    t1, t2 = sm[6][:, 0:4], sm[7][:, 0:4]
                    if si_ < 2:
                        dr, di = fst[:, si_, d, tsl, 0], fst[:, si_, d, tsl, 1]
                        dsl = fst.s
                    else:
                        dr, di = sloc[:, d, 0, tsl], sloc[:, d, 1, tsl]
                        dsl = sloc.s
                    rs = [XR.s, XI.s, tab.s, sm[6].s, sm[7].s, dsl]
                    TTop(t1, glr, cL, ALU.mult, rs, [sm[6].s])
                    TTop(t2, gli, sL, ALU.mult, rs, [sm[7].s])
                    TTop(dr, t1, t2, ALU.subtract, rs, [dsl])
                    TTop(t1, glr, sL, ALU.mult, rs, [sm[6].s])
                    TTop(t2, gli, cL, ALU.mult, rs, [sm[7].s])
                    TTop(di, t1, t2, ALU.add, rs, [dsl])
        if _stop(4.8):
            return
        for seq in range(2):
            for d in range(2):
                SDMA(bass.AP(ns_s5.tensor, ns_s5[seq, l, d].offset, [[2, 128], [256, 16], [1, 2]]), fst[:, seq, d], [fst.s], [])

        if _stop(5):
            return
        ccs_in, ccs_out = Slot("cc5in"), Slot("cc5out")
        SDMA(cc5_in[l], sloc[:].rearrange("p a b c -> p (a b c)"), [sloc.s], [ccs_in])
        P.dma("pool", lambda e: e.collective_compute("AllGather", ALU.bypass, replica_groups=[[0, 1, 2, 3], [4, 5, 6, 7]],
                                                     ins=[cc5_in[l]], outs=[cc5_out[l]]),
              reads=[ccs_in], writes=[ccs_out], inc=1)
        SDMA(gat5[:].rearrange("p r a b c -> p r (a b c)"), cc5_out[l].rearrange("(r p) c -> p r c", p=128), [ccs_out], [gat5.s])

        old_r2 = new_r2
        ar["off"] = R2
        LA = aalloc([4, 128], F32, "LA")
        LB = aalloc([4, 128], F32, "LB")
        mnat = [aalloc([4, 128], BF16, "mnat%d" % i) for i in range(2)]
        Dacc = aalloc([8, 128], F32, "Dacc")
        new_r2 = [LA.s, LB.s, mnat[0].s, mnat[1].s, Dacc.s]
        P.dve(lambda e: e.memset(fence_t[:], 0.0), reads=[], writes=old_r2 + new_r2 + [fence_t.s])

        for d in range(2):
            for q in range(4):
                gp0 = q * 4
                cv, g8_, cs_ = Ct_(gp0)
                lifted(cv[:, g8_:g8_ + 4, d, 0, :], cv[:, g8_:g8_ + 4, d, 1, :], craw[:, d, 0], craw[:, d, 1], d, E_C[d], -1, gp0, [cs_])
        for q in range(4):
            gp0 = q * 4
            for d in range(2):
                lifted(mnat[0][:], mnat[1][:], bb[:, d, 0], bb[:, d, 1], d, E_N[d], +1, gp0, [mnat[0].s, mnat[1].s])
                for gi in range(8):
                    gl, g2 = gi // 2, gi % 2
                    gp = gp0 + gl
                    cv, g8_, cs_ = Ct_(gp)
                    bk = psum()
                    for ri in range(2):
                        MM(bk[:, 0:128], mnat[ri][64 * g2:64 * g2 + 64, gl, :], cv[64 * g2:64 * g2 + 64, g8_, d, ri, :],
                           (ri == 0), (ri == 1), [mnat[ri].s, cs_], [bk.s])
                    msk = M5[:, d * 128:(d + 1) * 128]
                    if d == 0:
                        TTop(Dacc[:, gi, :], bk[:, 0:128], msk, ALU.mult, [bk.s, c3.s], [Dacc.s])
                    else:
                        tmpv = LA[:, gl, :] if g2 == 0 else LB[:, gl, :]
                        tmps = LA.s if g2 == 0 else LB.s
                        TTop(tmpv, bk[:, 0:128], msk, ALU.mult, [bk.s, c3.s, mnat[0].s, mnat[1].s], [tmps])
                        TTop(Dacc[:, gi, :], Dacc[:, gi, :], tmpv, ALU.add, [tmps, Dacc.s], [Dacc.s])
                        g = 2 * gp + g2
                        STTop(DtV[:, g, :], ident_f, dsk[:, g:g + 1], Dacc[:, gi, :], ALU.mult, ALU.add,
                              [cst_f.s, dsk.s, Dacc.s], [DtS.s])

        old_w0 = [Bt.s, bb.s, craw.s, pw.s] + new_r2 + [XR.s, XI.s, A1.s, B2.s, C2.s, RC.s]
        ar["off"] = W0
        DH = aalloc([16, 2, 2, 64], BF16, "DH")
        W1 = ar["off"]
        tsc2 = [aalloc([4, 128], F32, "tscb%d" % i) for i in range(3)]
        tsi2 = aalloc([4, 128], F32, "tsib")
        TRt = aalloc([16, 64], F32, "TRt")
        TIt = aalloc([16, 64], F32, "TIt")
        U1 = aalloc([16, 64], F32, "U1")
        U2 = aalloc([16, 64], F32, "U2")
        pc = [aalloc([16], F32, "pc%d" % i) for i in range(6)]
        new_w0 = [DH.s, tsi2.s, TRt.s, TIt.s, U1.s, U2.s] + [t_.s for t_ in tsc2] + [p_.s for p_ in pc]
        P.dve(lambda e: e.memset(fence_t[:], 0.0), reads=[], writes=old_w0 + new_w0 + [fence_t.s])

        def cmul(dr, di, ar_, ai_, br_, bi_, t1, t2, rs, ws):
            TTop(t1, ar_, br_, ALU.mult, rs, ws)
            TTop(t2, ai_, bi_, ALU.mult, rs, ws)
            TTop(dr, t1, t2, ALU.subtract, rs, ws)
            TTop(t1, ar_, bi_, ALU.mult, rs, ws)
            TTop(t2, ai_, br_, ALU.mult, rs, ws)
            TTop(di, t1, t2, ALU.add, rs, ws)

        for d in range(2):
            tables(d, tsc2, tsi2)
            atr, ati, cr_, ci_, t1, t2 = (p_[:] for p_ in pc)
            ws = [p_.s for p_ in pc] + [sm[0].s, sm[1].s]
            rs = [tab.s, gat5.s, hin.s, hent.s, cst_f.s] + ws
            TTop(atr, tab[:, 0, :, 64], tab[:, 2, :, 64], ALU.mult, rs, ws)
            TTop(ati, tab[:, 1, :, 64], tab[:, 2, :, 64], ALU.mult, rs, ws)
            ranks = [0, 1, 2, 3] if d == 0 else [3, 2, 1, 0]
            CPY(cr_, hin[:, d, :, 0], rs, ws)
            CPY(ci_, hin[:, d, :, 1], rs, ws)
            TSop(hent[:, d, 0], cr_, cst_f[:, 384 + ranks[0]:385 + ranks[0]], None, ALU.mult, None, rs, [hent.s])
            TSop(hent[:, d, 1], ci_, cst_f[:, 384 + ranks[0]:385 + ranks[0]], None, ALU.mult, None, rs, [hent.s])
            for i in range(3):
                r = ranks[i]
                nr_, ni_ = sm[0][:], sm[1][:]
                cmul(nr_, ni_, atr, ati, cr_, ci_, t1, t2, rs, ws)
                TTop(cr_, nr_, gat5[:, r, d, 0, :], ALU.add, rs, ws)
                TTop(ci_, ni_, gat5[:, r, d, 1, :], ALU.add, rs, ws)
                rn = ranks[i + 1]
                STTop(hent[:, d, 0], cr_, cst_f[:, 384 + rn:385 + rn], hent[:, d, 0], ALU.mult, ALU.add, rs, [hent.s])
                STTop(hent[:, d, 1], ci_, cst_f[:, 384 + rn:385 + rn], hent[:, d, 1], ALU.mult, ALU.add, rs, [hent.s])
            rs = [tab.s, hent.s, TRt.s, TIt.s, U1.s, U2.s]
            TTop(TRt[:], tab[:, 0, :, 0:64], tab[:, 2, :, 0:64], ALU.mult, rs, [TRt.s])
            TTop(TIt[:], tab[:, 1, :, 0:64], tab[:, 2, :, 0:64], ALU.mult, rs, [TIt.s])
            her = hent[:, d, 0].unsqueeze(2).broadcast_to([128, 16, 64])
            hei = hent[:, d, 1].unsqueeze(2).broadcast_to([128, 16, 64])
            dhr = DH[:, :, d, 0, :] if d == 0 else DH[:, :, d, 0, ::-1]
            dhi = DH[:, :, d, 1, :] if d == 0 else DH[:, :, d, 1, ::-1]
            TTop(U1[:], TRt[:], her, ALU.mult, rs, [U1.s])
            TTop(U2[:], TIt[:], hei, ALU.mult, rs, [U2.s])
            TTop(dhr, U1[:], U2[:], ALU.subtract, rs, [DH.s])
            TTop(U1[:], TRt[:], hei, ALU.mult, rs, [U1.s])
            TTop(U2[:], TIt[:], her, ALU.mult, rs, [U2.s])
            TTop(dhi, U1[:], U2[:], ALU.add, rs, [DH.s])

        if _stop(6):
            return
        old_w1 = new_w0[1:]
        ar["off"] = W1
        YA = aalloc([NG, NCK], BF16, "YA")
        yT = aalloc([4, T], BF16, "yT5")
        gl_t = [aalloc([512], F32, "gl%d" % i) for i in range(4)]
        new_w1 = [YA.s, yT.s] + [g_.s for g_ in gl_t]
        P.dve(lambda e: e.memset(fence_t[:], 0.0), reads=[], writes=old_w1 + new_w1 + [fence_t.s])

        for g0 in range(0, NG, 4):
            bk = psum()
            for gi in range(4):
                g = g0 + gi
                gp, g2 = g // 2, g % 2
                cv, g8_, cs_ = Ct_(gp)
                cols = slice(gi * 128, (gi + 1) * 128)
                MM(bk[:, cols], DtV[:, g, :], UT[:, g, :], (gi == 0), False, [DtS.s, UT.s], [bk.s])
                for d in range(2):
                    for ri in range(2):
                        MM(bk[:, cols], cv[64 * g2:64 * g2 + 64, g8_, d, ri, :], Hb[64 * g2:64 * g2 + 64, d, ri, gp, :], False, False,
                           [cs_, Hb.s], [bk.s])
                for d in range(2):
                    for ri in range(2):
                        MM(bk[:, gi * 128 + 64:(gi + 1) * 128], cv[64 * g2:64 * g2 + 64, g8_, d, ri, :],
                           DH[64 * g2:64 * g2 + 64, gp, d, ri, :], False, (d == 1 and ri == 1), [cs_, DH.s], [bk.s])
            xs_, sq_, u_, sg_ = gl_t
            ACTop(xs_[:], bk[:], AF.Copy, [bk.s], [xs_.s])
            ACTop(sq_[:], bk[:], AF.Square, [bk.s], [sq_.s])
            TSop(sq_[:], sq_[:], 0.044715, 1.0, ALU.mult, ALU.add, [sq_.s], [sq_.s])
            TTop(u_[:], sq_[:], xs_[:], ALU.mult, [sq_.s, xs_.s], [u_.s])
            ACTop(sg_[:], u_[:], AF.Sigmoid, [u_.s], [sg_.s], scale=2.0 * math.sqrt(2.0 / math.pi))
            TTop(YA[:, g0:g0 + 4, :].rearrange("p g n -> p (g n)"), xs_[:], sg_[:], ALU.mult, [xs_.s, sg_.s], [YA.s])

        for ct in range(4):
            for t0 in range(0, 8, 4):
                bk = psum()
                for ti in range(4):
                    t_ = t0 + ti
                    for g8 in range(8):
                        g = ct * 8 + g8
                        MM(bk[:, ti * 128:(ti + 1) * 128], asel[:, t_, 112 - 16 * g8:240 - 16 * g8],
                           YA[:, g, :], (ti == 0 and g8 == 0), (g8 == 7), [asel.s, YA.s], [bk.s])
                ACTop(yT[:, ct, :].rearrange("p (n t) -> p t n", t=8)[:, t0:t0 + 4, :],
                      bk[:].rearrange("p (t n) -> p t n", n=128), AF.Copy, [bk.s], [yT.s])

        wgl, wgls = wload(s5_w_glu[l].rearrange("(k p) c -> p k c", p=128), 4, 512, slot=0)
        for c2 in range(4):
            for hf in range(2):
                bk = psum()
                for ct in range(4):
                    MM(bk[:], wgl[:, ct, c2 * 128:(c2 + 1) * 128], yT[:, ct, hf * 512:(hf + 1) * 512], (ct == 0), (ct == 3),
                       [wgls, yT.s], [bk.s])
                sgl = gl_t[(c2 * 2 + hf) % 2]
                ACTop(sgl[:], bk[:], AF.Sigmoid, [bk.s], [sgl.s])
                TTop(mixT[:, 4 + c2, hf * 512:(hf + 1) * 512], yT[:, c2, hf * 512:(hf + 1) * 512], sgl[:], ALU.mult,
                     [yT.s, sgl.s], [mixs[4 + c2][hf]])


    for l in range(DEPTH):
        compute_mod(l, 0)

        def mod_rest(l=l):
            compute_mod(l, 1, slots=[1, 2, 3, 1])
            compute_mod(l, 2, slots=[2, 3, 1, 2])
        norm_mod(l, 0, 0)
        if STAGE["s5"]:
            s5(l, after_u=mod_rest)
        else:
            mod_rest()
            for k in range(4, 8):
                for h in range(2):
                    P.dve(lambda e, k=k, h=h: e.memset(mixT[:, k, h * 512:(h + 1) * 512], 0.0), writes=[mixs[k][h]])
        if STAGE["hg"]:
            hgrn(l)
        else:
            for k in range(0, 4):
                for h in range(2):
                    P.dve(lambda e, k=k, h=h: e.memset(mixT[:, k, h * 512:(h + 1) * 512], 0.0), writes=[mixs[k][h]])
        residual_proj(w_out[l], l, 16, mixT, mixs, KD)
        norm_mod(l, 1, 24)
        ffn(l)

    nfin32 = sb([128, KD], F32, "nfin32")
    P.dve(lambda e: e.tensor_scalar(out=nfin32[:], in0=nfin[:], scalar1=32.0, scalar2=None, op0=ALU.mult),
          reads=[nfin.s], writes=[nfin32.s])
    aphase()
    ytok = [aalloc([D], F32, "ytok%d" % j) for j in range(4)]
    yT = aalloc([512], F32, "yT")
    afence()
    for h in range(2):
        rms_stats(h)
        for k in range(KD):
            P.dve(lambda e, k=k, h=h: e.scalar_tensor_tensor(
                out=yT[:], in0=xT[:, k, h * 512:(h + 1) * 512], scalar=nfin32[:, k:k + 1], in1=rstd[:],
                op0=ALU.mult, op1=ALU.mult), reads=[xs[k][h], nfin32.s, rstd.s], writes=[yT.s])
            bk = psum()
            for j in range(4):
                P.pe(lambda e, bk=bk, j=j: e.transpose(bk[:, j * 128:(j + 1) * 128], yT[:, j * 128:(j + 1) * 128], ident_f),
                     reads=[yT.s, cst_f.s], writes=[bk.s])
            for j in range(4):
                P.act(lambda e, bk=bk, j=j, k=k: e.activation(out=ytok[j][:, k * 128:(k + 1) * 128],
                                                               in_=bk[:, j * 128:(j + 1) * 128], func=AF.Copy),
                      reads=[bk.s], writes=[ytok[j].s])
        for j in range(4):
            tt = h * 4 + j
            P.dma("sp", lambda e, j=j, tt=tt: e.dma_start(out=y_out[tt * 128:(tt + 1) * 128, :], in_=ytok[j][:]),
                  reads=[ytok[j].s], writes=[])

    with nc.allow_non_contiguous_dma(reason="small strided parameter loads"):
        P.emit(st)
    st.close()
    return nc


def _consts(core):
    q = core % 4
    c = np.zeros((128, 512), np.float32)
    c[:, 0:128] = np.eye(128, dtype=np.float32)
    j = np.arange(128)[:, None]
    i = np.arange(128)[None, :]
    same = (j // CH) == (i // CH)
    c[:, 128:256] = (same & (j <= i)).astype(np.float32)
    c[:, 256:384] = (same & (j >= i)).astype(np.float32)
    c[:, 384 + q] = 1.0
    nf = D // 4
    p = np.arange(128, dtype=np.float32)
    for par in range(2):
        kf = par * 128 + p
        c[:, 388 + par] = (1.0 / (np.float32(10000.0) ** (kf / np.float32(nf)))).astype(np.float32)
    t = q * 512 + np.arange(512)
    c2 = np.zeros((128, 1024), np.float32)
    c2[:, 0:512] = (t // 64).astype(np.float32)[None, :]
    c2[:, 512:1024] = (t % 64).astype(np.float32)[None, :]
    c3 = np.zeros((128, 1024), np.float32)
    for p_ in range(128):
        c3[p_, 112 + p_ % 16] = 1.0
        c3[p_, 240 + p_ // 16] = 1.0
    for g4 in range(4):
        for cc in range(16):
            c3[(2 * g4) * 16 + cc, 480 + g4 * 16 + cc] = 1.0
            c3[(2 * g4 + 1) * 16 + cc, 544 + g4 * 16 + cc] = 1.0
    c3[:, 608:625] = np.arange(-8, 9, dtype=np.float32)[None, :]
    c3[:, 640:705] = (8.0 * np.arange(65, dtype=np.float32))[None, :]
    sI = (np.arange(128) // 16)[:, None]
    tI = (np.arange(128) // 16)[None, :]
    c3[:, 768:896] = (sI <= tI).astype(np.float32)
    c3[:, 896:1024] = (sI >= tI).astype(np.float32)
    return c, c2, c3


_NC_CACHE = {}


def kernel(**inp):
    inp = {k: np.asarray(v) for k, v in inp.items()}
    if "nc" not in _NC_CACHE:
        _NC_CACHE["nc"] = build_program()
    nc = _NC_CACHE["nc"]
    xp = inp["x_prompt"]
    xsm = inp["x_sample"]
    in_maps = []
    shared = {k: np.ascontiguousarray(inp[k], dtype=np.float32) for k in
              ("w_mod", "b_mod", "norm_mix", "norm_ffn", "norm_final", "w_in", "w_out", "w_gate", "w_up", "w_down",
               "hg_lb_logits", "hg_norm", "s5_lam_re", "s5_lam_im", "s5_log_dt", "s5_b_re", "s5_b_im", "s5_c_re", "s5_c_im",
               "s5_d", "s5_w_glu")}
    for core in range(8):
        b, q = core // 4, core % 4
        x = np.concatenate([xp[2 * core], xp[2 * core + 1], xsm[b, q * 512:(q + 1) * 512]], axis=0)
        cond = np.stack([inp["c_ctx"], inp["c"][b]], axis=0)
        c1, c2, c3 = _consts(core)
        m = dict(shared)
        m.update({"x": np.ascontiguousarray(x, dtype=np.float32), "cond": np.ascontiguousarray(cond, dtype=np.float32),
                  "st_hg": np.ascontiguousarray(inp["state_hgrn"][b], dtype=np.float32), "cst": c1, "cst2": c2, "cst3": c3,
                  "st_s5": np.ascontiguousarray(inp["state_s5"][b], dtype=np.float32)})
        in_maps.append(m)
    res = run_bass_kernel_spmd(nc, in_maps, core_ids=list(range(8)))
    outs = res.results
    _DBG["outs"] = outs
    y_prompt = np.zeros_like(xp)
    y_sample = np.zeros_like(xsm)
    ns_hg = np.zeros((16, DEPTH, 2, NH, DK, DK), np.float32)
    ns_s5 = np.zeros((16, DEPTH, 2, NG, SP, 2), np.float32)
    for core in range(8):
        b, q = core // 4, core % 4
        y = outs[core]["y"]
        y_prompt[2 * core] = y[0:256]
        y_prompt[2 * core + 1] = y[256:512]
        y_sample[b, q * 512:(q + 1) * 512] = y[512:1024]
        ns_hg[2 * core:2 * core + 2] = outs[core]["ns_hg"]
        ns_s5[2 * core:2 * core + 2] = outs[core]["ns_s5"]
    return (y_prompt, y_sample, ns_hg, ns_s5)
```

```python
import math
from contextlib import ExitStack

import numpy as np
import concourse.bass as bass
import concourse.mybir as mybir
from concourse.bass_utils import run_bass_kernel_spmd

F32 = mybir.dt.float32
BF16 = mybir.dt.bfloat16
AF = mybir.ActivationFunctionType
ALU = mybir.AluOpType

ENGS = ("pe", "act", "dve", "pool", "sp")
SAME_ENGINE_RAW_DIST = 2


class Slot:
    __slots__ = ("name", "w", "r", "al")

    def __init__(self, name):
        self.name = name
        self.w = None
        self.r = []
        self.al = [self]


def alias(*slots):
    grp = []
    for s in slots:
        for a in s.al:
            if a not in grp:
                grp.append(a)
    for s in grp:
        s.al = grp


class Op:
    __slots__ = ("eng", "fn", "deps", "raw", "dma", "idx", "milestone", "mcount", "dsem", "dval", "inc", "eidx")


class Prog:
    def __init__(self, nc, n_dma_sems=8, sync_same_engine=True):
        self.nc = nc
        self.ops = []
        self.n_dma_sems = n_dma_sems
        self.sync_same = sync_same_engine

    def add(self, eng, fn, reads=(), writes=(), dma=False, inc=16):
        op = Op()
        op.eng, op.fn, op.dma, op.inc = eng, fn, dma, inc
        op.deps = set()
        op.raw = set()
        op.milestone = False
        op.mcount = 0
        op.dsem = None
        op.dval = 0
        op.idx = len(self.ops)
        for s0 in reads:
            for s in s0.al:
                if s.w is not None:
                    op.deps.add(s.w)
                    op.raw.add(s.w)
        for s0 in writes:
            for s in s0.al:
                if s.w is not None:
                    op.deps.add(s.w)
                op.deps.update(s.r)
        for s in reads:
            s.r.append(op.idx)
        for s in writes:
            s.w = op.idx
            s.r = []
        op.deps.discard(op.idx)
        self.ops.append(op)
        return op

    def pe(self, fn, reads=(), writes=()):
        return self.add("pe", fn, reads, writes)

    def act(self, fn, reads=(), writes=()):
        return self.add("act", fn, reads, writes)

    def dve(self, fn, reads=(), writes=()):
        return self.add("dve", fn, reads, writes)

    def dma(self, eng, fn, reads=(), writes=(), inc=16):
        return self.add(eng, fn, reads, writes, dma=True, inc=inc)

    def emit(self, stack):
        nc = self.nc
        ops = self.ops
        ecount = {e: 0 for e in ENGS}
        for op in ops:
            op.eidx = ecount[op.eng]
            ecount[op.eng] += 1

        def needs_sync(op, dop):
            if dop.dma or op.dma or dop.eng != op.eng:
                return True
            if dop.eng == "pe" or not self.sync_same:
                return False
            return (dop.idx in op.raw) and (op.eidx - dop.eidx < SAME_ENGINE_RAW_DIST)

        self.needs_sync = needs_sync
        for op in ops:
            for d in op.deps:
                dop = ops[d]
                if dop.dma:
                    continue
                if not needs_sync(op, dop):
                    continue
                dop.milestone = True
        cnt = {e: 0 for e in ENGS}
        for op in ops:
            if not op.dma and op.milestone:
                cnt[op.eng] += 1
            op.mcount = cnt[op.eng]
        esem = {e: stack.enter_context(nc.semaphore("s_" + e)) for e in ENGS}
        dsems = {e: None for e in ENGS}
        dcount = {}
        dn = {e: 0 for e in ENGS}
        for op in ops:
            if op.dma:
                if dsems[op.eng] is None:
                    dsems[op.eng] = [stack.enter_context(nc.semaphore("d_%s_%d" % (op.eng, i)))
                                     for i in range(self.n_dma_sems)]
                k = dn[op.eng]
                dn[op.eng] += 1
                op.dsem = (op.eng, k % self.n_dma_sems)
                dcount[op.dsem] = dcount.get(op.dsem, 0) + op.inc
                op.dval = dcount[op.dsem]
        per = {e: [o for o in ops if o.eng == e] for e in ENGS}
        block = stack.enter_context(nc.Block())
        sync_same = self.sync_same

        def make(e):
            def body(eng):
                waited = {}

                def wait(key, sem, val):
                    if waited.get(key, 0) >= val:
                        return
                    waited[key] = val
                    eng.wait_ge(sem, val)

                for op in per[e]:
                    for d in sorted(op.deps):
                        dop = ops[d]
                        if dop.dma:
                            wait(("d",) + dop.dsem, dsems[dop.dsem[0]][dop.dsem[1]], dop.dval)
                        else:
                            if not self.needs_sync(op, dop):
                                continue
                            wait(("e", dop.eng), esem[dop.eng], dop.mcount)
                    if op.dma:
                        prev = op.dval - op.inc
                        if prev > 0:
                            wait(("d",) + op.dsem, dsems[op.dsem[0]][op.dsem[1]], prev)
                        ins = op.fn(eng)
                        ins.then_inc(dsems[op.dsem[0]][op.dsem[1]], op.inc)
                    else:
                        ins = op.fn(eng)
                        if op.milestone:
                            ins.then_inc(esem[e], 1)
                if dsems[e] is not None:
                    for i, s in enumerate(dsems[e]):
                        v = dcount.get((e, i), 0)
                        if v:
                            wait(("d", e, i), s, v)
            return body

        block.tensor(make("pe"))
        block.scalar(make("act"))
        block.vector(make("dve"))
        block.gpsimd(make("pool"))
        block.sync(make("sp"))


D = 1024
KD = 8
T = 1024
NT = 8
DEPTH = 2
HG_W = 512
NH = 4
DK = 128
S5_W = 512
NG = 32
SP = 64
IN_W = 3072
DFF = 2816
NF = 22
EPS = 1e-6
CH = 32
NCH = T // CH
SEGS = [(0, 8), (8, 16), (16, 32)]
WSLOT = 4096
N_WSLOT = 4
CCW = 1096
ARENA_B = 91 * 1024 // 2

I32 = mybir.dt.int32
STAGE = {"hg": True, "s5": True, "s5_stop": 99}
DEBUG = False
_DBG = {}


class TT:
    def __init__(self, t, slot):
        self.t = t
        self.s = slot

    def __getitem__(self, k):
        return self.t[k]


def build_program():
    nc = bass.Bass("TRN2", target_bir_lowering=False)
    st = ExitStack()
    P = Prog(nc)

    def din(name, shape):
        return nc.dram_tensor(name, list(shape), F32, kind="ExternalInput").ap()

    def dout(name, shape):
        return nc.dram_tensor(name, list(shape), F32, kind="ExternalOutput").ap()

    x_in = din("x", [T, D])
    cond_in = din("cond", [2, D])
    w_mod = din("w_mod", [DEPTH, D, 6 * D])
    b_mod = din("b_mod", [DEPTH, 6 * D])
    norm_mix = din("norm_mix", [DEPTH, D])
    norm_ffn = din("norm_ffn", [DEPTH, D])
    norm_final = din("norm_final", [D])
    w_in = din("w_in", [DEPTH, D, IN_W])
    w_out = din("w_out", [DEPTH, D, D])
    w_gate = din("w_gate", [DEPTH, D, DFF])
    w_up = din("w_up", [DEPTH, D, DFF])
    w_down = din("w_down", [DEPTH, DFF, D])
    hg_lb = din("hg_lb_logits", [2, DEPTH, HG_W])
    hg_norm = din("hg_norm", [DEPTH, DK])
    st_hg = din("st_hg", [DEPTH, 2, NH, DK, DK])
    cst = din("cst", [128, 512])
    cst2_in = din("cst2", [128, 1024])
    cst3_in = din("cst3", [128, 1024])
    s5_lam_re = din("s5_lam_re", [DEPTH, 2, NG, SP])
    s5_lam_im = din("s5_lam_im", [DEPTH, 2, NG, SP])
    s5_log_dt = din("s5_log_dt", [DEPTH, 2, NG])
    s5_b_re = din("s5_b_re", [DEPTH, 2, NG, SP, 16])
    s5_b_im = din("s5_b_im", [DEPTH, 2, NG, SP, 16])
    s5_c_re = din("s5_c_re", [DEPTH, 2, NG, 16, SP])
    s5_c_im = din("s5_c_im", [DEPTH, 2, NG, 16, SP])
    s5_d = din("s5_d", [DEPTH, NG, 16])
    s5_w_glu = din("s5_w_glu", [DEPTH, S5_W, S5_W])
    st_s5 = din("st_s5", [DEPTH, 2, NG, SP, 2])
    y_out = dout("y", [T, D])
    ns_hg = dout("ns_hg", [2, DEPTH, 2, NH, DK, DK])
    ns_s5 = dout("ns_s5", [2, DEPTH, 2, NG, SP, 2])
    cc5_in = [nc.dram_tensor("cc5_in%d" % i, [128, 64], F32, kind="Internal").ap() for i in range(DEPTH)]
    cc5_out = [nc.dram_tensor("cc5_out%d" % i, [4 * 128, 64], F32, kind="Internal").ap() for i in range(DEPTH)]
    cc_in = [nc.dram_tensor("cc_in%d" % i, [128, CCW], F32, kind="Internal").ap() for i in range(2 * DEPTH)]
    cc_out = [nc.dram_tensor("cc_out%d" % i, [4 * 128, CCW], F32, kind="Internal").ap() for i in range(2 * DEPTH)]

    _n = [0]
    dbg_list = []

    def dbg(name, ap, slot, shape, dtype=F32):
        if not DEBUG:
            return
        t = nc.dram_tensor("dbg_" + name, list(shape), dtype, kind="ExternalOutput").ap()
        P.dma("sp", lambda e: e.dma_start(out=t, in_=ap), reads=[slot], writes=[])

    def sb(shape, dtype, name=None):
        _n[0] += 1
        name = "sb_" + (name or "t%d" % _n[0])
        t = st.enter_context(nc.sbuf_tensor(name, list(shape), dtype))
        return TT(t, Slot(name))

    banks = [TT(st.enter_context(nc.psum_tensor("ps%d" % i, [128, 512], F32)), Slot("ps%d" % i)) for i in range(8)]
    _pb = [0]

    def psum():
        b = banks[_pb[0] % 6]
        _pb[0] += 1
        return b

    wslots = [sb([128, WSLOT], BF16, "wslot%d" % i) for i in range(N_WSLOT)]
    _ws = [0]

    def wload(src_ap, a, b, slot=None):
        if slot is None:
            w = wslots[_ws[0] % N_WSLOT]
            _ws[0] += 1
        else:
            w = wslots[slot]
        view = bass.AP(w.t, 0, [[WSLOT, 128], [b, a], [1, b]])
        P.dma("pool", lambda e, v=view, s=src_ap: e.dma_start(out=v, in_=s), writes=[w.s])
        return view, w.s

    arena_t = st.enter_context(nc.sbuf_tensor("arena", [128, ARENA_B], BF16))
    ar = {"off": 0, "live": []}

    def aalloc(free_shape, dtype, name):
        n = 1
        for v in free_shape:
            n *= v
        nb = n * (1 if dtype == BF16 else 2)
        nb = (nb + 1) // 2 * 2
        off = ar["off"]
        assert off + nb <= ARENA_B, ("arena overflow", name, off, nb)
        ar["off"] = off + nb
        v = arena_t[:, off:off + nb]
        if dtype != BF16:
            v = v.bitcast(dtype)
        if len(free_shape) > 1:
            names = "abcdefg"[:len(free_shape)]
            kw = {names[i]: free_shape[i] for i in range(1, len(free_shape))}
            v = v.rearrange("p (%s) -> p %s" % (" ".join(names), " ".join(names)), **kw)
        s = Slot(name)
        ar["live"].append(s)
        return TT(v, s)

    fence_t = sb([128, 2], F32, "fence")

    def aphase(new_names_hint=None):
        old = ar["live"]
        ar["live"] = []
        ar["off"] = 0
        ar["pending"] = old

    def afence():
        old = ar.get("pending", [])
        new = list(ar["live"])
        P.dve(lambda e: e.memset(fence_t[:], 0.0), reads=[], writes=old + new + [fence_t.s])
        ar["pending"] = []

    cst_f = sb([128, 512], F32, "cst_f")
    P.dma("sp", lambda e: e.dma_start(out=cst_f[:], in_=cst), writes=[cst_f.s])
    ident_f = cst_f[:, 0:128]
    ident_b = sb([128, 128], BF16, "ident_b")
    P.dve(lambda e: e.tensor_copy(out=ident_b[:], in_=cst_f[:, 0:128]), reads=[cst_f.s], writes=[ident_b.s])
    ones_b = sb([128, 128], BF16, "ones_b")
    P.dve(lambda e: e.memset(ones_b[:], 1.0), writes=[ones_b.s])
    zeros_f = sb([128, 128], F32, "zeros_f")
    P.dve(lambda e: e.memset(zeros_f[:], 0.0), writes=[zeros_f.s])
    epsc = sb([128, 2], F32, "epsc")
    P.dve(lambda e: e.memset(epsc[:, 0:1], float(D * EPS)), writes=[epsc.s])
    P.dve(lambda e: e.memset(epsc[:, 1:2], float(DK * EPS)), reads=[epsc.s], writes=[epsc.s])
    lnc = sb([128, 1], F32, "lnc")
    P.dve(lambda e: e.memset(lnc[:], float(math.log(DK ** -0.5))), writes=[lnc.s])

    xT = sb([128, KD, T], F32, "xT")
    xs = [[Slot("xT%d_%d" % (k, h)) for h in range(2)] for k in range(KD)]
    hT = sb([128, KD, T], BF16, "hT")
    hs = [[Slot("hT%d_%d" % (k, h)) for h in range(2)] for k in range(KD)]
    mixT = sb([128, KD, T], BF16, "mixT")
    mixs = [[Slot("mix%d_%d" % (k, h)) for h in range(2)] for k in range(KD)]

    def sin_turns(out, u, ki, kf, ap=lambda t: t[:]):
        P.dve(lambda e: e.tensor_copy(out=ap(ki), in_=ap(u)), reads=[u.s], writes=[ki.s])
        P.dve(lambda e: e.tensor_copy(out=ap(kf), in_=ap(ki)), reads=[ki.s], writes=[kf.s])
        P.dve(lambda e: e.tensor_tensor(out=ap(kf), in0=ap(u), in1=ap(kf), op=ALU.subtract), reads=[u.s, kf.s], writes=[kf.s])
        P.act(lambda e: e.activation(out=ap(out), in_=ap(kf), func=AF.Sin, scale=2 * math.pi), reads=[kf.s], writes=[out.s])

    aphase()
    xtok = [aalloc([D], F32, "xtok%d" % i) for i in range(NT)]
    posarg = aalloc([512], F32, "posarg")
    cst2 = aalloc([1024], F32, "cst2")
    posk_i = aalloc([512], I32, "posk_i")
    posk_f = aalloc([512], F32, "posk_f")
    afence()
    P.dma("sp", lambda e: e.dma_start(out=cst2[:], in_=cst2_in), writes=[cst2.s])
    for tt in range(NT):
        P.dma("sp", lambda e, tt=tt: e.dma_start(out=xtok[tt][:], in_=x_in[tt * 128:(tt + 1) * 128, :]),
              writes=[xtok[tt].s])
    for k in range(KD):
        for h in range(2):
            b = psum()
            for j in range(4):
                tt = h * 4 + j
                P.pe(lambda e, b=b, j=j, tt=tt, k=k: e.transpose(b[:, j * 128:(j + 1) * 128],
                                                                   xtok[tt][:, k * 128:(k + 1) * 128], ident_f),
                     reads=[xtok[tt].s, cst_f.s], writes=[b.s])
            P.act(lambda e, b=b, k=k, h=h: e.activation(out=xT[:, k, h * 512:(h + 1) * 512], in_=b[:], func=AF.Copy),
                  reads=[b.s], writes=[xs[k][h]])

    posi = aalloc_late = None
    for k in range(KD):
        blk = k // 2
        pos_src = cst2[:, 0:512] if blk < 2 else cst2[:, 512:1024]
        om = cst_f[:, 388 + (k % 2):389 + (k % 2)]
        P.dve(lambda e, pos_src=pos_src, om=om: e.tensor_scalar(
            out=posarg[:], in0=pos_src, scalar1=om, scalar2=1.0 / (2 * math.pi), op0=ALU.mult, op1=ALU.mult),
            reads=[cst2.s, cst_f.s], writes=[posarg.s])
        if blk % 2 == 1:
            P.dve(lambda e: e.tensor_scalar(out=posarg[:], in0=posarg[:], scalar1=0.25, scalar2=None, op0=ALU.add),
                  reads=[posarg.s], writes=[posarg.s])
        sin_turns(posarg, posarg, posk_i, posk_f)
        P.dve(lambda e, k=k: e.tensor_tensor(out=xT[:, k, 512:1024], in0=xT[:, k, 512:1024], in1=posarg[:], op=ALU.add),
              reads=[posarg.s, xs[k][1]], writes=[xs[k][1]])

    condf = sb([128, KD, 2], F32, "condf")
    condb = sb([128, KD, 2], BF16, "condb")
    for j in range(2):
        P.dma("sp", lambda e, j=j: e.dma_start(out=condf[:, :, j], in_=cond_in[j].rearrange("(k p) -> p k", p=128)),
              writes=[condf.s])
    P.act(lambda e: e.activation(out=condb[:], in_=condf[:], func=AF.Silu), reads=[condf.s], writes=[condb.s])

    nmix = sb([128, DEPTH, KD], F32, "nmix")
    nffn = sb([128, DEPTH, KD], F32, "nffn")
    nfin = sb([128, KD], F32, "nfin")
    bmod = sb([128, DEPTH, 48], F32, "bmod")
    P.dma("sp", lambda e: e.dma_start(out=nmix[:], in_=norm_mix.rearrange("l (k p) -> p l k", p=128)), writes=[nmix.s])
    P.dma("sp", lambda e: e.dma_start(out=nffn[:], in_=norm_ffn.rearrange("l (k p) -> p l k", p=128)), writes=[nffn.s])
    P.dma("sp", lambda e: e.dma_start(out=nfin[:], in_=norm_final.rearrange("(k p) -> p k", p=128)), writes=[nfin.s])
    P.dma("sp", lambda e: e.dma_start(out=bmod[:], in_=b_mod.rearrange("l (k p) -> p l k", p=128)), writes=[bmod.s])

    lbl = sb([128, 2, DEPTH, NH], F32, "lbl")
    for d in range(2):
        for l in range(DEPTH):
            P.dma("sp", lambda e, d=d, l=l: e.dma_start(out=lbl[:, d, l, :], in_=hg_lb[d, l].rearrange("(h p) -> p h", p=128)),
                  writes=[lbl.s])
    lb = sb([128, DEPTH, 2, NH], F32, "lb")
    oml = sb([128, DEPTH, 2, NH], F32, "oml")
    noml = sb([128, DEPTH, 2, NH], F32, "noml")
    P.dve(lambda e: e.memset(lb[:], 0.0), writes=[lb.s])
    P.dve(lambda e: e.tensor_tensor(out=lb[:, 1], in0=lbl[:, :, 1, :], in1=lbl[:, :, 0, :], op=ALU.subtract),
          reads=[lbl.s, lb.s], writes=[lb.s])
    P.act(lambda e: e.activation(out=lb[:, 1], in_=lb[:, 1], func=AF.Sigmoid), reads=[lb.s], writes=[lb.s])
    P.dve(lambda e: e.tensor_scalar(out=oml[:], in0=lb[:], scalar1=-1.0, scalar2=1.0, op0=ALU.mult, op1=ALU.add),
          reads=[lb.s], writes=[oml.s])
    P.dve(lambda e: e.tensor_scalar(out=noml[:], in0=lb[:], scalar1=1.0, scalar2=-1.0, op0=ALU.mult, op1=ALU.add),
          reads=[lb.s], writes=[noml.s])
    gn = sb([128, DEPTH], F32, "gn")
    P.dma("sp", lambda e: e.dma_start(out=gn[:], in_=hg_norm.rearrange("l p -> p l")), writes=[gn.s])
    P.dve(lambda e: e.tensor_scalar(out=gn[:], in0=gn[:], scalar1=float(math.sqrt(DK)), scalar2=None, op0=ALU.mult),
          reads=[gn.s], writes=[gn.s])

    par_all = sb([128, DEPTH, 2, 5, 16], F32, "par_all")
    for l_ in range(DEPTH):
        for d_ in range(2):
            P.dma("sp", lambda e, l_=l_, d_=d_: e.dma_start(
                out=par_all[:, l_, d_, 0, :], in_=s5_lam_re[l_, d_].rearrange("(gp g2) p -> (g2 p) gp", g2=2)), writes=[par_all.s])
            P.dma("sp", lambda e, l_=l_, d_=d_: e.dma_start(
                out=par_all[:, l_, d_, 1, :], in_=s5_lam_im[l_, d_].rearrange("(gp g2) p -> (g2 p) gp", g2=2)), writes=[par_all.s])
            for g2_ in range(2):
                P.dma("sp", lambda e, l_=l_, d_=d_, g2_=g2_: e.dma_start(
                    out=par_all[64 * g2_:64 * g2_ + 64, l_, d_, 2, :],
                    in_=s5_log_dt[l_, d_].rearrange("(gp g2) -> g2 gp", g2=2)[g2_].partition_broadcast(64)), writes=[par_all.s])
    mod = sb([128, DEPTH, 48, 2], F32, "mod")
    coef = sb([128, DEPTH, 2, KD, 2], F32, "coef")

    mod_s = [[Slot("mod%d_%d" % (l_, p_)) for p_ in range(3)] for l_ in range(DEPTH)]
    coef_s = [[Slot("coef%d_%d" % (l_, n_)) for n_ in range(2)] for l_ in range(DEPTH)]

    def compute_mod(l, part, slots=None):
        bk = psum()
        for ci_, cc in enumerate(range(part * 4, part * 4 + 4)):
            wv, wsl = wload(w_mod[l][:, cc * 512:(cc + 1) * 512].rearrange("(k p) c -> p k c", p=128), KD, 512,
                            slot=None if slots is None else slots[ci_])
            for j in range(4):
                ft = cc * 4 + j - part * 16
                for k in range(KD):
                    P.pe(lambda e, wv=wv, j=j, k=k, ft=ft, bk=bk: e.matmul(
                        bk[:, ft * 2:ft * 2 + 2], wv[:, k, j * 128:(j + 1) * 128], condb[:, k, :],
                        start=(k == 0), stop=(k == KD - 1)),
                        reads=[wsl, condb.s], writes=[bk.s])
        t0_ = part * 16
        P.dve(lambda e, bk=bk: e.tensor_tensor(
            out=mod[:, l, t0_:t0_ + 16], in0=bk[:, 0:32].rearrange("p (f c) -> p f c", c=2),
            in1=bmod[:, l, t0_:t0_ + 16].unsqueeze(2).broadcast_to([128, 16, 2]), op=ALU.add),
            reads=[bk.s, bmod.s], writes=[mod_s[l][part]])
        for n, (gt, base, prt) in enumerate(((nmix, 8, 0), (nffn, 32, 2))):
            if prt != part:
                continue
            P.dve(lambda e, n=n, base=base: e.tensor_scalar(
                out=coef[:, l, n], in0=mod[:, l, base:base + 8, :], scalar1=1.0, scalar2=32.0,
                op0=ALU.add, op1=ALU.mult), reads=[mod_s[l][part]], writes=[coef_s[l][n]])
            P.dve(lambda e, n=n, gt=gt: e.tensor_tensor(
                out=coef[:, l, n], in0=coef[:, l, n], in1=gt[:, l].unsqueeze(2).broadcast_to([128, KD, 2]),
                op=ALU.mult), reads=[coef_s[l][n], gt.s], writes=[coef_s[l][n]])

    sq = [sb([128, 512], BF16, "sq%d" % i) for i in range(2)]
    rstd = sb([128, 512], F32, "rstd")
    tmpn = [sb([128, 512], F32, "tmpn%d" % i) for i in range(2)]

    def rms_stats(h):
        bk = psum()
        for k in range(KD):
            s = sq[k % 2]
            P.act(lambda e, k=k, h=h, s=s: e.activation(out=s[:], in_=xT[:, k, h * 512:(h + 1) * 512], func=AF.Square),
                  reads=[xs[k][h]], writes=[s.s])
            P.pe(lambda e, k=k, bk=bk, s=s: e.matmul(bk[:], ones_b[:], s[:], start=(k == 0), stop=(k == KD - 1)),
                 reads=[s.s, ones_b.s], writes=[bk.s])
        P.act(lambda e, bk=bk: e.activation(out=rstd[:], in_=bk[:], func=AF.Ln, bias=epsc[:, 0:1]),
              reads=[bk.s, epsc.s], writes=[rstd.s])
        P.act(lambda e: e.activation(out=rstd[:], in_=rstd[:], func=AF.Exp, scale=-0.5), reads=[rstd.s], writes=[rstd.s])

    def norm_mod(l, n, shift_base):
        for h in range(2):
            rms_stats(h)
            for k in range(KD):
                tm = tmpn[k % 2]
                P.dve(lambda e, k=k, h=h, tm=tm: e.tensor_tensor(out=tm[:], in0=xT[:, k, h * 512:(h + 1) * 512],
                                                                 in1=rstd[:], op=ALU.mult),
                      reads=[xs[k][h], rstd.s], writes=[tm.s])
                P.act(lambda e, k=k, h=h, l=l, n=n, tm=tm: e.activation(
                    out=hT[:, k, h * 512:(h + 1) * 512], in_=tm[:], func=AF.Identity,
                    bias=mod[:, l, shift_base + k, h:h + 1], scale=coef[:, l, n, k, h:h + 1]),
                    reads=[tm.s, mod_s[l][shift_base // 16], coef_s[l][n]], writes=[hs[k][h]])

    def residual_proj(wdram, l, gate_base, srcT, src_slots, nk):
        per = WSLOT // 256
        for c4 in range(4):
            wv = []
            for k0 in range(0, nk, per):
                kk = min(per, nk - k0)
                v, s = wload(wdram[k0 * 128:(k0 + kk) * 128, c4 * 256:(c4 + 1) * 256].rearrange("(k p) c -> p k c", p=128),
                             kk, 256)
                wv.append((k0, kk, v, s))
            for j in range(2):
                dt_ = c4 * 2 + j
                for h in range(2):
                    bk = psum()
                    for (k0, kk, v, s) in wv:
                        for k in range(kk):
                            kg = k0 + k
                            P.pe(lambda e, v=v, k=k, j=j, kg=kg, h=h, bk=bk: e.matmul(
                                bk[:], v[:, k, j * 128:(j + 1) * 128], srcT[:, kg, h * 512:(h + 1) * 512],
                                start=(kg == 0), stop=(kg == nk - 1)),
                                reads=[s, src_slots[kg][h]], writes=[bk.s])
                    P.dve(lambda e, bk=bk, dt_=dt_, h=h, l=l: e.scalar_tensor_tensor(
                        out=xT[:, dt_, h * 512:(h + 1) * 512], in0=bk[:], scalar=mod[:, l, gate_base + dt_, h:h + 1],
                        in1=xT[:, dt_, h * 512:(h + 1) * 512], op0=ALU.mult, op1=ALU.add),
                        reads=[bk.s, mod_s[l][gate_base // 16], xs[dt_][h]], writes=[xs[dt_][h]])


    def ffn(l):
        aphase()
        h1 = aalloc([NF, T], BF16, "h1")
        sgate = [aalloc([512], F32, "sgate%d" % i) for i in range(2)]
        h1s = [[Slot("h1_%d_%d" % (f, h)) for h in range(2)] for f in range(NF)]
        ar["live"].extend([s for row in h1s for s in row])
        afence()
        it = 0
        for c in range(6):
            ncol = 512 if c < 5 else 256
            vg, sg_ = wload(w_gate[l][:, c * 512:c * 512 + ncol].rearrange("(k p) c -> p k c", p=128), KD, ncol)
            vu, su_ = wload(w_up[l][:, c * 512:c * 512 + ncol].rearrange("(k p) c -> p k c", p=128), KD, ncol)
            for j in range(ncol // 128):
                f = c * 4 + j
                for h in range(2):
                    bg = psum()
                    bu = psum()
                    for k in range(KD):
                        P.pe(lambda e, vg=vg, k=k, j=j, h=h, bg=bg: e.matmul(
                            bg[:], vg[:, k, j * 128:(j + 1) * 128], hT[:, k, h * 512:(h + 1) * 512],
                            start=(k == 0), stop=(k == KD - 1)), reads=[sg_, hs[k][h]], writes=[bg.s])
                    for k in range(KD):
                        P.pe(lambda e, vu=vu, k=k, j=j, h=h, bu=bu: e.matmul(
                            bu[:], vu[:, k, j * 128:(j + 1) * 128], hT[:, k, h * 512:(h + 1) * 512],
                            start=(k == 0), stop=(k == KD - 1)), reads=[su_, hs[k][h]], writes=[bu.s])
                    sgt = sgate[it % 2]
                    it += 1
                    P.act(lambda e, bg=bg, sgt=sgt: e.activation(out=sgt[:], in_=bg[:], func=AF.Silu),
                          reads=[bg.s], writes=[sgt.s])
                    P.dve(lambda e, bu=bu, f=f, h=h, sgt=sgt: e.tensor_tensor(
                        out=h1[:, f, h * 512:(h + 1) * 512], in0=sgt[:], in1=bu[:], op=ALU.mult),
                        reads=[sgt.s, bu.s], writes=[h1s[f][h]])
        residual_proj(w_down[l], l, 40, h1, h1s, NF)

    def proj_feat(wv, wsl, c0, h):
        bk = psum()
        for k in range(KD):
            P.pe(lambda e, k=k, bk=bk: e.matmul(bk[:], wv[:, k, c0:c0 + 128], hT[:, k, h * 512:(h + 1) * 512],
                                                start=(k == 0), stop=(k == KD - 1)),
                 reads=[wsl, hs[k][h]], writes=[bk.s])
        return bk

    def hgrn(l):
        aphase()
        qk = [[[aalloc([T], BF16, "qk%d%d%d" % (h, d, w)) for w in range(2)] for d in range(2)] for h in range(NH)]
        kendT = [[aalloc([NT, DK], BF16, "kendT%d%d" % (h, d)) for d in range(2)] for h in range(NH)]
        V = [aalloc([HG_W], BF16, "V%d" % tt) for tt in range(NT)]
        gch = aalloc([NH * 2, NCH], F32, "gch")
        gch_s = [[Slot("gch%d%d" % (h, d)) for d in range(2)] for h in range(NH)]
        ar["live"].extend([s for row in gch_s for s in row])
        R1 = ar["off"]
        qs = [aalloc([512], F32, "qs%d" % hf) for hf in range(2)]
        rmask = aalloc([512], F32, "rmask")
        tmp = [[aalloc([512], F32, "gt%d_%d" % (i, j)) for j in range(5)] for i in range(2)]
        kend_t = [aalloc([512], BF16, "kend%d" % i) for i in range(2)]
        R1_end = ar["off"]
        afence()
        P.dve(lambda e: e.memset(rmask[:], 1.0), writes=[rmask.s])
        P.dve(lambda e: e.memset(rmask[:, 0:512:CH], 0.0), reads=[rmask.s], writes=[rmask.s])

        wv_iv, ws_iv = None, None

        def load_in(c):
            return wload(w_in[l][:, c * 512:(c + 1) * 512].rearrange("(k p) c -> p k c", p=128), KD, 512)

        wq, wqs = load_in(0)
        wf = [None, None]
        wf[0] = load_in(1)
        wf[1] = load_in(2)
        def gate_chain(h, d, hf, tset, ke):
            t_sig, t_a, t_b, t_c, t_e = tset
            bk = proj_feat(wf[d][0], wf[d][1], h * 128, hf)
            lb_ = lb[:, l, d, h:h + 1]
            oml_ = oml[:, l, d, h:h + 1]
            noml_ = noml[:, l, d, h:h + 1]
            P.act(lambda e, bk=bk, t_sig=t_sig: e.activation(out=t_sig[:], in_=bk[:], func=AF.Sigmoid),
                  reads=[bk.s], writes=[t_sig.s])
            yield
            P.dve(lambda e, t_sig=t_sig, t_a=t_a, lb_=lb_, oml_=oml_: e.tensor_scalar(
                out=t_a[:], in0=t_sig[:], scalar1=oml_, scalar2=lb_, op0=ALU.mult, op1=ALU.add),
                reads=[t_sig.s, lb.s, oml.s], writes=[t_a.s])
            yield
            P.act(lambda e, t_a=t_a: e.activation(out=t_a[:], in_=t_a[:], func=AF.Ln), reads=[t_a.s], writes=[t_a.s])
            yield
            P.dve(lambda e, t_a=t_a, t_b=t_b: e.tensor_tensor_scan(
                out=t_b[:], data0=rmask[:], data1=t_a[:], initial=0.0, op0=ALU.mult, op1=ALU.add),
                reads=[t_a.s, rmask.s], writes=[t_b.s])
            yield
            P.dve(lambda e, t_sig=t_sig, noml_=noml_, oml_=oml_: e.tensor_scalar(
                out=t_sig[:], in0=t_sig[:], scalar1=noml_, scalar2=oml_, op0=ALU.mult, op1=ALU.add),
                reads=[t_sig.s, noml.s, oml.s], writes=[t_sig.s])
            yield
            P.act(lambda e, t_b=t_b, h=h, d=d, hf=hf: e.activation(
                out=gch[:, h * 2 + d, hf * (512 // CH):(hf + 1) * (512 // CH)], in_=t_b[:, CH - 1:512:CH], func=AF.Exp),
                reads=[t_b.s], writes=[gch_s[h][d]])
            tb3 = t_b[:].rearrange("p (c j) -> p c j", j=CH)
            tc3 = t_c[:].rearrange("p (c j) -> p c j", j=CH)
            ta3 = t_a[:].rearrange("p (c j) -> p c j", j=CH)
            tot_b = tb3[:, :, CH - 1:CH].broadcast_to([128, 512 // CH, CH])
            if d == 0:
                P.dve(lambda e, tc3=tc3, tb3=tb3, tot_b=tot_b: e.tensor_tensor(
                    out=tc3, in0=tb3, in1=tot_b, op=ALU.subtract), reads=[t_b.s], writes=[t_c.s])
            else:
                P.dve(lambda e, tc3=tc3, ta3=ta3, tb3=tb3: e.tensor_tensor(
                    out=tc3, in0=ta3, in1=tb3, op=ALU.subtract), reads=[t_a.s, t_b.s], writes=[t_c.s])
                P.dve(lambda e, tc3=tc3, tb3=tb3, tot_b=tot_b, ta3=ta3: e.tensor_tensor(
                    out=ta3, in0=tc3, in1=tot_b, op=ALU.add), reads=[t_c.s, t_b.s], writes=[t_a.s])
            beta = t_b if d == 0 else t_a
            yield
            P.act(lambda e, beta=beta, t_e=t_e: e.activation(out=t_e[:], in_=beta[:], func=AF.Exp, bias=lnc[:, 0:1]),
                  reads=[beta.s, lnc.s], writes=[t_e.s])
            yield
            P.dve(lambda e, t_e=t_e, h=h, d=d, hf=hf: e.tensor_tensor(
                out=qk[h][d][0][:, hf * 512:(hf + 1) * 512], in0=qs[hf][:], in1=t_e[:], op=ALU.mult),
                reads=[t_e.s, qs[hf].s], writes=[qk[h][d][0].s])
            yield
            P.dve(lambda e, beta=beta, t_e=t_e: e.tensor_scalar(out=t_e[:], in0=beta[:], scalar1=-75.0, scalar2=None, op0=ALU.max),
                  reads=[beta.s], writes=[t_e.s])
            yield
            P.act(lambda e, t_e=t_e: e.activation(out=t_e[:], in_=t_e[:], func=AF.Exp, scale=-1.0),
                  reads=[t_e.s], writes=[t_e.s])
            yield
            P.dve(lambda e, t_e=t_e, t_sig=t_sig, h=h, d=d, hf=hf: e.tensor_tensor(
                out=qk[h][d][1][:, hf * 512:(hf + 1) * 512], in0=t_sig[:], in1=t_e[:], op=ALU.mult),
                reads=[t_e.s, t_sig.s], writes=[qk[h][d][1].s])
            yield
            P.act(lambda e, t_c=t_c, t_e=t_e: e.activation(out=t_e[:], in_=t_c[:], func=AF.Exp, scale=-1.0),
                  reads=[t_c.s], writes=[t_e.s])
            yield
            P.dve(lambda e, t_e=t_e, t_sig=t_sig, ke=ke: e.tensor_tensor(
                out=ke[:], in0=t_sig[:], in1=t_e[:], op=ALU.mult), reads=[t_e.s, t_sig.s], writes=[ke.s])
            yield
            bk2 = psum()
            yield
            for j in range(4):
                P.pe(lambda e, bk2=bk2, j=j, ke=ke: e.matmul(bk2[:, j * 128:(j + 1) * 128], ke[:, j * 128:(j + 1) * 128],
                                                             ident_b[:], start=True, stop=True),
                     reads=[ke.s, ident_b.s], writes=[bk2.s])
            yield
            P.act(lambda e, bk2=bk2, h=h, d=d, hf=hf: e.activation(
                out=kendT[h][d][:, hf * 4:(hf + 1) * 4, :], in_=bk2[:].rearrange("p (j k) -> p j k", k=128), func=AF.Copy),
                reads=[bk2.s], writes=[kendT[h][d].s])
            yield

        def interleave(gens):
            gens = list(gens)
            while gens:
                for g_ in list(gens):
                    try:
                        next(g_)
                    except StopIteration:
                        gens.remove(g_)

        for h in range(NH):
            for hf in range(2):
                bk = proj_feat(wq, wqs, h * 128, hf)
                P.act(lambda e, bk=bk, hf=hf: e.activation(out=qs[hf][:], in_=bk[:], func=AF.Silu),
                      reads=[bk.s], writes=[qs[hf].s])
            for d in range(2):
                interleave([gate_chain(h, d, 0, tmp[0], kend_t[0]), gate_chain(h, d, 1, tmp[1], kend_t[1])])
        wiv, wivs = load_in(3)
        for tt in range(NT):
            bk = psum()
            hf = tt // 4
            for k in range(KD):
                P.pe(lambda e, k=k, bk=bk, tt=tt: e.matmul(bk[:], hT[:, k, tt * 128:(tt + 1) * 128], wiv[:, k, :],
                                                            start=(k == 0), stop=(k == KD - 1)),
                     reads=[wivs, hs[k][hf]], writes=[bk.s])
            P.act(lambda e, bk=bk, tt=tt: e.activation(out=V[tt][:], in_=bk[:], func=AF.Copy), reads=[bk.s], writes=[V[tt].s])

        old_tmp = [t.s for grp in tmp for t in grp] + [k.s for k in kend_t] + [q.s for q in qs] + [rmask.s]
        ar["off"] = R1
        S = [aalloc([DK], F32, "S%d" % i) for i in range(3)]
        Sent = aalloc([NCH, DK], BF16, "Sent")
        o_t = aalloc([512], F32, "o_t")
        on_t = aalloc([512], F32, "on_t")
        sg_t = aalloc([512], F32, "sg_t")
        sq_t = aalloc([512], BF16, "sq_t")
        PT = [aalloc([128], BF16, "PT%d" % i) for i in range(2)]
        gat = aalloc([4, DK], F32, "gat")
        gatG = aalloc([4, 8], F32, "gatG")
        s0t = aalloc([DK], F32, "s0t")
        Pc = [aalloc([DK], F32, "Pc%d" % i) for i in range(2)]
        Sinit = aalloc([2 * NH, DK], F32, "Sinit")
        Sinit_s = [[Slot("Sinit%d%d" % (h, d)) for d in range(2)] for h in range(NH)]
        gtot = aalloc([NH * 2, NCH // 2], F32, "gtot")
        new2 = [t.s for t in S + PT + Pc] + [Sent.s, o_t.s, on_t.s, sg_t.s, sq_t.s, gat.s, gatG.s, s0t.s, Sinit.s, gtot.s] + \
               [s_ for row in Sinit_s for s_ in row]
        ar["live"].extend([s_ for row in Sinit_s for s_ in row])
        P.dve(lambda e: e.memset(fence_t[:], 0.0), reads=[], writes=old_tmp + new2 + [fence_t.s])

        CPT = 128 // CH

        def u_matmul(h, d, c):
            tt, p0 = c // CPT, (c % CPT) * CH
            bk = psum()
            P.pe(lambda e, bk=bk: e.matmul(bk[:, 0:128], kendT[h][d][p0:p0 + CH, tt, :],
                                           V[tt][p0:p0 + CH, h * 128:(h + 1) * 128],
                                           start=True, stop=True, tile_position=(p0, 0)),
                 reads=[kendT[h][d].s, V[tt].s], writes=[bk.s])
            return bk

        def scan_order(c0, c1, d):
            return list(range(c0, c1)) if d == 0 else list(range(c1 - 1, c0 - 1, -1))

        si = [0]
        SC0, SC1 = SEGS[2]

        ci = l * 2
        ccs_in, ccs_out = Slot("ccin"), Slot("ccout")
        def rec1_chain(h, d, Sx):
            hd = h * 2 + d
            order = scan_order(SC0, SC1, d)
            for i, c in enumerate(order):
                bk = u_matmul(h, d, c)
                yield
                if i == 0:
                    P.dve(lambda e, bk=bk, Sx=Sx: e.tensor_copy(out=Sx[:], in_=bk[:, 0:128]), reads=[bk.s], writes=[Sx.s])
                else:
                    P.dve(lambda e, bk=bk, Sx=Sx, c=c, hd=hd: e.scalar_tensor_tensor(
                        out=Sx[:], in0=Sx[:], scalar=gch[:, hd, c:c + 1], in1=bk[:, 0:128],
                        op0=ALU.mult, op1=ALU.add), reads=[bk.s, Sx.s, gch_s[h][d]], writes=[Sx.s])
                yield
            P.dma("sp", lambda e, Sx=Sx, hd=hd: e.dma_start(out=cc_in[ci][:, hd * 128:(hd + 1) * 128], in_=Sx[:]),
                  reads=[Sx.s], writes=[ccs_in])
            P.dve(lambda e, hd=hd: e.tensor_tensor_scan(
                out=gtot[:, hd, :], data0=gch[:, hd, SC0:SC1], data1=zeros_f[:, 0:SC1 - SC0], initial=1.0,
                op0=ALU.mult, op1=ALU.add), reads=[gch_s[h][d], zeros_f.s], writes=[gtot.s])
            yield

        for h in range(NH):
            interleave([rec1_chain(h, 0, S[0]), rec1_chain(h, 1, S[1])])
        P.dma("sp", lambda e: e.dma_start(out=cc_in[ci][:, 1024:1032], in_=gtot[:, :, SC1 - SC0 - 1]),
              reads=[gtot.s], writes=[ccs_in])
        P.dma("sp", lambda e: e.dma_start(out=cc_in[ci][:, 1032:CCW], in_=zeros_f[:, 0:CCW - 1032]),
              reads=[zeros_f.s], writes=[ccs_in])
        P.dma("pool", lambda e: e.collective_compute("AllGather", ALU.bypass, replica_groups=[[0, 1, 2, 3], [4, 5, 6, 7]],
                                                     ins=[cc_in[ci]], outs=[cc_out[ci]]),
              reads=[ccs_in], writes=[ccs_out], inc=1)
        ccv = cc_out[ci].rearrange("(r p) c -> p r c", p=128)
        P.dma("sp", lambda e: e.dma_start(out=gatG[:], in_=ccv[:, :, 1024:1032]), reads=[ccs_out], writes=[gatG.s])
        for h in range(NH):
            for d in range(2):
                hd = h * 2 + d
                col = slice(hd * 128, (hd + 1) * 128)
                P.dma("sp", lambda e, col=col: e.dma_start(out=gat[:], in_=ccv[:, :, col]), reads=[ccs_out], writes=[gat.s])
                P.dma("sp", lambda e, d=d, h=h: e.dma_start(out=s0t[:], in_=st_hg[l, d, h]), writes=[s0t.s])
                dst = Sinit[:, hd, :]
                ranks = [0, 1, 2, 3] if d == 0 else [3, 2, 1, 0]
                prev, prev_s = s0t[:], s0t.s
                P.dve(lambda e, dst=dst, prev=prev, r=ranks[0]: e.tensor_scalar(
                    out=dst, in0=prev, scalar1=cst_f[:, 384 + r:385 + r], scalar2=None, op0=ALU.mult),
                    reads=[prev_s, cst_f.s], writes=[Sinit_s[h][d]])
                for i in range(3):
                    r = ranks[i]
                    nxt = Pc[i % 2]
                    P.dve(lambda e, nxt=nxt, prev=prev, r=r, hd=hd: e.scalar_tensor_tensor(
                        out=nxt[:], in0=prev, scalar=gatG[:, r, hd:hd + 1], in1=gat[:, r, :],
                        op0=ALU.mult, op1=ALU.add), reads=[prev_s, gat.s, gatG.s], writes=[nxt.s])
                    rn = ranks[i + 1]
                    P.dve(lambda e, nxt=nxt, dst=dst, rn=rn: e.scalar_tensor_tensor(
                        out=dst, in0=nxt[:], scalar=cst_f[:, 384 + rn:385 + rn], in1=dst, op0=ALU.mult, op1=ALU.add),
                        reads=[nxt.s, cst_f.s, Sinit_s[h][d]], writes=[Sinit_s[h][d]])
                    prev, prev_s = nxt[:], nxt.s

        wg_v, wg_s = load_in(4)
        it2 = 0
        for h in range(NH):
            bo = [banks[6], banks[7]]
            for d in range(2):
                hd = h * 2 + d
                def seg_chain(h, d, c0, c1, Sx):
                    hd = h * 2 + d
                    order = scan_order(c0, c1, d)
                    is_sample = (c0 == SC0)
                    for i, c in enumerate(order):
                        if i == 0:
                            src = Sinit[:, hd, :] if is_sample else zeros_f[:]
                            src_s = Sinit_s[h][d] if is_sample else zeros_f.s
                        else:
                            src, src_s = Sx[:], Sx.s
                        P.act(lambda e, src=src, c=c: e.activation(out=Sent[:, c, :], in_=src, func=AF.Copy),
                              reads=[src_s], writes=[Sent.s])
                        yield
                        last = (i == len(order) - 1)
                        if last and is_sample:
                            continue
                        bk = u_matmul(h, d, c)
                        yield
                        if i == 0 and not is_sample:
                            P.dve(lambda e, bk=bk, Sx=Sx: e.tensor_copy(out=Sx[:], in_=bk[:, 0:128]), reads=[bk.s], writes=[Sx.s])
                        else:
                            P.dve(lambda e, bk=bk, Sx=Sx, src=src, c=c, hd=hd: e.scalar_tensor_tensor(
                                out=Sx[:], in0=src, scalar=gch[:, hd, c:c + 1], in1=bk[:, 0:128],
                                op0=ALU.mult, op1=ALU.add), reads=[bk.s, src_s, gch_s[h][d]], writes=[Sx.s])
                        yield
                    if not is_sample:
                        seq = 0 if c0 == 0 else 1
                        P.dma("sp", lambda e, Sx=Sx, seq=seq, d=d, h=h: e.dma_start(out=ns_hg[seq, l, d, h], in_=Sx[:]),
                              reads=[Sx.s], writes=[])
                    yield

                interleave([seg_chain(h, d, SEGS[i_][0], SEGS[i_][1], S[i_]) for i_ in range(3)])
                for hf in range(2):
                    for j in range(4):
                        tt = hf * 4 + j
                        tok = slice(tt * 128, (tt + 1) * 128)
                        bs = psum()
                        P.pe(lambda e, bs=bs, tok=tok, d=d, h=h: e.matmul(bs[:, 0:128], qk[h][d][1][:, tok], qk[h][d][0][:, tok],
                                                                    start=True, stop=True),
                             reads=[qk[h][d][0].s, qk[h][d][1].s], writes=[bs.s])
                        pt = PT[it2 % 2]
                        it2 += 1
                        P.dve(lambda e, bs=bs, pt=pt, d=d: e.tensor_tensor(
                            out=pt[:], in0=bs[:, 0:128], in1=cst_f[:, 128 + d * 128:256 + d * 128], op=ALU.mult),
                            reads=[bs.s, cst_f.s], writes=[pt.s])
                        oc = slice(j * 128, (j + 1) * 128)
                        P.pe(lambda e, pt=pt, tt=tt, oc=oc, hf=hf, d=d, j=j, h=h: e.matmul(
                            bo[hf][:, oc], V[tt][:, h * 128:(h + 1) * 128], pt[:], start=(d == 0 and j == 0), stop=False),
                            reads=[V[tt].s, pt.s], writes=[bo[hf].s])
                        for sub in range(CPT):
                            c = tt * CPT + sub
                            cs = slice(j * 128 + sub * CH, j * 128 + (sub + 1) * CH)
                            ts = slice(tt * 128 + sub * CH, tt * 128 + (sub + 1) * CH)
                            P.pe(lambda e, c=c, cs=cs, ts=ts, hf=hf, d=d, h=h: e.matmul(
                                bo[hf][:, cs], Sent[:, c, :], qk[h][d][0][:, ts], start=False, stop=(d == 1)),
                                reads=[Sent.s, qk[h][d][0].s], writes=[bo[hf].s])
            for hf in range(2):
                P.act(lambda e, hf=hf: e.activation(out=o_t[:], in_=bo[hf][:], func=AF.Copy), reads=[bo[hf].s], writes=[o_t.s])
                P.act(lambda e, hf=hf: e.activation(out=sq_t[:], in_=bo[hf][:], func=AF.Square), reads=[bo[hf].s], writes=[sq_t.s])
                if l == 0 and h == 0 and hf == 0:
                    dbg("o", o_t[:], o_t.s, [128, 512])
                    dbg("qf", qk[0][0][0][:], qk[0][0][0].s, [128, T], BF16)
                    dbg("kf", qk[0][0][1][:], qk[0][0][1].s, [128, T], BF16)
                    dbg("qb", qk[0][1][0][:], qk[0][1][0].s, [128, T], BF16)
                    dbg("kb", qk[0][1][1][:], qk[0][1][1].s, [128, T], BF16)
                br = psum()
                P.pe(lambda e, br=br: e.matmul(br[:], ones_b[:], sq_t[:], start=True, stop=True),
                     reads=[sq_t.s, ones_b.s], writes=[br.s])
                P.act(lambda e, br=br: e.activation(out=on_t[:], in_=br[:], func=AF.Ln, bias=epsc[:, 1:2]),
                      reads=[br.s, epsc.s], writes=[on_t.s])
                P.act(lambda e: e.activation(out=on_t[:], in_=on_t[:], func=AF.Exp, scale=-0.5), reads=[on_t.s], writes=[on_t.s])
                P.dve(lambda e: e.tensor_tensor(out=on_t[:], in0=o_t[:], in1=on_t[:], op=ALU.mult),
                      reads=[o_t.s, on_t.s], writes=[on_t.s])
                bg = proj_feat(wg_v, wg_s, h * 128, hf)
                P.act(lambda e, bg=bg: e.activation(out=sg_t[:], in_=bg[:], func=AF.Silu), reads=[bg.s], writes=[sg_t.s])
                P.dve(lambda e, hf=hf, h=h: e.scalar_tensor_tensor(
                    out=mixT[:, h, hf * 512:(hf + 1) * 512], in0=on_t[:], scalar=gn[:, l:l + 1], in1=sg_t[:],
                    op0=ALU.mult, op1=ALU.mult), reads=[on_t.s, sg_t.s, gn.s], writes=[mixs[h][hf]])

    def TTop(out, in0, in1, op, reads, writes):
        return P.dve(lambda e: e.tensor_tensor(out=out, in0=in0, in1=in1, op=op), reads=reads, writes=writes)

    def TSop(out, in0, s1, s2, op0, op1, reads, writes):
        if s2 is None:
            return P.dve(lambda e: e.tensor_scalar(out=out, in0=in0, scalar1=s1, scalar2=None, op0=op0), reads=reads, writes=writes)
        return P.dve(lambda e: e.tensor_scalar(out=out, in0=in0, scalar1=s1, scalar2=s2, op0=op0, op1=op1), reads=reads, writes=writes)

    def STTop(out, in0, scalar, in1, op0, op1, reads, writes):
        return P.dve(lambda e: e.scalar_tensor_tensor(out=out, in0=in0, scalar=scalar, in1=in1, op0=op0, op1=op1),
                     reads=reads, writes=writes)

    def ACTop(out, in_, func, reads, writes, bias=None, scale=None):
        kw = {}
        if bias is not None:
            kw["bias"] = bias
        if scale is not None:
            kw["scale"] = scale
        return P.act(lambda e: e.activation(out=out, in_=in_, func=func, **kw), reads=reads, writes=writes)

    def MM(out, lhsT, rhs, start, stop, reads, writes, tp=None):
        if tp is None:
            return P.pe(lambda e: e.matmul(out, lhsT, rhs, start=start, stop=stop), reads=reads, writes=writes)
        return P.pe(lambda e: e.matmul(out, lhsT, rhs, start=start, stop=stop, tile_position=tp), reads=reads, writes=writes)

    def CPY(out, in_, reads, writes):
        return P.dve(lambda e: e.tensor_copy(out=out, in_=in_), reads=reads, writes=writes)

    def MSET(out, val, reads, writes):
        return P.dve(lambda e: e.memset(out, val), reads=reads, writes=writes)

    def SDMA(out, in_, reads, writes):
        return P.dma("sp", lambda e: e.dma_start(out=out, in_=in_), reads=reads, writes=writes)

    NCK = 128
    SEG8 = [(0, 32), (32, 64), (64, 128)]
    TWO_PI = 2.0 * math.pi

    def s5(l, after_u=None):
        aphase()
        c3 = aalloc([1024], F32, "c3")
        asel = aalloc([8, 240], BF16, "asel")
        UT = aalloc([NG, NCK], BF16, "UT")
        Hb = aalloc([2, 2, 16, NCK], BF16, "Hb")
        par = TT(par_all[:, l], par_all.s)
        tab = aalloc([3, 16, 65], F32, "tab")
        hin = aalloc([2, 16, 2], F32, "hin")
        sloc = aalloc([2, 2, 16], F32, "sloc")
        hent = aalloc([2, 2, 16], F32, "hent")
        fst = aalloc([2, 2, 16, 2], F32, "fst")
        dsk = aalloc([NG], F32, "dsk")
        gat5 = aalloc([4, 2, 2, 16], F32, "gat5")
        sm = [aalloc([16], F32, "sm%d" % i) for i in range(8)]
        W0 = ar["off"]
        Bt = aalloc([NG, 2, 64], BF16, "Bt")
        bb = aalloc([2, 2, 16, 16], F32, "bb")
        craw = aalloc([2, 2, 16, 16], F32, "craw")
        pw = aalloc([2, 2, 16, 17], F32, "pw")
        R2 = ar["off"]
        uT = aalloc([4, T], BF16, "uT")
        cnat = aalloc([16, 64], F32, "cnat")
        prs = [aalloc([16, 17], F32, "prs%d" % i) for i in range(3)]
        pri = aalloc([16, 17], I32, "pri")
        afence()
        CtS = [wslots[1], wslots[2]]
        CtV = [bass.AP(w.t, 0, [[WSLOT, 128], [512, 8], [256, 2], [128, 2], [1, 128]]) for w in CtS]
        DtS = wslots[3]
        DtV = bass.AP(DtS.t, 0, [[WSLOT, 128], [128, NG], [1, 128]])

        def Ct_(gp):
            return CtV[gp // 8], gp % 8, CtS[gp // 8].s

        def _stop(k):
            if STAGE["s5_stop"] <= k:
                for kk in range(4, 8):
                    for hh in range(2):
                        MSET(mixT[:, kk, hh * 512:(hh + 1) * 512], 0.0, [], [mixs[kk][hh]])
                return True
            return False

        SDMA(c3[:], cst3_in, [], [c3.s])
        for g8 in range(8):
            TSop(asel[:, g8, :], c3[:, 0:240], c3[:, 240 + g8:241 + g8], None, ALU.mult, None, [c3.s], [asel.s])
        EV = c3[:, 608:625]
        K8 = c3[:, 640:705]
        R_even = c3[:, 480:544]
        R_odd = c3[:, 544:608]
        M5 = c3[:, 768:1024]
        wu, wus = wload(w_in[l][:, 2560:3072].rearrange("(k p) c -> p k c", p=128), KD, 512, slot=0)
        for ct in range(4):
            for hf in range(2):
                bk = proj_feat(wu, wus, ct * 128, hf)
                ACTop(uT[:, ct, hf * 512:(hf + 1) * 512], bk[:], AF.Copy, [bk.s], [uT.s])
        for g0 in range(0, NG, 4):
            bk = psum()
            for gi in range(4):
                g = g0 + gi
                ct, g8 = g // 8, g % 8
                for s_ in range(8):
                    MM(bk[:, gi * 128:(gi + 1) * 128], asel[:, g8, 112 - 16 * s_:240 - 16 * s_],
                       uT[:, ct, s_:T:8], (gi == 0 and s_ == 0), (s_ == 7), [asel.s, uT.s], [bk.s])
            ACTop(UT[:, g0:g0 + 4, :], bk[:].rearrange("p (g n) -> p g n", n=128), AF.Copy, [bk.s], [UT.s])
        if after_u is not None:
            after_u()

        for d in range(2):
            for ri, src in enumerate((s5_b_re, s5_b_im)):
                SDMA(bb[:, d, ri], src[l, d].rearrange("(gp g2) p c -> (g2 p) gp c", g2=2), [], [bb.s])
            SDMA(hin[:, d], bass.AP(st_s5.tensor, st_s5[l, d].offset, [[2, 128], [256, 16], [1, 2]]), [], [hin.s])
        for s_ in range(8):
            SDMA(dsk[16 * s_:16 * s_ + 16, :], s5_d[l].rearrange("g c -> c g"), [], [dsk.s])
        for d in range(2):
            for ri, src in enumerate((s5_c_re, s5_c_im)):
                x0 = (d * 2 + ri) * 4
                SDMA(cnat[:, x0:x0 + 4, :], src[l, d].rearrange("(ct g8) c p -> (g8 c) ct p", g8=8), [], [cnat.s])
        for d in range(2):
            for ri in range(2):
                bk = psum()
                for ct in range(4):
                    x = (d * 2 + ri) * 4 + ct
                    MM(bk[0:64, ct * 64:(ct + 1) * 64], cnat[:, x, :], R_even, True, True, [cnat.s, c3.s], [bk.s], tp=(0, 0))
                    MM(bk[64:128, ct * 64:(ct + 1) * 64], cnat[:, x, :], R_odd, True, True, [cnat.s, c3.s], [bk.s], tp=(0, 64))
                ACTop(craw[:, d, ri].rearrange("p a b -> p (a b)"), bk[:, 0:256], AF.Copy, [bk.s], [craw.s])

        if _stop(1):
            return
        for d in range(2):
            lr, li, dt_, a_, th_ = (par[:, d, i, :] for i in range(5))
            TSop(lr, lr, -1e-4, None, ALU.min, None, [par.s], [par.s])
            ACTop(dt_, dt_, AF.Exp, [par.s], [par.s])
            TTop(a_, lr, dt_, ALU.mult, [par.s], [par.s])
            TTop(th_, li, dt_, ALU.mult, [par.s], [par.s])
            TSop(th_, th_, 1.0 / TWO_PI, None, ALU.mult, None, [par.s], [par.s])

        def powers(out_r, out_i, out_m, a_ap, th_ap, evals, ng_, ne, tr, ti_, tk_i, tk_f, rs, ws):
            sh = [128, ng_, ne]
            ev_b = evals.unsqueeze(1).broadcast_to(sh)
            TTop(tr, th_ap.unsqueeze(2).broadcast_to(sh), ev_b, ALU.mult, rs + ws, ws)
            TSop(ti_, tr, 0.25, None, ALU.add, None, ws, ws)
            for (dst, src) in ((out_i, tr), (out_r, ti_)):
                CPY(tk_i, src, ws, ws)
                CPY(tk_f, tk_i, ws, ws)
                TTop(tk_f, src, tk_f, ALU.subtract, ws, ws)
                ACTop(dst, tk_f, AF.Sin, ws, ws, scale=TWO_PI)
            TTop(tr, a_ap.unsqueeze(2).broadcast_to(sh), ev_b, ALU.mult, rs + ws, ws)
            ACTop(out_m, tr, AF.Exp, ws, ws)

        for d in range(2):
            ws = [pw.s, pri.s] + [p_.s for p_ in prs]
            tkf_ = cnat[:].rearrange("p a b -> p (a b)")[:, 0:272].rearrange("p (a b) -> p a b", b=17)
            powers(pw[:, d, 0], pw[:, d, 1], prs[2][:], par[:, d, 3, :], par[:, d, 4, :], EV, 16, 17, prs[0][:], prs[1][:],
                   pri[:], tkf_, [par.s, c3.s, craw.s], ws + [cnat.s])
            TTop(pw[:, d, 0], pw[:, d, 0], prs[2][:], ALU.mult, ws, ws)
            TTop(pw[:, d, 1], pw[:, d, 1], prs[2][:], ALU.mult, ws, ws)

        for d in range(2):
            lr, li = par[:, d, 0, :], par[:, d, 1, :]
            abr, abi = pw[:, d, 0, :, 9], pw[:, d, 1, :, 9]
            nr, den, zr, zi, t1, t2 = (sm[i][:] for i in range(6))
            ws = [s_.s for s_ in sm]
            rs = [par.s, pw.s] + ws
            TSop(nr, abr, -1.0, None, ALU.add, None, rs, ws)
            TTop(t1, lr, lr, ALU.mult, rs, ws)
            TTop(t2, li, li, ALU.mult, rs, ws)
            TTop(den, t1, t2, ALU.add, rs, ws)
            P.dve(lambda e, den=den: e.reciprocal(out=den, in_=den), reads=rs, writes=ws)
            TTop(t1, nr, lr, ALU.mult, rs, ws)
            TTop(t2, abi, li, ALU.mult, rs, ws)
            TTop(zr, t1, t2, ALU.add, rs, ws)
            TTop(zr, zr, den, ALU.mult, rs, ws)
            TTop(t1, abi, lr, ALU.mult, rs, ws)
            TTop(t2, nr, li, ALU.mult, rs, ws)
            TTop(zi, t1, t2, ALU.subtract, rs, ws)
            TTop(zi, zi, den, ALU.mult, rs, ws)
            zrb = zr.unsqueeze(2).broadcast_to([128, 16, 16])
            zib = zi.unsqueeze(2).broadcast_to([128, 16, 16])
            cf = cnat[:].rearrange("p a b -> p (a b)")
            t3 = cf[:, 0:256].rearrange("p (a b) -> p a b", b=16)
            t4 = cf[:, 256:512].rearrange("p (a b) -> p a b", b=16)
            t5 = cf[:, 512:768].rearrange("p (a b) -> p a b", b=16)
            br_, bi_ = bb[:, d, 0], bb[:, d, 1]
            rs2 = rs + [bb.s, cnat.s, craw.s]
            ws2 = [bb.s, cnat.s]
            TTop(t3, br_, zrb, ALU.mult, rs2, ws2)
            TTop(t4, bi_, zib, ALU.mult, rs2, ws2)
            TTop(t5, br_, zib, ALU.mult, rs2, ws2)
            TTop(t3, t3, t4, ALU.subtract, rs2, ws2)
            TTop(t4, bi_, zrb, ALU.mult, rs2, ws2)
            TTop(bi_, t4, t5, ALU.add, rs2, ws2)
            CPY(br_, t3, rs2, ws2)

        if l == 0:
            dbg("par", par[:].rearrange("p a b c -> p (a b c)"), par.s, [128, 160])
            dbg("bb", bb[:].rearrange("p a b c d -> p (a b c d)"), bb.s, [128, 1024])
            dbg("pw", pw[:].rearrange("p a b c d -> p (a b c d)"), pw.s, [128, 1088])
            dbg("craw", craw[:].rearrange("p a b c d -> p (a b c d)"), craw.s, [128, 1024])
        def lifted(dst_r, dst_i, coef_r, coef_i, d, e_idx, conj_sign, gp0, ws):
            sh = [128, 4, 8, 16]
            pr = pw[:, d, 0, gp0:gp0 + 4, e_idx].unsqueeze(3).broadcast_to(sh)
            pi_ = pw[:, d, 1, gp0:gp0 + 4, e_idx].unsqueeze(3).broadcast_to(sh)
            cr = coef_r[:, gp0:gp0 + 4, :].unsqueeze(2).broadcast_to(sh)
            ci = coef_i[:, gp0:gp0 + 4, :].unsqueeze(2).broadcast_to(sh)
            t1 = LA[:].rearrange("p a (j c) -> p a j c", c=16)
            t2 = LB[:].rearrange("p a (j c) -> p a j c", c=16)
            rs = [pw.s, bb.s, craw.s, LA.s, LB.s]
            TTop(t1, cr, pr, ALU.mult, rs, [LA.s])
            TTop(t2, ci, pi_, ALU.mult, rs, [LB.s])
            TTop(dst_r.rearrange("p a (j c) -> p a j c", c=16), t1, t2, ALU.subtract, rs, ws)
            TTop(t1, cr, pi_, ALU.mult, rs, [LA.s])
            TTop(t2, ci, pr, ALU.mult, rs, [LB.s])
            if conj_sign > 0:
                TTop(dst_i.rearrange("p a (j c) -> p a j c", c=16), t1, t2, ALU.add, rs, ws)
            else:
                STTop(dst_i.rearrange("p a (j c) -> p a j c", c=16), t1, -1.0, t2, ALU.mult, ALU.subtract, rs, ws)

        E_B = [slice(15, 7, -1), slice(8, 16)]
        E_C = [slice(9, 17), slice(16, 8, -1)]
        E_N = [slice(7, None, -1), slice(0, 8)]

        if _stop(4):
            return
        old_r2 = [uT.s, cnat.s, pri.s] + [p_.s for p_ in prs]
        ar["off"] = R2
        XR = aalloc([4, NCK], F32, "XR")
        XI = aalloc([4, NCK], F32, "XI")
        A1 = aalloc([4, NCK], F32, "A1")
        B2 = aalloc([4, NCK], F32, "B2")
        C2 = aalloc([4, NCK], F32, "C2")
        RC = aalloc([4, NCK], F32, "RC")
        LA = aalloc([4, 128], F32, "LA2")
        LB = aalloc([4, 128], F32, "LB2")
        mnat = [aalloc([4, 128], BF16, "mnat2_%d" % i) for i in range(2)]
        XRb = aalloc([4, NCK], F32, "XRb")
        XIb = aalloc([4, NCK], F32, "XIb")
        new_r2 = [XR.s, XI.s, A1.s, B2.s, C2.s, RC.s, LA.s, LB.s, mnat[0].s, mnat[1].s, XRb.s, XIb.s]
        tsc = [A1, B2, C2]
        tsi = RC
        P.dve(lambda e: e.memset(fence_t[:], 0.0), reads=[], writes=old_r2 + new_r2 + [fence_t.s])

        def tables(d, tsc, tsi):
            ws = [tab.s, tsi.s] + [t_.s for t_ in tsc]
            for q in range(4):
                g_ = slice(q * 4, q * 4 + 4)
                powers(tab[:, 0, g_, :], tab[:, 1, g_, :], tab[:, 2, g_, :], par[:, d, 3, g_], par[:, d, 4, g_], K8, 4, 65,
                       tsc[0][:, :, 0:65], tsc[1][:, :, 0:65], tsi[:, :, 0:65].bitcast(I32), tsc[2][:, :, 0:65], [par.s, c3.s], ws)

        def seg_views(buf, gsl, n0, n1, d, shift):
            if d == 0:
                if shift == 0:
                    return buf[:, gsl, n0:n1]
                return buf[:, gsl, n0 + 1:n1] if shift > 0 else buf[:, gsl, n0:n1 - 1]
            lo = None if n0 == 0 else n0 - 1
            if shift == 0:
                return buf[:, gsl, n1 - 1:lo:-1]
            if shift > 0:
                return buf[:, gsl, n1 - 2:lo:-1]
            return buf[:, gsl, n1 - 1:n0:-1]

        for d in range(2):
            tables(d, tsc, tsi)
            if l == 0 and d == 0:
                dbg("tab", tab[:].rearrange("p a b c -> p (a b c)"), tab.s, [128, 3 * 16 * 65])
            if _stop(4.2):
                return
            def front(q, XR, XI):
                gp0 = q * 4
                tsl = slice(gp0, gp0 + 4)
                lifted(mnat[0][:], mnat[1][:], bb[:, d, 0], bb[:, d, 1], d, E_B[d], +1, gp0, [mnat[0].s, mnat[1].s])
                for ri in range(2):
                    bk = psum()
                    for gl in range(4):
                        MM(bk[:, gl * 128:(gl + 1) * 128], mnat[ri][:, gl, :], ident_b[:], True, True,
                           [mnat[ri].s, ident_b.s], [bk.s])
                    ACTop(Bt[:, 2 * gp0:2 * gp0 + 8, ri, :], bk[:].rearrange("p (g q) -> p g q", q=64), AF.Copy, [bk.s], [Bt.s])
                for gl in range(4):
                    gp = gp0 + gl
                    bk = psum()
                    for g2 in range(2):
                        g = 2 * gp + g2
                        for ri in range(2):
                            MM(bk[64 * g2:64 * g2 + 64, ri * 128:(ri + 1) * 128], Bt[:, g, ri, :], UT[:, g, :], True, True,
                               [Bt.s, UT.s], [bk.s], tp=(0, 64 * g2))
                    ACTop(XR[:, gl, :], bk[:, 0:128], AF.Copy, [bk.s], [XR.s])
                    ACTop(XI[:, gl, :], bk[:, 128:256], AF.Copy, [bk.s], [XI.s])

            def back(q, XR, XI):
                gp0 = q * 4
                tsl = slice(gp0, gp0 + 4)
                CPY(RC[:], tab[:, 2, tsl, 1:2].broadcast_to([128, 4, NCK]), [tab.s], [RC.s])
                for (n0, n1) in SEG8:
                    first = n0 if d == 0 else n1 - 1
                    MSET(RC[:, :, first:first + 1], 0.0, [RC.s], [RC.s])
                gsl = slice(0, 4)
                for (n0, n1) in SEG8:
                    L = n1 - n0
                    xr, xi = seg_views(XR, gsl, n0, n1, d, 0), seg_views(XI, gsl, n0, n1, d, 0)
                    a1, b2, c2 = seg_views(A1, gsl, n0, n1, d, 0), seg_views(B2, gsl, n0, n1, d, 0), seg_views(C2, gsl, n0, n1, d, 0)
                    cs_, sn_ = tab[:, 0, tsl, 1:L + 1], tab[:, 1, tsl, 1:L + 1]
                    rs = [XR.s, XI.s, tab.s, A1.s, B2.s, C2.s]
                    TTop(a1, xr, cs_, ALU.mult, rs, [A1.s])
                    TTop(c2, xi, sn_, ALU.mult, rs, [C2.s])
                    TTop(a1, a1, c2, ALU.add, rs, [A1.s])
                    TTop(b2, xi, cs_, ALU.mult, rs, [B2.s])
                    TTop(c2, xr, sn_, ALU.mult, rs, [C2.s])
                    TTop(b2, b2, c2, ALU.subtract, rs, [B2.s])


                def fl(t_):
                    v = t_[:].rearrange("p a b -> p (a b)")
                    return v if d == 0 else v[:, ::-1]
                o_r, o_i, i_r, i_i, cf_ = fl(XR), fl(XI), fl(A1), fl(B2), fl(RC)
                P.dve(lambda e, o_r=o_r, i_r=i_r, cf_=cf_: e.tensor_tensor_scan(out=o_r, data0=cf_, data1=i_r, initial=0.0,
                                                                                  op0=ALU.mult, op1=ALU.add),
                      reads=[RC.s, A1.s], writes=[XR.s])
                P.dve(lambda e, o_i=o_i, i_i=i_i, cf_=cf_: e.tensor_tensor_scan(out=o_i, data0=cf_, data1=i_i, initial=0.0,
                                                                                  op0=ALU.mult, op1=ALU.add),
                      reads=[RC.s, B2.s], writes=[XI.s])
                for si_, (n0, n1) in enumerate(SEG8):
                    L = n1 - n0
                    gr, gi_ = seg_views(XR, gsl, n0, n1, d, -1), seg_views(XI, gsl, n0, n1, d, -1)
                    a1, b2 = seg_views(A1, gsl, n0, n1, d, 1), seg_views(B2, gsl, n0, n1, d, 1)
                    hr = seg_views(Hb[:, d, 0], tsl, n0, n1, d, 1)
                    hi = seg_views(Hb[:, d, 1], tsl, n0, n1, d, 1)
                    cs_, sn_ = tab[:, 0, tsl, 1:L], tab[:, 1, tsl, 1:L]
                    rs = [XR.s, XI.s, tab.s, A1.s, B2.s]
                    TTop(a1, gr, cs_, ALU.mult, rs, [A1.s])
                    TTop(b2, gi_, sn_, ALU.mult, rs, [B2.s])
                    TTop(hr, a1, b2, ALU.subtract, rs, [Hb.s])
                    TTop(a1, gr, sn_, ALU.mult, rs, [A1.s])
                    TTop(b2, gi_, cs_, ALU.mult, rs, [B2.s])
                    TTop(hi, a1, b2, ALU.add, rs, [Hb.s])
                    first = n0 if d == 0 else n1 - 1
                    MSET(Hb[:, d, :, tsl, first:first + 1], 0.0, [Hb.s], [Hb.s])
                    last = n1 - 1 if d == 0 else n0
                    glr, gli = XR[:, :, last], XI[:, :, last]
                    cL, sL = tab[:, 0, tsl, L], tab[:, 1, tsl, L]
                    t1, t2 = sm[6][:, 0:4], sm[7][:, 0:4]
                    if si_ < 2:
                        dr, di = fst[:, si_, d, tsl, 0], fst[:, si_, d, tsl, 1]
                        dsl = fst.s
                    else:
                        dr, di = sloc[:, d, 0, tsl], sloc[:, d, 1, tsl]
                        dsl = sloc.s
                    rs = [XR.s, XI.s, tab.s, sm[6].s, sm[7].s, dsl]
                    TTop(t1, glr, cL, ALU.mult, rs, [sm[6].s])
                    TTop(t2, gli, sL, ALU.mult, rs, [sm[7].s])
                    TTop(dr, t1, t2, ALU.subtract, rs, [dsl])
                    TTop(t1, glr, sL, ALU.mult, rs, [sm[6].s])
                    TTop(t2, gli, cL, ALU.mult, rs, [sm[7].s])
                    TTop(di, t1, t2, ALU.add, rs, [dsl])

            xbufs = [(XR, XI), (XRb, XIb)]
            front(0, *xbufs[0])
            for q in range(4):
                if q < 3:
                    front(q + 1, *xbufs[(q + 1) % 2])
                back(q, *xbufs[q % 2])
        if _stop(4.8):
            return
        for seq in range(2):
            for d in range(2):
                SDMA(bass.AP(ns_s5.tensor, ns_s5[seq, l, d].offset, [[2, 128], [256, 16], [1, 2]]), fst[:, seq, d], [fst.s], [])

        if _stop(5):
            return
        ccs_in, ccs_out = Slot("cc5in"), Slot("cc5out")
        SDMA(cc5_in[l], sloc[:].rearrange("p a b c -> p (a b c)"), [sloc.s], [ccs_in])
        P.dma("pool", lambda e: e.collective_compute("AllGather", ALU.bypass, replica_groups=[[0, 1, 2, 3], [4, 5, 6, 7]],
                                                     ins=[cc5_in[l]], outs=[cc5_out[l]]),
              reads=[ccs_in], writes=[ccs_out], inc=1)
        SDMA(gat5[:].rearrange("p r a b c -> p r (a b c)"), cc5_out[l].rearrange("(r p) c -> p r c", p=128), [ccs_out], [gat5.s])

        old_r2 = new_r2
        ar["off"] = R2
        LA = aalloc([4, 128], F32, "LA")
        LB = aalloc([4, 128], F32, "LB")
        mnat = [aalloc([4, 128], BF16, "mnat%d" % i) for i in range(2)]
        Dacc = aalloc([8, 128], F32, "Dacc")
        new_r2 = [LA.s, LB.s, mnat[0].s, mnat[1].s, Dacc.s]
        P.dve(lambda e: e.memset(fence_t[:], 0.0), reads=[], writes=old_r2 + new_r2 + [fence_t.s])

        for d in range(2):
            for q in range(4):
                gp0 = q * 4
                cv, g8_, cs_ = Ct_(gp0)
                lifted(cv[:, g8_:g8_ + 4, d, 0, :], cv[:, g8_:g8_ + 4, d, 1, :], craw[:, d, 0], craw[:, d, 1], d, E_C[d], -1, gp0, [cs_])
        for q in range(4):
            gp0 = q * 4
            for d in range(2):
                lifted(mnat[0][:], mnat[1][:], bb[:, d, 0], bb[:, d, 1], d, E_N[d], +1, gp0, [mnat[0].s, mnat[1].s])
                for gi in range(8):
                    gl, g2 = gi // 2, gi % 2
                    gp = gp0 + gl
                    cv, g8_, cs_ = Ct_(gp)
                    bk = psum()
                    for ri in range(2):
                        MM(bk[:, 0:128], mnat[ri][64 * g2:64 * g2 + 64, gl, :], cv[64 * g2:64 * g2 + 64, g8_, d, ri, :],
                           (ri == 0), (ri == 1), [mnat[ri].s, cs_], [bk.s])
                    msk = M5[:, d * 128:(d + 1) * 128]
                    if d == 0:
                        TTop(Dacc[:, gi, :], bk[:, 0:128], msk, ALU.mult, [bk.s, c3.s], [Dacc.s])
                    else:
                        tmpv = LA[:, gl, :] if g2 == 0 else LB[:, gl, :]
                        tmps = LA.s if g2 == 0 else LB.s
                        TTop(tmpv, bk[:, 0:128], msk, ALU.mult, [bk.s, c3.s, mnat[0].s, mnat[1].s], [tmps])
                        TTop(Dacc[:, gi, :], Dacc[:, gi, :], tmpv, ALU.add, [tmps, Dacc.s], [Dacc.s])
                        g = 2 * gp + g2
                        STTop(DtV[:, g, :], ident_f, dsk[:, g:g + 1], Dacc[:, gi, :], ALU.mult, ALU.add,
                              [cst_f.s, dsk.s, Dacc.s], [DtS.s])

        old_w0 = [Bt.s, bb.s, craw.s, pw.s] + new_r2 + [XR.s, XI.s, A1.s, B2.s, C2.s, RC.s]
        ar["off"] = W0
        DH = aalloc([16, 2, 2, 64], BF16, "DH")
        W1 = ar["off"]
        tsc2 = [aalloc([4, 128], F32, "tscb%d" % i) for i in range(3)]
        tsi2 = aalloc([4, 128], F32, "tsib")
        TRt = aalloc([16, 64], F32, "TRt")
        TIt = aalloc([16, 64], F32, "TIt")
        U1 = aalloc([16, 64], F32, "U1")
        U2 = aalloc([16, 64], F32, "U2")
        pc = [aalloc([16], F32, "pc%d" % i) for i in range(6)]
        new_w0 = [DH.s, tsi2.s, TRt.s, TIt.s, U1.s, U2.s] + [t_.s for t_ in tsc2] + [p_.s for p_ in pc]
        P.dve(lambda e: e.memset(fence_t[:], 0.0), reads=[], writes=old_w0 + new_w0 + [fence_t.s])

        def cmul(dr, di, ar_, ai_, br_, bi_, t1, t2, rs, ws):
            TTop(t1, ar_, br_, ALU.mult, rs, ws)
            TTop(t2, ai_, bi_, ALU.mult, rs, ws)
            TTop(dr, t1, t2, ALU.subtract, rs, ws)
            TTop(t1, ar_, bi_, ALU.mult, rs, ws)
            TTop(t2, ai_, br_, ALU.mult, rs, ws)
            TTop(di, t1, t2, ALU.add, rs, ws)

        for d in range(2):
            tables(d, tsc2, tsi2)
            atr, ati, cr_, ci_, t1, t2 = (p_[:] for p_ in pc)
            ws = [p_.s for p_ in pc] + [sm[0].s, sm[1].s]
            rs = [tab.s, gat5.s, hin.s, hent.s, cst_f.s] + ws
            TTop(atr, tab[:, 0, :, 64], tab[:, 2, :, 64], ALU.mult, rs, ws)
            TTop(ati, tab[:, 1, :, 64], tab[:, 2, :, 64], ALU.mult, rs, ws)
            ranks = [0, 1, 2, 3] if d == 0 else [3, 2, 1, 0]
            CPY(cr_, hin[:, d, :, 0], rs, ws)
            CPY(ci_, hin[:, d, :, 1], rs, ws)
            TSop(hent[:, d, 0], cr_, cst_f[:, 384 + ranks[0]:385 + ranks[0]], None, ALU.mult, None, rs, [hent.s])
            TSop(hent[:, d, 1], ci_, cst_f[:, 384 + ranks[0]:385 + ranks[0]], None, ALU.mult, None, rs, [hent.s])
            for i in range(3):
                r = ranks[i]
                nr_, ni_ = sm[0][:], sm[1][:]
                cmul(nr_, ni_, atr, ati, cr_, ci_, t1, t2, rs, ws)
                TTop(cr_, nr_, gat5[:, r, d, 0, :], ALU.add, rs, ws)
                TTop(ci_, ni_, gat5[:, r, d, 1, :], ALU.add, rs, ws)
                rn = ranks[i + 1]
                STTop(hent[:, d, 0], cr_, cst_f[:, 384 + rn:385 + rn], hent[:, d, 0], ALU.mult, ALU.add, rs, [hent.s])
                STTop(hent[:, d, 1], ci_, cst_f[:, 384 + rn:385 + rn], hent[:, d, 1], ALU.mult, ALU.add, rs, [hent.s])
            rs = [tab.s, hent.s, TRt.s, TIt.s, U1.s, U2.s]
            TTop(TRt[:], tab[:, 0, :, 0:64], tab[:, 2, :, 0:64], ALU.mult, rs, [TRt.s])
            TTop(TIt[:], tab[:, 1, :, 0:64], tab[:, 2, :, 0:64], ALU.mult, rs, [TIt.s])
            her = hent[:, d, 0].unsqueeze(2).broadcast_to([128, 16, 64])
            hei = hent[:, d, 1].unsqueeze(2).broadcast_to([128, 16, 64])
            dhr = DH[:, :, d, 0, :] if d == 0 else DH[:, :, d, 0, ::-1]
            dhi = DH[:, :, d, 1, :] if d == 0 else DH[:, :, d, 1, ::-1]
            TTop(U1[:], TRt[:], her, ALU.mult, rs, [U1.s])
            TTop(U2[:], TIt[:], hei, ALU.mult, rs, [U2.s])
            TTop(dhr, U1[:], U2[:], ALU.subtract, rs, [DH.s])
            TTop(U1[:], TRt[:], hei, ALU.mult, rs, [U1.s])
            TTop(U2[:], TIt[:], her, ALU.mult, rs, [U2.s])
            TTop(dhi, U1[:], U2[:], ALU.add, rs, [DH.s])

        if _stop(6):
            return
        old_w1 = new_w0[1:]
        ar["off"] = W1
        YA = aalloc([NG, NCK], BF16, "YA")
        yT = aalloc([4, T], BF16, "yT5")
        gl_t = [aalloc([512], F32, "gl%d" % i) for i in range(4)]
        new_w1 = [YA.s, yT.s] + [g_.s for g_ in gl_t]
        P.dve(lambda e: e.memset(fence_t[:], 0.0), reads=[], writes=old_w1 + new_w1 + [fence_t.s])

        for g0 in range(0, NG, 4):
            bk = psum()
            for gi in range(4):
                g = g0 + gi
                gp, g2 = g // 2, g % 2
                cv, g8_, cs_ = Ct_(gp)
                cols = slice(gi * 128, (gi + 1) * 128)
                MM(bk[:, cols], DtV[:, g, :], UT[:, g, :], (gi == 0), False, [DtS.s, UT.s], [bk.s])
                for d in range(2):
                    for ri in range(2):
                        MM(bk[:, cols], cv[64 * g2:64 * g2 + 64, g8_, d, ri, :], Hb[64 * g2:64 * g2 + 64, d, ri, gp, :], False, False,
                           [cs_, Hb.s], [bk.s])
                for d in range(2):
                    for ri in range(2):
                        MM(bk[:, gi * 128 + 64:(gi + 1) * 128], cv[64 * g2:64 * g2 + 64, g8_, d, ri, :],
                           DH[64 * g2:64 * g2 + 64, gp, d, ri, :], False, (d == 1 and ri == 1), [cs_, DH.s], [bk.s])
            xs_, sq_, u_, sg_ = gl_t
            ACTop(xs_[:], bk[:], AF.Copy, [bk.s], [xs_.s])
            ACTop(sq_[:], bk[:], AF.Square, [bk.s], [sq_.s])
            TSop(sq_[:], sq_[:], 0.044715, 1.0, ALU.mult, ALU.add, [sq_.s], [sq_.s])
            TTop(u_[:], sq_[:], xs_[:], ALU.mult, [sq_.s, xs_.s], [u_.s])
            ACTop(sg_[:], u_[:], AF.Sigmoid, [u_.s], [sg_.s], scale=2.0 * math.sqrt(2.0 / math.pi))
            TTop(YA[:, g0:g0 + 4, :].rearrange("p g n -> p (g n)"), xs_[:], sg_[:], ALU.mult, [xs_.s, sg_.s], [YA.s])

        for ct in range(4):
            for t0 in range(0, 8, 4):
                bk = psum()
                for ti in range(4):
                    t_ = t0 + ti
                    for g8 in range(8):
                        g = ct * 8 + g8
                        MM(bk[:, ti * 128:(ti + 1) * 128], asel[:, t_, 112 - 16 * g8:240 - 16 * g8],
                           YA[:, g, :], (ti == 0 and g8 == 0), (g8 == 7), [asel.s, YA.s], [bk.s])
                ACTop(yT[:, ct, :].rearrange("p (n t) -> p t n", t=8)[:, t0:t0 + 4, :],
                      bk[:].rearrange("p (t n) -> p t n", n=128), AF.Copy, [bk.s], [yT.s])

        wgl, wgls = wload(s5_w_glu[l].rearrange("(k p) c -> p k c", p=128), 4, 512, slot=0)
        for c2 in range(4):
            for hf in range(2):
                bk = psum()
                for ct in range(4):
                    MM(bk[:], wgl[:, ct, c2 * 128:(c2 + 1) * 128], yT[:, ct, hf * 512:(hf + 1) * 512], (ct == 0), (ct == 3),
                       [wgls, yT.s], [bk.s])
                sgl = gl_t[(c2 * 2 + hf) % 2]
                ACTop(sgl[:], bk[:], AF.Sigmoid, [bk.s], [sgl.s])
                TTop(mixT[:, 4 + c2, hf * 512:(hf + 1) * 512], yT[:, c2, hf * 512:(hf + 1) * 512], sgl[:], ALU.mult,
                     [yT.s, sgl.s], [mixs[4 + c2][hf]])


    for l in range(DEPTH):
        compute_mod(l, 0)

        def mod_rest(l=l):
            compute_mod(l, 1, slots=[1, 2, 3, 1])
            compute_mod(l, 2, slots=[2, 3, 1, 2])
        norm_mod(l, 0, 0)
        if STAGE["s5"]:
            s5(l, after_u=mod_rest)
        else:
            mod_rest()
            for k in range(4, 8):
                for h in range(2):
                    P.dve(lambda e, k=k, h=h: e.memset(mixT[:, k, h * 512:(h + 1) * 512], 0.0), writes=[mixs[k][h]])
        if STAGE["hg"]:
            hgrn(l)
        else:
            for k in range(0, 4):
                for h in range(2):
                    P.dve(lambda e, k=k, h=h: e.memset(mixT[:, k, h * 512:(h + 1) * 512], 0.0), writes=[mixs[k][h]])
        residual_proj(w_out[l], l, 16, mixT, mixs, KD)
        norm_mod(l, 1, 24)
        ffn(l)

    nfin32 = sb([128, KD], F32, "nfin32")
    P.dve(lambda e: e.tensor_scalar(out=nfin32[:], in0=nfin[:], scalar1=32.0, scalar2=None, op0=ALU.mult),
          reads=[nfin.s], writes=[nfin32.s])
    aphase()
    ytok = [aalloc([D], F32, "ytok%d" % j) for j in range(4)]
    yT = aalloc([512], F32, "yT")
    afence()
    for h in range(2):
        rms_stats(h)
        for k in range(KD):
            P.dve(lambda e, k=k, h=h: e.scalar_tensor_tensor(
                out=yT[:], in0=xT[:, k, h * 512:(h + 1) * 512], scalar=nfin32[:, k:k + 1], in1=rstd[:],
                op0=ALU.mult, op1=ALU.mult), reads=[xs[k][h], nfin32.s, rstd.s], writes=[yT.s])
            bk = psum()
            for j in range(4):
                P.pe(lambda e, bk=bk, j=j: e.transpose(bk[:, j * 128:(j + 1) * 128], yT[:, j * 128:(j + 1) * 128], ident_f),
                     reads=[yT.s, cst_f.s], writes=[bk.s])
            for j in range(4):
                P.act(lambda e, bk=bk, j=j, k=k: e.activation(out=ytok[j][:, k * 128:(k + 1) * 128],
                                                               in_=bk[:, j * 128:(j + 1) * 128], func=AF.Copy),
                      reads=[bk.s], writes=[ytok[j].s])
        for j in range(4):
            tt = h * 4 + j
            P.dma("sp", lambda e, j=j, tt=tt: e.dma_start(out=y_out[tt * 128:(tt + 1) * 128, :], in_=ytok[j][:]),
                  reads=[ytok[j].s], writes=[])

    with nc.allow_non_contiguous_dma(reason="small strided parameter loads"):
        P.emit(st)
    st.close()
    return nc


def _consts(core):
    q = core % 4
    c = np.zeros((128, 512), np.float32)
    c[:, 0:128] = np.eye(128, dtype=np.float32)
    j = np.arange(128)[:, None]
    i = np.arange(128)[None, :]
    same = (j // CH) == (i // CH)
    c[:, 128:256] = (same & (j <= i)).astype(np.float32)
    c[:, 256:384] = (same & (j >= i)).astype(np.float32)
    c[:, 384 + q] = 1.0
    nf = D // 4
    p = np.arange(128, dtype=np.float32)
    for par in range(2):
        kf = par * 128 + p
        c[:, 388 + par] = (1.0 / (np.float32(10000.0) ** (kf / np.float32(nf)))).astype(np.float32)
    t = q * 512 + np.arange(512)
    c2 = np.zeros((128, 1024), np.float32)
    c2[:, 0:512] = (t // 64).astype(np.float32)[None, :]
    c2[:, 512:1024] = (t % 64).astype(np.float32)[None, :]
    c3 = np.zeros((128, 1024), np.float32)
    for p_ in range(128):
        c3[p_, 112 + p_ % 16] = 1.0
        c3[p_, 240 + p_ // 16] = 1.0
    for g4 in range(4):
        for cc in range(16):
            c3[(2 * g4) * 16 + cc, 480 + g4 * 16 + cc] = 1.0
            c3[(2 * g4 + 1) * 16 + cc, 544 + g4 * 16 + cc] = 1.0
    c3[:, 608:625] = np.arange(-8, 9, dtype=np.float32)[None, :]
    c3[:, 640:705] = (8.0 * np.arange(65, dtype=np.float32))[None, :]
    sI = (np.arange(128) // 16)[:, None]
    tI = (np.arange(128) // 16)[None, :]
    c3[:, 768:896] = (sI <= tI).astype(np.float32)
    c3[:, 896:1024] = (sI >= tI).astype(np.float32)
    return c, c2, c3


_NC_CACHE = {}


def kernel(**inp):
    inp = {k: np.asarray(v) for k, v in inp.items()}
    if "nc" not in _NC_CACHE:
        _NC_CACHE["nc"] = build_program()
    nc = _NC_CACHE["nc"]
    xp = inp["x_prompt"]
    xsm = inp["x_sample"]
    in_maps = []
    shared = {k: np.ascontiguousarray(inp[k], dtype=np.float32) for k in
              ("w_mod", "b_mod", "norm_mix", "norm_ffn", "norm_final", "w_in", "w_out", "w_gate", "w_up", "w_down",
               "hg_lb_logits", "hg_norm", "s5_lam_re", "s5_lam_im", "s5_log_dt", "s5_b_re", "s5_b_im", "s5_c_re", "s5_c_im",
               "s5_d", "s5_w_glu")}
    for core in range(8):
        b, q = core // 4, core % 4
        x = np.concatenate([xp[2 * core], xp[2 * core + 1], xsm[b, q * 512:(q + 1) * 512]], axis=0)
        cond = np.stack([inp["c_ctx"], inp["c"][b]], axis=0)
        c1, c2, c3 = _consts(core)
        m = dict(shared)
        m.update({"x": np.ascontiguousarray(x, dtype=np.float32), "cond": np.ascontiguousarray(cond, dtype=np.float32),
                  "st_hg": np.ascontiguousarray(inp["state_hgrn"][b], dtype=np.float32), "cst": c1, "cst2": c2, "cst3": c3,
                  "st_s5": np.ascontiguousarray(inp["state_s5"][b], dtype=np.float32)})
        in_maps.append(m)
    res = run_bass_kernel_spmd(nc, in_maps, core_ids=list(range(8)))
    outs = res.results
    _DBG["outs"] = outs
    y_prompt = np.zeros_like(xp)
    y_sample = np.zeros_like(xsm)
    ns_hg = np.zeros((16, DEPTH, 2, NH, DK, DK), np.float32)
    ns_s5 = np.zeros((16, DEPTH, 2, NG, SP, 2), np.float32)
    for core in range(8):
        b, q = core // 4, core % 4
        y = outs[core]["y"]
        y_prompt[2 * core] = y[0:256]
        y_prompt[2 * core + 1] = y[256:512]
        y_sample[b, q * 512:(q + 1) * 512] = y[512:1024]
        ns_hg[2 * core:2 * core + 2] = outs[core]["ns_hg"]
        ns_s5[2 * core:2 * core + 2] = outs[core]["ns_s5"]
    return (y_prompt, y_sample, ns_hg, ns_s5)
```

```python
import math
from contextlib import ExitStack

import numpy as np
import concourse.bass as bass
import concourse.mybir as mybir
from concourse.bass_utils import run_bass_kernel_spmd

F32 = mybir.dt.float32
BF16 = mybir.dt.bfloat16
AF = mybir.ActivationFunctionType
ALU = mybir.AluOpType

ENGS = ("pe", "act", "dve", "pool", "sp")
SAME_ENGINE_RAW_DIST = 2


class Slot:
    __slots__ = ("name", "w", "r", "al")

    def __init__(self, name):
        self.name = name
        self.w = None
        self.r = []
        self.al = [self]


def alias(*slots):
    grp = []
    for s in slots:
        for a in s.al:
            if a not in grp:
                grp.append(a)
    for s in grp:
        s.al = grp


class Op:
    __slots__ = ("eng", "fn", "deps", "raw", "dma", "idx", "milestone", "mcount", "dsem", "dval", "inc", "eidx")


class Prog:
    def __init__(self, nc, n_dma_sems=8, sync_same_engine=True):
        self.nc = nc
        self.ops = []
        self.n_dma_sems = n_dma_sems
        self.sync_same = sync_same_engine

    def add(self, eng, fn, reads=(), writes=(), dma=False, inc=16):
        op = Op()
        op.eng, op.fn, op.dma, op.inc = eng, fn, dma, inc
        op.deps = set()
        op.raw = set()
        op.milestone = False
        op.mcount = 0
        op.dsem = None
        op.dval = 0
        op.idx = len(self.ops)
        for s0 in reads:
            for s in s0.al:
                if s.w is not None:
                    op.deps.add(s.w)
                    op.raw.add(s.w)
        for s0 in writes:
            for s in s0.al:
                if s.w is not None:
                    op.deps.add(s.w)
                op.deps.update(s.r)
        for s in reads:
            s.r.append(op.idx)
        for s in writes:
            s.w = op.idx
            s.r = []
        op.deps.discard(op.idx)
        self.ops.append(op)
        return op

    def pe(self, fn, reads=(), writes=()):
        return self.add("pe", fn, reads, writes)

    def act(self, fn, reads=(), writes=()):
        return self.add("act", fn, reads, writes)

    def dve(self, fn, reads=(), writes=()):
        return self.add("dve", fn, reads, writes)

    def dma(self, eng, fn, reads=(), writes=(), inc=16):
        return self.add(eng, fn, reads, writes, dma=True, inc=inc)

    def emit(self, stack):
        nc = self.nc
        ops = self.ops
        ecount = {e: 0 for e in ENGS}
        for op in ops:
            op.eidx = ecount[op.eng]
            ecount[op.eng] += 1

        def needs_sync(op, dop):
            if dop.dma or op.dma or dop.eng != op.eng:
                return True
            if dop.eng == "pe" or not self.sync_same:
                return False
            return (dop.idx in op.raw) and (op.eidx - dop.eidx < SAME_ENGINE_RAW_DIST)

        self.needs_sync = needs_sync
        for op in ops:
            for d in op.deps:
                dop = ops[d]
                if dop.dma:
                    continue
                if not needs_sync(op, dop):
                    continue
                dop.milestone = True
        cnt = {e: 0 for e in ENGS}
        for op in ops:
            if not op.dma and op.milestone:
                cnt[op.eng] += 1
            op.mcount = cnt[op.eng]
        esem = {e: stack.enter_context(nc.semaphore("s_" + e)) for e in ENGS}
        dsems = {e: None for e in ENGS}
        dcount = {}
        dn = {e: 0 for e in ENGS}
        for op in ops:
            if op.dma:
                if dsems[op.eng] is None:
                    dsems[op.eng] = [stack.enter_context(nc.semaphore("d_%s_%d" % (op.eng, i)))
                                     for i in range(self.n_dma_sems)]
                k = dn[op.eng]
                dn[op.eng] += 1
                op.dsem = (op.eng, k % self.n_dma_sems)
                dcount[op.dsem] = dcount.get(op.dsem, 0) + op.inc
                op.dval = dcount[op.dsem]
        per = {e: [o for o in ops if o.eng == e] for e in ENGS}
        block = stack.enter_context(nc.Block())
        sync_same = self.sync_same

        def make(e):
            def body(eng):
                waited = {}

                def wait(key, sem, val):
                    if waited.get(key, 0) >= val:
                        return
                    waited[key] = val
                    eng.wait_ge(sem, val)

                for op in per[e]:
                    for d in sorted(op.deps):
                        dop = ops[d]
                        if dop.dma:
                            wait(("d",) + dop.dsem, dsems[dop.dsem[0]][dop.dsem[1]], dop.dval)
                        else:
                            if not self.needs_sync(op, dop):
                                continue
                            wait(("e", dop.eng), esem[dop.eng], dop.mcount)
                    if op.dma:
                        prev = op.dval - op.inc
                        if prev > 0:
                            wait(("d",) + op.dsem, dsems[op.dsem[0]][op.dsem[1]], prev)
                        ins = op.fn(eng)
                        ins.then_inc(dsems[op.dsem[0]][op.dsem[1]], op.inc)
                    else:
                        ins = op.fn(eng)
                        if op.milestone:
                            ins.then_inc(esem[e], 1)
                if dsems[e] is not None:
                    for i, s in enumerate(dsems[e]):
                        v = dcount.get((e, i), 0)
                        if v:
                            wait(("d", e, i), s, v)
            return body

        block.tensor(make("pe"))
        block.scalar(make("act"))
        block.vector(make("dve"))
        block.gpsimd(make("pool"))
        block.sync(make("sp"))


D = 1024
KD = 8
T = 1024
NT = 8
DEPTH = 2
HG_W = 512
NH = 4
DK = 128
S5_W = 512
NG = 32
SP = 64
IN_W = 3072
DFF = 2816
NF = 22
EPS = 1e-6
CH = 32
NCH = T // CH
SEGS = [(0, 8), (8, 16), (16, 32)]
WSLOT = 4096
N_WSLOT = 4
CCW = 1096
ARENA_B = 91 * 1024 // 2

I32 = mybir.dt.int32
STAGE = {"hg": True, "s5": True, "s5_stop": 99}
DEBUG = False
_DBG = {}


class TT:
    def __init__(self, t, slot):
        self.t = t
        self.s = slot

    def __getitem__(self, k):
        return self.t[k]


def build_program():
    nc = bass.Bass("TRN2", target_bir_lowering=False)
    st = ExitStack()
    P = Prog(nc)

    def din(name, shape):
        return nc.dram_tensor(name, list(shape), F32, kind="ExternalInput").ap()

    def dout(name, shape):
        return nc.dram_tensor(name, list(shape), F32, kind="ExternalOutput").ap()

    x_in = din("x", [T, D])
    cond_in = din("cond", [2, D])
    w_mod = din("w_mod", [DEPTH, D, 6 * D])
    b_mod = din("b_mod", [DEPTH, 6 * D])
    norm_mix = din("norm_mix", [DEPTH, D])
    norm_ffn = din("norm_ffn", [DEPTH, D])
    norm_final = din("norm_final", [D])
    w_in = din("w_in", [DEPTH, D, IN_W])
    w_out = din("w_out", [DEPTH, D, D])
    w_gate = din("w_gate", [DEPTH, D, DFF])
    w_up = din("w_up", [DEPTH, D, DFF])
    w_down = din("w_down", [DEPTH, DFF, D])
    hg_lb = din("hg_lb_logits", [2, DEPTH, HG_W])
    hg_norm = din("hg_norm", [DEPTH, DK])
    st_hg = din("st_hg", [DEPTH, 2, NH, DK, DK])
    cst = din("cst", [128, 512])
    cst2_in = din("cst2", [128, 1024])
    cst3_in = din("cst3", [128, 1024])
    s5_lam_re = din("s5_lam_re", [DEPTH, 2, NG, SP])
    s5_lam_im = din("s5_lam_im", [DEPTH, 2, NG, SP])
    s5_log_dt = din("s5_log_dt", [DEPTH, 2, NG])
    s5_b_re = din("s5_b_re", [DEPTH, 2, NG, SP, 16])
    s5_b_im = din("s5_b_im", [DEPTH, 2, NG, SP, 16])
    s5_c_re = din("s5_c_re", [DEPTH, 2, NG, 16, SP])
    s5_c_im = din("s5_c_im", [DEPTH, 2, NG, 16, SP])
    s5_d = din("s5_d", [DEPTH, NG, 16])
    s5_w_glu = din("s5_w_glu", [DEPTH, S5_W, S5_W])
    st_s5 = din("st_s5", [DEPTH, 2, NG, SP, 2])
    y_out = dout("y", [T, D])
    ns_hg = dout("ns_hg", [2, DEPTH, 2, NH, DK, DK])
    ns_s5 = dout("ns_s5", [2, DEPTH, 2, NG, SP, 2])
    cc5_in = [nc.dram_tensor("cc5_in%d" % i, [128, 64], F32, kind="Internal").ap() for i in range(DEPTH)]
    cc5_out = [nc.dram_tensor("cc5_out%d" % i, [4 * 128, 64], F32, kind="Internal").ap() for i in range(DEPTH)]
    cc_in = [nc.dram_tensor("cc_in%d" % i, [128, CCW], F32, kind="Internal").ap() for i in range(2 * DEPTH)]
    cc_out = [nc.dram_tensor("cc_out%d" % i, [4 * 128, CCW], F32, kind="Internal").ap() for i in range(2 * DEPTH)]

    _n = [0]
    dbg_list = []

    def dbg(name, ap, slot, shape, dtype=F32):
        if not DEBUG:
            return
        t = nc.dram_tensor("dbg_" + name, list(shape), dtype, kind="ExternalOutput").ap()
        P.dma("sp", lambda e: e.dma_start(out=t, in_=ap), reads=[slot], writes=[])

    def sb(shape, dtype, name=None):
        _n[0] += 1
        name = "sb_" + (name or "t%d" % _n[0])
        t = st.enter_context(nc.sbuf_tensor(name, list(shape), dtype))
        return TT(t, Slot(name))

    banks = [TT(st.enter_context(nc.psum_tensor("ps%d" % i, [128, 512], F32)), Slot("ps%d" % i)) for i in range(8)]
    _pb = [0]

    def psum():
        b = banks[_pb[0] % 6]
        _pb[0] += 1
        return b

    wslots = [sb([128, WSLOT], BF16, "wslot%d" % i) for i in range(N_WSLOT)]
    _ws = [0]

    def wload(src_ap, a, b, slot=None):
        if slot is None:
            w = wslots[_ws[0] % N_WSLOT]
            _ws[0] += 1
        else:
            w = wslots[slot]
        view = bass.AP(w.t, 0, [[WSLOT, 128], [b, a], [1, b]])
        P.dma("pool", lambda e, v=view, s=src_ap: e.dma_start(out=v, in_=s), writes=[w.s])
        return view, w.s

    arena_t = st.enter_context(nc.sbuf_tensor("arena", [128, ARENA_B], BF16))
    ar = {"off": 0, "live": []}

    def aalloc(free_shape, dtype, name):
        n = 1
        for v in free_shape:
            n *= v
        nb = n * (1 if dtype == BF16 else 2)
        nb = (nb + 1) // 2 * 2
        off = ar["off"]
        assert off + nb <= ARENA_B, ("arena overflow", name, off, nb)
        ar["off"] = off + nb
        v = arena_t[:, off:off + nb]
        if dtype != BF16:
            v = v.bitcast(dtype)
        if len(free_shape) > 1:
            names = "abcdefg"[:len(free_shape)]
            kw = {names[i]: free_shape[i] for i in range(1, len(free_shape))}
            v = v.rearrange("p (%s) -> p %s" % (" ".join(names), " ".join(names)), **kw)
        s = Slot(name)
        ar["live"].append(s)
        return TT(v, s)

    fence_t = sb([128, 2], F32, "fence")

    def aphase(new_names_hint=None):
        old = ar["live"]
        ar["live"] = []
        ar["off"] = 0
        ar["pending"] = old

    def afence():
        old = ar.get("pending", [])
        new = list(ar["live"])
        P.dve(lambda e: e.memset(fence_t[:], 0.0), reads=[], writes=old + new + [fence_t.s])
        ar["pending"] = []

    cst_f = sb([128, 512], F32, "cst_f")
    P.dma("sp", lambda e: e.dma_start(out=cst_f[:], in_=cst), writes=[cst_f.s])
    ident_f = cst_f[:, 0:128]
    ident_b = sb([128, 128], BF16, "ident_b")
    P.dve(lambda e: e.tensor_copy(out=ident_b[:], in_=cst_f[:, 0:128]), reads=[cst_f.s], writes=[ident_b.s])
    ones_b = sb([128, 128], BF16, "ones_b")
    P.dve(lambda e: e.memset(ones_b[:], 1.0), writes=[ones_b.s])
    zeros_f = sb([128, 128], F32, "zeros_f")
    P.dve(lambda e: e.memset(zeros_f[:], 0.0), writes=[zeros_f.s])
    epsc = sb([128, 2], F32, "epsc")
    P.dve(lambda e: e.memset(epsc[:, 0:1], float(D * EPS)), writes=[epsc.s])
    P.dve(lambda e: e.memset(epsc[:, 1:2], float(DK * EPS)), reads=[epsc.s], writes=[epsc.s])
    lnc = sb([128, 1], F32, "lnc")
    P.dve(lambda e: e.memset(lnc[:], float(math.log(DK ** -0.5))), writes=[lnc.s])

    xT = sb([128, KD, T], F32, "xT")
    xs = [[Slot("xT%d_%d" % (k, h)) for h in range(2)] for k in range(KD)]
    hT = sb([128, KD, T], BF16, "hT")
    hs = [[Slot("hT%d_%d" % (k, h)) for h in range(2)] for k in range(KD)]
    mixT = sb([128, KD, T], BF16, "mixT")
    mixs = [[Slot("mix%d_%d" % (k, h)) for h in range(2)] for k in range(KD)]

    def sin_turns(out, u, ki, kf, ap=lambda t: t[:]):
        P.dve(lambda e: e.tensor_copy(out=ap(ki), in_=ap(u)), reads=[u.s], writes=[ki.s])
        P.dve(lambda e: e.tensor_copy(out=ap(kf), in_=ap(ki)), reads=[ki.s], writes=[kf.s])
        P.dve(lambda e: e.tensor_tensor(out=ap(kf), in0=ap(u), in1=ap(kf), op=ALU.subtract), reads=[u.s, kf.s], writes=[kf.s])
        P.act(lambda e: e.activation(out=ap(out), in_=ap(kf), func=AF.Sin, scale=2 * math.pi), reads=[kf.s], writes=[out.s])

    aphase()
    xtok = [aalloc([D], F32, "xtok%d" % i) for i in range(NT)]
    posarg = aalloc([512], F32, "posarg")
    cst2 = aalloc([1024], F32, "cst2")
    posk_i = aalloc([512], I32, "posk_i")
    posk_f = aalloc([512], F32, "posk_f")
    afence()
    P.dma("sp", lambda e: e.dma_start(out=cst2[:], in_=cst2_in), writes=[cst2.s])
    for tt in range(NT):
        P.dma("sp", lambda e, tt=tt: e.dma_start(out=xtok[tt][:], in_=x_in[tt * 128:(tt + 1) * 128, :]),
              writes=[xtok[tt].s])
    for k in range(KD):
        for h in range(2):
            b = psum()
            for j in range(4):
                tt = h * 4 + j
                P.pe(lambda e, b=b, j=j, tt=tt, k=k: e.transpose(b[:, j * 128:(j + 1) * 128],
                                                                   xtok[tt][:, k * 128:(k + 1) * 128], ident_f),
                     reads=[xtok[tt].s, cst_f.s], writes=[b.s])
            P.act(lambda e, b=b, k=k, h=h: e.activation(out=xT[:, k, h * 512:(h + 1) * 512], in_=b[:], func=AF.Copy),
                  reads=[b.s], writes=[xs[k][h]])

    posi = aalloc_late = None
    for k in range(KD):
        blk = k // 2
        pos_src = cst2[:, 0:512] if blk < 2 else cst2[:, 512:1024]
        om = cst_f[:, 388 + (k % 2):389 + (k % 2)]
        P.dve(lambda e, pos_src=pos_src, om=om: e.tensor_scalar(
            out=posarg[:], in0=pos_src, scalar1=om, scalar2=1.0 / (2 * math.pi), op0=ALU.mult, op1=ALU.mult),
            reads=[cst2.s, cst_f.s], writes=[posarg.s])
        if blk % 2 == 1:
            P.dve(lambda e: e.tensor_scalar(out=posarg[:], in0=posarg[:], scalar1=0.25, scalar2=None, op0=ALU.add),
                  reads=[posarg.s], writes=[posarg.s])
        sin_turns(posarg, posarg, posk_i, posk_f)
        P.dve(lambda e, k=k: e.tensor_tensor(out=xT[:, k, 512:1024], in0=xT[:, k, 512:1024], in1=posarg[:], op=ALU.add),
              reads=[posarg.s, xs[k][1]], writes=[xs[k][1]])

    condf = sb([128, KD, 2], F32, "condf")
    condb = sb([128, KD, 2], BF16, "condb")
    for j in range(2):
        P.dma("sp", lambda e, j=j: e.dma_start(out=condf[:, :, j], in_=cond_in[j].rearrange("(k p) -> p k", p=128)),
              writes=[condf.s])
    P.act(lambda e: e.activation(out=condb[:], in_=condf[:], func=AF.Silu), reads=[condf.s], writes=[condb.s])

    nmix = sb([128, DEPTH, KD], F32, "nmix")
    nffn = sb([128, DEPTH, KD], F32, "nffn")
    nfin = sb([128, KD], F32, "nfin")
    bmod = sb([128, DEPTH, 48], F32, "bmod")
    P.dma("sp", lambda e: e.dma_start(out=nmix[:], in_=norm_mix.rearrange("l (k p) -> p l k", p=128)), writes=[nmix.s])
    P.dma("sp", lambda e: e.dma_start(out=nffn[:], in_=norm_ffn.rearrange("l (k p) -> p l k", p=128)), writes=[nffn.s])
    P.dma("sp", lambda e: e.dma_start(out=nfin[:], in_=norm_final.rearrange("(k p) -> p k", p=128)), writes=[nfin.s])
    P.dma("sp", lambda e: e.dma_start(out=bmod[:], in_=b_mod.rearrange("l (k p) -> p l k", p=128)), writes=[bmod.s])

    lbl = sb([128, 2, DEPTH, NH], F32, "lbl")
    for d in range(2):
        for l in range(DEPTH):
            P.dma("sp", lambda e, d=d, l=l: e.dma_start(out=lbl[:, d, l, :], in_=hg_lb[d, l].rearrange("(h p) -> p h", p=128)),
                  writes=[lbl.s])
    lb = sb([128, DEPTH, 2, NH], F32, "lb")
    oml = sb([128, DEPTH, 2, NH], F32, "oml")
    noml = sb([128, DEPTH, 2, NH], F32, "noml")
    P.dve(lambda e: e.memset(lb[:], 0.0), writes=[lb.s])
    P.dve(lambda e: e.tensor_tensor(out=lb[:, 1], in0=lbl[:, :, 1, :], in1=lbl[:, :, 0, :], op=ALU.subtract),
          reads=[lbl.s, lb.s], writes=[lb.s])
    P.act(lambda e: e.activation(out=lb[:, 1], in_=lb[:, 1], func=AF.Sigmoid), reads=[lb.s], writes=[lb.s])
    P.dve(lambda e: e.tensor_scalar(out=oml[:], in0=lb[:], scalar1=-1.0, scalar2=1.0, op0=ALU.mult, op1=ALU.add),
          reads=[lb.s], writes=[oml.s])
    P.dve(lambda e: e.tensor_scalar(out=noml[:], in0=lb[:], scalar1=1.0, scalar2=-1.0, op0=ALU.mult, op1=ALU.add),
          reads=[lb.s], writes=[noml.s])
    gn = sb([128, DEPTH], F32, "gn")
    P.dma("sp", lambda e: e.dma_start(out=gn[:], in_=hg_norm.rearrange("l p -> p l")), writes=[gn.s])
    P.dve(lambda e: e.tensor_scalar(out=gn[:], in0=gn[:], scalar1=float(math.sqrt(DK)), scalar2=None, op0=ALU.mult),
          reads=[gn.s], writes=[gn.s])

    par_all = sb([128, DEPTH, 2, 5, 16], F32, "par_all")
    for l_ in range(DEPTH):
        for d_ in range(2):
            P.dma("sp", lambda e, l_=l_, d_=d_: e.dma_start(
                out=par_all[:, l_, d_, 0, :], in_=s5_lam_re[l_, d_].rearrange("(gp g2) p -> (g2 p) gp", g2=2)), writes=[par_all.s])
            P.dma("sp", lambda e, l_=l_, d_=d_: e.dma_start(
                out=par_all[:, l_, d_, 1, :], in_=s5_lam_im[l_, d_].rearrange("(gp g2) p -> (g2 p) gp", g2=2)), writes=[par_all.s])
            for g2_ in range(2):
                P.dma("sp", lambda e, l_=l_, d_=d_, g2_=g2_: e.dma_start(
                    out=par_all[64 * g2_:64 * g2_ + 64, l_, d_, 2, :],
                    in_=s5_log_dt[l_, d_].rearrange("(gp g2) -> g2 gp", g2=2)[g2_].partition_broadcast(64)), writes=[par_all.s])
    mod = sb([128, DEPTH, 48, 2], F32, "mod")
    coef = sb([128, DEPTH, 2, KD, 2], F32, "coef")

    mod_s = [[Slot("mod%d_%d" % (l_, p_)) for p_ in range(3)] for l_ in range(DEPTH)]
    coef_s = [[Slot("coef%d_%d" % (l_, n_)) for n_ in range(2)] for l_ in range(DEPTH)]

    def compute_mod(l, part, slots=None):
        bk = psum()
        for ci_, cc in enumerate(range(part * 4, part * 4 + 4)):
            wv, wsl = wload(w_mod[l][:, cc * 512:(cc + 1) * 512].rearrange("(k p) c -> p k c", p=128), KD, 512,
                            slot=None if slots is None else slots[ci_])
            for j in range(4):
                ft = cc * 4 + j - part * 16
                for k in range(KD):
                    P.pe(lambda e, wv=wv, j=j, k=k, ft=ft, bk=bk: e.matmul(
                        bk[:, ft * 2:ft * 2 + 2], wv[:, k, j * 128:(j + 1) * 128], condb[:, k, :],
                        start=(k == 0), stop=(k == KD - 1)),
                        reads=[wsl, condb.s], writes=[bk.s])
        t0_ = part * 16
        P.dve(lambda e, bk=bk: e.tensor_tensor(
            out=mod[:, l, t0_:t0_ + 16], in0=bk[:, 0:32].rearrange("p (f c) -> p f c", c=2),
            in1=bmod[:, l, t0_:t0_ + 16].unsqueeze(2).broadcast_to([128, 16, 2]), op=ALU.add),
            reads=[bk.s, bmod.s], writes=[mod_s[l][part]])
        for n, (gt, base, prt) in enumerate(((nmix, 8, 0), (nffn, 32, 2))):
            if prt != part:
                continue
            P.dve(lambda e, n=n, base=base: e.tensor_scalar(
                out=coef[:, l, n], in0=mod[:, l, base:base + 8, :], scalar1=1.0, scalar2=32.0,
                op0=ALU.add, op1=ALU.mult), reads=[mod_s[l][part]], writes=[coef_s[l][n]])
            P.dve(lambda e, n=n, gt=gt: e.tensor_tensor(
                out=coef[:, l, n], in0=coef[:, l, n], in1=gt[:, l].unsqueeze(2).broadcast_to([128, KD, 2]),
                op=ALU.mult), reads=[coef_s[l][n], gt.s], writes=[coef_s[l][n]])

    sq = [sb([128, 512], BF16, "sq%d" % i) for i in range(2)]
    rstd = [sb([128, 512], F32, "rstd%d" % i) for i in range(2)]
    tmpn = [sb([128, 512], F32, "tmpn%d" % i) for i in range(2)]

    def interleave_gens(gens):
        gens = list(gens)
        while gens:
            for g_ in list(gens):
                try:
                    next(g_)
                except StopIteration:
                    gens.remove(g_)

    def rms_stats_gen(h):
        bk = psum()
        for k in range(KD):
            s_ = sq[h]
            P.act(lambda e, k=k, s_=s_: e.activation(out=s_[:], in_=xT[:, k, h * 512:(h + 1) * 512], func=AF.Square),
                  reads=[xs[k][h]], writes=[s_.s])
            yield
            P.pe(lambda e, k=k, bk=bk, s_=s_: e.matmul(bk[:], ones_b[:], s_[:], start=(k == 0), stop=(k == KD - 1)),
                 reads=[s_.s, ones_b.s], writes=[bk.s])
            yield
        P.act(lambda e, bk=bk: e.activation(out=rstd[h][:], in_=bk[:], func=AF.Ln, bias=epsc[:, 0:1]),
              reads=[bk.s, epsc.s], writes=[rstd[h].s])
        yield
        P.act(lambda e: e.activation(out=rstd[h][:], in_=rstd[h][:], func=AF.Exp, scale=-0.5), reads=[rstd[h].s], writes=[rstd[h].s])
        yield

    def rms_stats(h):
        for _ in rms_stats_gen(h):
            pass

    def norm_half_gen(l, n, shift_base, h):
        yield from rms_stats_gen(h)
        for k in range(KD):
            tm = tmpn[h]
            P.dve(lambda e, k=k, tm=tm: e.tensor_tensor(out=tm[:], in0=xT[:, k, h * 512:(h + 1) * 512],
                                                       in1=rstd[h][:], op=ALU.mult),
                  reads=[xs[k][h], rstd[h].s], writes=[tm.s])
            yield
            P.act(lambda e, k=k, tm=tm: e.activation(
                out=hT[:, k, h * 512:(h + 1) * 512], in_=tm[:], func=AF.Identity,
                bias=mod[:, l, shift_base + k, h:h + 1], scale=coef[:, l, n, k, h:h + 1]),
                reads=[tm.s, mod_s[l][shift_base // 16], coef_s[l][n]], writes=[hs[k][h]])
            yield

    def norm_mod(l, n, shift_base):
        interleave_gens([norm_half_gen(l, n, shift_base, 0), norm_half_gen(l, n, shift_base, 1)])

    def residual_proj(wdram, l, gate_base, srcT, src_slots, nk):
        per = WSLOT // 256
        for c4 in range(4):
            wv = []
            for k0 in range(0, nk, per):
                kk = min(per, nk - k0)
                v, s = wload(wdram[k0 * 128:(k0 + kk) * 128, c4 * 256:(c4 + 1) * 256].rearrange("(k p) c -> p k c", p=128),
                             kk, 256)
                wv.append((k0, kk, v, s))
            for j in range(2):
                dt_ = c4 * 2 + j
                for h in range(2):
                    bk = psum()
                    for (k0, kk, v, s) in wv:
                        for k in range(kk):
                            kg = k0 + k
                            P.pe(lambda e, v=v, k=k, j=j, kg=kg, h=h, bk=bk: e.matmul(
                                bk[:], v[:, k, j * 128:(j + 1) * 128], srcT[:, kg, h * 512:(h + 1) * 512],
                                start=(kg == 0), stop=(kg == nk - 1)),
                                reads=[s, src_slots[kg][h]], writes=[bk.s])
                    P.dve(lambda e, bk=bk, dt_=dt_, h=h, l=l: e.scalar_tensor_tensor(
                        out=xT[:, dt_, h * 512:(h + 1) * 512], in0=bk[:], scalar=mod[:, l, gate_base + dt_, h:h + 1],
                        in1=xT[:, dt_, h * 512:(h + 1) * 512], op0=ALU.mult, op1=ALU.add),
                        reads=[bk.s, mod_s[l][gate_base // 16], xs[dt_][h]], writes=[xs[dt_][h]])


    def ffn(l):
        aphase()
        h1 = aalloc([NF, T], BF16, "h1")
        sgate = [aalloc([512], F32, "sgate%d" % i) for i in range(2)]
        h1s = [[Slot("h1_%d_%d" % (f, h)) for h in range(2)] for f in range(NF)]
        ar["live"].extend([s for row in h1s for s in row])
        afence()
        it = 0
        for c in range(6):
            ncol = 512 if c < 5 else 256
            vg, sg_ = wload(w_gate[l][:, c * 512:c * 512 + ncol].rearrange("(k p) c -> p k c", p=128), KD, ncol)
            vu, su_ = wload(w_up[l][:, c * 512:c * 512 + ncol].rearrange("(k p) c -> p k c", p=128), KD, ncol)
            for j in range(ncol // 128):
                f = c * 4 + j
                for h in range(2):
                    bg = psum()
                    bu = psum()
                    for k in range(KD):
                        P.pe(lambda e, vg=vg, k=k, j=j, h=h, bg=bg: e.matmul(
                            bg[:], vg[:, k, j * 128:(j + 1) * 128], hT[:, k, h * 512:(h + 1) * 512],
                            start=(k == 0), stop=(k == KD - 1)), reads=[sg_, hs[k][h]], writes=[bg.s])
                    for k in range(KD):
                        P.pe(lambda e, vu=vu, k=k, j=j, h=h, bu=bu: e.matmul(
                            bu[:], vu[:, k, j * 128:(j + 1) * 128], hT[:, k, h * 512:(h + 1) * 512],
                            start=(k == 0), stop=(k == KD - 1)), reads=[su_, hs[k][h]], writes=[bu.s])
                    sgt = sgate[it % 2]
                    it += 1
                    P.act(lambda e, bg=bg, sgt=sgt: e.activation(out=sgt[:], in_=bg[:], func=AF.Silu),
                          reads=[bg.s], writes=[sgt.s])
                    P.dve(lambda e, bu=bu, f=f, h=h, sgt=sgt: e.tensor_tensor(
                        out=h1[:, f, h * 512:(h + 1) * 512], in0=sgt[:], in1=bu[:], op=ALU.mult),
                        reads=[sgt.s, bu.s], writes=[h1s[f][h]])
        residual_proj(w_down[l], l, 40, h1, h1s, NF)

    def proj_feat(wv, wsl, c0, h):
        bk = psum()
        for k in range(KD):
            P.pe(lambda e, k=k, bk=bk: e.matmul(bk[:], wv[:, k, c0:c0 + 128], hT[:, k, h * 512:(h + 1) * 512],
                                                start=(k == 0), stop=(k == KD - 1)),
                 reads=[wsl, hs[k][h]], writes=[bk.s])
        return bk

    def hgrn(l):
        aphase()
        qk = [[[aalloc([T], BF16, "qk%d%d%d" % (h, d, w)) for w in range(2)] for d in range(2)] for h in range(NH)]
        kendT = [[aalloc([NT, DK], BF16, "kendT%d%d" % (h, d)) for d in range(2)] for h in range(NH)]
        V = [aalloc([HG_W], BF16, "V%d" % tt) for tt in range(NT)]
        gch = aalloc([NH * 2, NCH], F32, "gch")
        gch_s = [[Slot("gch%d%d" % (h, d)) for d in range(2)] for h in range(NH)]
        ar["live"].extend([s for row in gch_s for s in row])
        R1 = ar["off"]
        qs = [aalloc([512], F32, "qs%d" % hf) for hf in range(2)]
        rmask = aalloc([512], F32, "rmask")
        tmp = [[aalloc([512], F32, "gt%d_%d" % (i, j)) for j in range(5)] for i in range(2)]
        kend_t = [aalloc([512], BF16, "kend%d" % i) for i in range(2)]
        R1_end = ar["off"]
        afence()
        P.dve(lambda e: e.memset(rmask[:], 1.0), writes=[rmask.s])
        P.dve(lambda e: e.memset(rmask[:, 0:512:CH], 0.0), reads=[rmask.s], writes=[rmask.s])

        wv_iv, ws_iv = None, None

        def load_in(c):
            return wload(w_in[l][:, c * 512:(c + 1) * 512].rearrange("(k p) c -> p k c", p=128), KD, 512)

        wq, wqs = load_in(0)
        wf = [None, None]
        wf[0] = load_in(1)
        wf[1] = load_in(2)
        def gate_chain(h, d, hf, tset, ke):
            t_sig, t_a, t_b, t_c, t_e = tset
            bk = proj_feat(wf[d][0], wf[d][1], h * 128, hf)
            lb_ = lb[:, l, d, h:h + 1]
            oml_ = oml[:, l, d, h:h + 1]
            noml_ = noml[:, l, d, h:h + 1]
            P.act(lambda e, bk=bk, t_sig=t_sig: e.activation(out=t_sig[:], in_=bk[:], func=AF.Sigmoid),
                  reads=[bk.s], writes=[t_sig.s])
            yield
            P.dve(lambda e, t_sig=t_sig, t_a=t_a, lb_=lb_, oml_=oml_: e.tensor_scalar(
                out=t_a[:], in0=t_sig[:], scalar1=oml_, scalar2=lb_, op0=ALU.mult, op1=ALU.add),
                reads=[t_sig.s, lb.s, oml.s], writes=[t_a.s])
            yield
            P.act(lambda e, t_a=t_a: e.activation(out=t_a[:], in_=t_a[:], func=AF.Ln), reads=[t_a.s], writes=[t_a.s])
            yield
            P.dve(lambda e, t_a=t_a, t_b=t_b: e.tensor_tensor_scan(
                out=t_b[:], data0=rmask[:], data1=t_a[:], initial=0.0, op0=ALU.mult, op1=ALU.add),
                reads=[t_a.s, rmask.s], writes=[t_b.s])
            yield
            P.dve(lambda e, t_sig=t_sig, noml_=noml_, oml_=oml_: e.tensor_scalar(
                out=t_sig[:], in0=t_sig[:], scalar1=noml_, scalar2=oml_, op0=ALU.mult, op1=ALU.add),
                reads=[t_sig.s, noml.s, oml.s], writes=[t_sig.s])
            yield
            P.act(lambda e, t_b=t_b, h=h, d=d, hf=hf: e.activation(
                out=gch[:, h * 2 + d, hf * (512 // CH):(hf + 1) * (512 // CH)], in_=t_b[:, CH - 1:512:CH], func=AF.Exp),
                reads=[t_b.s], writes=[gch_s[h][d]])
            tb3 = t_b[:].rearrange("p (c j) -> p c j", j=CH)
            tc3 = t_c[:].rearrange("p (c j) -> p c j", j=CH)
            ta3 = t_a[:].rearrange("p (c j) -> p c j", j=CH)
            tot_b = tb3[:, :, CH - 1:CH].broadcast_to([128, 512 // CH, CH])
            if d == 0:
                P.dve(lambda e, tc3=tc3, tb3=tb3, tot_b=tot_b: e.tensor_tensor(
                    out=tc3, in0=tb3, in1=tot_b, op=ALU.subtract), reads=[t_b.s], writes=[t_c.s])
            else:
                P.dve(lambda e, tc3=tc3, ta3=ta3, tb3=tb3: e.tensor_tensor(
                    out=tc3, in0=ta3, in1=tb3, op=ALU.subtract), reads=[t_a.s, t_b.s], writes=[t_c.s])
                P.dve(lambda e, tc3=tc3, tb3=tb3, tot_b=tot_b, ta3=ta3: e.tensor_tensor(
                    out=ta3, in0=tc3, in1=tot_b, op=ALU.add), reads=[t_c.s, t_b.s], writes=[t_a.s])
            beta = t_b if d == 0 else t_a
            yield
            P.act(lambda e, beta=beta, t_e=t_e: e.activation(out=t_e[:], in_=beta[:], func=AF.Exp, bias=lnc[:, 0:1]),
                  reads=[beta.s, lnc.s], writes=[t_e.s])
            yield
            P.dve(lambda e, t_e=t_e, h=h, d=d, hf=hf: e.tensor_tensor(
                out=qk[h][d][0][:, hf * 512:(hf + 1) * 512], in0=qs[hf][:], in1=t_e[:], op=ALU.mult),
                reads=[t_e.s, qs[hf].s], writes=[qk[h][d][0].s])
            yield
            P.dve(lambda e, beta=beta, t_e=t_e: e.tensor_scalar(out=t_e[:], in0=beta[:], scalar1=-75.0, scalar2=None, op0=ALU.max),
                  reads=[beta.s], writes=[t_e.s])
            yield
            P.act(lambda e, t_e=t_e: e.activation(out=t_e[:], in_=t_e[:], func=AF.Exp, scale=-1.0),
                  reads=[t_e.s], writes=[t_e.s])
            yield
            P.dve(lambda e, t_e=t_e, t_sig=t_sig, h=h, d=d, hf=hf: e.tensor_tensor(
                out=qk[h][d][1][:, hf * 512:(hf + 1) * 512], in0=t_sig[:], in1=t_e[:], op=ALU.mult),
                reads=[t_e.s, t_sig.s], writes=[qk[h][d][1].s])
            yield
            P.act(lambda e, t_c=t_c, t_e=t_e: e.activation(out=t_e[:], in_=t_c[:], func=AF.Exp, scale=-1.0),
                  reads=[t_c.s], writes=[t_e.s])
            yield
            P.dve(lambda e, t_e=t_e, t_sig=t_sig, ke=ke: e.tensor_tensor(
                out=ke[:], in0=t_sig[:], in1=t_e[:], op=ALU.mult), reads=[t_e.s, t_sig.s], writes=[ke.s])
            yield
            bk2 = psum()
            yield
            for j in range(4):
                P.pe(lambda e, bk2=bk2, j=j, ke=ke: e.matmul(bk2[:, j * 128:(j + 1) * 128], ke[:, j * 128:(j + 1) * 128],
                                                             ident_b[:], start=True, stop=True),
                     reads=[ke.s, ident_b.s], writes=[bk2.s])
            yield
            P.act(lambda e, bk2=bk2, h=h, d=d, hf=hf: e.activation(
                out=kendT[h][d][:, hf * 4:(hf + 1) * 4, :], in_=bk2[:].rearrange("p (j k) -> p j k", k=128), func=AF.Copy),
                reads=[bk2.s], writes=[kendT[h][d].s])
            yield

        def interleave(gens):
            gens = list(gens)
            while gens:
                for g_ in list(gens):
                    try:
                        next(g_)
                    except StopIteration:
                        gens.remove(g_)

        for h in range(NH):
            for hf in range(2):
                bk = proj_feat(wq, wqs, h * 128, hf)
                P.act(lambda e, bk=bk, hf=hf: e.activation(out=qs[hf][:], in_=bk[:], func=AF.Silu),
                      reads=[bk.s], writes=[qs[hf].s])
            for d in range(2):
                interleave([gate_chain(h, d, 0, tmp[0], kend_t[0]), gate_chain(h, d, 1, tmp[1], kend_t[1])])
        wiv, wivs = load_in(3)
        for tt in range(NT):
            bk = psum()
            hf = tt // 4
            for k in range(KD):
                P.pe(lambda e, k=k, bk=bk, tt=tt: e.matmul(bk[:], hT[:, k, tt * 128:(tt + 1) * 128], wiv[:, k, :],
                                                            start=(k == 0), stop=(k == KD - 1)),
                     reads=[wivs, hs[k][hf]], writes=[bk.s])
            P.act(lambda e, bk=bk, tt=tt: e.activation(out=V[tt][:], in_=bk[:], func=AF.Copy), reads=[bk.s], writes=[V[tt].s])

        old_tmp = [t.s for grp in tmp for t in grp] + [k.s for k in kend_t] + [q.s for q in qs] + [rmask.s]
        ar["off"] = R1
        S = [aalloc([DK], F32, "S%d" % i) for i in range(3)]
        Sent = aalloc([NCH, DK], BF16, "Sent")
        o_t = aalloc([512], F32, "o_t")
        on_t = aalloc([512], F32, "on_t")
        sg_t = aalloc([512], F32, "sg_t")
        sq_t = aalloc([512], BF16, "sq_t")
        PT = [aalloc([128], BF16, "PT%d" % i) for i in range(2)]
        gat = aalloc([4, DK], F32, "gat")
        gatG = aalloc([4, 8], F32, "gatG")
        s0t = aalloc([DK], F32, "s0t")
        Pc = [aalloc([DK], F32, "Pc%d" % i) for i in range(2)]
        Sinit = aalloc([2 * NH, DK], F32, "Sinit")
        Sinit_s = [[Slot("Sinit%d%d" % (h, d)) for d in range(2)] for h in range(NH)]
        gtot = aalloc([NH * 2, NCH // 2], F32, "gtot")
        new2 = [t.s for t in S + PT + Pc] + [Sent.s, o_t.s, on_t.s, sg_t.s, sq_t.s, gat.s, gatG.s, s0t.s, Sinit.s, gtot.s] + \
               [s_ for row in Sinit_s for s_ in row]
        ar["live"].extend([s_ for row in Sinit_s for s_ in row])
        P.dve(lambda e: e.memset(fence_t[:], 0.0), reads=[], writes=old_tmp + new2 + [fence_t.s])

        CPT = 128 // CH

        def u_matmul(h, d, c):
            tt, p0 = c // CPT, (c % CPT) * CH
            bk = psum()
            P.pe(lambda e, bk=bk: e.matmul(bk[:, 0:128], kendT[h][d][p0:p0 + CH, tt, :],
                                           V[tt][p0:p0 + CH, h * 128:(h + 1) * 128],
                                           start=True, stop=True, tile_position=(p0, 0)),
                 reads=[kendT[h][d].s, V[tt].s], writes=[bk.s])
            return bk

        def scan_order(c0, c1, d):
            return list(range(c0, c1)) if d == 0 else list(range(c1 - 1, c0 - 1, -1))

        si = [0]
        SC0, SC1 = SEGS[2]

        ci = l * 2
        ccs_in, ccs_out = Slot("ccin"), Slot("ccout")
        def rec1_chain(h, d, Sx):
            hd = h * 2 + d
            order = scan_order(SC0, SC1, d)
            for i, c in enumerate(order):
                bk = u_matmul(h, d, c)
                yield
                if i == 0:
                    P.dve(lambda e, bk=bk, Sx=Sx: e.tensor_copy(out=Sx[:], in_=bk[:, 0:128]), reads=[bk.s], writes=[Sx.s])
                else:
                    P.dve(lambda e, bk=bk, Sx=Sx, c=c, hd=hd: e.scalar_tensor_tensor(
                        out=Sx[:], in0=Sx[:], scalar=gch[:, hd, c:c + 1], in1=bk[:, 0:128],
                        op0=ALU.mult, op1=ALU.add), reads=[bk.s, Sx.s, gch_s[h][d]], writes=[Sx.s])
                yield
            P.dma("sp", lambda e, Sx=Sx, hd=hd: e.dma_start(out=cc_in[ci][:, hd * 128:(hd + 1) * 128], in_=Sx[:]),
                  reads=[Sx.s], writes=[ccs_in])
            P.dve(lambda e, hd=hd: e.tensor_tensor_scan(
                out=gtot[:, hd, :], data0=gch[:, hd, SC0:SC1], data1=zeros_f[:, 0:SC1 - SC0], initial=1.0,
                op0=ALU.mult, op1=ALU.add), reads=[gch_s[h][d], zeros_f.s], writes=[gtot.s])
            yield

        for h in range(NH):
            interleave([rec1_chain(h, 0, S[0]), rec1_chain(h, 1, S[1])])
        P.dma("sp", lambda e: e.dma_start(out=cc_in[ci][:, 1024:1032], in_=gtot[:, :, SC1 - SC0 - 1]),
              reads=[gtot.s], writes=[ccs_in])
        P.dma("sp", lambda e: e.dma_start(out=cc_in[ci][:, 1032:CCW], in_=zeros_f[:, 0:CCW - 1032]),
              reads=[zeros_f.s], writes=[ccs_in])
        P.dma("pool", lambda e: e.collective_compute("AllGather", ALU.bypass, replica_groups=[[0, 1, 2, 3], [4, 5, 6, 7]],
                                                     ins=[cc_in[ci]], outs=[cc_out[ci]]),
              reads=[ccs_in], writes=[ccs_out], inc=1)
        ccv = cc_out[ci].rearrange("(r p) c -> p r c", p=128)
        P.dma("sp", lambda e: e.dma_start(out=gatG[:], in_=ccv[:, :, 1024:1032]), reads=[ccs_out], writes=[gatG.s])
        for h in range(NH):
            for d in range(2):
                hd = h * 2 + d
                col = slice(hd * 128, (hd + 1) * 128)
                P.dma("sp", lambda e, col=col: e.dma_start(out=gat[:], in_=ccv[:, :, col]), reads=[ccs_out], writes=[gat.s])
                P.dma("sp", lambda e, d=d, h=h: e.dma_start(out=s0t[:], in_=st_hg[l, d, h]), writes=[s0t.s])
                dst = Sinit[:, hd, :]
                ranks = [0, 1, 2, 3] if d == 0 else [3, 2, 1, 0]
                prev, prev_s = s0t[:], s0t.s
                P.dve(lambda e, dst=dst, prev=prev, r=ranks[0]: e.tensor_scalar(
                    out=dst, in0=prev, scalar1=cst_f[:, 384 + r:385 + r], scalar2=None, op0=ALU.mult),
                    reads=[prev_s, cst_f.s], writes=[Sinit_s[h][d]])
                for i in range(3):
                    r = ranks[i]
                    nxt = Pc[i % 2]
                    P.dve(lambda e, nxt=nxt, prev=prev, r=r, hd=hd: e.scalar_tensor_tensor(
                        out=nxt[:], in0=prev, scalar=gatG[:, r, hd:hd + 1], in1=gat[:, r, :],
                        op0=ALU.mult, op1=ALU.add), reads=[prev_s, gat.s, gatG.s], writes=[nxt.s])
                    rn = ranks[i + 1]
                    P.dve(lambda e, nxt=nxt, dst=dst, rn=rn: e.scalar_tensor_tensor(
                        out=dst, in0=nxt[:], scalar=cst_f[:, 384 + rn:385 + rn], in1=dst, op0=ALU.mult, op1=ALU.add),
                        reads=[nxt.s, cst_f.s, Sinit_s[h][d]], writes=[Sinit_s[h][d]])
                    prev, prev_s = nxt[:], nxt.s

        wg_v, wg_s = load_in(4)
        it2 = 0
        for h in range(NH):
            bo = [banks[6], banks[7]]
            for d in range(2):
                hd = h * 2 + d
                def seg_chain(h, d, c0, c1, Sx):
                    hd = h * 2 + d
                    order = scan_order(c0, c1, d)
                    is_sample = (c0 == SC0)
                    for i, c in enumerate(order):
                        if i == 0:
                            src = Sinit[:, hd, :] if is_sample else zeros_f[:]
                            src_s = Sinit_s[h][d] if is_sample else zeros_f.s
                        else:
                            src, src_s = Sx[:], Sx.s
                        P.act(lambda e, src=src, c=c: e.activation(out=Sent[:, c, :], in_=src, func=AF.Copy),
                              reads=[src_s], writes=[Sent.s])
                        yield
                        last = (i == len(order) - 1)
                        if last and is_sample:
                            continue
                        bk = u_matmul(h, d, c)
                        yield
                        if i == 0 and not is_sample:
                            P.dve(lambda e, bk=bk, Sx=Sx: e.tensor_copy(out=Sx[:], in_=bk[:, 0:128]), reads=[bk.s], writes=[Sx.s])
                        else:
                            P.dve(lambda e, bk=bk, Sx=Sx, src=src, c=c, hd=hd: e.scalar_tensor_tensor(
                                out=Sx[:], in0=src, scalar=gch[:, hd, c:c + 1], in1=bk[:, 0:128],
                                op0=ALU.mult, op1=ALU.add), reads=[bk.s, src_s, gch_s[h][d]], writes=[Sx.s])
                        yield
                    if not is_sample:
                        seq = 0 if c0 == 0 else 1
                        P.dma("sp", lambda e, Sx=Sx, seq=seq, d=d, h=h: e.dma_start(out=ns_hg[seq, l, d, h], in_=Sx[:]),
                              reads=[Sx.s], writes=[])
                    yield

                interleave([seg_chain(h, d, SEGS[i_][0], SEGS[i_][1], S[i_]) for i_ in range(3)])
                def scores_mm(tt, h=h, d=d):
                    tok = slice(tt * 128, (tt + 1) * 128)
                    bs = psum()
                    P.pe(lambda e, bs=bs, tok=tok, d=d, h=h: e.matmul(bs[:, 0:128], qk[h][d][1][:, tok], qk[h][d][0][:, tok],
                                                                start=True, stop=True),
                         reads=[qk[h][d][0].s, qk[h][d][1].s], writes=[bs.s])
                    return bs

                pend = scores_mm(0)
                for hf in range(2):
                    for j in range(4):
                        tt = hf * 4 + j
                        bs = pend
                        pend = scores_mm(tt + 1) if tt < NT - 1 else None
                        pt = PT[it2 % 2]
                        it2 += 1
                        P.dve(lambda e, bs=bs, pt=pt, d=d: e.tensor_tensor(
                            out=pt[:], in0=bs[:, 0:128], in1=cst_f[:, 128 + d * 128:256 + d * 128], op=ALU.mult),
                            reads=[bs.s, cst_f.s], writes=[pt.s])
                        oc = slice(j * 128, (j + 1) * 128)
                        P.pe(lambda e, pt=pt, tt=tt, oc=oc, hf=hf, d=d, j=j, h=h: e.matmul(
                            bo[hf][:, oc], V[tt][:, h * 128:(h + 1) * 128], pt[:], start=(d == 0 and j == 0), stop=False),
                            reads=[V[tt].s, pt.s], writes=[bo[hf].s])
                        for sub in range(CPT):
                            c = tt * CPT + sub
                            cs = slice(j * 128 + sub * CH, j * 128 + (sub + 1) * CH)
                            ts = slice(tt * 128 + sub * CH, tt * 128 + (sub + 1) * CH)
                            P.pe(lambda e, c=c, cs=cs, ts=ts, hf=hf, d=d, h=h: e.matmul(
                                bo[hf][:, cs], Sent[:, c, :], qk[h][d][0][:, ts], start=False, stop=(d == 1)),
                                reads=[Sent.s, qk[h][d][0].s], writes=[bo[hf].s])
            for hf in range(2):
                P.act(lambda e, hf=hf: e.activation(out=o_t[:], in_=bo[hf][:], func=AF.Copy), reads=[bo[hf].s], writes=[o_t.s])
                P.act(lambda e, hf=hf: e.activation(out=sq_t[:], in_=bo[hf][:], func=AF.Square), reads=[bo[hf].s], writes=[sq_t.s])
                if l == 0 and h == 0 and hf == 0:
                    dbg("o", o_t[:], o_t.s, [128, 512])
                    dbg("qf", qk[0][0][0][:], qk[0][0][0].s, [128, T], BF16)
                    dbg("kf", qk[0][0][1][:], qk[0][0][1].s, [128, T], BF16)
                    dbg("qb", qk[0][1][0][:], qk[0][1][0].s, [128, T], BF16)
                    dbg("kb", qk[0][1][1][:], qk[0][1][1].s, [128, T], BF16)
                br = psum()
                P.pe(lambda e, br=br: e.matmul(br[:], ones_b[:], sq_t[:], start=True, stop=True),
                     reads=[sq_t.s, ones_b.s], writes=[br.s])
                P.act(lambda e, br=br: e.activation(out=on_t[:], in_=br[:], func=AF.Ln, bias=epsc[:, 1:2]),
                      reads=[br.s, epsc.s], writes=[on_t.s])
                P.act(lambda e: e.activation(out=on_t[:], in_=on_t[:], func=AF.Exp, scale=-0.5), reads=[on_t.s], writes=[on_t.s])
                P.dve(lambda e: e.tensor_tensor(out=on_t[:], in0=o_t[:], in1=on_t[:], op=ALU.mult),
                      reads=[o_t.s, on_t.s], writes=[on_t.s])
                bg = proj_feat(wg_v, wg_s, h * 128, hf)
                P.act(lambda e, bg=bg: e.activation(out=sg_t[:], in_=bg[:], func=AF.Silu), reads=[bg.s], writes=[sg_t.s])
                P.dve(lambda e, hf=hf, h=h: e.scalar_tensor_tensor(
                    out=mixT[:, h, hf * 512:(hf + 1) * 512], in0=on_t[:], scalar=gn[:, l:l + 1], in1=sg_t[:],
                    op0=ALU.mult, op1=ALU.mult), reads=[on_t.s, sg_t.s, gn.s], writes=[mixs[h][hf]])

    def TTop(out, in0, in1, op, reads, writes):
        return P.dve(lambda e: e.tensor_tensor(out=out, in0=in0, in1=in1, op=op), reads=reads, writes=writes)

    def TSop(out, in0, s1, s2, op0, op1, reads, writes):
        if s2 is None:
            return P.dve(lambda e: e.tensor_scalar(out=out, in0=in0, scalar1=s1, scalar2=None, op0=op0), reads=reads, writes=writes)
        return P.dve(lambda e: e.tensor_scalar(out=out, in0=in0, scalar1=s1, scalar2=s2, op0=op0, op1=op1), reads=reads, writes=writes)

    def STTop(out, in0, scalar, in1, op0, op1, reads, writes):
        return P.dve(lambda e: e.scalar_tensor_tensor(out=out, in0=in0, scalar=scalar, in1=in1, op0=op0, op1=op1),
                     reads=reads, writes=writes)

    def ACTop(out, in_, func, reads, writes, bias=None, scale=None):
        kw = {}
        if bias is not None:
            kw["bias"] = bias
        if scale is not None:
            kw["scale"] = scale
        return P.act(lambda e: e.activation(out=out, in_=in_, func=func, **kw), reads=reads, writes=writes)

    def MM(out, lhsT, rhs, start, stop, reads, writes, tp=None):
        if tp is None:
            return P.pe(lambda e: e.matmul(out, lhsT, rhs, start=start, stop=stop), reads=reads, writes=writes)
        return P.pe(lambda e: e.matmul(out, lhsT, rhs, start=start, stop=stop, tile_position=tp), reads=reads, writes=writes)

    def CPY(out, in_, reads, writes):
        return P.dve(lambda e: e.tensor_copy(out=out, in_=in_), reads=reads, writes=writes)

    def MSET(out, val, reads, writes):
        return P.dve(lambda e: e.memset(out, val), reads=reads, writes=writes)

    def SDMA(out, in_, reads, writes):
        return P.dma("sp", lambda e: e.dma_start(out=out, in_=in_), reads=reads, writes=writes)

    NCK = 128
    SEG8 = [(0, 32), (32, 64), (64, 128)]
    TWO_PI = 2.0 * math.pi

    def s5(l, after_u=None):
        aphase()
        c3 = aalloc([1024], F32, "c3")
        asel = aalloc([8, 240], BF16, "asel")
        UT = aalloc([NG, NCK], BF16, "UT")
        Hb = aalloc([2, 2, 16, NCK], BF16, "Hb")
        par = TT(par_all[:, l], par_all.s)
        tab = aalloc([3, 16, 65], F32, "tab")
        hin = aalloc([2, 16, 2], F32, "hin")
        sloc = aalloc([2, 2, 16], F32, "sloc")
        hent = aalloc([2, 2, 16], F32, "hent")
        fst = aalloc([2, 2, 16, 2], F32, "fst")
        dsk = aalloc([NG], F32, "dsk")
        gat5 = aalloc([4, 2, 2, 16], F32, "gat5")
        sm = [aalloc([16], F32, "sm%d" % i) for i in range(8)]
        W0 = ar["off"]
        Bt = aalloc([NG, 2, 64], BF16, "Bt")
        bb = aalloc([2, 2, 16, 16], F32, "bb")
        craw = aalloc([2, 2, 16, 16], F32, "craw")
        pw = aalloc([2, 2, 16, 17], F32, "pw")
        R2 = ar["off"]
        uT = aalloc([4, T], BF16, "uT")
        cnat = aalloc([16, 64], F32, "cnat")
        prs = [aalloc([16, 17], F32, "prs%d" % i) for i in range(3)]
        pri = aalloc([16, 17], I32, "pri")
        afence()
        CtS = [wslots[1], wslots[2]]
        CtV = [bass.AP(w.t, 0, [[WSLOT, 128], [512, 8], [256, 2], [128, 2], [1, 128]]) for w in CtS]
        DtS = wslots[3]
        DtV = bass.AP(DtS.t, 0, [[WSLOT, 128], [128, NG], [1, 128]])

        def Ct_(gp):
            return CtV[gp // 8], gp % 8, CtS[gp // 8].s

        def _stop(k):
            if STAGE["s5_stop"] <= k:
                for kk in range(4, 8):
                    for hh in range(2):
                        MSET(mixT[:, kk, hh * 512:(hh + 1) * 512], 0.0, [], [mixs[kk][hh]])
                return True
            return False

        SDMA(c3[:], cst3_in, [], [c3.s])
        for g8 in range(8):
            TSop(asel[:, g8, :], c3[:, 0:240], c3[:, 240 + g8:241 + g8], None, ALU.mult, None, [c3.s], [asel.s])
        EV = c3[:, 608:625]
        K8 = c3[:, 640:705]
        R_even = c3[:, 480:544]
        R_odd = c3[:, 544:608]
        M5 = c3[:, 768:1024]
        wu, wus = wload(w_in[l][:, 2560:3072].rearrange("(k p) c -> p k c", p=128), KD, 512, slot=0)
        for ct in range(4):
            for hf in range(2):
                bk = proj_feat(wu, wus, ct * 128, hf)
                ACTop(uT[:, ct, hf * 512:(hf + 1) * 512], bk[:], AF.Copy, [bk.s], [uT.s])
        for g0 in range(0, NG, 4):
            bk = psum()
            for gi in range(4):
                g = g0 + gi
                ct, g8 = g // 8, g % 8
                for s_ in range(8):
                    MM(bk[:, gi * 128:(gi + 1) * 128], asel[:, g8, 112 - 16 * s_:240 - 16 * s_],
                       uT[:, ct, s_:T:8], (gi == 0 and s_ == 0), (s_ == 7), [asel.s, uT.s], [bk.s])
            ACTop(UT[:, g0:g0 + 4, :], bk[:].rearrange("p (g n) -> p g n", n=128), AF.Copy, [bk.s], [UT.s])
        if after_u is not None:
            after_u()

        for d in range(2):
            for ri, src in enumerate((s5_b_re, s5_b_im)):
                SDMA(bb[:, d, ri], src[l, d].rearrange("(gp g2) p c -> (g2 p) gp c", g2=2), [], [bb.s])
            SDMA(hin[:, d], bass.AP(st_s5.tensor, st_s5[l, d].offset, [[2, 128], [256, 16], [1, 2]]), [], [hin.s])
        for s_ in range(8):
            SDMA(dsk[16 * s_:16 * s_ + 16, :], s5_d[l].rearrange("g c -> c g"), [], [dsk.s])
        for d in range(2):
            for ri, src in enumerate((s5_c_re, s5_c_im)):
                x0 = (d * 2 + ri) * 4
                SDMA(cnat[:, x0:x0 + 4, :], src[l, d].rearrange("(ct g8) c p -> (g8 c) ct p", g8=8), [], [cnat.s])
        for d in range(2):
            for ri in range(2):
                bk = psum()
                for ct in range(4):
                    x = (d * 2 + ri) * 4 + ct
                    MM(bk[0:64, ct * 64:(ct + 1) * 64], cnat[:, x, :], R_even, True, True, [cnat.s, c3.s], [bk.s], tp=(0, 0))
                    MM(bk[64:128, ct * 64:(ct + 1) * 64], cnat[:, x, :], R_odd, True, True, [cnat.s, c3.s], [bk.s], tp=(0, 64))
                ACTop(craw[:, d, ri].rearrange("p a b -> p (a b)"), bk[:, 0:256], AF.Copy, [bk.s], [craw.s])

        if _stop(1):
            return
        for d in range(2):
            lr, li, dt_, a_, th_ = (par[:, d, i, :] for i in range(5))
            TSop(lr, lr, -1e-4, None, ALU.min, None, [par.s], [par.s])
            ACTop(dt_, dt_, AF.Exp, [par.s], [par.s])
            TTop(a_, lr, dt_, ALU.mult, [par.s], [par.s])
            TTop(th_, li, dt_, ALU.mult, [par.s], [par.s])
            TSop(th_, th_, 1.0 / TWO_PI, None, ALU.mult, None, [par.s], [par.s])

        def powers(out_r, out_i, out_m, a_ap, th_ap, evals, ng_, ne, tr, ti_, tk_i, tk_f, rs, ws):
            sh = [128, ng_, ne]
            ev_b = evals.unsqueeze(1).broadcast_to(sh)
            TTop(tr, th_ap.unsqueeze(2).broadcast_to(sh), ev_b, ALU.mult, rs + ws, ws)
            TSop(ti_, tr, 0.25, None, ALU.add, None, ws, ws)
            for (dst, src) in ((out_i, tr), (out_r, ti_)):
                CPY(tk_i, src, ws, ws)
                CPY(tk_f, tk_i, ws, ws)
                TTop(tk_f, src, tk_f, ALU.subtract, ws, ws)
                ACTop(dst, tk_f, AF.Sin, ws, ws, scale=TWO_PI)
            TTop(tr, a_ap.unsqueeze(2).broadcast_to(sh), ev_b, ALU.mult, rs + ws, ws)
            ACTop(out_m, tr, AF.Exp, ws, ws)

        for d in range(2):
            ws = [pw.s, pri.s] + [p_.s for p_ in prs]
            tkf_ = cnat[:].rearrange("p a b -> p (a b)")[:, 0:272].rearrange("p (a b) -> p a b", b=17)
            powers(pw[:, d, 0], pw[:, d, 1], prs[2][:], par[:, d, 3, :], par[:, d, 4, :], EV, 16, 17, prs[0][:], prs[1][:],
                   pri[:], tkf_, [par.s, c3.s, craw.s], ws + [cnat.s])
            TTop(pw[:, d, 0], pw[:, d, 0], prs[2][:], ALU.mult, ws, ws)
            TTop(pw[:, d, 1], pw[:, d, 1], prs[2][:], ALU.mult, ws, ws)

        for d in range(2):
            lr, li = par[:, d, 0, :], par[:, d, 1, :]
            abr, abi = pw[:, d, 0, :, 9], pw[:, d, 1, :, 9]
            nr, den, zr, zi, t1, t2 = (sm[i][:] for i in range(6))
            ws = [s_.s for s_ in sm]
            rs = [par.s, pw.s] + ws
            TSop(nr, abr, -1.0, None, ALU.add, None, rs, ws)
            TTop(t1, lr, lr, ALU.mult, rs, ws)
            TTop(t2, li, li, ALU.mult, rs, ws)
            TTop(den, t1, t2, ALU.add, rs, ws)
            P.dve(lambda e, den=den: e.reciprocal(out=den, in_=den), reads=rs, writes=ws)
            TTop(t1, nr, lr, ALU.mult, rs, ws)
            TTop(t2, abi, li, ALU.mult, rs, ws)
            TTop(zr, t1, t2, ALU.add, rs, ws)
            TTop(zr, zr, den, ALU.mult, rs, ws)
            TTop(t1, abi, lr, ALU.mult, rs, ws)
            TTop(t2, nr, li, ALU.mult, rs, ws)
            TTop(zi, t1, t2, ALU.subtract, rs, ws)
            TTop(zi, zi, den, ALU.mult, rs, ws)
            zrb = zr.unsqueeze(2).broadcast_to([128, 16, 16])
            zib = zi.unsqueeze(2).broadcast_to([128, 16, 16])
            cf = cnat[:].rearrange("p a b -> p (a b)")
            t3 = cf[:, 0:256].rearrange("p (a b) -> p a b", b=16)
            t4 = cf[:, 256:512].rearrange("p (a b) -> p a b", b=16)
            t5 = cf[:, 512:768].rearrange("p (a b) -> p a b", b=16)
            br_, bi_ = bb[:, d, 0], bb[:, d, 1]
            rs2 = rs + [bb.s, cnat.s, craw.s]
            ws2 = [bb.s, cnat.s]
            TTop(t3, br_, zrb, ALU.mult, rs2, ws2)
            TTop(t4, bi_, zib, ALU.mult, rs2, ws2)
            TTop(t5, br_, zib, ALU.mult, rs2, ws2)
            TTop(t3, t3, t4, ALU.subtract, rs2, ws2)
            TTop(t4, bi_, zrb, ALU.mult, rs2, ws2)
            TTop(bi_, t4, t5, ALU.add, rs2, ws2)
            CPY(br_, t3, rs2, ws2)

        if l == 0:
            dbg("par", par[:].rearrange("p a b c -> p (a b c)"), par.s, [128, 160])
            dbg("bb", bb[:].rearrange("p a b c d -> p (a b c d)"), bb.s, [128, 1024])
            dbg("pw", pw[:].rearrange("p a b c d -> p (a b c d)"), pw.s, [128, 1088])
            dbg("craw", craw[:].rearrange("p a b c d -> p (a b c d)"), craw.s, [128, 1024])
        def lifted(dst_r, dst_i, coef_r, coef_i, d, e_idx, conj_sign, gp0, ws):
            sh = [128, 4, 8, 16]
            pr = pw[:, d, 0, gp0:gp0 + 4, e_idx].unsqueeze(3).broadcast_to(sh)
            pi_ = pw[:, d, 1, gp0:gp0 + 4, e_idx].unsqueeze(3).broadcast_to(sh)
            cr = coef_r[:, gp0:gp0 + 4, :].unsqueeze(2).broadcast_to(sh)
            ci = coef_i[:, gp0:gp0 + 4, :].unsqueeze(2).broadcast_to(sh)
            t1 = LA[:].rearrange("p a (j c) -> p a j c", c=16)
            t2 = LB[:].rearrange("p a (j c) -> p a j c", c=16)
            rs = [pw.s, bb.s, craw.s, LA.s, LB.s]
            TTop(t1, cr, pr, ALU.mult, rs, [LA.s])
            TTop(t2, ci, pi_, ALU.mult, rs, [LB.s])
            TTop(dst_r.rearrange("p a (j c) -> p a j c", c=16), t1, t2, ALU.subtract, rs, ws)
            TTop(t1, cr, pi_, ALU.mult, rs, [LA.s])
            TTop(t2, ci, pr, ALU.mult, rs, [LB.s])
            if conj_sign > 0:
                TTop(dst_i.rearrange("p a (j c) -> p a j c", c=16), t1, t2, ALU.add, rs, ws)
            else:
                STTop(dst_i.rearrange("p a (j c) -> p a j c", c=16), t1, -1.0, t2, ALU.mult, ALU.subtract, rs, ws)

        E_B = [slice(15, 7, -1), slice(8, 16)]
        E_C = [slice(9, 17), slice(16, 8, -1)]
        E_N = [slice(7, None, -1), slice(0, 8)]

        if _stop(4):
            return
        old_r2 = [uT.s, cnat.s, pri.s] + [p_.s for p_ in prs]
        ar["off"] = R2
        XR = aalloc([4, NCK], F32, "XR")
        XI = aalloc([4, NCK], F32, "XI")
        A1 = aalloc([4, NCK], F32, "A1")
        B2 = aalloc([4, NCK], F32, "B2")
        C2 = aalloc([4, NCK], F32, "C2")
        RC = aalloc([4, NCK], F32, "RC")
        LA = aalloc([4, 128], F32, "LA2")
        LB = aalloc([4, 128], F32, "LB2")
        mnat = [aalloc([4, 128], BF16, "mnat2_%d" % i) for i in range(2)]
        XRb = aalloc([4, NCK], F32, "XRb")
        XIb = aalloc([4, NCK], F32, "XIb")
        new_r2 = [XR.s, XI.s, A1.s, B2.s, C2.s, RC.s, LA.s, LB.s, mnat[0].s, mnat[1].s, XRb.s, XIb.s]
        tsc = [A1, B2, C2]
        tsi = RC
        P.dve(lambda e: e.memset(fence_t[:], 0.0), reads=[], writes=old_r2 + new_r2 + [fence_t.s])

        def tables(d, tsc, tsi):
            ws = [tab.s, tsi.s] + [t_.s for t_ in tsc]
            for q in range(4):
                g_ = slice(q * 4, q * 4 + 4)
                powers(tab[:, 0, g_, :], tab[:, 1, g_, :], tab[:, 2, g_, :], par[:, d, 3, g_], par[:, d, 4, g_], K8, 4, 65,
                       tsc[0][:, :, 0:65], tsc[1][:, :, 0:65], tsi[:, :, 0:65].bitcast(I32), tsc[2][:, :, 0:65], [par.s, c3.s], ws)

        def seg_views(buf, gsl, n0, n1, d, shift):
            if d == 0:
                if shift == 0:
                    return buf[:, gsl, n0:n1]
                return buf[:, gsl, n0 + 1:n1] if shift > 0 else buf[:, gsl, n0:n1 - 1]
            lo = None if n0 == 0 else n0 - 1
            if shift == 0:
                return buf[:, gsl, n1 - 1:lo:-1]
            if shift > 0:
                return buf[:, gsl, n1 - 2:lo:-1]
            return buf[:, gsl, n1 - 1:n0:-1]

        for d in range(2):
            tables(d, tsc, tsi)
            if l == 0 and d == 0:
                dbg("tab", tab[:].rearrange("p a b c -> p (a b c)"), tab.s, [128, 3 * 16 * 65])
            if _stop(4.2):
                return
            def front(q, XR, XI):
                gp0 = q * 4
                tsl = slice(gp0, gp0 + 4)
                lifted(mnat[0][:], mnat[1][:], bb[:, d, 0], bb[:, d, 1], d, E_B[d], +1, gp0, [mnat[0].s, mnat[1].s])
                for ri in range(2):
                    bk = psum()
                    for gl in range(4):
                        MM(bk[:, gl * 128:(gl + 1) * 128], mnat[ri][:, gl, :], ident_b[:], True, True,
                           [mnat[ri].s, ident_b.s], [bk.s])
                    ACTop(Bt[:, 2 * gp0:2 * gp0 + 8, ri, :], bk[:].rearrange("p (g q) -> p g q", q=64), AF.Copy, [bk.s], [Bt.s])
                for gl in range(4):
                    gp = gp0 + gl
                    bk = psum()
                    for g2 in range(2):
                        g = 2 * gp + g2
                        for ri in range(2):
                            MM(bk[64 * g2:64 * g2 + 64, ri * 128:(ri + 1) * 128], Bt[:, g, ri, :], UT[:, g, :], True, True,
                               [Bt.s, UT.s], [bk.s], tp=(0, 64 * g2))
                    ACTop(XR[:, gl, :], bk[:, 0:128], AF.Copy, [bk.s], [XR.s])
                    ACTop(XI[:, gl, :], bk[:, 128:256], AF.Copy, [bk.s], [XI.s])

            def back(q, XR, XI):
                gp0 = q * 4
                tsl = slice(gp0, gp0 + 4)
                CPY(RC[:], tab[:, 2, tsl, 1:2].broadcast_to([128, 4, NCK]), [tab.s], [RC.s])
                for (n0, n1) in SEG8:
                    first = n0 if d == 0 else n1 - 1
                    MSET(RC[:, :, first:first + 1], 0.0, [RC.s], [RC.s])
                gsl = slice(0, 4)
                for (n0, n1) in SEG8:
                    L = n1 - n0
                    xr, xi = seg_views(XR, gsl, n0, n1, d, 0), seg_views(XI, gsl, n0, n1, d, 0)
                    a1, b2, c2 = seg_views(A1, gsl, n0, n1, d, 0), seg_views(B2, gsl, n0, n1, d, 0), seg_views(C2, gsl, n0, n1, d, 0)
                    cs_, sn_ = tab[:, 0, tsl, 1:L + 1], tab[:, 1, tsl, 1:L + 1]
                    rs = [XR.s, XI.s, tab.s, A1.s, B2.s, C2.s]
                    TTop(a1, xr, cs_, ALU.mult, rs, [A1.s])
                    TTop(c2, xi, sn_, ALU.mult, rs, [C2.s])
                    TTop(a1, a1, c2, ALU.add, rs, [A1.s])
                    TTop(b2, xi, cs_, ALU.mult, rs, [B2.s])
                    TTop(c2, xr, sn_, ALU.mult, rs, [C2.s])
                    TTop(b2, b2, c2, ALU.subtract, rs, [B2.s])


                def fl(t_):
                    v = t_[:].rearrange("p a b -> p (a b)")
                    return v if d == 0 else v[:, ::-1]
                o_r, o_i, i_r, i_i, cf_ = fl(XR), fl(XI), fl(A1), fl(B2), fl(RC)
                P.dve(lambda e, o_r=o_r, i_r=i_r, cf_=cf_: e.tensor_tensor_scan(out=o_r, data0=cf_, data1=i_r, initial=0.0,
                                                                                  op0=ALU.mult, op1=ALU.add),
                      reads=[RC.s, A1.s], writes=[XR.s])
                P.dve(lambda e, o_i=o_i, i_i=i_i, cf_=cf_: e.tensor_tensor_scan(out=o_i, data0=cf_, data1=i_i, initial=0.0,
                                                                                  op0=ALU.mult, op1=ALU.add),
                      reads=[RC.s, B2.s], writes=[XI.s])
                for si_, (n0, n1) in enumerate(SEG8):
                    L = n1 - n0
                    gr, gi_ = seg_views(XR, gsl, n0, n1, d, -1), seg_views(XI, gsl, n0, n1, d, -1)
                    a1, b2 = seg_views(A1, gsl, n0, n1, d, 1), seg_views(B2, gsl, n0, n1, d, 1)
                    hr = seg_views(Hb[:, d, 0], tsl, n0, n1, d, 1)
                    hi = seg_views(Hb[:, d, 1], tsl, n0, n1, d, 1)
                    cs_, sn_ = tab[:, 0, tsl, 1:L], tab[:, 1, tsl, 1:L]
                    rs = [XR.s, XI.s, tab.s, A1.s, B2.s]
                    TTop(a1, gr, cs_, ALU.mult, rs, [A1.s])
                    TTop(b2, gi_, sn_, ALU.mult, rs, [B2.s])
                    TTop(hr, a1, b2, ALU.subtract, rs, [Hb.s])
                    TTop(a1, gr, sn_, ALU.mult, rs, [A1.s])
                    TTop(b2, gi_, cs_, ALU.mult, rs, [B2.s])
                    TTop(hi, a1, b2, ALU.add, rs, [Hb.s])
                    first = n0 if d == 0 else n1 - 1
                    MSET(Hb[:, d, :, tsl, first:first + 1], 0.0, [Hb.s], [Hb.s])
                    last = n1 - 1 if d == 0 else n0
                    glr, gli = XR[:, :, last], XI[:, :, last]
                    cL, sL = tab[:, 0, tsl, L], tab[:, 1, tsl, L]
                    t1, t2 = sm[6][:, 0:4], sm[7][:, 0:4]
                    if si_ < 2:
                        dr, di = fst[:, si_, d, tsl, 0], fst[:, si_, d, tsl, 1]
                        dsl = fst.s
                    else:
                        dr, di = sloc[:, d, 0, tsl], sloc[:, d, 1, tsl]
                        dsl = sloc.s
                    rs = [XR.s, XI.s, tab.s, sm[6].s, sm[7].s, dsl]
                    TTop(t1, glr, cL, ALU.mult, rs, [sm[6].s])
                    TTop(t2, gli, sL, ALU.mult, rs, [sm[7].s])
                    TTop(dr, t1, t2, ALU.subtract, rs, [dsl])
                    TTop(t1, glr, sL, ALU.mult, rs, [sm[6].s])
                    TTop(t2, gli, cL, ALU.mult, rs, [sm[7].s])
                    TTop(di, t1, t2, ALU.add, rs, [dsl])

            xbufs = [(XR, XI), (XRb, XIb)]
            front(0, *xbufs[0])
            for q in range(4):
                if q < 3:
                    front(q + 1, *xbufs[(q + 1) % 2])
                back(q, *xbufs[q % 2])
        if _stop(4.8):
            return
        for seq in range(2):
            for d in range(2):
                SDMA(bass.AP(ns_s5.tensor, ns_s5[seq, l, d].offset, [[2, 128], [256, 16], [1, 2]]), fst[:, seq, d], [fst.s], [])

        if _stop(5):
            return
        ccs_in, ccs_out = Slot("cc5in"), Slot("cc5out")
        SDMA(cc5_in[l], sloc[:].rearrange("p a b c -> p (a b c)"), [sloc.s], [ccs_in])
        P.dma("pool", lambda e: e.collective_compute("AllGather", ALU.bypass, replica_groups=[[0, 1, 2, 3], [4, 5, 6, 7]],
                                                     ins=[cc5_in[l]], outs=[cc5_out[l]]),
              reads=[ccs_in], writes=[ccs_out], inc=1)
        SDMA(gat5[:].rearrange("p r a b c -> p r (a b c)"), cc5_out[l].rearrange("(r p) c -> p r c", p=128), [ccs_out], [gat5.s])

        old_r2 = new_r2
        ar["off"] = R2
        LA = aalloc([4, 128], F32, "LA")
        LB = aalloc([4, 128], F32, "LB")
        mnat = [aalloc([4, 128], BF16, "mnat%d" % i) for i in range(2)]
        Dacc = aalloc([8, 128], F32, "Dacc")
        new_r2 = [LA.s, LB.s, mnat[0].s, mnat[1].s, Dacc.s]
        P.dve(lambda e: e.memset(fence_t[:], 0.0), reads=[], writes=old_r2 + new_r2 + [fence_t.s])

        for d in range(2):
            for q in range(4):
                gp0 = q * 4
                cv, g8_, cs_ = Ct_(gp0)
                lifted(cv[:, g8_:g8_ + 4, d, 0, :], cv[:, g8_:g8_ + 4, d, 1, :], craw[:, d, 0], craw[:, d, 1], d, E_C[d], -1, gp0, [cs_])
        for q in range(4):
            gp0 = q * 4
            for d in range(2):
                lifted(mnat[0][:], mnat[1][:], bb[:, d, 0], bb[:, d, 1], d, E_N[d], +1, gp0, [mnat[0].s, mnat[1].s])
                for gi in range(8):
                    gl, g2 = gi // 2, gi % 2
                    gp = gp0 + gl
                    cv, g8_, cs_ = Ct_(gp)
                    bk = psum()
                    for ri in range(2):
                        MM(bk[:, 0:128], mnat[ri][64 * g2:64 * g2 + 64, gl, :], cv[64 * g2:64 * g2 + 64, g8_, d, ri, :],
                           (ri == 0), (ri == 1), [mnat[ri].s, cs_], [bk.s])
                    msk = M5[:, d * 128:(d + 1) * 128]
                    if d == 0:
                        TTop(Dacc[:, gi, :], bk[:, 0:128], msk, ALU.mult, [bk.s, c3.s], [Dacc.s])
                    else:
                        tmpv = LA[:, gl, :] if g2 == 0 else LB[:, gl, :]
                        tmps = LA.s if g2 == 0 else LB.s
                        TTop(tmpv, bk[:, 0:128], msk, ALU.mult, [bk.s, c3.s, mnat[0].s, mnat[1].s], [tmps])
                        TTop(Dacc[:, gi, :], Dacc[:, gi, :], tmpv, ALU.add, [tmps, Dacc.s], [Dacc.s])
                        g = 2 * gp + g2
                        STTop(DtV[:, g, :], ident_f, dsk[:, g:g + 1], Dacc[:, gi, :], ALU.mult, ALU.add,
                              [cst_f.s, dsk.s, Dacc.s], [DtS.s])

        old_w0 = [Bt.s, bb.s, craw.s, pw.s] + new_r2 + [XR.s, XI.s, A1.s, B2.s, C2.s, RC.s]
        ar["off"] = W0
        DH = aalloc([16, 2, 2, 64], BF16, "DH")
        W1 = ar["off"]
        tsc2 = [aalloc([4, 128], F32, "tscb%d" % i) for i in range(3)]
        tsi2 = aalloc([4, 128], F32, "tsib")
        TRt = aalloc([16, 64], F32, "TRt")
        TIt = aalloc([16, 64], F32, "TIt")
        U1 = aalloc([16, 64], F32, "U1")
        U2 = aalloc([16, 64], F32, "U2")
        pc = [aalloc([16], F32, "pc%d" % i) for i in range(6)]
        new_w0 = [DH.s, tsi2.s, TRt.s, TIt.s, U1.s, U2.s] + [t_.s for t_ in tsc2] + [p_.s for p_ in pc]
        P.dve(lambda e: e.memset(fence_t[:], 0.0), reads=[], writes=old_w0 + new_w0 + [fence_t.s])

        def cmul(dr, di, ar_, ai_, br_, bi_, t1, t2, rs, ws):
            TTop(t1, ar_, br_, ALU.mult, rs, ws)
            TTop(t2, ai_, bi_, ALU.mult, rs, ws)
            TTop(dr, t1, t2, ALU.subtract, rs, ws)
            TTop(t1, ar_, bi_, ALU.mult, rs, ws)
            TTop(t2, ai_, br_, ALU.mult, rs, ws)
            TTop(di, t1, t2, ALU.add, rs, ws)

        for d in range(2):
            tables(d, tsc2, tsi2)
            atr, ati, cr_, ci_, t1, t2 = (p_[:] for p_ in pc)
            ws = [p_.s for p_ in pc] + [sm[0].s, sm[1].s]
            rs = [tab.s, gat5.s, hin.s, hent.s, cst_f.s] + ws
            TTop(atr, tab[:, 0, :, 64], tab[:, 2, :, 64], ALU.mult, rs, ws)
            TTop(ati, tab[:, 1, :, 64], tab[:, 2, :, 64], ALU.mult, rs, ws)
            ranks = [0, 1, 2, 3] if d == 0 else [3, 2, 1, 0]
            CPY(cr_, hin[:, d, :, 0], rs, ws)
            CPY(ci_, hin[:, d, :, 1], rs, ws)
            TSop(hent[:, d, 0], cr_, cst_f[:, 384 + ranks[0]:385 + ranks[0]], None, ALU.mult, None, rs, [hent.s])
            TSop(hent[:, d, 1], ci_, cst_f[:, 384 + ranks[0]:385 + ranks[0]], None, ALU.mult, None, rs, [hent.s])
            for i in range(3):
                r = ranks[i]
                nr_, ni_ = sm[0][:], sm[1][:]
                cmul(nr_, ni_, atr, ati, cr_, ci_, t1, t2, rs, ws)
                TTop(cr_, nr_, gat5[:, r, d, 0, :], ALU.add, rs, ws)
                TTop(ci_, ni_, gat5[:, r, d, 1, :], ALU.add, rs, ws)
                rn = ranks[i + 1]
                STTop(hent[:, d, 0], cr_, cst_f[:, 384 + rn:385 + rn], hent[:, d, 0], ALU.mult, ALU.add, rs, [hent.s])
                STTop(hent[:, d, 1], ci_, cst_f[:, 384 + rn:385 + rn], hent[:, d, 1], ALU.mult, ALU.add, rs, [hent.s])
            rs = [tab.s, hent.s, TRt.s, TIt.s, U1.s, U2.s]
            TTop(TRt[:], tab[:, 0, :, 0:64], tab[:, 2, :, 0:64], ALU.mult, rs, [TRt.s])
            TTop(TIt[:], tab[:, 1, :, 0:64], tab[:, 2, :, 0:64], ALU.mult, rs, [TIt.s])
            her = hent[:, d, 0].unsqueeze(2).broadcast_to([128, 16, 64])
            hei = hent[:, d, 1].unsqueeze(2).broadcast_to([128, 16, 64])
            dhr = DH[:, :, d, 0, :] if d == 0 else DH[:, :, d, 0, ::-1]
            dhi = DH[:, :, d, 1, :] if d == 0 else DH[:, :, d, 1, ::-1]
            TTop(U1[:], TRt[:], her, ALU.mult, rs, [U1.s])
            TTop(U2[:], TIt[:], hei, ALU.mult, rs, [U2.s])
            TTop(dhr, U1[:], U2[:], ALU.subtract, rs, [DH.s])
            TTop(U1[:], TRt[:], hei, ALU.mult, rs, [U1.s])
            TTop(U2[:], TIt[:], her, ALU.mult, rs, [U2.s])
            TTop(dhi, U1[:], U2[:], ALU.add, rs, [DH.s])

        if _stop(6):
            return
        old_w1 = new_w0[1:]
        ar["off"] = W1
        YA = aalloc([NG, NCK], BF16, "YA")
        yT = aalloc([4, T], BF16, "yT5")
        gl_t = [aalloc([512], F32, "gl%d" % i) for i in range(4)]
        new_w1 = [YA.s, yT.s] + [g_.s for g_ in gl_t]
        P.dve(lambda e: e.memset(fence_t[:], 0.0), reads=[], writes=old_w1 + new_w1 + [fence_t.s])

        for g0 in range(0, NG, 4):
            bk = psum()
            for gi in range(4):
                g = g0 + gi
                gp, g2 = g // 2, g % 2
                cv, g8_, cs_ = Ct_(gp)
                cols = slice(gi * 128, (gi + 1) * 128)
                MM(bk[:, cols], DtV[:, g, :], UT[:, g, :], (gi == 0), False, [DtS.s, UT.s], [bk.s])
                for d in range(2):
                    for ri in range(2):
                        MM(bk[:, cols], cv[64 * g2:64 * g2 + 64, g8_, d, ri, :], Hb[64 * g2:64 * g2 + 64, d, ri, gp, :], False, False,
                           [cs_, Hb.s], [bk.s])
                for d in range(2):
                    for ri in range(2):
                        MM(bk[:, gi * 128 + 64:(gi + 1) * 128], cv[64 * g2:64 * g2 + 64, g8_, d, ri, :],
                           DH[64 * g2:64 * g2 + 64, gp, d, ri, :], False, (d == 1 and ri == 1), [cs_, DH.s], [bk.s])
            xs_, sq_, u_, sg_ = gl_t
            ACTop(xs_[:], bk[:], AF.Copy, [bk.s], [xs_.s])
            ACTop(sq_[:], bk[:], AF.Square, [bk.s], [sq_.s])
            TSop(sq_[:], sq_[:], 0.044715, 1.0, ALU.mult, ALU.add, [sq_.s], [sq_.s])
            TTop(u_[:], sq_[:], xs_[:], ALU.mult, [sq_.s, xs_.s], [u_.s])
            ACTop(sg_[:], u_[:], AF.Sigmoid, [u_.s], [sg_.s], scale=2.0 * math.sqrt(2.0 / math.pi))
            TTop(YA[:, g0:g0 + 4, :].rearrange("p g n -> p (g n)"), xs_[:], sg_[:], ALU.mult, [xs_.s, sg_.s], [YA.s])

        for ct in range(4):
            for t0 in range(0, 8, 4):
                bk = psum()
                for ti in range(4):
                    t_ = t0 + ti
                    for g8 in range(8):
                        g = ct * 8 + g8
                        MM(bk[:, ti * 128:(ti + 1) * 128], asel[:, t_, 112 - 16 * g8:240 - 16 * g8],
                           YA[:, g, :], (ti == 0 and g8 == 0), (g8 == 7), [asel.s, YA.s], [bk.s])
                ACTop(yT[:, ct, :].rearrange("p (n t) -> p t n", t=8)[:, t0:t0 + 4, :],
                      bk[:].rearrange("p (t n) -> p t n", n=128), AF.Copy, [bk.s], [yT.s])

        wgl, wgls = wload(s5_w_glu[l].rearrange("(k p) c -> p k c", p=128), 4, 512, slot=0)
        for c2 in range(4):
            for hf in range(2):
                bk = psum()
                for ct in range(4):
                    MM(bk[:], wgl[:, ct, c2 * 128:(c2 + 1) * 128], yT[:, ct, hf * 512:(hf + 1) * 512], (ct == 0), (ct == 3),
                       [wgls, yT.s], [bk.s])
                sgl = gl_t[(c2 * 2 + hf) % 2]
                ACTop(sgl[:], bk[:], AF.Sigmoid, [bk.s], [sgl.s])
                TTop(mixT[:, 4 + c2, hf * 512:(hf + 1) * 512], yT[:, c2, hf * 512:(hf + 1) * 512], sgl[:], ALU.mult,
                     [yT.s, sgl.s], [mixs[4 + c2][hf]])


    for l in range(DEPTH):
        compute_mod(l, 0)

        def mod_rest(l=l):
            compute_mod(l, 1, slots=[1, 2, 3, 1])
            compute_mod(l, 2, slots=[2, 3, 1, 2])
        norm_mod(l, 0, 0)
        if STAGE["s5"]:
            s5(l, after_u=mod_rest)
        else:
            mod_rest()
            for k in range(4, 8):
                for h in range(2):
                    P.dve(lambda e, k=k, h=h: e.memset(mixT[:, k, h * 512:(h + 1) * 512], 0.0), writes=[mixs[k][h]])
        if STAGE["hg"]:
            hgrn(l)
        else:
            for k in range(0, 4):
                for h in range(2):
                    P.dve(lambda e, k=k, h=h: e.memset(mixT[:, k, h * 512:(h + 1) * 512], 0.0), writes=[mixs[k][h]])
        residual_proj(w_out[l], l, 16, mixT, mixs, KD)
        norm_mod(l, 1, 24)
        ffn(l)

    nfin32 = sb([128, KD], F32, "nfin32")
    P.dve(lambda e: e.tensor_scalar(out=nfin32[:], in0=nfin[:], scalar1=32.0, scalar2=None, op0=ALU.mult),
          reads=[nfin.s], writes=[nfin32.s])
    aphase()
    ytok = [aalloc([D], F32, "ytok%d" % j) for j in range(4)]
    yT = aalloc([512], F32, "yT")
    afence()
    for h in range(2):
        rms_stats(h)
        for k in range(KD):
            P.dve(lambda e, k=k, h=h: e.scalar_tensor_tensor(
                out=yT[:], in0=xT[:, k, h * 512:(h + 1) * 512], scalar=nfin32[:, k:k + 1], in1=rstd[h][:],
                op0=ALU.mult, op1=ALU.mult), reads=[xs[k][h], nfin32.s, rstd[h].s], writes=[yT.s])
            bk = psum()
            for j in range(4):
                P.pe(lambda e, bk=bk, j=j: e.transpose(bk[:, j * 128:(j + 1) * 128], yT[:, j * 128:(j + 1) * 128], ident_f),
                     reads=[yT.s, cst_f.s], writes=[bk.s])
            for j in range(4):
                P.act(lambda e, bk=bk, j=j, k=k: e.activation(out=ytok[j][:, k * 128:(k + 1) * 128],
                                                               in_=bk[:, j * 128:(j + 1) * 128], func=AF.Copy),
                      reads=[bk.s], writes=[ytok[j].s])
        for j in range(4):
            tt = h * 4 + j
            P.dma("sp", lambda e, j=j, tt=tt: e.dma_start(out=y_out[tt * 128:(tt + 1) * 128, :], in_=ytok[j][:]),
                  reads=[ytok[j].s], writes=[])

    with nc.allow_non_contiguous_dma(reason="small strided parameter loads"):
        P.emit(st)
    st.close()
    return nc


def _consts(core):
    q = core % 4
    c = np.zeros((128, 512), np.float32)
    c[:, 0:128] = np.eye(128, dtype=np.float32)
    j = np.arange(128)[:, None]
    i = np.arange(128)[None, :]
    same = (j // CH) == (i // CH)
    c[:, 128:256] = (same & (j <= i)).astype(np.float32)
    c[:, 256:384] = (same & (j >= i)).astype(np.float32)
    c[:, 384 + q] = 1.0
    nf = D // 4
    p = np.arange(128, dtype=np.float32)
    for par in range(2):
        kf = par * 128 + p
        c[:, 388 + par] = (1.0 / (np.float32(10000.0) ** (kf / np.float32(nf)))).astype(np.float32)
    t = q * 512 + np.arange(512)
    c2 = np.zeros((128, 1024), np.float32)
    c2[:, 0:512] = (t // 64).astype(np.float32)[None, :]
    c2[:, 512:1024] = (t % 64).astype(np.float32)[None, :]
    c3 = np.zeros((128, 1024), np.float32)
    for p_ in range(128):
        c3[p_, 112 + p_ % 16] = 1.0
        c3[p_, 240 + p_ // 16] = 1.0
    for g4 in range(4):
        for cc in range(16):
            c3[(2 * g4) * 16 + cc, 480 + g4 * 16 + cc] = 1.0
            c3[(2 * g4 + 1) * 16 + cc, 544 + g4 * 16 + cc] = 1.0
    c3[:, 608:625] = np.arange(-8, 9, dtype=np.float32)[None, :]
    c3[:, 640:705] = (8.0 * np.arange(65, dtype=np.float32))[None, :]
    sI = (np.arange(128) // 16)[:, None]
    tI = (np.arange(128) // 16)[None, :]
    c3[:, 768:896] = (sI <= tI).astype(np.float32)
    c3[:, 896:1024] = (sI >= tI).astype(np.float32)
    return c, c2, c3


_NC_CACHE = {}


def kernel(**inp):
    inp = {k: np.asarray(v) for k, v in inp.items()}
    if "nc" not in _NC_CACHE:
        _NC_CACHE["nc"] = build_program()
    nc = _NC_CACHE["nc"]
    xp = inp["x_prompt"]
    xsm = inp["x_sample"]
    in_maps = []
    shared = {k: np.ascontiguousarray(inp[k], dtype=np.float32) for k in
              ("w_mod", "b_mod", "norm_mix", "norm_ffn", "norm_final", "w_in", "w_out", "w_gate", "w_up", "w_down",
               "hg_lb_logits", "hg_norm", "s5_lam_re", "s5_lam_im", "s5_log_dt", "s5_b_re", "s5_b_im", "s5_c_re", "s5_c_im",
               "s5_d", "s5_w_glu")}
    for core in range(8):
        b, q = core // 4, core % 4
        x = np.concatenate([xp[2 * core], xp[2 * core + 1], xsm[b, q * 512:(q + 1) * 512]], axis=0)
        cond = np.stack([inp["c_ctx"], inp["c"][b]], axis=0)
        c1, c2, c3 = _consts(core)
        m = dict(shared)
        m.update({"x": np.ascontiguousarray(x, dtype=np.float32), "cond": np.ascontiguousarray(cond, dtype=np.float32),
                  "st_hg": np.ascontiguousarray(inp["state_hgrn"][b], dtype=np.float32), "cst": c1, "cst2": c2, "cst3": c3,
                  "st_s5": np.ascontiguousarray(inp["state_s5"][b], dtype=np.float32)})
        in_maps.append(m)
    res = run_bass_kernel_spmd(nc, in_maps, core_ids=list(range(8)))
    outs = res.results
    _DBG["outs"] = outs
    y_prompt = np.zeros_like(xp)
    y_sample = np.zeros_like(xsm)
    ns_hg = np.zeros((16, DEPTH, 2, NH, DK, DK), np.float32)
    ns_s5 = np.zeros((16, DEPTH, 2, NG, SP, 2), np.float32)
    for core in range(8):
        b, q = core // 4, core % 4
        y = outs[core]["y"]
        y_prompt[2 * core] = y[0:256]
        y_prompt[2 * core + 1] = y[256:512]
        y_sample[b, q * 512:(q + 1) * 512] = y[512:1024]
        ns_hg[2 * core:2 * core + 2] = outs[core]["ns_hg"]
        ns_s5[2 * core:2 * core + 2] = outs[core]["ns_s5"]
    return (y_prompt, y_sample, ns_hg, ns_s5)
```
